# Optimizing a Trainium2 kernel written in Bass

```python
import math
import jax, jax.numpy as jnp
from jax import lax
import numpy as np


D_MODEL = 1024
BATCH = 16
SEQ = 256
DEPTH = 2
DEC_BATCH = 2
DEC_SEQ = 2048
PAST_LEN = 256

GRID_W = 64
ROPE_THETA = 10000.0
Q_BLOCK = 128
EPS = 1e-6
N_MOD = 9
D_FF = ((8 * D_MODEL // 3 + 127) // 128) * 128

W_SCONV = D_MODEL // 4
SCONV_K = 3
N_HEADS_DIFF = 4
DV_DIFF = D_MODEL // 4 // N_HEADS_DIFF
DK_DIFF = DV_DIFF // 2
W_CCM = D_MODEL // 4
CCM_K = 31
N_HEADS_GQA = 4
N_KV_GQA = 2
HD_GQA = D_MODEL // 4 // N_HEADS_GQA
G_GQA = N_HEADS_GQA // N_KV_GQA

SPLIT_SIZES = (W_SCONV, W_SCONV, W_SCONV,
               N_HEADS_DIFF * 2 * DK_DIFF, N_HEADS_DIFF * 2 * DK_DIFF, N_HEADS_DIFF * DV_DIFF,
               W_CCM, W_CCM,
               N_HEADS_GQA * HD_GQA, N_KV_GQA * HD_GQA, N_KV_GQA * HD_GQA)
MIX_IN = 3 * W_SCONV + 2 * N_HEADS_DIFF * 2 * DK_DIFF + N_HEADS_DIFF * DV_DIFF + 2 * W_CCM + N_HEADS_GQA * HD_GQA + 2 * N_KV_GQA * HD_GQA
MIX_WIDTH = W_SCONV + N_HEADS_DIFF * DV_DIFF + W_CCM + N_HEADS_GQA * HD_GQA

kernel_name = "hybrid_diffusion_parallel_groups_step"


def rms_norm(x, g):
    xf = x.astype(jnp.float32)
    y = xf * lax.rsqrt(jnp.mean(xf * xf, axis=-1, keepdims=True) + EPS)
    return (y * g.astype(jnp.float32)).astype(x.dtype)


def layer_norm(x, g, b):
    xf = x.astype(jnp.float32)
    mu = jnp.mean(xf, axis=-1, keepdims=True)
    var = jnp.mean(jnp.square(xf - mu), axis=-1, keepdims=True)
    y = (xf - mu) * lax.rsqrt(var + EPS)
    return (y * g.astype(jnp.float32) + b.astype(jnp.float32)).astype(x.dtype)


def depthwise_conv(x, w):
    k = w.shape[0]
    return lax.conv_general_dilated(
        x, w[:, None, :].astype(x.dtype), window_strides=(1,), padding=[(k // 2, k // 2)],
        dimension_numbers=("NWC", "WIO", "NWC"), feature_group_count=x.shape[-1])


def swiglu(h, wg, wu, wd):
    return (jax.nn.silu(h @ wg) * (h @ wu)) @ wd


def axial_angles(n_tokens, rot_dim):
    n_rows = n_tokens // GRID_W
    row = jnp.repeat(jnp.arange(n_rows, dtype=jnp.float32), GRID_W)
    col = jnp.tile(jnp.arange(GRID_W, dtype=jnp.float32), n_rows)
    n_freq = rot_dim // 4
    inv = ROPE_THETA ** (-jnp.arange(n_freq, dtype=jnp.float32) / n_freq)
    return row[:, None] * inv, col[:, None] * inv


def _rotate(x, ang):
    m = x.shape[-1] // 2
    shape = (ang.shape[0],) + (1,) * (x.ndim - 3) + (m,)
    cos = jnp.cos(ang).reshape(shape).astype(x.dtype)
    sin = jnp.sin(ang).reshape(shape).astype(x.dtype)
    x1, x2 = x[..., :m], x[..., m:]
    return jnp.concatenate([x1 * cos - x2 * sin, x1 * sin + x2 * cos], axis=-1)


def axial_rope(x, angles):
    ang_row, ang_col = angles
    half = x.shape[-1] // 2
    return jnp.concatenate([_rotate(x[..., :half], ang_row), _rotate(x[..., half:], ang_col)], axis=-1)


def _to_blocks(q):
    b, t = q.shape[:2]
    return q.reshape((b, t // Q_BLOCK, Q_BLOCK) + q.shape[2:]).swapaxes(0, 1)


def _from_blocks(o):
    nb, b = o.shape[:2]
    return o.swapaxes(0, 1).reshape((b, nb * Q_BLOCK) + o.shape[3:])


def diff_attention(q, k, v, lam):
    scale = q.shape[-1] ** -0.5

    def block(qb):
        s = jnp.einsum("bqhmd,bshmd->bhmqs", qb, k).astype(jnp.float32) * scale
        p = jax.nn.softmax(s, axis=-1)
        w = p[:, :, 0] - lam * p[:, :, 1]
        return jnp.einsum("bhqs,bshd->bqhd", w.astype(v.dtype), v)

    return _from_blocks(lax.map(block, _to_blocks(q)))


def gqa_attention(q, k, v):
    scale = q.shape[-1] ** -0.5

    def block(qb):
        s = jnp.einsum("bqkgd,bskd->bkgqs", qb, k).astype(jnp.float32) * scale
        p = jax.nn.softmax(s, axis=-1)
        return jnp.einsum("bkgqs,bskd->bqkgd", p.astype(v.dtype), v)

    return _from_blocks(lax.map(block, _to_blocks(q)))


def token_mix(h, l, p, angles, ctx_kv):
    bsz, t, _ = h.shape
    points = []
    acc = 0
    for s in SPLIT_SIZES[:-1]:
        acc += s
        points.append(acc)
    proj = h @ p["w_mix_in"][l]
    a_b, a_c, a_h, b_q, b_k, b_v, c_a, c_g, d_q, d_k, d_v = jnp.split(proj, points, axis=-1)

    a_out = a_b * depthwise_conv(a_c * a_h, p["sconv_w"][l])

    bq = b_q.reshape(bsz, t, N_HEADS_DIFF, 2, DK_DIFF)
    bk = b_k.reshape(bsz, t, N_HEADS_DIFF, 2, DK_DIFF)
    bv = b_v.reshape(bsz, t, N_HEADS_DIFF, DV_DIFF)
    dq = rms_norm(d_q.reshape(bsz, t, N_HEADS_GQA, HD_GQA), p["gqa_qnorm"][l])
    dk = rms_norm(d_k.reshape(bsz, t, N_KV_GQA, HD_GQA), p["gqa_knorm"][l])
    dv = d_v.reshape(bsz, t, N_KV_GQA, HD_GQA)
    own = (bk, bv, dk, dv)

    if angles is not None:
        ang_diff, ang_gqa = angles
        bq = axial_rope(bq, ang_diff)
        bk = axial_rope(bk, ang_diff)
        dq = axial_rope(dq, ang_gqa)
        dk = axial_rope(dk, ang_gqa)
    if ctx_kv is not None:
        cbk, cbv, cdk, cdv = ctx_kv
        bk = jnp.concatenate([cbk, bk], axis=1)
        bv = jnp.concatenate([cbv, bv], axis=1)
        dk = jnp.concatenate([cdk, dk], axis=1)
        dv = jnp.concatenate([cdv, dv], axis=1)

    lam_init = 0.8 - 0.6 * math.exp(-0.3 * l)
    f32 = jnp.float32
    lam = (jnp.exp(jnp.sum(p["diff_lq1"][l].astype(f32) * p["diff_lk1"][l].astype(f32)))
           - jnp.exp(jnp.sum(p["diff_lq2"][l].astype(f32) * p["diff_lk2"][l].astype(f32)))
           + lam_init)
    b_out = diff_attention(bq, bk, bv, lam)
    b_out = (rms_norm(b_out, p["diff_subln"][l]) * (1.0 - lam_init)).reshape(bsz, t, -1)

    u = c_a * jax.nn.sigmoid(c_g)
    u = depthwise_conv(u, p["ccm_dw_w"][l]) + p["ccm_dw_b"][l]
    u = jax.nn.silu(layer_norm(u, p["ccm_ln_g"][l], p["ccm_ln_b"][l]))
    c_out = u @ p["ccm_pw"][l]

    d_out = gqa_attention(dq.reshape(bsz, t, N_KV_GQA, G_GQA, HD_GQA), dk, dv).reshape(bsz, t, -1)

    out = jnp.concatenate([a_out, b_out, c_out, d_out], axis=-1) @ p["w_mix_out"][l]
    return out, own


def trunk_layer(x, cond, l, p, angles, ctx_kv):
    m = jax.nn.silu(cond) @ p["w_ada"][l] + p["b_ada"][l]
    m = m.reshape(m.shape[0], 1, N_MOD, D_MODEL)
    shift, scale, gate = m[:, :, 0::3], m[:, :, 1::3], m[:, :, 2::3]

    def pre(y, i):
        return rms_norm(y, p["norm_pre"][l, i]) * (1.0 + scale[:, :, i]) + shift[:, :, i]

    def post(y, i):
        return gate[:, :, i] * rms_norm(y, p["norm_post"][l, i])

    x = x + 0.5 * post(swiglu(pre(x, 0), p["ffn1_gate"][l], p["ffn1_up"][l], p["ffn1_down"][l]), 0)
    y, own = token_mix(pre(x, 1), l, p, angles, ctx_kv)
    x = x + post(y, 1)
    x = x + 0.5 * post(swiglu(pre(x, 2), p["ffn2_gate"][l], p["ffn2_up"][l], p["ffn2_down"][l]), 2)
    return x, own


def setup_inputs(seed: int = 0) -> dict:
    key = jax.random.key(seed)
    ks = iter(jax.random.split(key, 48))
    D = D_MODEL

    def nrm(shape, s):
        return jax.random.normal(next(ks), shape, jnp.float32) * s

    def gain(shape):
        return 1.0 + nrm(shape, 0.02)

    return {
        "x_prompt": nrm((BATCH, SEQ, D), 1.0),
        "x_sample": nrm((DEC_BATCH, DEC_SEQ, D), 1.0),
        "cache_diff_k": nrm((DEC_BATCH, DEPTH, PAST_LEN, N_HEADS_DIFF, 2, DK_DIFF), 1.0),
        "cache_diff_v": nrm((DEC_BATCH, DEPTH, PAST_LEN, N_HEADS_DIFF, DV_DIFF), 1.0),
        "cache_gqa_k": nrm((DEC_BATCH, DEPTH, PAST_LEN, N_KV_GQA, HD_GQA), 1.0),
        "cache_gqa_v": nrm((DEC_BATCH, DEPTH, PAST_LEN, N_KV_GQA, HD_GQA), 1.0),
        "c": nrm((DEC_BATCH, D), 1.0),
        "c_ctx": nrm((D,), 1.0),
        "w_ada": nrm((DEPTH, D, N_MOD * D), 0.5 * D ** -0.5),
        "b_ada": nrm((DEPTH, N_MOD * D), 0.02),
        "norm_pre": gain((DEPTH, 3, D)),
        "norm_post": gain((DEPTH, 3, D)),
        "ffn1_gate": nrm((DEPTH, D, D_FF), D ** -0.5),
        "ffn1_up": nrm((DEPTH, D, D_FF), D ** -0.5),
        "ffn1_down": nrm((DEPTH, D_FF, D), D_FF ** -0.5),
        "ffn2_gate": nrm((DEPTH, D, D_FF), D ** -0.5),
        "ffn2_up": nrm((DEPTH, D, D_FF), D ** -0.5),
        "ffn2_down": nrm((DEPTH, D_FF, D), D_FF ** -0.5),
        "w_mix_in": nrm((DEPTH, D, MIX_IN), D ** -0.5),
        "w_mix_out": nrm((DEPTH, MIX_WIDTH, D), MIX_WIDTH ** -0.5),
        "sconv_w": nrm((DEPTH, SCONV_K, W_SCONV), SCONV_K ** -0.5),
        "diff_lq1": nrm((DEPTH, DK_DIFF), 0.1),
        "diff_lk1": nrm((DEPTH, DK_DIFF), 0.1),
        "diff_lq2": nrm((DEPTH, DK_DIFF), 0.1),
        "diff_lk2": nrm((DEPTH, DK_DIFF), 0.1),
        "diff_subln": gain((DEPTH, DV_DIFF)),
        "ccm_dw_w": nrm((DEPTH, CCM_K, W_CCM), CCM_K ** -0.5),
        "ccm_dw_b": nrm((DEPTH, W_CCM), 0.02),
        "ccm_ln_g": gain((DEPTH, W_CCM)),
        "ccm_ln_b": nrm((DEPTH, W_CCM), 0.02),
        "ccm_pw": nrm((DEPTH, W_CCM, W_CCM), W_CCM ** -0.5),
        "gqa_qnorm": gain((DEPTH, HD_GQA)),
        "gqa_knorm": gain((DEPTH, HD_GQA)),
    }


def reference(x_prompt, x_sample, cache_diff_k, cache_diff_v, cache_gqa_k, cache_gqa_v, c, c_ctx,
              w_ada, b_ada, norm_pre, norm_post,
              ffn1_gate, ffn1_up, ffn1_down, ffn2_gate, ffn2_up, ffn2_down,
              w_mix_in, w_mix_out, sconv_w,
              diff_lq1, diff_lk1, diff_lq2, diff_lk2, diff_subln,
              ccm_dw_w, ccm_dw_b, ccm_ln_g, ccm_ln_b, ccm_pw,
              gqa_qnorm, gqa_knorm):
    p = dict(w_ada=w_ada, b_ada=b_ada, norm_pre=norm_pre, norm_post=norm_post,
             ffn1_gate=ffn1_gate, ffn1_up=ffn1_up, ffn1_down=ffn1_down,
             ffn2_gate=ffn2_gate, ffn2_up=ffn2_up, ffn2_down=ffn2_down,
             w_mix_in=w_mix_in, w_mix_out=w_mix_out, sconv_w=sconv_w,
             diff_lq1=diff_lq1, diff_lk1=diff_lk1, diff_lq2=diff_lq2, diff_lk2=diff_lk2,
             diff_subln=diff_subln, ccm_dw_w=ccm_dw_w, ccm_dw_b=ccm_dw_b,
             ccm_ln_g=ccm_ln_g, ccm_ln_b=ccm_ln_b, ccm_pw=ccm_pw,
             gqa_qnorm=gqa_qnorm, gqa_knorm=gqa_knorm)

    x = x_prompt
    dks, dvs, gks, gvs = [], [], [], []
    for l in range(DEPTH):
        x, own = trunk_layer(x, c_ctx[None, :], l, p, None, None)
        dks.append(own[0])
        dvs.append(own[1])
        gks.append(own[2])
        gvs.append(own[3])
    y_prompt = x
    new_diff_k = jnp.stack(dks, axis=1)
    new_diff_v = jnp.stack(dvs, axis=1)
    new_gqa_k = jnp.stack(gks, axis=1)
    new_gqa_v = jnp.stack(gvs, axis=1)

    t_lat = x_sample.shape[1]
    angles = (axial_angles(t_lat, DK_DIFF), axial_angles(t_lat, HD_GQA))
    x = x_sample
    for l in range(DEPTH):
        ctx_kv = (cache_diff_k[:, l], cache_diff_v[:, l], cache_gqa_k[:, l], cache_gqa_v[:, l])
        x, _ = trunk_layer(x, c, l, p, angles, ctx_kv)
    y_sample = x

    return (y_prompt, y_sample, new_diff_k, new_diff_v, new_gqa_k, new_gqa_v)
```

```python
import math
from contextlib import ExitStack
import numpy as np
import concourse.bass as bass
import concourse.mybir as mybir
from concourse.bass_utils import run_bass_kernel_spmd

F32 = mybir.dt.float32
BF16 = mybir.dt.bfloat16
ALU = mybir.AluOpType
AF = mybir.ActivationFunctionType

D = 1024
DFF = 2816
NFC = 22
DEPTH = 2
T = 512
EPS = 1e-6
NCORES = 8
NKS = 2304
BROWS = 784
NW = 3
NDS = 12
DEBUG = False


class Buf:
    __slots__ = ("name", "w", "r", "psum")

    def __init__(self, name, psum=False):
        self.name = name
        self.w = None
        self.r = {}
        self.psum = psum


class Op:
    __slots__ = ("eng", "emit", "deps", "needs_inc", "ticket", "kind", "dsem", "dval", "pre")

    def __init__(self, eng, emit, kind):
        self.eng = eng
        self.emit = emit
        self.deps = []
        self.needs_inc = False
        self.ticket = 0
        self.kind = kind
        self.dsem = None
        self.dval = 0
        self.pre = None


class Sched:
    ENGS = ("pe", "act", "dve", "pool", "sp")

    def __init__(self):
        self.ops = {e: [] for e in self.ENGS}
        self.ndma = {e: 0 for e in self.ENGS}
        self.ncc = 0

    def add(self, eng, emit, reads=(), writes=(), kind="c"):
        op = Op(eng, emit, kind)
        deps = {}

        def dep(o):
            if o is None or o is op:
                return
            if eng == "pe" and kind == "c" and o.eng == "pe" and o.kind == "c":
                return
            deps[id(o)] = o

        for b in reads:
            dep(b.w)
            if b.psum:
                for k_, o in b.r.items():
                    if k_ != eng:
                        dep(o)
        for b in writes:
            dep(b.w)
            for o in b.r.values():
                dep(o)
        for b in writes:
            b.w = op
            b.r = {}
        for b in reads:
            if kind == "c":
                b.r[eng] = op
            else:
                b.r[("d", id(op))] = op
        op.deps = list(deps.values())
        for o in op.deps:
            if o.kind == "c":
                o.needs_inc = True
        if kind == "d":
            n = self.ndma[eng]
            self.ndma[eng] = n + 1
            op.dsem = (eng, n % NDS)
            op.dval = 16 * (n // NDS + 1)
            if n >= NDS:
                op.pre = (op.dsem, 16 * (n // NDS))
        elif kind == "cc":
            self.ncc += 1
            op.dsem = ("cc", self.ncc - 1)
            op.dval = 1
        self.ops[eng].append(op)
        return op

    def finalize(self):
        for e in self.ENGS:
            t = 0
            for op in self.ops[e]:
                if op.kind == "c" and op.needs_inc:
                    t += 1
                    op.ticket = t

    def replay(self, e, eng, sems):
        waited = {}

        def wait(key, val):
            if waited.get(key, 0) < val:
                eng.wait_ge(sems[key], val)
                waited[key] = val

        for op in self.ops[e]:
            for o in op.deps:
                if o.kind == "c":
                    wait((o.eng, "c"), o.ticket)
                else:
                    wait(o.dsem, o.dval)
            if op.pre is not None:
                wait(op.pre[0], op.pre[1])
            ins = op.emit(eng)
            if op.kind == "d":
                ins.then_inc(sems[op.dsem], 16)
            elif op.kind == "cc":
                ins.then_inc(sems[op.dsem])
            elif op.needs_inc:
                ins.then_inc(sems[(e, "c")], 1)
        n = self.ndma[e]
        for k in range(min(n, NDS)):
            cnt = (n - 1 - k) // NDS + 1
            wait((e, k), 16 * cnt)
        if e == "pool":
            for k in range(self.ncc):
                wait(("cc", k), 1)


class _Stop(Exception):
    pass


def build_program(stop=None, substop=None):
    nc = bass.Bass("TRN2", target_bir_lowering=False)
    S = Sched()
    es = ExitStack()

    def din(name, shape, dt=F32):
        return nc.dram_tensor(name, list(shape), dt, kind="ExternalInput").ap()

    def dout(name, shape, dt=F32):
        return nc.dram_tensor(name, list(shape), dt, kind="ExternalOutput").ap()

    d_x = [din("xp", [128, 8, T]), din("xs", [128, 8, T])]
    d_cnd = din("cnd", [128, 8, 2])
    d_wada = din("wada", [DEPTH, 18, 128, 8 * 512])
    d_bada = din("bada", [DEPTH, 128, 144])
    d_gpre = din("gpre", [DEPTH, 128, 24])
    d_gpost = din("gpost", [DEPTH, 128, 24])
    d_wgu = din("wgu", [DEPTH, 2, 11, 128, 4096])
    d_wdn = din("wdn", [DEPTH, 2, 8, 128, NFC * 128])
    d_wmi = din("wmi", [DEPTH, 5, 128, 4096])
    d_wmo = din("wmo", [DEPTH, 3, 128, 4096])
    d_sconv = din("sconvw", [DEPTH, 128, 6])
    d_ccw = din("ccw", [DEPTH, 128, 62])
    d_ccv = din("ccv", [DEPTH, 128, 6])
    d_ccpw = din("ccpw", [DEPTH, 128, 512])
    d_lam = din("lamv", [DEPTH, 128, 128])
    d_subln = din("subln", [DEPTH, 128, 1])
    d_qkn = din("qkn", [DEPTH, 128, 2])
    d_kng = din("kng", [DEPTH, 128, 64])
    d_ckd = din("ckd", [DEPTH, 128, 2 * 256])
    d_ckg = din("ckg", [DEPTH, 128, 256])
    d_cvd = din("cvd", [DEPTH, 128, 2 * 256])
    d_cvg = din("cvg", [DEPTH, 128, 2 * 128])
    d_rope = din("rope", [128, 4 * T])
    d_cmat = din("cmat", [128, 5 * 128])
    d_msk = din("msk", [128, 16])
    d_y = [dout("yp", [128, 8, T]), dout("ys", [128, 8, T])]
    d_ndk = dout("ndk", [DEPTH, T, 256])
    d_ndv = dout("ndv", [DEPTH, T, 256])
    d_ngk = dout("ngk", [DEPTH, T, 128])
    d_ngv = dout("ngv", [DEPTH, T, 128])
    if DEBUG:
        d_dbg = dout("dbg", [8, 128, 8 * T])
        d_dbgb = dout("dbgb", [4, 128, 8 * T], BF16)
    d_bounce = [nc.dram_tensor("bounce%d" % l, [BROWS, 512], BF16).ap() for l in range(DEPTH)]
    d_gath = [nc.dram_tensor("gath%d" % l, [4 * BROWS, 512], BF16).ap() for l in range(DEPTH)]
    B_bounce = [Buf("bounce%d" % l) for l in range(DEPTH)]
    B_gath = [Buf("gath%d" % l) for l in range(DEPTH)]

    def sb(name, shape, dt=F32):
        return es.enter_context(nc.sbuf_tensor("s_" + name, list(shape), dt))

    xt = [sb("xP", [128, 8, T]), sb("xS", [128, 8, T])]
    Bx = [[Buf("x%d_%d" % (g, c)) for c in range(8)] for g in range(2)]
    h1 = sb("h", [128, 8, T], BF16)
    ht = [h1, h1]
    Bh1 = [Buf("h_%d" % c) for c in range(8)]
    Bh = [Bh1, Bh1]
    y1 = sb("y", [128, 8, T])
    yt = [y1, y1]
    By1 = [Buf("y_%d" % c) for c in range(8)]
    By = [By1, By1]
    wsl = [sb("ws%d" % k, [128, 4096], BF16) for k in range(NW)]
    Bws = [Buf("ws%d" % k) for k in range(NW)]
    ps = [es.enter_context(nc.psum_tensor("ps%d" % k, [128, 512], F32)) for k in range(8)]
    Bps = [Buf("ps%d" % k, psum=True) for k in range(8)]
    hmid1 = sb("hmid", [128, NFC, T], BF16)
    Bhm1 = [Buf("hm_%d" % f) for f in range(NFC)]

    NTP = 6
    tpool = [sb("tp%d" % k, [128, T]) for k in range(NTP)]
    Btp = [Buf("tp%d" % k) for k in range(NTP)]
    tpc = [0]

    rstd_t = sb("rstd_t", [128, T]); Brstd_t = Buf("rstd_t")

    def nt():
        k = tpc[0] % NTP
        tpc[0] += 1
        return tpool[k], Btp[k]

    cnd = sb("cnd", [128, 8, 2]); Bcnd = Buf("cnd")
    scnd = sb("scnd", [128, 8, 2], BF16); Bscnd = Buf("scnd")
    bada_l = [sb("bada%d" % l, [128, 144]) for l in range(DEPTH)]; Bbada_l = [Buf("bada%d" % l) for l in range(DEPTH)]
    modt_l = [sb("modt%d" % l, [128, 72, 2]) for l in range(DEPTH)]; Bmod_l = [[Buf("mod%d_%d" % (l, i)) for i in range(3)] for l in range(DEPTH)]
    gpre_l = [sb("gpre%d" % l, [128, 3, 8]) for l in range(DEPTH)]; gpost_l = [sb("gpost%d" % l, [128, 3, 8]) for l in range(DEPTH)]
    Bgp_l = [Buf("gprepost%d" % l) for l in range(DEPTH)]
    Amod_l = [sb("Amod%d" % l, [128, 2, 3, 8]) for l in range(DEPTH)]; Gmod_l = [sb("Gmod%d" % l, [128, 2, 3, 8]) for l in range(DEPTH)]
    BAG_l = [[Buf("AG%d_%d" % (l, i)) for i in range(3)] for l in range(DEPTH)]
    BGG_l = [[Buf("GG%d_%d" % (l, i)) for i in range(3)] for l in range(DEPTH)]
    Bmodg_l = [[Buf("modg%d_%d" % (l, i)) for i in range(3)] for l in range(DEPTH)]
    cur = [0]
    sconvw = sb("sconvw", [128, 2, 3]); ccw = sb("ccw", [128, 2, 31]); ccv = sb("ccv", [128, 3, 2])
    Bsmall = Buf("small")
    ccpw = sb("ccpw", [128, 2, 256], BF16); Bccpw = Buf("ccpw")
    lamv = sb("lamv", [128, 4, 32]); lamt = sb("lamt", [128, 8]); Blam = Buf("lam")
    subln = sb("subln", [128, 1]); qkn = sb("qkn", [128, 2]); kng = sb("kng", [128, 64])
    rope = sb("rope", [128, 4, T]); Brope = Buf("rope")
    cmatf = sb("cmatf", [128, 5, 128]); cmat = sb("cmat", [128, 5, 128], BF16); Bcmat = Buf("cmat")
    msk = sb("msk", [128, 16]); Bmsk = Buf("msk")
    NDG = 6
    dgr = [sb("dgr%d" % k, [128, 128], BF16) for k in range(NDG)]; Bdgr = [Buf("dgr%d" % k) for k in range(NDG)]
    dgc = [0]

    hx = sb("hx", [128, 8, T], BF16)
    Bhx = [Buf("hx_%d" % c) for c in range(8)]
    abt = [hx[:, 0:2, :], hx[:, 2:4, :]]; Bab = [Bhx[0:2], Bhx[2:4]]
    ppad = [sb("ppadP", [128, 2, 2, 258], BF16), sb("ppadS", [128, 2, 514], BF16)]; Bpp = [Buf("pp0"), Buf("pp1")]
    upad = [sb("upadP", [128, 2, 2, 286], BF16), sb("upadS", [128, 2, 542], BF16)]; Bup = [Buf("up0"), Buf("up1")]
    cat1 = hx[:, 4:8, :]
    cat = [cat1, cat1]
    Bcat1 = Bhx[4:8]
    Bcat = [Bcat1, Bcat1]
    hb1 = sb("hb", [64, 8, T], BF16)
    hb = [hb1, hb1]
    Bhb1 = [Buf("hb_%d" % k) for k in range(8)]
    Bhb = [Bhb1, Bhb1]
    qpk = [sb("qpk%d" % g, [128, 4, T], BF16) for g in range(2)]
    Bqpk = [[Buf("qpk%d_%d" % (g, k)) for k in range(4)] for g in range(2)]
    qmr = [sb("qmr%d" % k, [128, T], BF16) for k in range(2)]; Bqmr = [Buf("qmr%d" % k) for k in range(2)]
    kdP = sb("kdP", [128, 2, T], BF16); kgP = sb("kgP", [128, T], BF16)
    kdS = sb("kdS", [128, 2, NKS], BF16); kgS = sb("kgS", [128, NKS], BF16)
    Bkd = [Buf("kdP"), Buf("kdS_own")]; Bkg = [Buf("kgP"), Buf("kgS_own")]
    Bkctx = Buf("kctx"); Bkgath = Buf("kgath")
    vaP = sb("vaP", [128, 4, 6, 128], BF16)
    vaSg = sb("vaSg", [128, 18, 2, 128], BF16)
    vaSd = hmid1[:].rearrange("p f t -> p (f t)")[:, 0:18 * 512].rearrange("p (k h d) -> p k h d", h=4, d=128)
    BvaSd = Bhm1[0:18]
    BvaP = Buf("vaP"); Bvgc = Buf("vgctx"); Bvgg = Buf("vggath")
    vtok = cat1.rearrange("p a t -> p (a t)")[:, 0:1536].rearrange("p (t v) -> p t v", v=384)
    edge = sb("edge", [128, 2, 32], BF16); Bedge = Buf("edge")
    egt = sb("egt", [128, 2, 4, 32], BF16); Begt = Buf("egt")
    halo = sb("halo", [128, 2, 2, 16]); Bhalo = Buf("halo")
    pT = [sb("pT%d" % k, [128, T], BF16) for k in range(2)]; BpT = [Buf("pT%d" % k) for k in range(2)]
    sqr = pT[0:2]; Bsqr = BpT[0:2]
    pT = pT + [hmid1[:, 18, :], hmid1[:, 19, :]]; BpT = BpT + [Bhm1[18], Bhm1[19]]
    vb = sb("vb", [128, 2, T], BF16); Bvb = Buf("vb")
    qbf = sb("qbf", [128, T], BF16); Bqbf = Buf("qbf")
    ktm = sb("ktm", [128, 8]); Bktm = Buf("ktm")

    sems = {}
    for e in Sched.ENGS:
        sems[(e, "c")] = es.enter_context(nc.semaphore("sc_" + e))
    for e in ("sp", "pool", "act"):
        for k in range(NDS):
            sems[(e, k)] = es.enter_context(nc.semaphore("sd_%s%d" % (e, k)))
    for k in range(DEPTH):
        sems[("cc", k)] = es.enter_context(nc.semaphore("scc%d" % k))

    def mm(outb, out, lb, lhsT, rb, rhs, start, stop):
        S.add("pe", lambda e: e.matmul(out, lhsT, rhs, start=start, stop=stop), reads=[lb, rb], writes=[outb])

    def act(out, in_, func, reads, writes, scale=None, bias=None):
        kw = {}
        if scale is not None:
            kw["scale"] = scale
        if bias is not None:
            kw["bias"] = bias
        S.add("act", lambda e: e.activation(out, in_, func, **kw), reads=reads, writes=writes)

    def tt(out, a, b, op, reads, writes, eng="dve"):
        S.add(eng, lambda e: e.tensor_tensor(out, a, b, op), reads=reads, writes=writes)

    def ts(out, a, s1, op0, reads, writes, eng="dve"):
        S.add(eng, lambda e: e.tensor_scalar(out, a, s1, None, op0), reads=reads, writes=writes)

    def stt(out, a, s, b, op0, op1, reads, writes):
        S.add("dve", lambda e: e.scalar_tensor_tensor(out, a, s, b, op0, op1), reads=reads, writes=writes)

    def cp(out, in_, reads, writes, eng="dve"):
        S.add(eng, lambda e: e.tensor_copy(out, in_), reads=reads, writes=writes)

    def dma(q, out, in_, reads, writes):
        S.add(q, lambda e: e.dma_start(out=out, in_=in_), reads=reads, writes=writes, kind="d")

    def memset(ap, val, writes, eng="dve"):
        S.add(eng, lambda e: e.memset(ap, val), reads=[], writes=writes)

    epsb = sb("epsb", [128, 1]); Bepsb = Buf("epsb")
    memset(epsb[:], EPS, [Bepsb])

    def rsqrt_chain(out, Bout, in_, mult, reads_in):
        np_ = out.shape[0]
        act(out, in_, AF.Ln, reads_in + [Bepsb], [Bout], scale=mult, bias=epsb[0:np_, 0:1])
        act(out, out, AF.Exp, [Bout], [Bout], scale=-0.5)

    wfree_list = list(range(NW))

    def wload(src_ap, nel=4096):
        k = wfree_list.pop(0)
        dma("pool", wsl[k][:, 0:nel].rearrange("p (a n) -> p a n", a=4), src_ap[:, 0:nel].rearrange("p (a n) -> p a n", a=4), [], [Bws[k]])
        return k

    def wfree(k):
        wfree_list.append(k)

    for g in range(2):
        for c in range(8):
            dma("sp", xt[g][:, c, :], d_x[g][:, c, :], [], [Bx[g][c]])
    dma("sp", cnd[:], d_cnd[:], [], [Bcnd])
    dma("sp", rope[:].rearrange("p a t -> p (a t)"), d_rope[:], [], [Brope])
    dma("sp", cmatf[:].rearrange("p a t -> p (a t)"), d_cmat[:], [], [Bcmat])
    dma("sp", msk[:], d_msk[:], [], [Bmsk])
    cp(cmat[:], cmatf[:], [Bcmat], [Bcmat])
    ones_b = cmat[:, 0, :]
    blk64 = cmat[:, 1, :]
    ident = cmat[:, 2, :]
    permd = cmat[:, 3, :]
    permg = cmat[:, 4, :]
    ones_f = cmatf[:, 0, :]
    act(scnd[:], cnd[:], AF.Silu, [Bcnd], [Bscnd])
    memset(vaP[:, :, :, 64:128], 1.0, [BvaP], eng="pool")
    memset(vaSg[:, :, :, 64:128], 1.0, [Bvgc, Bvgg], eng="pool")
    memset(ppad[0][:], 0.0, [Bpp[0]], eng="pool")
    memset(upad[0][:], 0.0, [Bup[0]], eng="pool")

    bankrr = [0]

    def nb():
        b = bankrr[0] % 4
        bankrr[0] += 1
        return b

    def sumsq_pre(g):
        for c in range(8):
            k = c % 2
            act(sqr[k][:], xt[g][:, c, :], AF.Square, [Bx[g][c]], [Bsqr[k]])
            mm(Bps[6], ps[6][:], Bcmat, ones_b, Bsqr[k], sqr[k][:], c == 0, c == 7)

    def pre_norm_gen(g, i, l=None, hb_=None):
        l = cur[0] if l is None else l
        hdst, Bhd = (h1, Bh1) if hb_ is None else hb_
        for c in range(8):
            k = c % 2
            act(sqr[k][:], xt[g][:, c, :], AF.Square, [Bx[g][c]], [Bsqr[k]])
            if c > 0:
                mm(Bps[6], ps[6][:], Bcmat, ones_b, Bsqr[1 - k], sqr[1 - k][:], c == 1, False)
            yield
        mm(Bps[6], ps[6][:], Bcmat, ones_b, Bsqr[1], sqr[1][:], False, True)
        rstd, Brstd = rstd_t, Brstd_t
        rsqrt_chain(rstd[:], Brstd, ps[6][:], 1.0 / D, [Bps[6]])
        for c in range(8):
            t_, Bt = nt()
            tt(t_[:], xt[g][:, c, :], rstd[:], ALU.mult, [Bx[g][c], Brstd], [Bt])
            act(hdst[:, c, :], t_[:], AF.Identity, [Bt, BAG_l[l][i], Bmod_l[l][i]], [Bhd[c]],
                scale=Amod_l[l][:, g, i, c:c + 1], bias=modt_l[l][:, (3 * i) * 8 + c, g:g + 1])

    def pre_norm(g, i, l=None, hb_=None):
        for _ in pre_norm_gen(g, i, l, hb_):
            pass

    def evac_y(g, oc, bank):
        act(yt[g][:, oc, :], ps[bank][:], AF.Copy, [Bps[bank]], [By[g][oc]])
        k = oc % 2
        act(sqr[k][:], ps[bank][:], AF.Square, [Bps[bank]], [Bsqr[k]])
        pend_sq.append((oc, k))

    pend_sq = []

    def flush_sq(everything=False):
        while pend_sq and (everything or len(pend_sq) > 0):
            oc, k = pend_sq.pop(0)
            mm(Bps[6], ps[6][:], Bcmat, ones_b, Bsqr[k], sqr[k][:], oc == 0, oc == 7)

    def post_norm(g, i, l=None):
        l = cur[0] if l is None else l
        rstd, Brstd = rstd_t, Brstd_t
        rsqrt_chain(rstd[:], Brstd, ps[6][:], 1.0 / D, [Bps[6]])
        for c in range(8):
            t_, Bt = nt()
            stt(t_[:], yt[g][:, c, :], Gmod_l[l][:, g, i, c:c + 1], rstd[:], ALU.mult, ALU.mult, [By[g][c], BGG_l[l][i], Brstd], [Bt])
            tt(xt[g][:, c, :], xt[g][:, c, :], t_[:], ALU.add, [Bx[g][c], Bt], [Bx[g][c]])

    def ffn_chain(jobs, hook=None):
        hsel = lambda g: (h1, Bh1) if g == 0 else (hx, Bhx)
        l0, w0, i0, g0 = jobs[0]
        pre_norm(g0, i0, l0, hsel(g0))
        for jn, (l, which, i, g) in enumerate(jobs):
            hsrc, Bhs = hsel(g)
            cnt = 0
            png = None
            if jn + 1 < len(jobs):
                ln, wn, in_, gn = jobs[jn + 1]
                png = pre_norm_gen(gn, in_, ln, hsel(gn))
            for f in range(11):
                k = wload(d_wgu[l, which, f])
                w = wsl[k][:].rearrange("p (a c n) -> p a c n", a=2, c=8)
                for j in range(2):
                    fc = 2 * f + j
                    bg, bu = (cnt % 2) * 2, (cnt % 2) * 2 + 1
                    cnt += 1
                    for c in range(8):
                        mm(Bps[bg], ps[bg][:], Bws[k], w[:, 0, c, j * 128:(j + 1) * 128], Bhs[c], hsrc[:, c, :], c == 0, c == 7)
                    for c in range(8):
                        mm(Bps[bu], ps[bu][:], Bws[k], w[:, 1, c, j * 128:(j + 1) * 128], Bhs[c], hsrc[:, c, :], c == 0, c == 7)
                    sk = cnt % 2
                    act(qmr[sk][:], ps[bg][:], AF.Silu, [Bps[bg]], [Bqmr[sk]])
                    tt(hmid1[:, fc, :], qmr[sk][:], ps[bu][:], ALU.mult, [Bqmr[sk], Bps[bu]], [Bhm1[fc]])
                wfree(k)
                if hook is not None:
                    hook()
                if png is not None and f >= 2:
                    next(png, None)
            if png is not None:
                for _ in png:
                    pass
            for oc in range(8):
                k = wload(d_wdn[l, which, oc], NFC * 128)
                w = wsl[k][:, 0:NFC * 128].rearrange("p (f n) -> p f n", f=NFC)
                bank = 4 + (oc % 2)
                for fc in range(NFC):
                    mm(Bps[bank], ps[bank][:], Bws[k], w[:, fc, :], Bhm1[fc], hmid1[:, fc, :], fc == 0, fc == NFC - 1)
                wfree(k)
                flush_sq()
                evac_y(g, oc, bank)
            flush_sq(True)
            post_norm(g, i, l)

    def ffn(l, which, i, g, hook=None):
        pre_norm(g, i)
        cnt = 0
        for f in range(11):
            k = wload(d_wgu[l, which, f])
            w = wsl[k][:].rearrange("p (a c n) -> p a c n", a=2, c=8)
            for j in range(2):
                fc = 2 * f + j
                bg, bu = (cnt % 2) * 2, (cnt % 2) * 2 + 1
                cnt += 1
                for c in range(8):
                    mm(Bps[bg], ps[bg][:], Bws[k], w[:, 0, c, j * 128:(j + 1) * 128], Bh[g][c], ht[g][:, c, :], c == 0, c == 7)
                for c in range(8):
                    mm(Bps[bu], ps[bu][:], Bws[k], w[:, 1, c, j * 128:(j + 1) * 128], Bh[g][c], ht[g][:, c, :], c == 0, c == 7)
                sk = cnt % 2
                act(qmr[sk][:], ps[bg][:], AF.Silu, [Bps[bg]], [Bqmr[sk]])
                tt(hmid1[:, fc, :], qmr[sk][:], ps[bu][:], ALU.mult, [Bqmr[sk], Bps[bu]], [Bhm1[fc]])
            wfree(k)
            if hook is not None:
                hook()
        for oc in range(8):
            k = wload(d_wdn[l, which, oc], NFC * 128)
            w = wsl[k][:, 0:NFC * 128].rearrange("p (f n) -> p f n", f=NFC)
            bank = 4 + (oc % 2)
            for fc in range(NFC):
                mm(Bps[bank], ps[bank][:], Bws[k], w[:, fc, :], Bhm1[fc], hmid1[:, fc, :], fc == 0, fc == NFC - 1)
            wfree(k)
            evac_y(g, oc, bank)
        post_norm(g, i)

    def ada_gen(l):
        bada, modt, gpre, gpost, Amod, Gmod = bada_l[l], modt_l[l], gpre_l[l], gpost_l[l], Amod_l[l], Gmod_l[l]
        Bbada, Bmod, Bgp, BAG = Bbada_l[l], Bmod_l[l], Bgp_l[l], BAG_l[l]
        dma("sp", bada[:], d_bada[l], [], [Bbada])
        dma("sp", gpre[:].rearrange("p a c -> p (a c)"), d_gpre[l], [], [Bgp])
        dma("sp", gpost[:].rearrange("p a c -> p (a c)"), d_gpost[l], [], [Bgp])
        for pa in range(18):
            k = wload(d_wada[l, pa])
            w = wsl[k][:].rearrange("p (c n) -> p c n", c=8)
            for cc in range(4):
                ci = pa * 4 + cc
                for c in range(8):
                    mm(Bps[7], ps[7][:, 2 * ci:2 * ci + 2], Bws[k], w[:, c, cc * 128:(cc + 1) * 128], Bscnd, scnd[:, c, :], c == 0, c == 7)
            wfree(k)
            if pa % 6 == 3:
                i = pa // 6
                lo, hi = (3 * i) * 16, (3 * i + 2) * 16
                tt(modt[:].rearrange("p a k -> p (a k)")[:, lo:hi], ps[7][:, lo:hi], bada[:, lo:hi], ALU.add, [Bps[7], Bbada], [Bmod[i]])
                for g in range(2):
                    stt(Amod[:, g, i, :], modt[:, (3 * i + 1) * 8:(3 * i + 2) * 8, g], 1.0, gpre[:, i, :], ALU.add, ALU.mult,
                        [Bmod[i], Bgp], [BAG[i]])
            if pa % 6 == 5:
                i = pa // 6
                lo, hi = (3 * i + 2) * 16, (3 * i + 3) * 16
                tt(modt[:].rearrange("p a k -> p (a k)")[:, lo:hi], ps[7][:, lo:hi], bada[:, lo:hi], ALU.add, [Bps[7], Bbada], [Bmodg_l[l][i]])
                for g in range(2):
                    stt(Gmod[:, g, i, :], modt[:, (3 * i + 2) * 8:(3 * i + 3) * 8, g], 1.0 if i == 1 else 0.5, gpost[:, i, :],
                        ALU.mult, ALU.mult, [Bmodg_l[l][i], Bgp], [BGG_l[l][i]])
            if pa < 17:
                yield

    def layer_consts(l):
        dma("sp", sconvw[:].rearrange("p a c -> p (a c)"), d_sconv[l], [], [Bsmall])
        dma("sp", ccw[:].rearrange("p a c -> p (a c)"), d_ccw[l], [], [Bsmall])
        dma("sp", ccv[:].rearrange("p a c -> p (a c)"), d_ccv[l], [], [Bsmall])
        dma("sp", subln[:], d_subln[l], [], [Bsmall])
        dma("sp", qkn[:], d_qkn[l], [], [Bsmall])
        dma("sp", kng[:], d_kng[l], [], [Bsmall])
        dma("sp", lamv[:].rearrange("p a c -> p (a c)"), d_lam[l], [], [Blam])
        dma("pool", ccpw[:].rearrange("p a c -> p (a c)"), d_ccpw[l], [], [Bccpw])
        dma("pool", kdS[:, :, 0:256], d_ckd[l].rearrange("p (c t) -> p c t", c=2), [], [Bkctx])
        dma("pool", kgS[:, 0:256], d_ckg[l], [], [Bkctx])
        dma("pool", vaSg[:, 0:2, :, 0:64], d_cvg[l].rearrange("p (c h d) -> p c h d", c=2, h=2), [], [Bvgc])
        lam_init = 0.8 - 0.6 * math.exp(-0.3 * l)
        tt(lamv[:, 0, :], lamv[:, 0, :], lamv[:, 1, :], ALU.mult, [Blam], [Blam])
        tt(lamv[:, 2, :], lamv[:, 2, :], lamv[:, 3, :], ALU.mult, [Blam], [Blam])
        S.add("dve", lambda e: e.tensor_reduce(lamt[:, 2:3], lamv[:, 0, :], mybir.AxisListType.X, ALU.add), reads=[Blam], writes=[Blam])
        S.add("dve", lambda e: e.tensor_reduce(lamt[:, 3:4], lamv[:, 2, :], mybir.AxisListType.X, ALU.add), reads=[Blam], writes=[Blam])
        act(lamt[:, 4:6], lamt[:, 2:4], AF.Exp, [Blam], [Blam])
        tt(lamt[:, 6:7], lamt[:, 5:6], lamt[:, 4:5], ALU.subtract, [Blam], [Blam])
        ts(lamt[:, 0:1], lamt[:, 6:7], -lam_init, ALU.add, [Blam], [Blam])
        ts(lamt[:, 1:2], subln[:], 1.0 - lam_init, ALU.mult, [Bsmall, Blam], [Blam])

    def ctx_v_diff(l):
        memset(vaSd[:, :, :, 64:128], 1.0, BvaSd, eng="pool")
        dma("pool", vaSd[:, 0:2, :, 0:64], d_cvd[l].rearrange("p (c h d) -> p c h d", c=2, h=4), [], BvaSd)

    def proj_fm(g, k, col, bank):
        w = wsl[k][:].rearrange("p (c n) -> p c n", c=8)
        for c in range(8):
            mm(Bps[bank], ps[bank][:], Bws[k], w[:, c, col:col + 128], Bh[g][c], ht[g][:, c, :], c == 0, c == 7)

    def proj_tm(g, k, col, ncol, tile_, bank):
        w = wsl[k][:].rearrange("p (c n) -> p c n", c=8)
        for c in range(8):
            mm(Bps[bank], ps[bank][:, 0:ncol], Bh[g][c], ht[g][:, c, tile_ * 128:(tile_ + 1) * 128], Bws[k], w[:, c, col:col + ncol], c == 0, c == 7)

    def rope_apply(src, src_reads, kind, dst, Bdst):
        ci, si, pm = (0, 1, permd) if kind == "d" else (2, 3, permg)
        act(qbf[:], src, AF.Copy, src_reads, [Bqbf])
        mm(Bps[6], ps[6][:], Bcmat, pm, Bqbf, qbf[:], True, True)
        t1, B1 = nt()
        t2, B2 = nt()
        tt(t1[:], src, rope[:, ci, :], ALU.mult, src_reads + [Brope], [B1])
        tt(t2[:], ps[6][:], rope[:, si, :], ALU.mult, [Bps[6], Brope], [B2])
        tt(dst, t1[:], t2[:], ALU.add, [B1, B2], [Bdst])

    def headnorm(bank, col):
        act(qbf[:], ps[bank][:], AF.Square, [Bps[bank]], [Bqbf])
        mm(Bps[7], ps[7][:], Bcmat, blk64, Bqbf, qbf[:], True, True)
        r_, Br = nt()
        rsqrt_chain(r_[:], Br, ps[7][:], 1.0 / 64, [Bps[7]])
        o_, Bo = nt()
        stt(o_[:], ps[bank][:], qkn[:, col:col + 1], r_[:], ALU.mult, ALU.mult, [Bps[bank], Bsmall, Br], [Bo])
        return o_[:], [Bo]

    def mix_in(l, g):
        for piece in range(5):
            if piece > 0:
                wfree(k)
            k = wload(d_wmi[l, piece])
            if piece == 0:
                for c in range(2):
                    b = nb(); proj_fm(g, k, c * 128, b)
                    ac, Bac = nt()
                    act(ac[:], ps[b][:], AF.Copy, [Bps[b]], [Bac])
                    b2 = nb(); proj_fm(g, k, 256 + c * 128, b2)
                    if g == 0:
                        tt(ppad[0][:, c, :, 1:257], ac[:].rearrange("p (s t) -> p s t", s=2), ps[b2][:].rearrange("p (s t) -> p s t", s=2),
                           ALU.mult, [Bac, Bps[b2]], [Bpp[0]])
                    else:
                        tt(ppad[1][:, c, 1:513], ac[:], ps[b2][:], ALU.mult, [Bac, Bps[b2]], [Bpp[1]])
            elif piece == 1:
                for c in range(2):
                    b = nb(); proj_fm(g, k, c * 128, b)
                    act(abt[g][:, c, :], ps[b][:], AF.Copy, [Bps[b]], Bab[g])
                for c in range(2):
                    b = nb(); proj_fm(g, k, 256 + c * 128, b)
                    if g == 0:
                        act(qpk[g][:, c, :], ps[b][:], AF.Copy, [Bps[b]], [Bqpk[g][c]])
                    else:
                        rope_apply(ps[b][:], [Bps[b]], "d", qpk[g][:, c, :], Bqpk[g][c])
            elif piece == 2:
                for c in range(2):
                    b = nb(); proj_fm(g, k, c * 128, b)
                    if g == 0:
                        act(kdP[:, c, :], ps[b][:], AF.Copy, [Bps[b]], [Bkd[0]])
                    else:
                        rope_apply(ps[b][:], [Bps[b]], "d", kdS[:, c, 256:256 + T], Bkd[1])
                for t_ in range(4):
                    b = nb()
                    rows = slice(t_ * 128, (t_ + 1) * 128)
                    if g == 0:
                        proj_tm(g, k, 0, 512, t_, b)
                        st, Bst = nt()
                        act(st[:], ps[b][:], AF.Copy, [Bps[b]], [Bst])
                        cp(vaP[:, t_, 0:4, 0:64], ps[b][:, 256:512].rearrange("p (h d) -> p h d", h=4), [Bps[b]], [BvaP])
                        dma("sp", d_ndk[l, rows, :], st[:, 0:256], [Bst], [])
                        dma("sp", d_ndv[l, rows, :], st[:, 256:512], [Bst], [])
                    else:
                        proj_tm(g, k, 256, 256, t_, b)
                        act(vtok[:, t_, 0:256], ps[b][:, 0:256], AF.Copy, [Bps[b]], Bcat1)
            elif piece == 3:
                for c in range(2):
                    b = nb(); proj_fm(g, k, 256 + c * 128, b)
                    sg, Bsg = nt()
                    act(sg[:], ps[b][:], AF.Sigmoid, [Bps[b]], [Bsg])
                    b2 = nb(); proj_fm(g, k, c * 128, b2)
                    if g == 0:
                        tt(upad[0][:, c, :, 15:271], sg[:].rearrange("p (s t) -> p s t", s=2),
                           ps[b2][:].rearrange("p (s t) -> p s t", s=2), ALU.mult, [Bsg, Bps[b2]], [Bup[0]])
                    else:
                        tt(upad[1][:, c, 15:527], sg[:], ps[b2][:], ALU.mult, [Bsg, Bps[b2]], [Bup[1]])
            else:
                for c in range(2):
                    b = nb(); proj_fm(g, k, c * 128, b)
                    src, Bsrc = headnorm(b, 0)
                    if g == 0:
                        cp(qpk[g][:, 2 + c, :], src, Bsrc, [Bqpk[g][2 + c]])
                    else:
                        rope_apply(src, Bsrc, "g", qpk[g][:, 2 + c, :], Bqpk[g][2 + c])
                b = nb(); proj_fm(g, k, 256, b)
                src, Bsrc = headnorm(b, 1)
                if g == 0:
                    cp(kgP[:], src, Bsrc, [Bkg[0]])
                else:
                    rope_apply(src, Bsrc, "g", kgS[:, 256:256 + T], Bkg[1])
                for t_ in range(4):
                    b = nb()
                    rows = slice(t_ * 128, (t_ + 1) * 128)
                    if g == 0:
                        proj_tm(g, k, 256, 256, t_, b)
                        st, Bst = nt()
                        act(st[:, 128:256], ps[b][:, 128:256], AF.Copy, [Bps[b]], [Bst])
                        cp(vaP[:, t_, 4:6, 0:64], ps[b][:, 128:256].rearrange("p (h d) -> p h d", h=2), [Bps[b]], [BvaP])
                        act(st[:, 256:384], ps[b][:, 0:128], AF.Square, [Bps[b]], [Bst])
                        S.add("dve", lambda e, st=st: e.tensor_reduce(ktm[:, 0:2], st[:, 256:384].rearrange("p (h d) -> p h d", h=2),
                                                                      mybir.AxisListType.X, ALU.add), reads=[Bst], writes=[Bktm])
                        rsqrt_chain(ktm[:, 2:4], Bktm, ktm[:, 0:2], 1.0 / 64, [Bktm])
                        for h in range(2):
                            stt(st[:, h * 64:(h + 1) * 64], ps[b][:, h * 64:(h + 1) * 64], ktm[:, 2 + h:3 + h], kng[:],
                                ALU.mult, ALU.mult, [Bps[b], Bktm, Bsmall], [Bst])
                        dma("sp", d_ngk[l, rows, :], st[:, 0:128], [Bst], [])
                        dma("sp", d_ngv[l, rows, :], st[:, 128:256], [Bst], [])
                    else:
                        proj_tm(g, k, 384, 128, t_, b)
                        act(vtok[:, t_, 256:384], ps[b][:, 0:128], AF.Copy, [Bps[b]], Bcat1)
            if substop is not None and piece == substop:
                raise _Stop()
        wfree(k)

    def exchange(l):
        for c in range(2):
            cp(edge[:, c, 0:15], upad[1][:, c, 15:30], [Bup[1]], [Bedge])
            cp(edge[:, c, 15:16], ppad[1][:, c, 1:2], [Bpp[1]], [Bedge])
            cp(edge[:, c, 16:31], upad[1][:, c, 512:527], [Bup[1]], [Bedge])
            cp(edge[:, c, 31:32], ppad[1][:, c, 512:513], [Bpp[1]], [Bedge])
        bo = d_bounce[l]
        dma("sp", bo[0:256, :].rearrange("(c p) t -> p c t", p=128), kdS[:, :, 256:256 + T], [Bkd[1]], [B_bounce[l]])
        dma("sp", bo[256:384, :], kgS[:, 256:256 + T], [Bkg[1]], [B_bounce[l]])
        vview = bo[384:768, :].rearrange("r c -> (r c)").rearrange("(tt p v) -> p tt v", p=128, v=384)
        dma("sp", vview, vtok, Bcat1, [B_bounce[l]])
        eview = bo[768:784, :].rearrange("r c -> (r c)").rearrange("(c p e) -> p c e", p=128, e=32)
        dma("sp", eview, edge[:], [Bedge], [B_bounce[l]])
        S.add("pool", lambda e: e.collective_compute("AllGather", ALU.bypass, replica_groups=[[0, 1, 2, 3], [4, 5, 6, 7]],
                                                     ins=[bo.opt()], outs=[d_gath[l].opt()]),
              reads=[B_bounce[l]], writes=[B_gath[l]], kind="cc")
    def exchange_recv(l):
        ga = d_gath[l]
        for r in range(4):
            base = r * BROWS
            ks = 256 + r * T
            dma("sp", kdS[:, :, ks:ks + T], ga[base:base + 256, :].rearrange("(c p) t -> p c t", p=128), [B_gath[l]], [Bkgath])
            dma("sp", kgS[:, ks:ks + T], ga[base + 256:base + 384, :], [B_gath[l]], [Bkgath])
            vv = ga[base + 384:base + 768, :].rearrange("r c -> (r c)").rearrange("(tt p v) -> p tt v", p=128, v=384)
            for t_ in range(4):
                ch = 2 + r * 4 + t_
                dma("sp", vaSd[:, ch, :, 0:64], vv[:, t_, 0:256].rearrange("p (h d) -> p h d", h=4), [B_gath[l]], BvaSd)
                dma("sp", vaSg[:, ch, :, 0:64], vv[:, t_, 256:384].rearrange("p (h d) -> p h d", h=2), [B_gath[l]], [Bvgg])
            ev = ga[base + 768:base + 784, :].rearrange("r c -> (r c)").rearrange("(c p e) -> p c e", p=128, e=32)
            dma("sp", egt[:, :, r, :], ev, [B_gath[l]], [Begt])

    def halos():
        for side, lo, mcol in ((0, 16, 8), (1, 0, 12)):
            ts(halo[:, :, side, :], egt[:, :, 0, lo:lo + 16], msk[:, mcol:mcol + 1], ALU.mult, [Begt, Bmsk], [Bhalo])
            for r in range(1, 4):
                stt(halo[:, :, side, :], egt[:, :, r, lo:lo + 16], msk[:, mcol + r:mcol + r + 1], halo[:, :, side, :], ALU.mult, ALU.add,
                    [Begt, Bmsk, Bhalo], [Bhalo])
        cp(upad[1][:, :, 0:15], halo[:, :, 0, 0:15], [Bhalo], [Bup[1]])
        cp(ppad[1][:, :, 0:1], halo[:, :, 0, 15:16], [Bhalo], [Bpp[1]])
        cp(upad[1][:, :, 527:542], halo[:, :, 1, 0:15], [Bhalo], [Bup[1]])
        cp(ppad[1][:, :, 513:514], halo[:, :, 1, 15:16], [Bhalo], [Bpp[1]])

    def attention(g, hook=None):
        nq, nkc = 512, (2 if g == 0 else 18)
        items = [(0, hm) for hm in range(12)]
        state = {"on_prev": None}

        def qinfo(hm):
            if hm < 8:
                return qpk[g][:, hm // 4, :], Bqpk[g][hm // 4], hm % 4, hm // 2, 32 ** -0.5
            sl = hm - 8
            return qpk[g][:, 2 + sl // 2, :], Bqpk[g][2 + sl // 2], 4 + sl % 2, 4 + sl % 2, 64 ** -0.5

        def emit_mask(idx):
            s_, hm = items[idx]
            qsrc, Bq, mcol, h, scale = qinfo(hm)
            qk = idx % 2
            ts(qmr[qk][:], qsrc, msk[:, mcol:mcol + 1], ALU.mult, [Bq, Bmsk], [Bqmr[qk]])

        def kv(s_, hm, h, kc):
            if g == 0:
                ksl = slice(s_ * 256 + kc * 128, s_ * 256 + (kc + 1) * 128)
                kT = kdP[:, hm // 4, ksl] if hm < 8 else kgP[:, ksl]
                Bk = [Bkd[0]] if hm < 8 else [Bkg[0]]
                return kT, Bk, vaP[:, s_ * 2 + kc, h, :], [BvaP]
            ksl = slice(kc * 128, (kc + 1) * 128)
            kT = kdS[:, hm // 4, ksl] if hm < 8 else kgS[:, ksl]
            Bk = [Bkctx] if kc < 2 else [Bkgath]
            if h < 4:
                return kT, Bk, vaSd[:, kc, h, :], BvaSd
            return kT, Bk, vaSg[:, kc, h - 4, :], [Bvgc if kc < 2 else Bvgg]

        def finalize(idx):
            s_, hm = items[idx]
            qs = slice(s_ * nq, (s_ + 1) * nq)
            qsrc, Bq, mcol, h, scale = qinfo(hm)
            ob = 4 + idx % 2
            r_, Br = nt()
            cp(r_[0:64, 0:nq], ps[ob][64:128, 0:nq], [Bps[ob]], [Br])
            act(r_[0:64, 0:nq], r_[0:64, 0:nq], AF.Ln, [Br], [Br])
            act(r_[0:64, 0:nq], r_[0:64, 0:nq], AF.Exp, [Br], [Br], scale=-1.0)
            if hm < 8:
                on, Bon = nt()
                tt(on[0:64, 0:nq], ps[ob][0:64, 0:nq], r_[0:64, 0:nq], ALU.mult, [Bps[ob], Br], [Bon])
                if hm % 2 == 0:
                    state["on_prev"] = (on, Bon)
                else:
                    on0, Bon0 = state["on_prev"]
                    d_, Bd = nt()
                    stt(d_[0:64, 0:nq], on[0:64, 0:nq], lamt[0:64, 0:1], on0[0:64, 0:nq], ALU.mult, ALU.add, [Bon0, Bon, Blam], [Bd])
                    act(qbf[0:64, 0:nq], d_[0:64, 0:nq], AF.Square, [Bd], [Bqbf])

                    def cont(d_=d_, Bd=Bd, h=h, qs=qs):
                        mm(Bps[6], ps[6][0:64, 0:nq], Bcmat, ones_b[0:64, 0:64], Bqbf, qbf[0:64, 0:nq], True, True)
                        rr, Brr = nt()
                        rsqrt_chain(rr[0:64, 0:nq], Brr, ps[6][0:64, 0:nq], 1.0 / 64, [Bps[6]])
                        stt(hb[g][:, h, qs], d_[0:64, 0:nq], lamt[0:64, 1:2], rr[0:64, 0:nq], ALU.mult, ALU.mult, [Bd, Blam, Brr], [Bhb[g][h]])
                    return cont
            else:
                sl = hm - 8
                tt(hb[g][:, 4 + sl, qs], ps[ob][0:64, 0:nq], r_[0:64, 0:nq], ALU.mult, [Bps[ob], Br], [Bhb[g][4 + sl]])
            return None

        contB = [None]

        def fin_step(idx):
            if contB[0] is not None:
                contB[0]()
                contB[0] = None
            if idx >= 0:
                contB[0] = finalize(idx)

        NPT = len(pT)
        stepc = [0]
        emit_mask(0)
        for idx, (s_, hm) in enumerate(items):
            qs = slice(s_ * nq, (s_ + 1) * nq)
            qsrc, Bq, mcol, h, scale = qinfo(hm)
            qk = idx % 2
            ob = 4 + idx % 2
            if idx + 1 < len(items):
                emit_mask(idx + 1)
            pend = []
            if g == 0:
                pks = []
                for kc in range(2):
                    sbk = stepc[0] % 4
                    pk = stepc[0] % NPT
                    stepc[0] += 1
                    pks.append(pk)
                    for s2 in range(2):
                        kT, Bk, va, Bv = kv(s2, hm, h, kc)
                        cs = slice(s2 * 256, (s2 + 1) * 256)
                        S.add("pe", lambda e, o=ps[sbk][:, cs], a=kT, b_=qmr[qk][:, cs]: e.matmul(o, a, b_, start=True, stop=True),
                              reads=Bk + [Bqmr[qk]], writes=[Bps[sbk]])
                    act(pT[pk][:], ps[sbk][:], AF.Exp, [Bps[sbk]], [BpT[pk]], scale=scale)
                fin_step(idx - 1)
                for s2 in range(2):
                    cs = slice(s2 * 256, (s2 + 1) * 256)
                    for kc in range(2):
                        kT, Bk, va, Bv = kv(s2, hm, h, kc)
                        S.add("pe", lambda e, o=ps[ob][:, cs], a=va, b_=pT[pks[kc]][:, cs], st_=(kc == 0), sp_=(kc == 1): e.matmul(o, a, b_, start=st_, stop=sp_),
                              reads=Bv + [BpT[pks[kc]]], writes=[Bps[ob]])
                continue

            def score(kc):
                kT, Bk, va, Bv = kv(s_, hm, h, kc)
                sbk = stepc[0] % 4
                pk = stepc[0] % NPT
                stepc[0] += 1
                S.add("pe", lambda e, o=ps[sbk][:, 0:nq], a=kT, b_=qmr[qk][:, qs]: e.matmul(o, a, b_, start=True, stop=True),
                      reads=Bk + [Bqmr[qk]], writes=[Bps[sbk]])
                act(pT[pk][:, 0:nq], ps[sbk][:, 0:nq], AF.Exp, [Bps[sbk]], [BpT[pk]], scale=scale)
                pend.append((kc, pk, va, Bv))

            def pv():
                kc, pk, va, Bv = pend.pop(0)
                st_, sp_ = (kc == 0), (kc == nkc - 1)
                S.add("pe", lambda e, o=ps[ob][:, 0:nq], a=va, b_=pT[pk][:, 0:nq], st_=st_, sp_=sp_: e.matmul(o, a, b_, start=st_, stop=sp_),
                      reads=Bv + [BpT[pk]], writes=[Bps[ob]])

            DEPTH_PIPE = 2
            for kc in range(nkc):
                score(kc)
                if len(pend) > DEPTH_PIPE:
                    pv()
            fin_step(idx - 1)
            while pend:
                pv()
            if hook is not None:
                hook()
        fin_step(len(items) - 1)
        fin_step(-1)

    def diag(wcol):
        k = dgc[0] % NDG
        dgc[0] += 1
        ts(dgr[k][:], ident, wcol, ALU.mult, [Bcmat, Bsmall], [Bdgr[k]])
        return dgr[k][:], Bdgr[k]

    def convs(g):
        nseq, n = (2, 256) if g == 0 else (1, 512)
        ucs = []
        for c in range(2):
            b = nb()
            for k in range(31):
                dg, Bdg = diag(ccw[:, c, k:k + 1])
                for s in range(nseq):
                    rhs = upad[0][:, c, s, k:k + n] if g == 0 else upad[1][:, c, k:k + n]
                    S.add("pe", lambda e, o=ps[4 + s][:, 0:n], a=dg, r=rhs, st_=(k == 0), sp_=(k == 30): e.matmul(o, a, r, start=st_, stop=sp_),
                          reads=[Bdg, Bup[g]], writes=[Bps[4 + s]])
            uc, Buc = nt()
            for s in range(nseq):
                act(uc[:, s * n:(s + 1) * n], ps[4 + s][:, 0:n], AF.Identity, [Bps[4 + s], Bsmall], [Buc], bias=ccv[:, 0, c:c + 1])
            ucq, Bucq = nt()
            act(ucq[:], uc[:], AF.Square, [Buc], [Bucq])
            ucs.append((uc, Buc, ucq, Bucq))
            for k in range(3):
                dg, Bdg = diag(sconvw[:, c, k:k + 1])
                for s in range(nseq):
                    rhs = ppad[0][:, c, s, k:k + n] if g == 0 else ppad[1][:, c, k:k + n]
                    S.add("pe", lambda e, o=ps[6 + s][:, 0:n], a=dg, r=rhs, st_=(k == 0), sp_=(k == 2): e.matmul(o, a, r, start=st_, stop=sp_),
                          reads=[Bdg, Bpp[g]], writes=[Bps[6 + s]])
            for s in range(nseq):
                tt(cat[g][:, c, s * n:(s + 1) * n], abt[g][:, c, s * n:(s + 1) * n], ps[6 + s][:, 0:n], ALU.mult, Bab[g] + [Bps[6 + s]], [Bcat[g][c]])
        b1 = nb()
        for c in range(2):
            mm(Bps[b1], ps[b1][:], Bcmat, ones_f, ucs[c][1], ucs[c][0][:], c == 0, c == 1)
        b2 = nb()
        for c in range(2):
            mm(Bps[b2], ps[b2][:], Bcmat, ones_f, ucs[c][3], ucs[c][2][:], c == 0, c == 1)
        mean, Bmean = nt()
        ts(mean[:], ps[b1][:], 1.0 / 256, ALU.mult, [Bps[b1]], [Bmean])
        var, Bvar = nt()
        tt(var[:], mean[:], mean[:], ALU.mult, [Bmean], [Bvar])
        stt(var[:], ps[b2][:], 1.0 / 256, var[:], ALU.mult, ALU.subtract, [Bps[b2], Bvar], [Bvar])
        rsqrt_chain(var[:], Bvar, var[:], 1.0, [Bvar])
        for c in range(2):
            uc, Buc, ucq, Bucq = ucs[c]
            tt(ucq[:], uc[:], mean[:], ALU.subtract, [Buc, Bmean], [Bucq])
            tt(ucq[:], ucq[:], var[:], ALU.mult, [Bucq, Bvar], [Bucq])
            act(vb[:, c, :], ucq[:], AF.Silu, [Bucq, Bsmall], [Bvb], scale=ccv[:, 1, c:c + 1], bias=ccv[:, 2, c:c + 1])
        for oc in range(2):
            b = nb()
            for c in range(2):
                mm(Bps[b], ps[b][:], Bccpw, ccpw[:, c, oc * 128:(oc + 1) * 128], Bvb, vb[:, c, :], c == 0, c == 1)
            act(cat[g][:, 2 + oc, :], ps[b][:], AF.Copy, [Bps[b]], [Bcat[g][2 + oc]])

    def mix_out(l, g):
        wk = [wload(d_wmo[l, a]) for a in range(3)]
        pieces = [(cat[g][:, 0, :], Bcat[g][0], 128), (cat[g][:, 1, :], Bcat[g][1], 128)]
        pieces += [(hb[g][:, h, :], Bhb[g][h], 64) for h in range(4)]
        pieces += [(cat[g][:, 2, :], Bcat[g][2], 128), (cat[g][:, 3, :], Bcat[g][3], 128)]
        pieces += [(hb[g][:, 4 + h, :], Bhb[g][4 + h], 64) for h in range(4)]
        for oc in range(8):
            bank = 4 + (oc % 2)
            for pi, (ap_, B_, kk) in enumerate(pieces):
                k = wk[pi // 4]
                w = wsl[k][:].rearrange("p (a n) -> p a n", a=4)
                mm(Bps[bank], ps[bank][:], Bws[k], w[0:kk, pi % 4, oc * 128:(oc + 1) * 128], B_, ap_[0:kk, :], pi == 0, pi == 11)
            flush_sq()
            evac_y(g, oc, bank)
        flush_sq(True)
        for k in wk:
            wfree(k)
        post_norm(g, 1)

    def dumpx(idx, g):
        if DEBUG:
            dma("sp", d_dbg[idx].rearrange("p (c t) -> p c t", c=8), xt[g][:], Bx[g], [])

    stage = [0]

    def chk():
        stage[0] += 1
        if stop is not None and stage[0] >= stop:
            raise _Stop()

    def mixer(l):
        cur[0] = l
        pre_norm(1, 1)
        mix_in(l, 1)
        exchange(l)
        pre_norm(0, 1)
        mix_in(l, 0)
        exchange_recv(l)
        attention(0)
        convs(0)
        mix_out(l, 0)
        halos()
        hook = None
        gen = None
        if l + 1 < DEPTH:
            gen = ada_gen(l + 1)
            cnt_h = [0]

            def hook():
                n = 2 if cnt_h[0] < 6 else 1
                cnt_h[0] += 1
                for _ in range(n):
                    next(gen, None)
        attention(1, hook)
        if gen is not None:
            for _ in gen:
                pass
        convs(1)
        mix_out(l, 1)

    try:
        gen0 = ada_gen(0)
        for _ in range(4):
            next(gen0, None)
        layer_consts(0)
        ffn_chain([(0, 0, 0, 0), (0, 0, 0, 1)], hook=lambda: next(gen0, None))
        for _ in gen0:
            pass
        ctx_v_diff(0)
        chk()
        mixer(0)
        chk()
        layer_consts(1)
        ffn_chain([(0, 1, 2, 0), (0, 1, 2, 1), (1, 0, 0, 0), (1, 0, 0, 1)])
        ctx_v_diff(1)
        mixer(1)
        ffn_chain([(1, 1, 2, 0), (1, 1, 2, 1)])
    except _Stop:
        pass
    for g in range(2):
        for c in range(8):
            dma("sp", d_y[g][:, c, :], xt[g][:, c, :], [Bx[g][c]], [])

    print('sbuf remaining', nc.sbuf_bytes_remaining, flush=True)
    S.finalize()
    with nc.Block() as block:
        @block.tensor
        def _(e):
            S.replay("pe", e, sems)

        @block.scalar
        def _(e):
            S.replay("act", e, sems)

        @block.vector
        def _(e):
            S.replay("dve", e, sems)

        @block.gpsimd
        def _(e):
            S.replay("pool", e, sems)

        @block.sync
        def _(e):
            S.replay("sp", e, sems)
    es.close()
    return nc


def _fm(a):
    t = a.shape[0]
    return np.ascontiguousarray(a.T.reshape(8, 128, t).transpose(1, 0, 2))


def _unfm(a):
    t = a.shape[2]
    return np.ascontiguousarray(a.transpose(1, 0, 2).reshape(1024, t).T)


def _kpieces(w, ncols_piece):
    n = w.shape[1]
    npieces = n // ncols_piece
    a = w.reshape(8, 128, npieces, ncols_piece).transpose(2, 1, 0, 3)
    return np.ascontiguousarray(a.reshape(npieces, 128, 8 * ncols_piece))


def _vec_fm(v, nch):
    return np.ascontiguousarray(v.reshape(nch, 128).T)


def _rope_tables():
    theta = np.float32(10000.0)
    tabs = {}
    for name, dd in (("d", 32), ("g", 64)):
        nf = dd // 4
        inv = (theta ** (-(np.arange(nf, dtype=np.float32) / np.float32(nf)))).astype(np.float32)
        tabs[name] = (dd, nf, inv)
    return tabs


def _host_inputs(inp):
    f32 = np.float32
    shared = {}
    L = DEPTH
    shared["wada"] = np.stack([_kpieces(inp["w_ada"][l], 512) for l in range(L)])
    shared["gpre"] = np.stack([np.concatenate([_vec_fm(inp["norm_pre"][l, i], 8) for i in range(3)], axis=1) for l in range(L)])
    shared["gpost"] = np.stack([np.concatenate([_vec_fm(inp["norm_post"][l, i], 8) for i in range(3)], axis=1) for l in range(L)])
    wgu = np.zeros((L, 2, 11, 128, 4096), f32)
    wdn = np.zeros((L, 2, 8, 128, NFC * 128), f32)
    for l in range(L):
        for wi, (gn, un, dn) in enumerate((("ffn1_gate", "ffn1_up", "ffn1_down"), ("ffn2_gate", "ffn2_up", "ffn2_down"))):
            gp = _kpieces(inp[gn][l], 256).reshape(11, 128, 2048)
            up = _kpieces(inp[un][l], 256).reshape(11, 128, 2048)
            wgu[l, wi] = np.concatenate([gp, up], axis=2)
            wd = inp[dn][l]
            a = wd.reshape(NFC, 128, 8, 128).transpose(2, 1, 0, 3)
            wdn[l, wi] = a.reshape(8, 128, NFC * 128)
    shared["wgu"] = wgu
    shared["wdn"] = wdn
    wmi = np.zeros((L, 5, 128, 4096), f32)
    wmo = np.zeros((L, 3, 128, 4096), f32)
    for l in range(L):
        w = inp["w_mix_in"][l].copy()
        dq = w[:, 2048:2304].reshape(1024, 4, 64)
        w[:, 2048:2304] = dq[:, [0, 2, 1, 3], :].reshape(1024, 256)
        w[:, 0:768] = np.concatenate([w[:, 256:512], w[:, 512:768], w[:, 0:256]], axis=1)
        wmi[l] = _kpieces(w, 512)
        wo = inp["w_mix_out"][l]
        pieces = np.zeros((12, 128, 1024), f32)
        pieces[0] = wo[0:128]; pieces[1] = wo[128:256]
        for h in range(4):
            pieces[2 + h, 0:64] = wo[256 + h * 64:256 + (h + 1) * 64]
        pieces[6] = wo[512:640]; pieces[7] = wo[640:768]
        for slot, h in enumerate((0, 2, 1, 3)):
            pieces[8 + slot, 0:64] = wo[768 + h * 64:768 + (h + 1) * 64]
        wmo[l] = pieces.reshape(3, 4, 128, 1024).transpose(0, 2, 1, 3).reshape(3, 128, 4096)
    shared["wmi"] = wmi
    shared["wmo"] = wmo
    shared["sconvw"] = np.stack([np.ascontiguousarray(inp["sconv_w"][l].reshape(3, 2, 128).transpose(2, 1, 0)).reshape(128, 6) for l in range(L)])
    shared["ccw"] = np.stack([np.ascontiguousarray(inp["ccm_dw_w"][l].reshape(31, 2, 128).transpose(2, 1, 0)).reshape(128, 62) for l in range(L)])
    shared["ccv"] = np.stack([np.concatenate([_vec_fm(inp[n][l], 2) for n in ("ccm_dw_b", "ccm_ln_g", "ccm_ln_b")], axis=1) for l in range(L)])
    shared["ccpw"] = np.stack([np.ascontiguousarray(inp["ccm_pw"][l].reshape(2, 128, 256).transpose(1, 0, 2)).reshape(128, 512) for l in range(L)])
    shared["lamv"] = np.stack([np.tile(np.concatenate([inp[n][l] for n in ("diff_lq1", "diff_lk1", "diff_lq2", "diff_lk2")])[None, :], (128, 1)) for l in range(L)])
    shared["subln"] = np.stack([np.tile(inp["diff_subln"][l], 2).reshape(128, 1) for l in range(L)])
    shared["qkn"] = np.stack([np.stack([np.tile(inp["gqa_qnorm"][l], 2), np.tile(inp["gqa_knorm"][l], 2)], axis=1) for l in range(L)])
    shared["kng"] = np.stack([np.tile(inp["gqa_knorm"][l][None, :], (128, 1)) for l in range(L)])
    cm = np.zeros((128, 5, 128), f32)
    cm[:, 0, :] = 1.0
    cm[0:64, 1, 0:64] = 1.0; cm[64:128, 1, 64:128] = 1.0
    cm[:, 2, :] = np.eye(128, dtype=f32)
    for i in range(128):
        cm[i ^ 8, 3, i] = 1.0
        cm[i ^ 16, 4, i] = 1.0
    shared["cmat"] = cm.reshape(128, 640)
    for k in shared:
        shared[k] = np.ascontiguousarray(shared[k], dtype=f32)

    tabs = _rope_tables()
    maps = []
    for i in range(NCORES):
        b, j = i // 4, i % 4
        m = dict(shared)
        m["xp"] = _fm(inp["x_prompt"][2 * i:2 * i + 2].reshape(512, 1024))
        m["xs"] = _fm(inp["x_sample"][b, 512 * j:512 * (j + 1)])
        m["cnd"] = np.ascontiguousarray(np.stack([_vec_fm(inp["c_ctx"], 8), _vec_fm(inp["c"][b], 8)], axis=2))
        m["bada"] = np.stack([np.repeat(_vec_fm(inp["b_ada"][l], 72), 2, axis=1) for l in range(L)])
        m["ckd"] = np.stack([np.ascontiguousarray(inp["cache_diff_k"][b, l].reshape(256, 2, 128).transpose(2, 1, 0)).reshape(128, 512) for l in range(L)])
        m["ckg"] = np.stack([np.ascontiguousarray(inp["cache_gqa_k"][b, l].reshape(256, 128).T) for l in range(L)])
        m["cvd"] = np.stack([np.ascontiguousarray(inp["cache_diff_v"][b, l].reshape(2, 128, 256).transpose(1, 0, 2)).reshape(128, 512) for l in range(L)])
        m["cvg"] = np.stack([np.ascontiguousarray(inp["cache_gqa_v"][b, l].reshape(2, 128, 128).transpose(1, 0, 2)).reshape(128, 256) for l in range(L)])
        tok = np.arange(512 * j, 512 * (j + 1))
        row = (tok // 64).astype(f32)
        col = (tok % 64).astype(f32)
        rt = np.zeros((128, 4, T), f32)
        for ti, name in enumerate(("d", "g")):
            dd, nf, inv = tabs[name]
            for p in range(128):
                f = p % dd
                pos = row if f < dd // 2 else col
                ang = (pos * inv[f % nf]).astype(f32)
                sgn = -1.0 if (f % (2 * nf)) < nf else 1.0
                rt[p, 2 * ti, :] = np.cos(ang)
                rt[p, 2 * ti + 1, :] = sgn * np.sin(ang)
        m["rope"] = rt.reshape(128, 4 * T)
        mk = np.zeros((128, 16), f32)
        for q in range(4):
            mk[32 * q:32 * (q + 1), q] = 1.0
        mk[0:64, 4] = 1.0
        mk[64:128, 5] = 1.0
        if j > 0:
            mk[:, 8 + j - 1] = 1.0
        if j < 3:
            mk[:, 12 + j + 1] = 1.0
        m["msk"] = mk
        for k in m:
            m[k] = np.ascontiguousarray(m[k], dtype=f32)
        maps.append(m)
    return maps


_NC_CACHE = {}


def kernel(**inputs):
    inp = {k: np.asarray(v) for k, v in inputs.items()}
    if "nc" not in _NC_CACHE:
        _NC_CACHE["nc"] = build_program()
    nc = _NC_CACHE["nc"]
    in_maps = _host_inputs(inp)
    res = run_bass_kernel_spmd(nc, in_maps, core_ids=list(range(NCORES)))
    R = res.results
    y_prompt = np.zeros((16, 256, 1024), np.float32)
    y_sample = np.zeros((2, 2048, 1024), np.float32)
    ndk = np.zeros((16, DEPTH, 256, 4, 2, 32), np.float32)
    ndv = np.zeros((16, DEPTH, 256, 4, 64), np.float32)
    ngk = np.zeros((16, DEPTH, 256, 2, 64), np.float32)
    ngv = np.zeros((16, DEPTH, 256, 2, 64), np.float32)
    for i in range(NCORES):
        b, j = i // 4, i % 4
        y_prompt[2 * i:2 * i + 2] = _unfm(R[i]["yp"]).reshape(2, 256, 1024)
        y_sample[b, 512 * j:512 * (j + 1)] = _unfm(R[i]["ys"])
        for l in range(DEPTH):
            ndk[2 * i:2 * i + 2, l] = R[i]["ndk"][l].reshape(2, 256, 4, 2, 32)
            ndv[2 * i:2 * i + 2, l] = R[i]["ndv"][l].reshape(2, 256, 4, 64)
            ngk[2 * i:2 * i + 2, l] = R[i]["ngk"][l].reshape(2, 256, 2, 64)
            ngv[2 * i:2 * i + 2, l] = R[i]["ngv"][l].reshape(2, 256, 2, 64)
    return (y_prompt, y_sample, ndk, ndv, ngk, ngv)
```

```python
import math
from contextlib import ExitStack
import numpy as np
import concourse.bass as bass
import concourse.mybir as mybir
from concourse.bass_utils import run_bass_kernel_spmd

F32 = mybir.dt.float32
BF16 = mybir.dt.bfloat16
ALU = mybir.AluOpType
AF = mybir.ActivationFunctionType

D = 1024
DFF = 2816
NFC = 22
DEPTH = 2
T = 512
EPS = 1e-6
NCORES = 8
NKS = 2304
BROWS = 784
NW = 3
NDS = 12
DEBUG = False


class Buf:
    __slots__ = ("name", "w", "r", "psum")

    def __init__(self, name, psum=False):
        self.name = name
        self.w = None
        self.r = {}
        self.psum = psum


class Op:
    __slots__ = ("eng", "emit", "deps", "needs_inc", "ticket", "kind", "dsem", "dval", "pre")

    def __init__(self, eng, emit, kind):
        self.eng = eng
        self.emit = emit
        self.deps = []
        self.needs_inc = False
        self.ticket = 0
        self.kind = kind
        self.dsem = None
        self.dval = 0
        self.pre = None


class Sched:
    ENGS = ("pe", "act", "dve", "pool", "sp")

    def __init__(self):
        self.ops = {e: [] for e in self.ENGS}
        self.ndma = {e: 0 for e in self.ENGS}
        self.ncc = 0

    def add(self, eng, emit, reads=(), writes=(), kind="c"):
        op = Op(eng, emit, kind)
        deps = {}

        def dep(o):
            if o is None or o is op:
                return
            if eng == "pe" and kind == "c" and o.eng == "pe" and o.kind == "c":
                return
            deps[id(o)] = o

        for b in reads:
            dep(b.w)
            if b.psum:
                for k_, o in b.r.items():
                    if k_ != eng:
                        dep(o)
        for b in writes:
            dep(b.w)
            for o in b.r.values():
                dep(o)
        for b in writes:
            b.w = op
            b.r = {}
        for b in reads:
            if kind == "c":
                b.r[eng] = op
            else:
                b.r[("d", id(op))] = op
        op.deps = list(deps.values())
        for o in op.deps:
            if o.kind == "c":
                o.needs_inc = True
        if kind == "d":
            n = self.ndma[eng]
            self.ndma[eng] = n + 1
            op.dsem = (eng, n % NDS)
            op.dval = 16 * (n // NDS + 1)
            if n >= NDS:
                op.pre = (op.dsem, 16 * (n // NDS))
        elif kind == "cc":
            self.ncc += 1
            op.dsem = ("cc", self.ncc - 1)
            op.dval = 1
        self.ops[eng].append(op)
        return op

    def finalize(self):
        for e in self.ENGS:
            t = 0
            for op in self.ops[e]:
                if op.kind == "c" and op.needs_inc:
                    t += 1
                    op.ticket = t

    def replay(self, e, eng, sems):
        waited = {}

        def wait(key, val):
            if waited.get(key, 0) < val:
                eng.wait_ge(sems[key], val)
                waited[key] = val

        for op in self.ops[e]:
            for o in op.deps:
                if o.kind == "c":
                    wait((o.eng, "c"), o.ticket)
                else:
                    wait(o.dsem, o.dval)
            if op.pre is not None:
                wait(op.pre[0], op.pre[1])
            ins = op.emit(eng)
            if op.kind == "d":
                ins.then_inc(sems[op.dsem], 16)
            elif op.kind == "cc":
                ins.then_inc(sems[op.dsem])
            elif op.needs_inc:
                ins.then_inc(sems[(e, "c")], 1)
        n = self.ndma[e]
        for k in range(min(n, NDS)):
            cnt = (n - 1 - k) // NDS + 1
            wait((e, k), 16 * cnt)
        if e == "pool":
            for k in range(self.ncc):
                wait(("cc", k), 1)


class _Stop(Exception):
    pass


def build_program(stop=None, substop=None):
    nc = bass.Bass("TRN2", target_bir_lowering=False)
    S = Sched()
    es = ExitStack()

    def din(name, shape, dt=F32):
        return nc.dram_tensor(name, list(shape), dt, kind="ExternalInput").ap()

    def dout(name, shape, dt=F32):
        return nc.dram_tensor(name, list(shape), dt, kind="ExternalOutput").ap()

    d_x = [din("xp", [128, 8, T]), din("xs", [128, 8, T])]
    d_cnd = din("cnd", [128, 8, 2])
    d_wada = din("wada", [DEPTH, 18, 128, 8 * 512])
    d_bada = din("bada", [DEPTH, 128, 144])
    d_gpre = din("gpre", [DEPTH, 128, 24])
    d_gpost = din("gpost", [DEPTH, 128, 24])
    d_wgu = din("wgu", [DEPTH, 2, 11, 128, 4096])
    d_wdn = din("wdn", [DEPTH, 2, 8, 128, NFC * 128])
    d_wmi = din("wmi", [DEPTH, 5, 128, 4096])
    d_wmo = din("wmo", [DEPTH, 3, 128, 4096])
    d_sconv = din("sconvw", [DEPTH, 128, 6])
    d_ccw = din("ccw", [DEPTH, 128, 62])
    d_ccv = din("ccv", [DEPTH, 128, 6])
    d_ccpw = din("ccpw", [DEPTH, 128, 512])
    d_lam = din("lamv", [DEPTH, 128, 128])
    d_subln = din("subln", [DEPTH, 128, 1])
    d_qkn = din("qkn", [DEPTH, 128, 2])
    d_kng = din("kng", [DEPTH, 128, 64])
    d_ckd = din("ckd", [DEPTH, 128, 2 * 256])
    d_ckg = din("ckg", [DEPTH, 128, 256])
    d_cvd = din("cvd", [DEPTH, 128, 2 * 256])
    d_cvg = din("cvg", [DEPTH, 128, 2 * 128])
    d_rope = din("rope", [128, 4 * T])
    d_cmat = din("cmat", [128, 5 * 128])
    d_msk = din("msk", [128, 16])
    d_y = [dout("yp", [128, 8, T]), dout("ys", [128, 8, T])]
    d_ndk = dout("ndk", [DEPTH, T, 256])
    d_ndv = dout("ndv", [DEPTH, T, 256])
    d_ngk = dout("ngk", [DEPTH, T, 128])
    d_ngv = dout("ngv", [DEPTH, T, 128])
    if DEBUG:
        d_dbg = dout("dbg", [8, 128, 8 * T])
        d_dbgb = dout("dbgb", [4, 128, 8 * T], BF16)
    d_bounce = [nc.dram_tensor("bounce%d" % l, [BROWS, 512], BF16).ap() for l in range(DEPTH)]
    d_gath = [nc.dram_tensor("gath%d" % l, [4 * BROWS, 512], BF16).ap() for l in range(DEPTH)]
    B_bounce = [Buf("bounce%d" % l) for l in range(DEPTH)]
    B_gath = [Buf("gath%d" % l) for l in range(DEPTH)]

    def sb(name, shape, dt=F32):
        return es.enter_context(nc.sbuf_tensor("s_" + name, list(shape), dt))

    xt = [sb("xP", [128, 8, T]), sb("xS", [128, 8, T])]
    Bx = [[Buf("x%d_%d" % (g, c)) for c in range(8)] for g in range(2)]
    h1 = sb("h", [128, 8, T], BF16)
    ht = [h1, h1]
    Bh1 = [Buf("h_%d" % c) for c in range(8)]
    Bh = [Bh1, Bh1]
    y1 = sb("y", [128, 8, T])
    yt = [y1, y1]
    By1 = [Buf("y_%d" % c) for c in range(8)]
    By = [By1, By1]
    wsl = [sb("ws%d" % k, [128, 4096], BF16) for k in range(NW)]
    Bws = [Buf("ws%d" % k) for k in range(NW)]
    ps = [es.enter_context(nc.psum_tensor("ps%d" % k, [128, 512], F32)) for k in range(8)]
    Bps = [Buf("ps%d" % k, psum=True) for k in range(8)]
    hmid1 = sb("hmid", [128, NFC, T], BF16)
    Bhm1 = [Buf("hm_%d" % f) for f in range(NFC)]

    NTP = 6
    tpool = [sb("tp%d" % k, [128, T]) for k in range(NTP)]
    Btp = [Buf("tp%d" % k) for k in range(NTP)]
    tpc = [0]

    rstd_t = sb("rstd_t", [128, T]); Brstd_t = Buf("rstd_t")

    def nt():
        k = tpc[0] % NTP
        tpc[0] += 1
        return tpool[k], Btp[k]

    cnd = sb("cnd", [128, 8, 2]); Bcnd = Buf("cnd")
    scnd = sb("scnd", [128, 8, 2], BF16); Bscnd = Buf("scnd")
    bada_l = [sb("bada%d" % l, [128, 144]) for l in range(DEPTH)]; Bbada_l = [Buf("bada%d" % l) for l in range(DEPTH)]
    modt_l = [sb("modt%d" % l, [128, 72, 2]) for l in range(DEPTH)]; Bmod_l = [[Buf("mod%d_%d" % (l, i)) for i in range(3)] for l in range(DEPTH)]
    gpre_l = [sb("gpre%d" % l, [128, 3, 8]) for l in range(DEPTH)]; gpost_l = [sb("gpost%d" % l, [128, 3, 8]) for l in range(DEPTH)]
    Bgp_l = [Buf("gprepost%d" % l) for l in range(DEPTH)]
    Amod_l = [sb("Amod%d" % l, [128, 2, 3, 8]) for l in range(DEPTH)]; Gmod_l = [sb("Gmod%d" % l, [128, 2, 3, 8]) for l in range(DEPTH)]
    BAG_l = [[Buf("AG%d_%d" % (l, i)) for i in range(3)] for l in range(DEPTH)]
    BGG_l = [[Buf("GG%d_%d" % (l, i)) for i in range(3)] for l in range(DEPTH)]
    Bmodg_l = [[Buf("modg%d_%d" % (l, i)) for i in range(3)] for l in range(DEPTH)]
    cur = [0]
    sconvw = sb("sconvw", [128, 2, 3]); ccw = sb("ccw", [128, 2, 31]); ccv = sb("ccv", [128, 3, 2])
    Bsmall = Buf("small")
    ccpw = sb("ccpw", [128, 2, 256], BF16); Bccpw = Buf("ccpw")
    lamv = sb("lamv", [128, 4, 32]); lamt = sb("lamt", [128, 8]); Blam = Buf("lam")
    subln = sb("subln", [128, 1]); qkn = sb("qkn", [128, 2]); kng = sb("kng", [128, 64])
    rope = sb("rope", [128, 4, T]); Brope = Buf("rope")
    cmatf = sb("cmatf", [128, 5, 128]); cmat = sb("cmat", [128, 5, 128], BF16); Bcmat = Buf("cmat")
    msk = sb("msk", [128, 16]); Bmsk = Buf("msk")
    NDG = 6
    dgr = [sb("dgr%d" % k, [128, 128], BF16) for k in range(NDG)]; Bdgr = [Buf("dgr%d" % k) for k in range(NDG)]
    dgc = [0]

    hx = sb("hx", [128, 8, T], BF16)
    Bhx = [Buf("hx_%d" % c) for c in range(8)]
    abt = [hx[:, 0:2, :], hx[:, 2:4, :]]; Bab = [Bhx[0:2], Bhx[2:4]]
    ppad = [sb("ppadP", [128, 2, 2, 258], BF16), sb("ppadS", [128, 2, 514], BF16)]; Bpp = [Buf("pp0"), Buf("pp1")]
    upad = [sb("upadP", [128, 2, 2, 286], BF16), sb("upadS", [128, 2, 542], BF16)]; Bup = [Buf("up0"), Buf("up1")]
    cat1 = hx[:, 4:8, :]
    cat = [cat1, cat1]
    Bcat1 = Bhx[4:8]
    Bcat = [Bcat1, Bcat1]
    hb1 = sb("hb", [64, 8, T], BF16)
    hb = [hb1, hb1]
    Bhb1 = [Buf("hb_%d" % k) for k in range(8)]
    Bhb = [Bhb1, Bhb1]
    qpk = [sb("qpk%d" % g, [128, 4, T], BF16) for g in range(2)]
    Bqpk = [[Buf("qpk%d_%d" % (g, k)) for k in range(4)] for g in range(2)]
    qmr = [sb("qmr%d" % k, [128, T], BF16) for k in range(2)]; Bqmr = [Buf("qmr%d" % k) for k in range(2)]
    kdP = sb("kdP", [128, 2, T], BF16); kgP = sb("kgP", [128, T], BF16)
    kdS = sb("kdS", [128, 2, NKS], BF16); kgS = sb("kgS", [128, NKS], BF16)
    Bkd = [Buf("kdP"), Buf("kdS_own")]; Bkg = [Buf("kgP"), Buf("kgS_own")]
    Bkctx = Buf("kctx"); Bkgath = Buf("kgath")
    vaP = sb("vaP", [128, 4, 6, 128], BF16)
    vaSg = sb("vaSg", [128, 18, 2, 128], BF16)
    vaSd = hmid1[:].rearrange("p f t -> p (f t)")[:, 0:18 * 512].rearrange("p (k h d) -> p k h d", h=4, d=128)
    BvaSd = Bhm1[0:18]
    BvaP = Buf("vaP"); Bvgc = Buf("vgctx"); Bvgg = Buf("vggath")
    vtok = cat1.rearrange("p a t -> p (a t)")[:, 0:1536].rearrange("p (t v) -> p t v", v=384)
    edge = sb("edge", [128, 2, 32], BF16); Bedge = Buf("edge")
    egt = sb("egt", [128, 2, 4, 32], BF16); Begt = Buf("egt")
    halo = sb("halo", [128, 2, 2, 16]); Bhalo = Buf("halo")
    pT = [sb("pT%d" % k, [128, T], BF16) for k in range(2)]; BpT = [Buf("pT%d" % k) for k in range(2)]
    sqr = pT[0:2]; Bsqr = BpT[0:2]
    pT = pT + [hmid1[:, 18, :], hmid1[:, 19, :]]; BpT = BpT + [Bhm1[18], Bhm1[19]]
    vb = sb("vb", [128, 2, T], BF16); Bvb = Buf("vb")
    qbf = sb("qbf", [128, T], BF16); Bqbf = Buf("qbf")
    ktm = sb("ktm", [128, 8]); Bktm = Buf("ktm")

    sems = {}
    for e in Sched.ENGS:
        sems[(e, "c")] = es.enter_context(nc.semaphore("sc_" + e))
    for e in ("sp", "pool", "act"):
        for k in range(NDS):
            sems[(e, k)] = es.enter_context(nc.semaphore("sd_%s%d" % (e, k)))
    for k in range(DEPTH):
        sems[("cc", k)] = es.enter_context(nc.semaphore("scc%d" % k))

    def mm(outb, out, lb, lhsT, rb, rhs, start, stop):
        S.add("pe", lambda e: e.matmul(out, lhsT, rhs, start=start, stop=stop), reads=[lb, rb], writes=[outb])

    def act(out, in_, func, reads, writes, scale=None, bias=None):
        kw = {}
        if scale is not None:
            kw["scale"] = scale
        if bias is not None:
            kw["bias"] = bias
        S.add("act", lambda e: e.activation(out, in_, func, **kw), reads=reads, writes=writes)

    def tt(out, a, b, op, reads, writes, eng="dve"):
        S.add(eng, lambda e: e.tensor_tensor(out, a, b, op), reads=reads, writes=writes)

    def ts(out, a, s1, op0, reads, writes, eng="dve"):
        S.add(eng, lambda e: e.tensor_scalar(out, a, s1, None, op0), reads=reads, writes=writes)

    def stt(out, a, s, b, op0, op1, reads, writes):
        S.add("dve", lambda e: e.scalar_tensor_tensor(out, a, s, b, op0, op1), reads=reads, writes=writes)

    def cp(out, in_, reads, writes, eng="dve"):
        S.add(eng, lambda e: e.tensor_copy(out, in_), reads=reads, writes=writes)

    def dma(q, out, in_, reads, writes):
        S.add(q, lambda e: e.dma_start(out=out, in_=in_), reads=reads, writes=writes, kind="d")

    def memset(ap, val, writes, eng="dve"):
        S.add(eng, lambda e: e.memset(ap, val), reads=[], writes=writes)

    epsb = sb("epsb", [128, 1]); Bepsb = Buf("epsb")
    memset(epsb[:], EPS, [Bepsb])

    def rsqrt_chain(out, Bout, in_, mult, reads_in):
        np_ = out.shape[0]
        act(out, in_, AF.Ln, reads_in + [Bepsb], [Bout], scale=mult, bias=epsb[0:np_, 0:1])
        act(out, out, AF.Exp, [Bout], [Bout], scale=-0.5)

    wfree_list = list(range(NW))

    def wload(src_ap, nel=4096):
        k = wfree_list.pop(0)
        dma("pool", wsl[k][:, 0:nel].rearrange("p (a n) -> p a n", a=4), src_ap[:, 0:nel].rearrange("p (a n) -> p a n", a=4), [], [Bws[k]])
        return k

    def wfree(k):
        wfree_list.append(k)

    for g in range(2):
        for c in range(8):
            dma("sp", xt[g][:, c, :], d_x[g][:, c, :], [], [Bx[g][c]])
    dma("sp", cnd[:], d_cnd[:], [], [Bcnd])
    dma("sp", rope[:].rearrange("p a t -> p (a t)"), d_rope[:], [], [Brope])
    dma("sp", cmatf[:].rearrange("p a t -> p (a t)"), d_cmat[:], [], [Bcmat])
    dma("sp", msk[:], d_msk[:], [], [Bmsk])
    cp(cmat[:], cmatf[:], [Bcmat], [Bcmat])
    ones_b = cmat[:, 0, :]
    blk64 = cmat[:, 1, :]
    ident = cmat[:, 2, :]
    permd = cmat[:, 3, :]
    permg = cmat[:, 4, :]
    ones_f = cmatf[:, 0, :]
    act(scnd[:], cnd[:], AF.Silu, [Bcnd], [Bscnd])
    memset(vaP[:, :, :, 64:128], 1.0, [BvaP], eng="pool")
    memset(vaSg[:, :, :, 64:128], 1.0, [Bvgc, Bvgg], eng="pool")
    memset(ppad[0][:], 0.0, [Bpp[0]], eng="pool")
    memset(upad[0][:], 0.0, [Bup[0]], eng="pool")

    bankrr = [0]

    def nb():
        b = bankrr[0] % 4
        bankrr[0] += 1
        return b

    def sumsq_pre(g):
        for c in range(8):
            k = c % 2
            act(sqr[k][:], xt[g][:, c, :], AF.Square, [Bx[g][c]], [Bsqr[k]])
            mm(Bps[6], ps[6][:], Bcmat, ones_b, Bsqr[k], sqr[k][:], c == 0, c == 7)

    def pre_norm_gen(g, i, l=None, hb_=None):
        l = cur[0] if l is None else l
        hdst, Bhd = (h1, Bh1) if hb_ is None else hb_
        for c in range(8):
            k = c % 2
            act(sqr[k][:], xt[g][:, c, :], AF.Square, [Bx[g][c]], [Bsqr[k]])
            if c > 0:
                mm(Bps[6], ps[6][:], Bcmat, ones_b, Bsqr[1 - k], sqr[1 - k][:], c == 1, False)
            yield
        mm(Bps[6], ps[6][:], Bcmat, ones_b, Bsqr[1], sqr[1][:], False, True)
        rstd, Brstd = rstd_t, Brstd_t
        rsqrt_chain(rstd[:], Brstd, ps[6][:], 1.0 / D, [Bps[6]])
        for c in range(8):
            t_, Bt = nt()
            tt(t_[:], xt[g][:, c, :], rstd[:], ALU.mult, [Bx[g][c], Brstd], [Bt])
            act(hdst[:, c, :], t_[:], AF.Identity, [Bt, BAG_l[l][i], Bmod_l[l][i]], [Bhd[c]],
                scale=Amod_l[l][:, g, i, c:c + 1], bias=modt_l[l][:, (3 * i) * 8 + c, g:g + 1])

    def pre_norm(g, i, l=None, hb_=None):
        for _ in pre_norm_gen(g, i, l, hb_):
            pass

    def evac_y(g, oc, bank):
        act(yt[g][:, oc, :], ps[bank][:], AF.Copy, [Bps[bank]], [By[g][oc]])
        k = oc % 2
        act(sqr[k][:], ps[bank][:], AF.Square, [Bps[bank]], [Bsqr[k]])
        pend_sq.append((oc, k))

    pend_sq = []

    def flush_sq(everything=False):
        while pend_sq and (everything or len(pend_sq) > 0):
            oc, k = pend_sq.pop(0)
            mm(Bps[6], ps[6][:], Bcmat, ones_b, Bsqr[k], sqr[k][:], oc == 0, oc == 7)

    def post_norm(g, i, l=None):
        l = cur[0] if l is None else l
        rstd, Brstd = rstd_t, Brstd_t
        rsqrt_chain(rstd[:], Brstd, ps[6][:], 1.0 / D, [Bps[6]])
        for c in range(8):
            t_, Bt = nt()
            stt(t_[:], yt[g][:, c, :], Gmod_l[l][:, g, i, c:c + 1], rstd[:], ALU.mult, ALU.mult, [By[g][c], BGG_l[l][i], Brstd], [Bt])
            tt(xt[g][:, c, :], xt[g][:, c, :], t_[:], ALU.add, [Bx[g][c], Bt], [Bx[g][c]])

    def ffn_chain(jobs, hook=None):
        hsel = lambda g: (h1, Bh1) if g == 0 else (hx, Bhx)
        l0, w0, i0, g0 = jobs[0]
        pre_norm(g0, i0, l0, hsel(g0))
        for jn, (l, which, i, g) in enumerate(jobs):
            hsrc, Bhs = hsel(g)
            cnt = 0
            png = None
            if jn + 1 < len(jobs):
                ln, wn, in_, gn = jobs[jn + 1]
                png = pre_norm_gen(gn, in_, ln, hsel(gn))
            for f in range(11):
                k = wload(d_wgu[l, which, f])
                w = wsl[k][:].rearrange("p (a c n) -> p a c n", a=2, c=8)
                for j in range(2):
                    fc = 2 * f + j
                    bg, bu = (cnt % 2) * 2, (cnt % 2) * 2 + 1
                    cnt += 1
                    for c in range(8):
                        mm(Bps[bg], ps[bg][:], Bws[k], w[:, 0, c, j * 128:(j + 1) * 128], Bhs[c], hsrc[:, c, :], c == 0, c == 7)
                    for c in range(8):
                        mm(Bps[bu], ps[bu][:], Bws[k], w[:, 1, c, j * 128:(j + 1) * 128], Bhs[c], hsrc[:, c, :], c == 0, c == 7)
                    sk = cnt % 2
                    act(qmr[sk][:], ps[bg][:], AF.Silu, [Bps[bg]], [Bqmr[sk]])
                    tt(hmid1[:, fc, :], qmr[sk][:], ps[bu][:], ALU.mult, [Bqmr[sk], Bps[bu]], [Bhm1[fc]])
                wfree(k)
                if hook is not None:
                    hook()
                if png is not None and f >= 2:
                    next(png, None)
            if png is not None:
                for _ in png:
                    pass
            for oc in range(8):
                k = wload(d_wdn[l, which, oc], NFC * 128)
                w = wsl[k][:, 0:NFC * 128].rearrange("p (f n) -> p f n", f=NFC)
                bank = 4 + (oc % 2)
                for fc in range(NFC):
                    mm(Bps[bank], ps[bank][:], Bws[k], w[:, fc, :], Bhm1[fc], hmid1[:, fc, :], fc == 0, fc == NFC - 1)
                wfree(k)
                flush_sq()
                evac_y(g, oc, bank)
            flush_sq(True)
            post_norm(g, i, l)

    def ffn(l, which, i, g, hook=None):
        pre_norm(g, i)
        cnt = 0
        for f in range(11):
            k = wload(d_wgu[l, which, f])
            w = wsl[k][:].rearrange("p (a c n) -> p a c n", a=2, c=8)
            for j in range(2):
                fc = 2 * f + j
                bg, bu = (cnt % 2) * 2, (cnt % 2) * 2 + 1
                cnt += 1
                for c in range(8):
                    mm(Bps[bg], ps[bg][:], Bws[k], w[:, 0, c, j * 128:(j + 1) * 128], Bh[g][c], ht[g][:, c, :], c == 0, c == 7)
                for c in range(8):
                    mm(Bps[bu], ps[bu][:], Bws[k], w[:, 1, c, j * 128:(j + 1) * 128], Bh[g][c], ht[g][:, c, :], c == 0, c == 7)
                sk = cnt % 2
                act(qmr[sk][:], ps[bg][:], AF.Silu, [Bps[bg]], [Bqmr[sk]])
                tt(hmid1[:, fc, :], qmr[sk][:], ps[bu][:], ALU.mult, [Bqmr[sk], Bps[bu]], [Bhm1[fc]])
            wfree(k)
            if hook is not None:
                hook()
        for oc in range(8):
            k = wload(d_wdn[l, which, oc], NFC * 128)
            w = wsl[k][:, 0:NFC * 128].rearrange("p (f n) -> p f n", f=NFC)
            bank = 4 + (oc % 2)
            for fc in range(NFC):
                mm(Bps[bank], ps[bank][:], Bws[k], w[:, fc, :], Bhm1[fc], hmid1[:, fc, :], fc == 0, fc == NFC - 1)
            wfree(k)
            evac_y(g, oc, bank)
        post_norm(g, i)

    def ada_gen(l):
        bada, modt, gpre, gpost, Amod, Gmod = bada_l[l], modt_l[l], gpre_l[l], gpost_l[l], Amod_l[l], Gmod_l[l]
        Bbada, Bmod, Bgp, BAG = Bbada_l[l], Bmod_l[l], Bgp_l[l], BAG_l[l]
        dma("sp", bada[:], d_bada[l], [], [Bbada])
        dma("sp", gpre[:].rearrange("p a c -> p (a c)"), d_gpre[l], [], [Bgp])
        dma("sp", gpost[:].rearrange("p a c -> p (a c)"), d_gpost[l], [], [Bgp])
        for pa in range(18):
            k = wload(d_wada[l, pa])
            w = wsl[k][:].rearrange("p (c n) -> p c n", c=8)
            for cc in range(4):
                ci = pa * 4 + cc
                for c in range(8):
                    mm(Bps[7], ps[7][:, 2 * ci:2 * ci + 2], Bws[k], w[:, c, cc * 128:(cc + 1) * 128], Bscnd, scnd[:, c, :], c == 0, c == 7)
            wfree(k)
            if pa % 6 == 3:
                i = pa // 6
                lo, hi = (3 * i) * 16, (3 * i + 2) * 16
                tt(modt[:].rearrange("p a k -> p (a k)")[:, lo:hi], ps[7][:, lo:hi], bada[:, lo:hi], ALU.add, [Bps[7], Bbada], [Bmod[i]])
                for g in range(2):
                    stt(Amod[:, g, i, :], modt[:, (3 * i + 1) * 8:(3 * i + 2) * 8, g], 1.0, gpre[:, i, :], ALU.add, ALU.mult,
                        [Bmod[i], Bgp], [BAG[i]])
            if pa % 6 == 5:
                i = pa // 6
                lo, hi = (3 * i + 2) * 16, (3 * i + 3) * 16
                tt(modt[:].rearrange("p a k -> p (a k)")[:, lo:hi], ps[7][:, lo:hi], bada[:, lo:hi], ALU.add, [Bps[7], Bbada], [Bmodg_l[l][i]])
                for g in range(2):
                    stt(Gmod[:, g, i, :], modt[:, (3 * i + 2) * 8:(3 * i + 3) * 8, g], 1.0 if i == 1 else 0.5, gpost[:, i, :],
                        ALU.mult, ALU.mult, [Bmodg_l[l][i], Bgp], [BGG_l[l][i]])
            if pa < 17:
                yield

    def layer_consts(l):
        dma("sp", sconvw[:].rearrange("p a c -> p (a c)"), d_sconv[l], [], [Bsmall])
        dma("sp", ccw[:].rearrange("p a c -> p (a c)"), d_ccw[l], [], [Bsmall])
        dma("sp", ccv[:].rearrange("p a c -> p (a c)"), d_ccv[l], [], [Bsmall])
        dma("sp", subln[:], d_subln[l], [], [Bsmall])
        dma("sp", qkn[:], d_qkn[l], [], [Bsmall])
        dma("sp", kng[:], d_kng[l], [], [Bsmall])
        dma("sp", lamv[:].rearrange("p a c -> p (a c)"), d_lam[l], [], [Blam])
        dma("pool", ccpw[:].rearrange("p a c -> p (a c)"), d_ccpw[l], [], [Bccpw])
        dma("pool", kdS[:, :, 0:256], d_ckd[l].rearrange("p (c t) -> p c t", c=2), [], [Bkctx])
        dma("pool", kgS[:, 0:256], d_ckg[l], [], [Bkctx])
        dma("pool", vaSg[:, 0:2, :, 0:64], d_cvg[l].rearrange("p (c h d) -> p c h d", c=2, h=2), [], [Bvgc])
        lam_init = 0.8 - 0.6 * math.exp(-0.3 * l)
        tt(lamv[:, 0, :], lamv[:, 0, :], lamv[:, 1, :], ALU.mult, [Blam], [Blam])
        tt(lamv[:, 2, :], lamv[:, 2, :], lamv[:, 3, :], ALU.mult, [Blam], [Blam])
        S.add("dve", lambda e: e.tensor_reduce(lamt[:, 2:3], lamv[:, 0, :], mybir.AxisListType.X, ALU.add), reads=[Blam], writes=[Blam])
        S.add("dve", lambda e: e.tensor_reduce(lamt[:, 3:4], lamv[:, 2, :], mybir.AxisListType.X, ALU.add), reads=[Blam], writes=[Blam])
        act(lamt[:, 4:6], lamt[:, 2:4], AF.Exp, [Blam], [Blam])
        tt(lamt[:, 6:7], lamt[:, 5:6], lamt[:, 4:5], ALU.subtract, [Blam], [Blam])
        ts(lamt[:, 0:1], lamt[:, 6:7], -lam_init, ALU.add, [Blam], [Blam])
        ts(lamt[:, 1:2], subln[:], 1.0 - lam_init, ALU.mult, [Bsmall, Blam], [Blam])

    def ctx_v_diff(l):
        memset(vaSd[:, :, :, 64:128], 1.0, BvaSd, eng="pool")
        dma("pool", vaSd[:, 0:2, :, 0:64], d_cvd[l].rearrange("p (c h d) -> p c h d", c=2, h=4), [], BvaSd)

    def proj_fm(g, k, col, bank):
        w = wsl[k][:].rearrange("p (c n) -> p c n", c=8)
        for c in range(8):
            mm(Bps[bank], ps[bank][:], Bws[k], w[:, c, col:col + 128], Bh[g][c], ht[g][:, c, :], c == 0, c == 7)

    def proj_tm(g, k, col, ncol, tile_, bank):
        w = wsl[k][:].rearrange("p (c n) -> p c n", c=8)
        for c in range(8):
            mm(Bps[bank], ps[bank][:, 0:ncol], Bh[g][c], ht[g][:, c, tile_ * 128:(tile_ + 1) * 128], Bws[k], w[:, c, col:col + ncol], c == 0, c == 7)

    def rope_apply(src, src_reads, kind, dst, Bdst):
        ci, si, pm = (0, 1, permd) if kind == "d" else (2, 3, permg)
        act(qbf[:], src, AF.Copy, src_reads, [Bqbf])
        mm(Bps[6], ps[6][:], Bcmat, pm, Bqbf, qbf[:], True, True)
        t1, B1 = nt()
        t2, B2 = nt()
        tt(t1[:], src, rope[:, ci, :], ALU.mult, src_reads + [Brope], [B1])
        tt(t2[:], ps[6][:], rope[:, si, :], ALU.mult, [Bps[6], Brope], [B2])
        tt(dst, t1[:], t2[:], ALU.add, [B1, B2], [Bdst])

    def headnorm(bank, col):
        act(qbf[:], ps[bank][:], AF.Square, [Bps[bank]], [Bqbf])
        mm(Bps[7], ps[7][:], Bcmat, blk64, Bqbf, qbf[:], True, True)
        r_, Br = nt()
        rsqrt_chain(r_[:], Br, ps[7][:], 1.0 / 64, [Bps[7]])
        o_, Bo = nt()
        stt(o_[:], ps[bank][:], qkn[:, col:col + 1], r_[:], ALU.mult, ALU.mult, [Bps[bank], Bsmall, Br], [Bo])
        return o_[:], [Bo]

    def mix_in(l, g):
        for piece in range(5):
            if piece > 0:
                wfree(k)
            k = wload(d_wmi[l, piece])
            if piece == 0:
                for c in range(2):
                    b = nb(); proj_fm(g, k, c * 128, b)
                    ac, Bac = nt()
                    act(ac[:], ps[b][:], AF.Copy, [Bps[b]], [Bac])
                    b2 = nb(); proj_fm(g, k, 256 + c * 128, b2)
                    if g == 0:
                        tt(ppad[0][:, c, :, 1:257], ac[:].rearrange("p (s t) -> p s t", s=2), ps[b2][:].rearrange("p (s t) -> p s t", s=2),
                           ALU.mult, [Bac, Bps[b2]], [Bpp[0]])
                    else:
                        tt(ppad[1][:, c, 1:513], ac[:], ps[b2][:], ALU.mult, [Bac, Bps[b2]], [Bpp[1]])
            elif piece == 1:
                for c in range(2):
                    b = nb(); proj_fm(g, k, c * 128, b)
                    act(abt[g][:, c, :], ps[b][:], AF.Copy, [Bps[b]], Bab[g])
                for c in range(2):
                    b = nb(); proj_fm(g, k, 256 + c * 128, b)
                    if g == 0:
                        act(qpk[g][:, c, :], ps[b][:], AF.Copy, [Bps[b]], [Bqpk[g][c]])
                    else:
                        rope_apply(ps[b][:], [Bps[b]], "d", qpk[g][:, c, :], Bqpk[g][c])
            elif piece == 2:
                for c in range(2):
                    b = nb(); proj_fm(g, k, c * 128, b)
                    if g == 0:
                        act(kdP[:, c, :], ps[b][:], AF.Copy, [Bps[b]], [Bkd[0]])
                    else:
                        rope_apply(ps[b][:], [Bps[b]], "d", kdS[:, c, 256:256 + T], Bkd[1])
                for t_ in range(4):
                    b = nb()
                    rows = slice(t_ * 128, (t_ + 1) * 128)
                    if g == 0:
                        proj_tm(g, k, 0, 512, t_, b)
                        st, Bst = nt()
                        act(st[:], ps[b][:], AF.Copy, [Bps[b]], [Bst])
                        cp(vaP[:, t_, 0:4, 0:64], ps[b][:, 256:512].rearrange("p (h d) -> p h d", h=4), [Bps[b]], [BvaP])
                        dma("sp", d_ndk[l, rows, :], st[:, 0:256], [Bst], [])
                        dma("sp", d_ndv[l, rows, :], st[:, 256:512], [Bst], [])
                    else:
                        proj_tm(g, k, 256, 256, t_, b)
                        act(vtok[:, t_, 0:256], ps[b][:, 0:256], AF.Copy, [Bps[b]], Bcat1)
            elif piece == 3:
                for c in range(2):
                    b = nb(); proj_fm(g, k, 256 + c * 128, b)
                    sg, Bsg = nt()
                    act(sg[:], ps[b][:], AF.Sigmoid, [Bps[b]], [Bsg])
                    b2 = nb(); proj_fm(g, k, c * 128, b2)
                    if g == 0:
                        tt(upad[0][:, c, :, 15:271], sg[:].rearrange("p (s t) -> p s t", s=2),
                           ps[b2][:].rearrange("p (s t) -> p s t", s=2), ALU.mult, [Bsg, Bps[b2]], [Bup[0]])
                    else:
                        tt(upad[1][:, c, 15:527], sg[:], ps[b2][:], ALU.mult, [Bsg, Bps[b2]], [Bup[1]])
            else:
                for c in range(2):
                    b = nb(); proj_fm(g, k, c * 128, b)
                    src, Bsrc = headnorm(b, 0)
                    if g == 0:
                        cp(qpk[g][:, 2 + c, :], src, Bsrc, [Bqpk[g][2 + c]])
                    else:
                        rope_apply(src, Bsrc, "g", qpk[g][:, 2 + c, :], Bqpk[g][2 + c])
                b = nb(); proj_fm(g, k, 256, b)
                src, Bsrc = headnorm(b, 1)
                if g == 0:
                    cp(kgP[:], src, Bsrc, [Bkg[0]])
                else:
                    rope_apply(src, Bsrc, "g", kgS[:, 256:256 + T], Bkg[1])
                for t_ in range(4):
                    b = nb()
                    rows = slice(t_ * 128, (t_ + 1) * 128)
                    if g == 0:
                        proj_tm(g, k, 256, 256, t_, b)
                        st, Bst = nt()
                        act(st[:, 128:256], ps[b][:, 128:256], AF.Copy, [Bps[b]], [Bst])
                        cp(vaP[:, t_, 4:6, 0:64], ps[b][:, 128:256].rearrange("p (h d) -> p h d", h=2), [Bps[b]], [BvaP])
                        act(st[:, 256:384], ps[b][:, 0:128], AF.Square, [Bps[b]], [Bst])
                        S.add("dve", lambda e, st=st: e.tensor_reduce(ktm[:, 0:2], st[:, 256:384].rearrange("p (h d) -> p h d", h=2),
                                                                      mybir.AxisListType.X, ALU.add), reads=[Bst], writes=[Bktm])
                        rsqrt_chain(ktm[:, 2:4], Bktm, ktm[:, 0:2], 1.0 / 64, [Bktm])
                        for h in range(2):
                            stt(st[:, h * 64:(h + 1) * 64], ps[b][:, h * 64:(h + 1) * 64], ktm[:, 2 + h:3 + h], kng[:],
                                ALU.mult, ALU.mult, [Bps[b], Bktm, Bsmall], [Bst])
                        dma("sp", d_ngk[l, rows, :], st[:, 0:128], [Bst], [])
                        dma("sp", d_ngv[l, rows, :], st[:, 128:256], [Bst], [])
                    else:
                        proj_tm(g, k, 384, 128, t_, b)
                        act(vtok[:, t_, 256:384], ps[b][:, 0:128], AF.Copy, [Bps[b]], Bcat1)
            if substop is not None and piece == substop:
                raise _Stop()
        wfree(k)

    def exchange(l):
        for c in range(2):
            cp(edge[:, c, 0:15], upad[1][:, c, 15:30], [Bup[1]], [Bedge])
            cp(edge[:, c, 15:16], ppad[1][:, c, 1:2], [Bpp[1]], [Bedge])
            cp(edge[:, c, 16:31], upad[1][:, c, 512:527], [Bup[1]], [Bedge])
            cp(edge[:, c, 31:32], ppad[1][:, c, 512:513], [Bpp[1]], [Bedge])
        bo = d_bounce[l]
        dma("sp", bo[0:256, :].rearrange("(c p) t -> p c t", p=128), kdS[:, :, 256:256 + T], [Bkd[1]], [B_bounce[l]])
        dma("sp", bo[256:384, :], kgS[:, 256:256 + T], [Bkg[1]], [B_bounce[l]])
        vview = bo[384:768, :].rearrange("r c -> (r c)").rearrange("(tt p v) -> p tt v", p=128, v=384)
        dma("sp", vview, vtok, Bcat1, [B_bounce[l]])
        eview = bo[768:784, :].rearrange("r c -> (r c)").rearrange("(c p e) -> p c e", p=128, e=32)
        dma("sp", eview, edge[:], [Bedge], [B_bounce[l]])
        S.add("pool", lambda e: e.collective_compute("AllGather", ALU.bypass, replica_groups=[[0, 1, 2, 3], [4, 5, 6, 7]],
                                                     ins=[bo.opt()], outs=[d_gath[l].opt()]),
              reads=[B_bounce[l]], writes=[B_gath[l]], kind="cc")
    def exchange_recv(l):
        ga = d_gath[l]
        for r in range(4):
            base = r * BROWS
            ks = 256 + r * T
            dma("sp", kdS[:, :, ks:ks + T], ga[base:base + 256, :].rearrange("(c p) t -> p c t", p=128), [B_gath[l]], [Bkgath])
            dma("sp", kgS[:, ks:ks + T], ga[base + 256:base + 384, :], [B_gath[l]], [Bkgath])
            vv = ga[base + 384:base + 768, :].rearrange("r c -> (r c)").rearrange("(tt p v) -> p tt v", p=128, v=384)
            for t_ in range(4):
                ch = 2 + r * 4 + t_
                dma("sp", vaSd[:, ch, :, 0:64], vv[:, t_, 0:256].rearrange("p (h d) -> p h d", h=4), [B_gath[l]], BvaSd)
                dma("sp", vaSg[:, ch, :, 0:64], vv[:, t_, 256:384].rearrange("p (h d) -> p h d", h=2), [B_gath[l]], [Bvgg])
            ev = ga[base + 768:base + 784, :].rearrange("r c -> (r c)").rearrange("(c p e) -> p c e", p=128, e=32)
            dma("sp", egt[:, :, r, :], ev, [B_gath[l]], [Begt])

    def halos():
        for side, lo, mcol in ((0, 16, 8), (1, 0, 12)):
            ts(halo[:, :, side, :], egt[:, :, 0, lo:lo + 16], msk[:, mcol:mcol + 1], ALU.mult, [Begt, Bmsk], [Bhalo])
            for r in range(1, 4):
                stt(halo[:, :, side, :], egt[:, :, r, lo:lo + 16], msk[:, mcol + r:mcol + r + 1], halo[:, :, side, :], ALU.mult, ALU.add,
                    [Begt, Bmsk, Bhalo], [Bhalo])
        cp(upad[1][:, :, 0:15], halo[:, :, 0, 0:15], [Bhalo], [Bup[1]])
        cp(ppad[1][:, :, 0:1], halo[:, :, 0, 15:16], [Bhalo], [Bpp[1]])
        cp(upad[1][:, :, 527:542], halo[:, :, 1, 0:15], [Bhalo], [Bup[1]])
        cp(ppad[1][:, :, 513:514], halo[:, :, 1, 15:16], [Bhalo], [Bpp[1]])

    def attention(g, hook=None):
        nq, nkc = 512, (2 if g == 0 else 18)
        items = [(0, hm) for hm in range(12)]
        state = {"on_prev": None}

        def qinfo(hm):
            if hm < 8:
                return qpk[g][:, hm // 4, :], Bqpk[g][hm // 4], hm % 4, hm // 2, 32 ** -0.5
            sl = hm - 8
            return qpk[g][:, 2 + sl // 2, :], Bqpk[g][2 + sl // 2], 4 + sl % 2, 4 + sl % 2, 64 ** -0.5

        def emit_mask(idx):
            s_, hm = items[idx]
            qsrc, Bq, mcol, h, scale = qinfo(hm)
            qk = idx % 2
            ts(qmr[qk][:], qsrc, msk[:, mcol:mcol + 1], ALU.mult, [Bq, Bmsk], [Bqmr[qk]])

        def kv(s_, hm, h, kc):
            if g == 0:
                ksl = slice(s_ * 256 + kc * 128, s_ * 256 + (kc + 1) * 128)
                kT = kdP[:, hm // 4, ksl] if hm < 8 else kgP[:, ksl]
                Bk = [Bkd[0]] if hm < 8 else [Bkg[0]]
                return kT, Bk, vaP[:, s_ * 2 + kc, h, :], [BvaP]
            ksl = slice(kc * 128, (kc + 1) * 128)
            kT = kdS[:, hm // 4, ksl] if hm < 8 else kgS[:, ksl]
            Bk = [Bkctx] if kc < 2 else [Bkgath]
            if h < 4:
                return kT, Bk, vaSd[:, kc, h, :], BvaSd
            return kT, Bk, vaSg[:, kc, h - 4, :], [Bvgc if kc < 2 else Bvgg]

        def finalize(idx):
            s_, hm = items[idx]
            qs = slice(s_ * nq, (s_ + 1) * nq)
            qsrc, Bq, mcol, h, scale = qinfo(hm)
            ob = 4 + idx % 2
            r_, Br = nt()
            cp(r_[0:64, 0:nq], ps[ob][64:128, 0:nq], [Bps[ob]], [Br])
            act(r_[0:64, 0:nq], r_[0:64, 0:nq], AF.Ln, [Br], [Br])
            act(r_[0:64, 0:nq], r_[0:64, 0:nq], AF.Exp, [Br], [Br], scale=-1.0)
            if hm < 8:
                on, Bon = nt()
                tt(on[0:64, 0:nq], ps[ob][0:64, 0:nq], r_[0:64, 0:nq], ALU.mult, [Bps[ob], Br], [Bon])
                if hm % 2 == 0:
                    state["on_prev"] = (on, Bon)
                else:
                    on0, Bon0 = state["on_prev"]
                    d_, Bd = nt()
                    stt(d_[0:64, 0:nq], on[0:64, 0:nq], lamt[0:64, 0:1], on0[0:64, 0:nq], ALU.mult, ALU.add, [Bon0, Bon, Blam], [Bd])
                    act(qbf[0:64, 0:nq], d_[0:64, 0:nq], AF.Square, [Bd], [Bqbf])

                    def cont(d_=d_, Bd=Bd, h=h, qs=qs):
                        mm(Bps[6], ps[6][0:64, 0:nq], Bcmat, ones_b[0:64, 0:64], Bqbf, qbf[0:64, 0:nq], True, True)
                        rr, Brr = nt()
                        rsqrt_chain(rr[0:64, 0:nq], Brr, ps[6][0:64, 0:nq], 1.0 / 64, [Bps[6]])
                        stt(hb[g][:, h, qs], d_[0:64, 0:nq], lamt[0:64, 1:2], rr[0:64, 0:nq], ALU.mult, ALU.mult, [Bd, Blam, Brr], [Bhb[g][h]])
                    return cont
            else:
                sl = hm - 8
                tt(hb[g][:, 4 + sl, qs], ps[ob][0:64, 0:nq], r_[0:64, 0:nq], ALU.mult, [Bps[ob], Br], [Bhb[g][4 + sl]])
            return None

        contB = [None]

        def fin_step(idx):
            if contB[0] is not None:
                contB[0]()
                contB[0] = None
            if idx >= 0:
                contB[0] = finalize(idx)

        NPT = len(pT)
        stepc = [0]
        emit_mask(0)
        for idx, (s_, hm) in enumerate(items):
            qs = slice(s_ * nq, (s_ + 1) * nq)
            qsrc, Bq, mcol, h, scale = qinfo(hm)
            qk = idx % 2
            ob = 4 + idx % 2
            if idx + 1 < len(items):
                emit_mask(idx + 1)
            pend = []
            if g == 0:
                pks = []
                for kc in range(2):
                    sbk = stepc[0] % 4
                    pk = stepc[0] % NPT
                    stepc[0] += 1
                    pks.append(pk)
                    for s2 in range(2):
                        kT, Bk, va, Bv = kv(s2, hm, h, kc)
                        cs = slice(s2 * 256, (s2 + 1) * 256)
                        S.add("pe", lambda e, o=ps[sbk][:, cs], a=kT, b_=qmr[qk][:, cs]: e.matmul(o, a, b_, start=True, stop=True),
                              reads=Bk + [Bqmr[qk]], writes=[Bps[sbk]])
                    act(pT[pk][:], ps[sbk][:], AF.Exp, [Bps[sbk]], [BpT[pk]], scale=scale)
                fin_step(idx - 1)
                for s2 in range(2):
                    cs = slice(s2 * 256, (s2 + 1) * 256)
                    for kc in range(2):
                        kT, Bk, va, Bv = kv(s2, hm, h, kc)
                        S.add("pe", lambda e, o=ps[ob][:, cs], a=va, b_=pT[pks[kc]][:, cs], st_=(kc == 0), sp_=(kc == 1): e.matmul(o, a, b_, start=st_, stop=sp_),
                              reads=Bv + [BpT[pks[kc]]], writes=[Bps[ob]])
                continue

            def score(kc):
                kT, Bk, va, Bv = kv(s_, hm, h, kc)
                sbk = stepc[0] % 4
                pk = stepc[0] % NPT
                stepc[0] += 1
                S.add("pe", lambda e, o=ps[sbk][:, 0:nq], a=kT, b_=qmr[qk][:, qs]: e.matmul(o, a, b_, start=True, stop=True),
                      reads=Bk + [Bqmr[qk]], writes=[Bps[sbk]])
                act(pT[pk][:, 0:nq], ps[sbk][:, 0:nq], AF.Exp, [Bps[sbk]], [BpT[pk]], scale=scale)
                pend.append((kc, pk, va, Bv))

            def pv():
                kc, pk, va, Bv = pend.pop(0)
                st_, sp_ = (kc == 0), (kc == nkc - 1)
                S.add("pe", lambda e, o=ps[ob][:, 0:nq], a=va, b_=pT[pk][:, 0:nq], st_=st_, sp_=sp_: e.matmul(o, a, b_, start=st_, stop=sp_),
                      reads=Bv + [BpT[pk]], writes=[Bps[ob]])

            DEPTH_PIPE = 2
            for kc in range(nkc):
                score(kc)
                if len(pend) > DEPTH_PIPE:
                    pv()
            fin_step(idx - 1)
            while pend:
                pv()
            if hook is not None:
                hook()
        fin_step(len(items) - 1)
        fin_step(-1)

    def diag(wcol):
        k = dgc[0] % NDG
        dgc[0] += 1
        ts(dgr[k][:], ident, wcol, ALU.mult, [Bcmat, Bsmall], [Bdgr[k]])
        return dgr[k][:], Bdgr[k]

    def convs(g):
        nseq, n = (2, 256) if g == 0 else (1, 512)
        ucs = []
        for c in range(2):
            b = nb()
            for k in range(31):
                dg, Bdg = diag(ccw[:, c, k:k + 1])
                for s in range(nseq):
                    rhs = upad[0][:, c, s, k:k + n] if g == 0 else upad[1][:, c, k:k + n]
                    S.add("pe", lambda e, o=ps[4 + s][:, 0:n], a=dg, r=rhs, st_=(k == 0), sp_=(k == 30): e.matmul(o, a, r, start=st_, stop=sp_),
                          reads=[Bdg, Bup[g]], writes=[Bps[4 + s]])
            uc, Buc = nt()
            for s in range(nseq):
                act(uc[:, s * n:(s + 1) * n], ps[4 + s][:, 0:n], AF.Identity, [Bps[4 + s], Bsmall], [Buc], bias=ccv[:, 0, c:c + 1])
            ucq, Bucq = nt()
            act(ucq[:], uc[:], AF.Square, [Buc], [Bucq])
            ucs.append((uc, Buc, ucq, Bucq))
            for k in range(3):
                dg, Bdg = diag(sconvw[:, c, k:k + 1])
                for s in range(nseq):
                    rhs = ppad[0][:, c, s, k:k + n] if g == 0 else ppad[1][:, c, k:k + n]
                    S.add("pe", lambda e, o=ps[6 + s][:, 0:n], a=dg, r=rhs, st_=(k == 0), sp_=(k == 2): e.matmul(o, a, r, start=st_, stop=sp_),
                          reads=[Bdg, Bpp[g]], writes=[Bps[6 + s]])
            for s in range(nseq):
                tt(cat[g][:, c, s * n:(s + 1) * n], abt[g][:, c, s * n:(s + 1) * n], ps[6 + s][:, 0:n], ALU.mult, Bab[g] + [Bps[6 + s]], [Bcat[g][c]])
        b1 = nb()
        for c in range(2):
            mm(Bps[b1], ps[b1][:], Bcmat, ones_f, ucs[c][1], ucs[c][0][:], c == 0, c == 1)
        b2 = nb()
        for c in range(2):
            mm(Bps[b2], ps[b2][:], Bcmat, ones_f, ucs[c][3], ucs[c][2][:], c == 0, c == 1)
        mean, Bmean = nt()
        ts(mean[:], ps[b1][:], 1.0 / 256, ALU.mult, [Bps[b1]], [Bmean])
        var, Bvar = nt()
        tt(var[:], mean[:], mean[:], ALU.mult, [Bmean], [Bvar])
        stt(var[:], ps[b2][:], 1.0 / 256, var[:], ALU.mult, ALU.subtract, [Bps[b2], Bvar], [Bvar])
        rsqrt_chain(var[:], Bvar, var[:], 1.0, [Bvar])
        for c in range(2):
            uc, Buc, ucq, Bucq = ucs[c]
            tt(ucq[:], uc[:], mean[:], ALU.subtract, [Buc, Bmean], [Bucq])
            tt(ucq[:], ucq[:], var[:], ALU.mult, [Bucq, Bvar], [Bucq])
            act(vb[:, c, :], ucq[:], AF.Silu, [Bucq, Bsmall], [Bvb], scale=ccv[:, 1, c:c + 1], bias=ccv[:, 2, c:c + 1])
        for oc in range(2):
            b = nb()
            for c in range(2):
                mm(Bps[b], ps[b][:], Bccpw, ccpw[:, c, oc * 128:(oc + 1) * 128], Bvb, vb[:, c, :], c == 0, c == 1)
            act(cat[g][:, 2 + oc, :], ps[b][:], AF.Copy, [Bps[b]], [Bcat[g][2 + oc]])

    def mix_out(l, g):
        wk = [wload(d_wmo[l, a]) for a in range(3)]
        pieces = [(cat[g][:, 0, :], Bcat[g][0], 128), (cat[g][:, 1, :], Bcat[g][1], 128)]
        pieces += [(hb[g][:, h, :], Bhb[g][h], 64) for h in range(4)]
        pieces += [(cat[g][:, 2, :], Bcat[g][2], 128), (cat[g][:, 3, :], Bcat[g][3], 128)]
        pieces += [(hb[g][:, 4 + h, :], Bhb[g][4 + h], 64) for h in range(4)]
        for oc in range(8):
            bank = 4 + (oc % 2)
            for pi, (ap_, B_, kk) in enumerate(pieces):
                k = wk[pi // 4]
                w = wsl[k][:].rearrange("p (a n) -> p a n", a=4)
                mm(Bps[bank], ps[bank][:], Bws[k], w[0:kk, pi % 4, oc * 128:(oc + 1) * 128], B_, ap_[0:kk, :], pi == 0, pi == 11)
            flush_sq()
            evac_y(g, oc, bank)
        flush_sq(True)
        for k in wk:
            wfree(k)
        post_norm(g, 1)

    def dumpx(idx, g):
        if DEBUG:
            dma("sp", d_dbg[idx].rearrange("p (c t) -> p c t", c=8), xt[g][:], Bx[g], [])

    stage = [0]

    def chk():
        stage[0] += 1
        if stop is not None and stage[0] >= stop:
            raise _Stop()

    def mixer(l):
        cur[0] = l
        pre_norm(1, 1)
        mix_in(l, 1)
        exchange(l)
        pre_norm(0, 1)
        mix_in(l, 0)
        attention(0)
        exchange_recv(l)
        convs(0)
        mix_out(l, 0)
        halos()
        hook = None
        gen = None
        if l + 1 < DEPTH:
            gen = ada_gen(l + 1)
            cnt_h = [0]

            def hook():
                n = 2 if cnt_h[0] < 6 else 1
                cnt_h[0] += 1
                for _ in range(n):
                    next(gen, None)
        attention(1, hook)
        if gen is not None:
            for _ in gen:
                pass
        convs(1)
        mix_out(l, 1)

    try:
        gen0 = ada_gen(0)
        for _ in range(4):
            next(gen0, None)
        layer_consts(0)
        ffn_chain([(0, 0, 0, 0), (0, 0, 0, 1)], hook=lambda: next(gen0, None))
        for _ in gen0:
            pass
        ctx_v_diff(0)
        chk()
        mixer(0)
        chk()
        layer_consts(1)
        ffn_chain([(0, 1, 2, 0), (0, 1, 2, 1), (1, 0, 0, 0), (1, 0, 0, 1)])
        ctx_v_diff(1)
        mixer(1)
        ffn_chain([(1, 1, 2, 0), (1, 1, 2, 1)])
    except _Stop:
        pass
    for g in range(2):
        for c in range(8):
            dma("sp", d_y[g][:, c, :], xt[g][:, c, :], [Bx[g][c]], [])

    print('sbuf remaining', nc.sbuf_bytes_remaining, flush=True)
    S.finalize()
    with nc.Block() as block:
        @block.tensor
        def _(e):
            S.replay("pe", e, sems)

        @block.scalar
        def _(e):
            S.replay("act", e, sems)

        @block.vector
        def _(e):
            S.replay("dve", e, sems)

        @block.gpsimd
        def _(e):
            S.replay("pool", e, sems)

        @block.sync
        def _(e):
            S.replay("sp", e, sems)
    es.close()
    return nc


def _fm(a):
    t = a.shape[0]
    return np.ascontiguousarray(a.T.reshape(8, 128, t).transpose(1, 0, 2))


def _unfm(a):
    t = a.shape[2]
    return np.ascontiguousarray(a.transpose(1, 0, 2).reshape(1024, t).T)


def _kpieces(w, ncols_piece):
    n = w.shape[1]
    npieces = n // ncols_piece
    a = w.reshape(8, 128, npieces, ncols_piece).transpose(2, 1, 0, 3)
    return np.ascontiguousarray(a.reshape(npieces, 128, 8 * ncols_piece))


def _vec_fm(v, nch):
    return np.ascontiguousarray(v.reshape(nch, 128).T)


def _rope_tables():
    theta = np.float32(10000.0)
    tabs = {}
    for name, dd in (("d", 32), ("g", 64)):
        nf = dd // 4
        inv = (theta ** (-(np.arange(nf, dtype=np.float32) / np.float32(nf)))).astype(np.float32)
        tabs[name] = (dd, nf, inv)
    return tabs


def _host_inputs(inp):
    f32 = np.float32
    shared = {}
    L = DEPTH
    shared["wada"] = np.stack([_kpieces(inp["w_ada"][l], 512) for l in range(L)])
    shared["gpre"] = np.stack([np.concatenate([_vec_fm(inp["norm_pre"][l, i], 8) for i in range(3)], axis=1) for l in range(L)])
    shared["gpost"] = np.stack([np.concatenate([_vec_fm(inp["norm_post"][l, i], 8) for i in range(3)], axis=1) for l in range(L)])
    wgu = np.zeros((L, 2, 11, 128, 4096), f32)
    wdn = np.zeros((L, 2, 8, 128, NFC * 128), f32)
    for l in range(L):
        for wi, (gn, un, dn) in enumerate((("ffn1_gate", "ffn1_up", "ffn1_down"), ("ffn2_gate", "ffn2_up", "ffn2_down"))):
            gp = _kpieces(inp[gn][l], 256).reshape(11, 128, 2048)
            up = _kpieces(inp[un][l], 256).reshape(11, 128, 2048)
            wgu[l, wi] = np.concatenate([gp, up], axis=2)
            wd = inp[dn][l]
            a = wd.reshape(NFC, 128, 8, 128).transpose(2, 1, 0, 3)
            wdn[l, wi] = a.reshape(8, 128, NFC * 128)
    shared["wgu"] = wgu
    shared["wdn"] = wdn
    wmi = np.zeros((L, 5, 128, 4096), f32)
    wmo = np.zeros((L, 3, 128, 4096), f32)
    for l in range(L):
        w = inp["w_mix_in"][l].copy()
        dq = w[:, 2048:2304].reshape(1024, 4, 64)
        w[:, 2048:2304] = dq[:, [0, 2, 1, 3], :].reshape(1024, 256)
        w[:, 0:768] = np.concatenate([w[:, 256:512], w[:, 512:768], w[:, 0:256]], axis=1)
        wmi[l] = _kpieces(w, 512)
        wo = inp["w_mix_out"][l]
        pieces = np.zeros((12, 128, 1024), f32)
        pieces[0] = wo[0:128]; pieces[1] = wo[128:256]
        for h in range(4):
            pieces[2 + h, 0:64] = wo[256 + h * 64:256 + (h + 1) * 64]
        pieces[6] = wo[512:640]; pieces[7] = wo[640:768]
        for slot, h in enumerate((0, 2, 1, 3)):
            pieces[8 + slot, 0:64] = wo[768 + h * 64:768 + (h + 1) * 64]
        wmo[l] = pieces.reshape(3, 4, 128, 1024).transpose(0, 2, 1, 3).reshape(3, 128, 4096)
    shared["wmi"] = wmi
    shared["wmo"] = wmo
    shared["sconvw"] = np.stack([np.ascontiguousarray(inp["sconv_w"][l].reshape(3, 2, 128).transpose(2, 1, 0)).reshape(128, 6) for l in range(L)])
    shared["ccw"] = np.stack([np.ascontiguousarray(inp["ccm_dw_w"][l].reshape(31, 2, 128).transpose(2, 1, 0)).reshape(128, 62) for l in range(L)])
    shared["ccv"] = np.stack([np.concatenate([_vec_fm(inp[n][l], 2) for n in ("ccm_dw_b", "ccm_ln_g", "ccm_ln_b")], axis=1) for l in range(L)])
    shared["ccpw"] = np.stack([np.ascontiguousarray(inp["ccm_pw"][l].reshape(2, 128, 256).transpose(1, 0, 2)).reshape(128, 512) for l in range(L)])
    shared["lamv"] = np.stack([np.tile(np.concatenate([inp[n][l] for n in ("diff_lq1", "diff_lk1", "diff_lq2", "diff_lk2")])[None, :], (128, 1)) for l in range(L)])
    shared["subln"] = np.stack([np.tile(inp["diff_subln"][l], 2).reshape(128, 1) for l in range(L)])
    shared["qkn"] = np.stack([np.stack([np.tile(inp["gqa_qnorm"][l], 2), np.tile(inp["gqa_knorm"][l], 2)], axis=1) for l in range(L)])
    shared["kng"] = np.stack([np.tile(inp["gqa_knorm"][l][None, :], (128, 1)) for l in range(L)])
    cm = np.zeros((128, 5, 128), f32)
    cm[:, 0, :] = 1.0
    cm[0:64, 1, 0:64] = 1.0; cm[64:128, 1, 64:128] = 1.0
    cm[:, 2, :] = np.eye(128, dtype=f32)
    for i in range(128):
        cm[i ^ 8, 3, i] = 1.0
        cm[i ^ 16, 4, i] = 1.0
    shared["cmat"] = cm.reshape(128, 640)
    for k in shared:
        shared[k] = np.ascontiguousarray(shared[k], dtype=f32)

    tabs = _rope_tables()
    maps = []
    for i in range(NCORES):
        b, j = i // 4, i % 4
        m = dict(shared)
        m["xp"] = _fm(inp["x_prompt"][2 * i:2 * i + 2].reshape(512, 1024))
        m["xs"] = _fm(inp["x_sample"][b, 512 * j:512 * (j + 1)])
        m["cnd"] = np.ascontiguousarray(np.stack([_vec_fm(inp["c_ctx"], 8), _vec_fm(inp["c"][b], 8)], axis=2))
        m["bada"] = np.stack([np.repeat(_vec_fm(inp["b_ada"][l], 72), 2, axis=1) for l in range(L)])
        m["ckd"] = np.stack([np.ascontiguousarray(inp["cache_diff_k"][b, l].reshape(256, 2, 128).transpose(2, 1, 0)).reshape(128, 512) for l in range(L)])
        m["ckg"] = np.stack([np.ascontiguousarray(inp["cache_gqa_k"][b, l].reshape(256, 128).T) for l in range(L)])
        m["cvd"] = np.stack([np.ascontiguousarray(inp["cache_diff_v"][b, l].reshape(2, 128, 256).transpose(1, 0, 2)).reshape(128, 512) for l in range(L)])
        m["cvg"] = np.stack([np.ascontiguousarray(inp["cache_gqa_v"][b, l].reshape(2, 128, 128).transpose(1, 0, 2)).reshape(128, 256) for l in range(L)])
        tok = np.arange(512 * j, 512 * (j + 1))
        row = (tok // 64).astype(f32)
        col = (tok % 64).astype(f32)
        rt = np.zeros((128, 4, T), f32)
        for ti, name in enumerate(("d", "g")):
            dd, nf, inv = tabs[name]
            for p in range(128):
                f = p % dd
                pos = row if f < dd // 2 else col
                ang = (pos * inv[f % nf]).astype(f32)
                sgn = -1.0 if (f % (2 * nf)) < nf else 1.0
                rt[p, 2 * ti, :] = np.cos(ang)
                rt[p, 2 * ti + 1, :] = sgn * np.sin(ang)
        m["rope"] = rt.reshape(128, 4 * T)
        mk = np.zeros((128, 16), f32)
        for q in range(4):
            mk[32 * q:32 * (q + 1), q] = 1.0
        mk[0:64, 4] = 1.0
        mk[64:128, 5] = 1.0
        if j > 0:
            mk[:, 8 + j - 1] = 1.0
        if j < 3:
            mk[:, 12 + j + 1] = 1.0
        m["msk"] = mk
        for k in m:
            m[k] = np.ascontiguousarray(m[k], dtype=f32)
        maps.append(m)
    return maps


_NC_CACHE = {}


def kernel(**inputs):
    inp = {k: np.asarray(v) for k, v in inputs.items()}
    if "nc" not in _NC_CACHE:
        _NC_CACHE["nc"] = build_program()
    nc = _NC_CACHE["nc"]
    in_maps = _host_inputs(inp)
    res = run_bass_kernel_spmd(nc, in_maps, core_ids=list(range(NCORES)))
    R = res.results
    y_prompt = np.zeros((16, 256, 1024), np.float32)
    y_sample = np.zeros((2, 2048, 1024), np.float32)
    ndk = np.zeros((16, DEPTH, 256, 4, 2, 32), np.float32)
    ndv = np.zeros((16, DEPTH, 256, 4, 64), np.float32)
    ngk = np.zeros((16, DEPTH, 256, 2, 64), np.float32)
    ngv = np.zeros((16, DEPTH, 256, 2, 64), np.float32)
    for i in range(NCORES):
        b, j = i // 4, i % 4
        y_prompt[2 * i:2 * i + 2] = _unfm(R[i]["yp"]).reshape(2, 256, 1024)
        y_sample[b, 512 * j:512 * (j + 1)] = _unfm(R[i]["ys"])
        for l in range(DEPTH):
            ndk[2 * i:2 * i + 2, l] = R[i]["ndk"][l].reshape(2, 256, 4, 2, 32)
            ndv[2 * i:2 * i + 2, l] = R[i]["ndv"][l].reshape(2, 256, 4, 64)
            ngk[2 * i:2 * i + 2, l] = R[i]["ngk"][l].reshape(2, 256, 2, 64)
            ngv[2 * i:2 * i + 2, l] = R[i]["ngv"][l].reshape(2, 256, 2, 64)
    return (y_prompt, y_sample, ndk, ndv, ngk, ngv)
```

```python
import math
from contextlib import ExitStack
import numpy as np
import concourse.bass as bass
import concourse.mybir as mybir
from concourse.bass_utils import run_bass_kernel_spmd

F32 = mybir.dt.float32
BF16 = mybir.dt.bfloat16
ALU = mybir.AluOpType
AF = mybir.ActivationFunctionType

D = 1024
DFF = 2816
NFC = 22
DEPTH = 2
T = 512
EPS = 1e-6
NCORES = 8
NKS = 2304
BROWS = 784
NW = 3
NDS = 12
DEBUG = False


class Buf:
    __slots__ = ("name", "w", "r", "psum")

    def __init__(self, name, psum=False):
        self.name = name
        self.w = None
        self.r = {}
        self.psum = psum


class Op:
    __slots__ = ("eng", "emit", "deps", "needs_inc", "ticket", "kind", "dsem", "dval", "pre")

    def __init__(self, eng, emit, kind):
        self.eng = eng
        self.emit = emit
        self.deps = []
        self.needs_inc = False
        self.ticket = 0
        self.kind = kind
        self.dsem = None
        self.dval = 0
        self.pre = None


class Sched:
    ENGS = ("pe", "act", "dve", "pool", "sp")

    def __init__(self):
        self.ops = {e: [] for e in self.ENGS}
        self.ndma = {e: 0 for e in self.ENGS}
        self.ncc = 0

    def add(self, eng, emit, reads=(), writes=(), kind="c"):
        op = Op(eng, emit, kind)
        deps = {}

        def dep(o):
            if o is None or o is op:
                return
            if eng == "pe" and kind == "c" and o.eng == "pe" and o.kind == "c":
                return
            deps[id(o)] = o

        for b in reads:
            dep(b.w)
            if b.psum:
                for k_, o in b.r.items():
                    if k_ != eng:
                        dep(o)
        for b in writes:
            dep(b.w)
            for o in b.r.values():
                dep(o)
        for b in writes:
            b.w = op
            b.r = {}
        for b in reads:
            if kind == "c":
                b.r[eng] = op
            else:
                b.r[("d", id(op))] = op
        op.deps = list(deps.values())
        for o in op.deps:
            if o.kind == "c":
                o.needs_inc = True
        if kind == "d":
            n = self.ndma[eng]
            self.ndma[eng] = n + 1
            op.dsem = (eng, n % NDS)
            op.dval = 16 * (n // NDS + 1)
            if n >= NDS:
                op.pre = (op.dsem, 16 * (n // NDS))
        elif kind == "cc":
            self.ncc += 1
            op.dsem = ("cc", self.ncc - 1)
            op.dval = 1
        self.ops[eng].append(op)
        return op

    def finalize(self):
        for e in self.ENGS:
            t = 0
            for op in self.ops[e]:
                if op.kind == "c" and op.needs_inc:
                    t += 1
                    op.ticket = t

    def replay(self, e, eng, sems):
        waited = {}

        def wait(key, val):
            if waited.get(key, 0) < val:
                eng.wait_ge(sems[key], val)
                waited[key] = val

        for op in self.ops[e]:
            for o in op.deps:
                if o.kind == "c":
                    wait((o.eng, "c"), o.ticket)
                else:
                    wait(o.dsem, o.dval)
            if op.pre is not None:
                wait(op.pre[0], op.pre[1])
            ins = op.emit(eng)
            if op.kind == "d":
                ins.then_inc(sems[op.dsem], 16)
            elif op.kind == "cc":
                ins.then_inc(sems[op.dsem])
            elif op.needs_inc:
                ins.then_inc(sems[(e, "c")], 1)
        n = self.ndma[e]
        for k in range(min(n, NDS)):
            cnt = (n - 1 - k) // NDS + 1
            wait((e, k), 16 * cnt)
        if e == "pool":
            for k in range(self.ncc):
                wait(("cc", k), 1)


class _Stop(Exception):
    pass


def build_program(stop=None, substop=None):
    nc = bass.Bass("TRN2", target_bir_lowering=False)
    S = Sched()
    es = ExitStack()

    def din(name, shape, dt=F32):
        return nc.dram_tensor(name, list(shape), dt, kind="ExternalInput").ap()

    def dout(name, shape, dt=F32):
        return nc.dram_tensor(name, list(shape), dt, kind="ExternalOutput").ap()

    d_x = [din("xp", [128, 8, T]), din("xs", [128, 8, T])]
    d_cnd = din("cnd", [128, 8, 2])
    d_wada = din("wada", [DEPTH, 18, 128, 8 * 512])
    d_bada = din("bada", [DEPTH, 128, 144])
    d_gpre = din("gpre", [DEPTH, 128, 24])
    d_gpost = din("gpost", [DEPTH, 128, 24])
    d_wgu = din("wgu", [DEPTH, 2, 11, 128, 4096])
    d_wdn = din("wdn", [DEPTH, 2, 8, 128, NFC * 128])
    d_wmi = din("wmi", [DEPTH, 5, 128, 4096])
    d_wmo = din("wmo", [DEPTH, 3, 128, 4096])
    d_sconv = din("sconvw", [DEPTH, 128, 6])
    d_ccw = din("ccw", [DEPTH, 128, 62])
    d_ccv = din("ccv", [DEPTH, 128, 6])
    d_ccpw = din("ccpw", [DEPTH, 128, 512])
    d_lam = din("lamv", [DEPTH, 128, 128])
    d_subln = din("subln", [DEPTH, 128, 1])
    d_qkn = din("qkn", [DEPTH, 128, 2])
    d_kng = din("kng", [DEPTH, 128, 64])
    d_ckd = din("ckd", [DEPTH, 128, 2 * 256])
    d_ckg = din("ckg", [DEPTH, 128, 256])
    d_cvd = din("cvd", [DEPTH, 128, 2 * 256])
    d_cvg = din("cvg", [DEPTH, 128, 2 * 128])
    d_rope = din("rope", [128, 4 * T])
    d_cmat = din("cmat", [128, 5 * 128])
    d_msk = din("msk", [128, 16])
    d_y = [dout("yp", [128, 8, T]), dout("ys", [128, 8, T])]
    d_ndk = dout("ndk", [DEPTH, T, 256])
    d_ndv = dout("ndv", [DEPTH, T, 256])
    d_ngk = dout("ngk", [DEPTH, T, 128])
    d_ngv = dout("ngv", [DEPTH, T, 128])
    if DEBUG:
        d_dbg = dout("dbg", [8, 128, 8 * T])
        d_dbgb = dout("dbgb", [4, 128, 8 * T], BF16)
    d_bounce = [nc.dram_tensor("bounce%d" % l, [BROWS, 512], BF16).ap() for l in range(DEPTH)]
    d_gath = [nc.dram_tensor("gath%d" % l, [4 * BROWS, 512], BF16).ap() for l in range(DEPTH)]
    B_bounce = [Buf("bounce%d" % l) for l in range(DEPTH)]
    B_gath = [Buf("gath%d" % l) for l in range(DEPTH)]

    def sb(name, shape, dt=F32):
        return es.enter_context(nc.sbuf_tensor("s_" + name, list(shape), dt))

    xt = [sb("xP", [128, 8, T]), sb("xS", [128, 8, T])]
    Bx = [[Buf("x%d_%d" % (g, c)) for c in range(8)] for g in range(2)]
    h1 = sb("h", [128, 8, T], BF16)
    ht = [h1, h1]
    Bh1 = [Buf("h_%d" % c) for c in range(8)]
    Bh = [Bh1, Bh1]
    y1 = sb("y", [128, 8, T])
    yt = [y1, y1]
    By1 = [Buf("y_%d" % c) for c in range(8)]
    By = [By1, By1]
    wsl = [sb("ws%d" % k, [128, 4096], BF16) for k in range(NW)]
    Bws = [Buf("ws%d" % k) for k in range(NW)]
    ps = [es.enter_context(nc.psum_tensor("ps%d" % k, [128, 512], F32)) for k in range(8)]
    Bps = [Buf("ps%d" % k, psum=True) for k in range(8)]
    hmid1 = sb("hmid", [128, NFC, T], BF16)
    Bhm1 = [Buf("hm_%d" % f) for f in range(NFC)]

    NTP = 6
    tpool = [sb("tp%d" % k, [128, T]) for k in range(NTP)]
    Btp = [Buf("tp%d" % k) for k in range(NTP)]
    tpc = [0]

    rstd_t = sb("rstd_t", [128, T]); Brstd_t = Buf("rstd_t")

    def nt():
        k = tpc[0] % NTP
        tpc[0] += 1
        return tpool[k], Btp[k]

    cnd = sb("cnd", [128, 8, 2]); Bcnd = Buf("cnd")
    scnd = sb("scnd", [128, 8, 2], BF16); Bscnd = Buf("scnd")
    bada_l = [sb("bada%d" % l, [128, 144]) for l in range(DEPTH)]; Bbada_l = [Buf("bada%d" % l) for l in range(DEPTH)]
    modt_l = [sb("modt%d" % l, [128, 72, 2]) for l in range(DEPTH)]; Bmod_l = [[Buf("mod%d_%d" % (l, i)) for i in range(3)] for l in range(DEPTH)]
    gpre_l = [sb("gpre%d" % l, [128, 3, 8]) for l in range(DEPTH)]; gpost_l = [sb("gpost%d" % l, [128, 3, 8]) for l in range(DEPTH)]
    Bgp_l = [Buf("gprepost%d" % l) for l in range(DEPTH)]
    Amod_l = [sb("Amod%d" % l, [128, 2, 3, 8]) for l in range(DEPTH)]; Gmod_l = [sb("Gmod%d" % l, [128, 2, 3, 8]) for l in range(DEPTH)]
    BAG_l = [[Buf("AG%d_%d" % (l, i)) for i in range(3)] for l in range(DEPTH)]
    BGG_l = [[Buf("GG%d_%d" % (l, i)) for i in range(3)] for l in range(DEPTH)]
    Bmodg_l = [[Buf("modg%d_%d" % (l, i)) for i in range(3)] for l in range(DEPTH)]
    cur = [0]
    sconvw = sb("sconvw", [128, 2, 3]); ccw = sb("ccw", [128, 2, 31]); ccv = sb("ccv", [128, 3, 2])
    Bsmall = Buf("small")
    ccpw = sb("ccpw", [128, 2, 256], BF16); Bccpw = Buf("ccpw")
    lamv = sb("lamv", [128, 4, 32]); lamt = sb("lamt", [128, 8]); Blam = Buf("lam")
    subln = sb("subln", [128, 1]); qkn = sb("qkn", [128, 2]); kng = sb("kng", [128, 64])
    rope = sb("rope", [128, 4, T]); Brope = Buf("rope")
    cmatf = sb("cmatf", [128, 5, 128]); cmat = sb("cmat", [128, 5, 128], BF16); Bcmat = Buf("cmat")
    msk = sb("msk", [128, 16]); Bmsk = Buf("msk")
    NDG = 6
    dgr = [sb("dgr%d" % k, [128, 128], BF16) for k in range(NDG)]; Bdgr = [Buf("dgr%d" % k) for k in range(NDG)]
    dgc = [0]

    hx = sb("hx", [128, 8, T], BF16)
    Bhx = [Buf("hx_%d" % c) for c in range(8)]
    abt = [hx[:, 0:2, :], hx[:, 2:4, :]]; Bab = [Bhx[0:2], Bhx[2:4]]
    ppad = [sb("ppadP", [128, 2, 2, 258], BF16), sb("ppadS", [128, 2, 514], BF16)]; Bpp = [Buf("pp0"), Buf("pp1")]
    upad = [sb("upadP", [128, 2, 2, 286], BF16), sb("upadS", [128, 2, 542], BF16)]; Bup = [Buf("up0"), Buf("up1")]
    cat1 = hx[:, 4:8, :]
    cat = [cat1, cat1]
    Bcat1 = Bhx[4:8]
    Bcat = [Bcat1, Bcat1]
    hb1 = sb("hb", [64, 8, T], BF16)
    hb = [hb1, hb1]
    Bhb1 = [Buf("hb_%d" % k) for k in range(8)]
    Bhb = [Bhb1, Bhb1]
    qpk = [sb("qpk%d" % g, [128, 4, T], BF16) for g in range(2)]
    Bqpk = [[Buf("qpk%d_%d" % (g, k)) for k in range(4)] for g in range(2)]
    qmr = [sb("qmr%d" % k, [128, T], BF16) for k in range(2)]; Bqmr = [Buf("qmr%d" % k) for k in range(2)]
    kdP = sb("kdP", [128, 2, T], BF16); kgP = sb("kgP", [128, T], BF16)
    kdS = sb("kdS", [128, 2, NKS], BF16); kgS = sb("kgS", [128, NKS], BF16)
    Bkd = [Buf("kdP"), Buf("kdS_own")]; Bkg = [Buf("kgP"), Buf("kgS_own")]
    Bkctx = Buf("kctx"); Bkgath = Buf("kgath")
    vaP = sb("vaP", [128, 4, 6, 128], BF16)
    vaSg = sb("vaSg", [128, 18, 2, 128], BF16)
    vaSd = hmid1[:].rearrange("p f t -> p (f t)")[:, 0:18 * 512].rearrange("p (k h d) -> p k h d", h=4, d=128)
    BvaSd = Bhm1[0:18]
    BvaP = Buf("vaP"); Bvgc = Buf("vgctx"); Bvgg = Buf("vggath")
    vtok = cat1.rearrange("p a t -> p (a t)")[:, 0:1536].rearrange("p (t v) -> p t v", v=384)
    edge = sb("edge", [128, 2, 32], BF16); Bedge = Buf("edge")
    egt = sb("egt", [128, 2, 4, 32], BF16); Begt = Buf("egt")
    halo = sb("halo", [128, 2, 2, 16]); Bhalo = Buf("halo")
    pT = [sb("pT%d" % k, [128, T], BF16) for k in range(2)]; BpT = [Buf("pT%d" % k) for k in range(2)]
    sqr = pT[0:2]; Bsqr = BpT[0:2]
    pT = pT + [hmid1[:, 18, :], hmid1[:, 19, :]]; BpT = BpT + [Bhm1[18], Bhm1[19]]
    vb = sb("vb", [128, 2, T], BF16); Bvb = Buf("vb")
    qbf = sb("qbf", [128, T], BF16); Bqbf = Buf("qbf")
    qbc = sb("qbc", [128, T], BF16); Bqbc = Buf("qbc")
    ktm = sb("ktm", [128, 8]); Bktm = Buf("ktm")

    sems = {}
    for e in Sched.ENGS:
        sems[(e, "c")] = es.enter_context(nc.semaphore("sc_" + e))
    for e in ("sp", "pool", "act"):
        for k in range(NDS):
            sems[(e, k)] = es.enter_context(nc.semaphore("sd_%s%d" % (e, k)))
    for k in range(DEPTH):
        sems[("cc", k)] = es.enter_context(nc.semaphore("scc%d" % k))

    def mm(outb, out, lb, lhsT, rb, rhs, start, stop):
        S.add("pe", lambda e: e.matmul(out, lhsT, rhs, start=start, stop=stop), reads=[lb, rb], writes=[outb])

    def act(out, in_, func, reads, writes, scale=None, bias=None):
        kw = {}
        if scale is not None:
            kw["scale"] = scale
        if bias is not None:
            kw["bias"] = bias
        S.add("act", lambda e: e.activation(out, in_, func, **kw), reads=reads, writes=writes)

    def tt(out, a, b, op, reads, writes, eng="dve"):
        S.add(eng, lambda e: e.tensor_tensor(out, a, b, op), reads=reads, writes=writes)

    def ts(out, a, s1, op0, reads, writes, eng="dve"):
        S.add(eng, lambda e: e.tensor_scalar(out, a, s1, None, op0), reads=reads, writes=writes)

    def stt(out, a, s, b, op0, op1, reads, writes):
        S.add("dve", lambda e: e.scalar_tensor_tensor(out, a, s, b, op0, op1), reads=reads, writes=writes)

    def cp(out, in_, reads, writes, eng="dve"):
        S.add(eng, lambda e: e.tensor_copy(out, in_), reads=reads, writes=writes)

    def dma(q, out, in_, reads, writes):
        S.add(q, lambda e: e.dma_start(out=out, in_=in_), reads=reads, writes=writes, kind="d")

    def memset(ap, val, writes, eng="dve"):
        S.add(eng, lambda e: e.memset(ap, val), reads=[], writes=writes)

    epsb = sb("epsb", [128, 1]); Bepsb = Buf("epsb")
    memset(epsb[:], EPS, [Bepsb])

    def rsqrt_chain(out, Bout, in_, mult, reads_in):
        np_ = out.shape[0]
        act(out, in_, AF.Ln, reads_in + [Bepsb], [Bout], scale=mult, bias=epsb[0:np_, 0:1])
        act(out, out, AF.Exp, [Bout], [Bout], scale=-0.5)

    wfree_list = list(range(NW))

    def wload(src_ap, nel=4096):
        k = wfree_list.pop(0)
        dma("pool", wsl[k][:, 0:nel].rearrange("p (a n) -> p a n", a=4), src_ap[:, 0:nel].rearrange("p (a n) -> p a n", a=4), [], [Bws[k]])
        return k

    def wfree(k):
        wfree_list.append(k)

    for g in range(2):
        for c in range(8):
            dma("sp", xt[g][:, c, :], d_x[g][:, c, :], [], [Bx[g][c]])
    dma("sp", cnd[:], d_cnd[:], [], [Bcnd])
    dma("sp", rope[:].rearrange("p a t -> p (a t)"), d_rope[:], [], [Brope])
    dma("sp", cmatf[:].rearrange("p a t -> p (a t)"), d_cmat[:], [], [Bcmat])
    dma("sp", msk[:], d_msk[:], [], [Bmsk])
    cp(cmat[:], cmatf[:], [Bcmat], [Bcmat])
    ones_b = cmat[:, 0, :]
    blk64 = cmat[:, 1, :]
    ident = cmat[:, 2, :]
    permd = cmat[:, 3, :]
    permg = cmat[:, 4, :]
    ones_f = cmatf[:, 0, :]
    act(scnd[:], cnd[:], AF.Silu, [Bcnd], [Bscnd])
    memset(vaP[:, :, :, 64:128], 1.0, [BvaP], eng="pool")
    memset(vaSg[:, :, :, 64:128], 1.0, [Bvgc, Bvgg], eng="pool")
    memset(ppad[0][:], 0.0, [Bpp[0]], eng="pool")
    memset(upad[0][:], 0.0, [Bup[0]], eng="pool")

    bankrr = [0]

    def nb():
        b = bankrr[0] % 4
        bankrr[0] += 1
        return b

    def sumsq_pre(g):
        for c in range(8):
            k = c % 2
            act(sqr[k][:], xt[g][:, c, :], AF.Square, [Bx[g][c]], [Bsqr[k]])
            mm(Bps[6], ps[6][:], Bcmat, ones_b, Bsqr[k], sqr[k][:], c == 0, c == 7)

    def pre_norm_gen(g, i, l=None, hb_=None):
        l = cur[0] if l is None else l
        hdst, Bhd = (h1, Bh1) if hb_ is None else hb_
        for c in range(8):
            k = c % 2
            act(sqr[k][:], xt[g][:, c, :], AF.Square, [Bx[g][c]], [Bsqr[k]])
            if c > 0:
                mm(Bps[6], ps[6][:], Bcmat, ones_b, Bsqr[1 - k], sqr[1 - k][:], c == 1, False)
            yield
        mm(Bps[6], ps[6][:], Bcmat, ones_b, Bsqr[1], sqr[1][:], False, True)
        rstd, Brstd = rstd_t, Brstd_t
        rsqrt_chain(rstd[:], Brstd, ps[6][:], 1.0 / D, [Bps[6]])
        for c in range(8):
            t_, Bt = nt()
            tt(t_[:], xt[g][:, c, :], rstd[:], ALU.mult, [Bx[g][c], Brstd], [Bt])
            act(hdst[:, c, :], t_[:], AF.Identity, [Bt, BAG_l[l][i], Bmod_l[l][i]], [Bhd[c]],
                scale=Amod_l[l][:, g, i, c:c + 1], bias=modt_l[l][:, (3 * i) * 8 + c, g:g + 1])

    def pre_norm(g, i, l=None, hb_=None):
        for _ in pre_norm_gen(g, i, l, hb_):
            pass

    def evac_y(g, oc, bank):
        act(yt[g][:, oc, :], ps[bank][:], AF.Copy, [Bps[bank]], [By[g][oc]])
        k = oc % 2
        act(sqr[k][:], ps[bank][:], AF.Square, [Bps[bank]], [Bsqr[k]])
        pend_sq.append((oc, k))

    pend_sq = []

    def flush_sq(everything=False):
        while pend_sq and (everything or len(pend_sq) > 0):
            oc, k = pend_sq.pop(0)
            mm(Bps[6], ps[6][:], Bcmat, ones_b, Bsqr[k], sqr[k][:], oc == 0, oc == 7)

    def post_norm(g, i, l=None):
        l = cur[0] if l is None else l
        rstd, Brstd = rstd_t, Brstd_t
        rsqrt_chain(rstd[:], Brstd, ps[6][:], 1.0 / D, [Bps[6]])
        for c in range(8):
            t_, Bt = nt()
            stt(t_[:], yt[g][:, c, :], Gmod_l[l][:, g, i, c:c + 1], rstd[:], ALU.mult, ALU.mult, [By[g][c], BGG_l[l][i], Brstd], [Bt])
            tt(xt[g][:, c, :], xt[g][:, c, :], t_[:], ALU.add, [Bx[g][c], Bt], [Bx[g][c]])

    def ffn_chain(jobs, hook=None):
        hsel = lambda g: (h1, Bh1) if g == 0 else (hx, Bhx)
        l0, w0, i0, g0 = jobs[0]
        pre_norm(g0, i0, l0, hsel(g0))
        for jn, (l, which, i, g) in enumerate(jobs):
            hsrc, Bhs = hsel(g)
            cnt = 0
            png = None
            if jn + 1 < len(jobs):
                ln, wn, in_, gn = jobs[jn + 1]
                png = pre_norm_gen(gn, in_, ln, hsel(gn))
            for f in range(11):
                k = wload(d_wgu[l, which, f])
                w = wsl[k][:].rearrange("p (a c n) -> p a c n", a=2, c=8)
                for j in range(2):
                    fc = 2 * f + j
                    bg, bu = (cnt % 2) * 2, (cnt % 2) * 2 + 1
                    cnt += 1
                    for c in range(8):
                        mm(Bps[bg], ps[bg][:], Bws[k], w[:, 0, c, j * 128:(j + 1) * 128], Bhs[c], hsrc[:, c, :], c == 0, c == 7)
                    for c in range(8):
                        mm(Bps[bu], ps[bu][:], Bws[k], w[:, 1, c, j * 128:(j + 1) * 128], Bhs[c], hsrc[:, c, :], c == 0, c == 7)
                    sk = cnt % 2
                    act(qmr[sk][:], ps[bg][:], AF.Silu, [Bps[bg]], [Bqmr[sk]])
                    tt(hmid1[:, fc, :], qmr[sk][:], ps[bu][:], ALU.mult, [Bqmr[sk], Bps[bu]], [Bhm1[fc]])
                wfree(k)
                if hook is not None:
                    hook()
                if png is not None and f >= 2:
                    next(png, None)
            if png is not None:
                for _ in png:
                    pass
            for oc in range(8):
                k = wload(d_wdn[l, which, oc], NFC * 128)
                w = wsl[k][:, 0:NFC * 128].rearrange("p (f n) -> p f n", f=NFC)
                bank = 4 + (oc % 2)
                for fc in range(NFC):
                    mm(Bps[bank], ps[bank][:], Bws[k], w[:, fc, :], Bhm1[fc], hmid1[:, fc, :], fc == 0, fc == NFC - 1)
                wfree(k)
                flush_sq()
                evac_y(g, oc, bank)
            flush_sq(True)
            post_norm(g, i, l)

    def ffn(l, which, i, g, hook=None):
        pre_norm(g, i)
        cnt = 0
        for f in range(11):
            k = wload(d_wgu[l, which, f])
            w = wsl[k][:].rearrange("p (a c n) -> p a c n", a=2, c=8)
            for j in range(2):
                fc = 2 * f + j
                bg, bu = (cnt % 2) * 2, (cnt % 2) * 2 + 1
                cnt += 1
                for c in range(8):
                    mm(Bps[bg], ps[bg][:], Bws[k], w[:, 0, c, j * 128:(j + 1) * 128], Bh[g][c], ht[g][:, c, :], c == 0, c == 7)
                for c in range(8):
                    mm(Bps[bu], ps[bu][:], Bws[k], w[:, 1, c, j * 128:(j + 1) * 128], Bh[g][c], ht[g][:, c, :], c == 0, c == 7)
                sk = cnt % 2
                act(qmr[sk][:], ps[bg][:], AF.Silu, [Bps[bg]], [Bqmr[sk]])
                tt(hmid1[:, fc, :], qmr[sk][:], ps[bu][:], ALU.mult, [Bqmr[sk], Bps[bu]], [Bhm1[fc]])
            wfree(k)
            if hook is not None:
                hook()
        for oc in range(8):
            k = wload(d_wdn[l, which, oc], NFC * 128)
            w = wsl[k][:, 0:NFC * 128].rearrange("p (f n) -> p f n", f=NFC)
            bank = 4 + (oc % 2)
            for fc in range(NFC):
                mm(Bps[bank], ps[bank][:], Bws[k], w[:, fc, :], Bhm1[fc], hmid1[:, fc, :], fc == 0, fc == NFC - 1)
            wfree(k)
            evac_y(g, oc, bank)
        post_norm(g, i)

    def ada_gen(l):
        bada, modt, gpre, gpost, Amod, Gmod = bada_l[l], modt_l[l], gpre_l[l], gpost_l[l], Amod_l[l], Gmod_l[l]
        Bbada, Bmod, Bgp, BAG = Bbada_l[l], Bmod_l[l], Bgp_l[l], BAG_l[l]
        dma("sp", bada[:], d_bada[l], [], [Bbada])
        dma("sp", gpre[:].rearrange("p a c -> p (a c)"), d_gpre[l], [], [Bgp])
        dma("sp", gpost[:].rearrange("p a c -> p (a c)"), d_gpost[l], [], [Bgp])
        for pa in range(18):
            k = wload(d_wada[l, pa])
            w = wsl[k][:].rearrange("p (c n) -> p c n", c=8)
            for cc in range(4):
                ci = pa * 4 + cc
                for c in range(8):
                    mm(Bps[7], ps[7][:, 2 * ci:2 * ci + 2], Bws[k], w[:, c, cc * 128:(cc + 1) * 128], Bscnd, scnd[:, c, :], c == 0, c == 7)
            wfree(k)
            if pa % 6 == 3:
                i = pa // 6
                lo, hi = (3 * i) * 16, (3 * i + 2) * 16
                tt(modt[:].rearrange("p a k -> p (a k)")[:, lo:hi], ps[7][:, lo:hi], bada[:, lo:hi], ALU.add, [Bps[7], Bbada], [Bmod[i]])
                for g in range(2):
                    stt(Amod[:, g, i, :], modt[:, (3 * i + 1) * 8:(3 * i + 2) * 8, g], 1.0, gpre[:, i, :], ALU.add, ALU.mult,
                        [Bmod[i], Bgp], [BAG[i]])
            if pa % 6 == 5:
                i = pa // 6
                lo, hi = (3 * i + 2) * 16, (3 * i + 3) * 16
                tt(modt[:].rearrange("p a k -> p (a k)")[:, lo:hi], ps[7][:, lo:hi], bada[:, lo:hi], ALU.add, [Bps[7], Bbada], [Bmodg_l[l][i]])
                for g in range(2):
                    stt(Gmod[:, g, i, :], modt[:, (3 * i + 2) * 8:(3 * i + 3) * 8, g], 1.0 if i == 1 else 0.5, gpost[:, i, :],
                        ALU.mult, ALU.mult, [Bmodg_l[l][i], Bgp], [BGG_l[l][i]])
            if pa < 17:
                yield

    def layer_consts(l):
        dma("sp", sconvw[:].rearrange("p a c -> p (a c)"), d_sconv[l], [], [Bsmall])
        dma("sp", ccw[:].rearrange("p a c -> p (a c)"), d_ccw[l], [], [Bsmall])
        dma("sp", ccv[:].rearrange("p a c -> p (a c)"), d_ccv[l], [], [Bsmall])
        dma("sp", subln[:], d_subln[l], [], [Bsmall])
        dma("sp", qkn[:], d_qkn[l], [], [Bsmall])
        dma("sp", kng[:], d_kng[l], [], [Bsmall])
        dma("sp", lamv[:].rearrange("p a c -> p (a c)"), d_lam[l], [], [Blam])
        dma("pool", ccpw[:].rearrange("p a c -> p (a c)"), d_ccpw[l], [], [Bccpw])
        dma("pool", kdS[:, :, 0:256], d_ckd[l].rearrange("p (c t) -> p c t", c=2), [], [Bkctx])
        dma("pool", kgS[:, 0:256], d_ckg[l], [], [Bkctx])
        dma("pool", vaSg[:, 0:2, :, 0:64], d_cvg[l].rearrange("p (c h d) -> p c h d", c=2, h=2), [], [Bvgc])
        lam_init = 0.8 - 0.6 * math.exp(-0.3 * l)
        tt(lamv[:, 0, :], lamv[:, 0, :], lamv[:, 1, :], ALU.mult, [Blam], [Blam])
        tt(lamv[:, 2, :], lamv[:, 2, :], lamv[:, 3, :], ALU.mult, [Blam], [Blam])
        S.add("dve", lambda e: e.tensor_reduce(lamt[:, 2:3], lamv[:, 0, :], mybir.AxisListType.X, ALU.add), reads=[Blam], writes=[Blam])
        S.add("dve", lambda e: e.tensor_reduce(lamt[:, 3:4], lamv[:, 2, :], mybir.AxisListType.X, ALU.add), reads=[Blam], writes=[Blam])
        act(lamt[:, 4:6], lamt[:, 2:4], AF.Exp, [Blam], [Blam])
        tt(lamt[:, 6:7], lamt[:, 5:6], lamt[:, 4:5], ALU.subtract, [Blam], [Blam])
        ts(lamt[:, 0:1], lamt[:, 6:7], -lam_init, ALU.add, [Blam], [Blam])
        ts(lamt[:, 1:2], subln[:], 1.0 - lam_init, ALU.mult, [Bsmall, Blam], [Blam])

    def ctx_v_diff(l):
        memset(vaSd[:, :, :, 64:128], 1.0, BvaSd, eng="pool")
        dma("pool", vaSd[:, 0:2, :, 0:64], d_cvd[l].rearrange("p (c h d) -> p c h d", c=2, h=4), [], BvaSd)

    dq = []

    def run_dq():
        todo = dq[:]
        del dq[:]
        for f_ in todo:
            f_()

    def proj_fm(g, k, col, bank):
        w = wsl[k][:].rearrange("p (c n) -> p c n", c=8)
        for c in range(8):
            mm(Bps[bank], ps[bank][:], Bws[k], w[:, c, col:col + 128], Bh[g][c], ht[g][:, c, :], c == 0, c == 7)
        run_dq()

    def proj_tm(g, k, col, ncol, tile_, bank):
        w = wsl[k][:].rearrange("p (c n) -> p c n", c=8)
        for c in range(8):
            mm(Bps[bank], ps[bank][:, 0:ncol], Bh[g][c], ht[g][:, c, tile_ * 128:(tile_ + 1) * 128], Bws[k], w[:, c, col:col + ncol], c == 0, c == 7)
        run_dq()

    def rope_apply(src, src_reads, kind, dst, Bdst):
        ci, si, pm = (0, 1, permd) if kind == "d" else (2, 3, permg)
        act(qbc[:], src, AF.Copy, src_reads, [Bqbc])

        def tail():
            mm(Bps[6], ps[6][:], Bcmat, pm, Bqbc, qbc[:], True, True)
            t1, B1 = nt()
            t2, B2 = nt()
            tt(t1[:], src, rope[:, ci, :], ALU.mult, src_reads + [Brope], [B1])
            tt(t2[:], ps[6][:], rope[:, si, :], ALU.mult, [Bps[6], Brope], [B2])
            tt(dst, t1[:], t2[:], ALU.add, [B1, B2], [Bdst])
        dq.append(tail)

    def headnorm(bank, col, then):
        act(qbf[:], ps[bank][:], AF.Square, [Bps[bank]], [Bqbf])

        def tail():
            mm(Bps[7], ps[7][:], Bcmat, blk64, Bqbf, qbf[:], True, True)
            r_, Br = nt()
            rsqrt_chain(r_[:], Br, ps[7][:], 1.0 / 64, [Bps[7]])
            o_, Bo = nt()
            stt(o_[:], ps[bank][:], qkn[:, col:col + 1], r_[:], ALU.mult, ALU.mult, [Bps[bank], Bsmall, Br], [Bo])
            then(o_[:], [Bo])
        dq.append(tail)

    def mix_in(l, g):
        for piece in range(5):
            if piece > 0:
                wfree(k)
            k = wload(d_wmi[l, piece])
            if piece == 0:
                for c in range(2):
                    b = nb(); proj_fm(g, k, c * 128, b)
                    ac, Bac = nt()
                    act(ac[:], ps[b][:], AF.Copy, [Bps[b]], [Bac])
                    b2 = nb(); proj_fm(g, k, 256 + c * 128, b2)
                    if g == 0:
                        tt(ppad[0][:, c, :, 1:257], ac[:].rearrange("p (s t) -> p s t", s=2), ps[b2][:].rearrange("p (s t) -> p s t", s=2),
                           ALU.mult, [Bac, Bps[b2]], [Bpp[0]])
                    else:
                        tt(ppad[1][:, c, 1:513], ac[:], ps[b2][:], ALU.mult, [Bac, Bps[b2]], [Bpp[1]])
            elif piece == 1:
                for c in range(2):
                    b = nb(); proj_fm(g, k, c * 128, b)
                    act(abt[g][:, c, :], ps[b][:], AF.Copy, [Bps[b]], Bab[g])
                for c in range(2):
                    b = nb(); proj_fm(g, k, 256 + c * 128, b)
                    if g == 0:
                        act(qpk[g][:, c, :], ps[b][:], AF.Copy, [Bps[b]], [Bqpk[g][c]])
                    else:
                        rope_apply(ps[b][:], [Bps[b]], "d", qpk[g][:, c, :], Bqpk[g][c])
            elif piece == 2:
                for c in range(2):
                    b = nb(); proj_fm(g, k, c * 128, b)
                    if g == 0:
                        act(kdP[:, c, :], ps[b][:], AF.Copy, [Bps[b]], [Bkd[0]])
                    else:
                        rope_apply(ps[b][:], [Bps[b]], "d", kdS[:, c, 256:256 + T], Bkd[1])
                for t_ in range(4):
                    b = nb()
                    rows = slice(t_ * 128, (t_ + 1) * 128)
                    if g == 0:
                        proj_tm(g, k, 0, 512, t_, b)
                        st, Bst = nt()
                        act(st[:], ps[b][:], AF.Copy, [Bps[b]], [Bst])
                        cp(vaP[:, t_, 0:4, 0:64], ps[b][:, 256:512].rearrange("p (h d) -> p h d", h=4), [Bps[b]], [BvaP])
                        dma("sp", d_ndk[l, rows, :], st[:, 0:256], [Bst], [])
                        dma("sp", d_ndv[l, rows, :], st[:, 256:512], [Bst], [])
                    else:
                        proj_tm(g, k, 256, 256, t_, b)
                        act(vtok[:, t_, 0:256], ps[b][:, 0:256], AF.Copy, [Bps[b]], Bcat1)
            elif piece == 3:
                for c in range(2):
                    b = nb(); proj_fm(g, k, 256 + c * 128, b)
                    sg, Bsg = nt()
                    act(sg[:], ps[b][:], AF.Sigmoid, [Bps[b]], [Bsg])
                    b2 = nb(); proj_fm(g, k, c * 128, b2)
                    if g == 0:
                        tt(upad[0][:, c, :, 15:271], sg[:].rearrange("p (s t) -> p s t", s=2),
                           ps[b2][:].rearrange("p (s t) -> p s t", s=2), ALU.mult, [Bsg, Bps[b2]], [Bup[0]])
                    else:
                        tt(upad[1][:, c, 15:527], sg[:], ps[b2][:], ALU.mult, [Bsg, Bps[b2]], [Bup[1]])
            else:
                for c in range(2):
                    b = nb(); proj_fm(g, k, c * 128, b)
                    if g == 0:
                        headnorm(b, 0, lambda src, Bsrc, c=c: cp(qpk[0][:, 2 + c, :], src, Bsrc, [Bqpk[0][2 + c]]))
                    else:
                        headnorm(b, 0, lambda src, Bsrc, c=c: rope_apply(src, Bsrc, "g", qpk[1][:, 2 + c, :], Bqpk[1][2 + c]))
                b = nb(); proj_fm(g, k, 256, b)
                if g == 0:
                    headnorm(b, 1, lambda src, Bsrc: cp(kgP[:], src, Bsrc, [Bkg[0]]))
                else:
                    headnorm(b, 1, lambda src, Bsrc: rope_apply(src, Bsrc, "g", kgS[:, 256:256 + T], Bkg[1]))
                for t_ in range(4):
                    b = nb()
                    rows = slice(t_ * 128, (t_ + 1) * 128)
                    if g == 0:
                        proj_tm(g, k, 256, 256, t_, b)
                        st, Bst = nt()
                        act(st[:, 128:256], ps[b][:, 128:256], AF.Copy, [Bps[b]], [Bst])
                        cp(vaP[:, t_, 4:6, 0:64], ps[b][:, 128:256].rearrange("p (h d) -> p h d", h=2), [Bps[b]], [BvaP])
                        act(st[:, 256:384], ps[b][:, 0:128], AF.Square, [Bps[b]], [Bst])
                        S.add("dve", lambda e, st=st: e.tensor_reduce(ktm[:, 0:2], st[:, 256:384].rearrange("p (h d) -> p h d", h=2),
                                                                      mybir.AxisListType.X, ALU.add), reads=[Bst], writes=[Bktm])
                        rsqrt_chain(ktm[:, 2:4], Bktm, ktm[:, 0:2], 1.0 / 64, [Bktm])
                        for h in range(2):
                            stt(st[:, h * 64:(h + 1) * 64], ps[b][:, h * 64:(h + 1) * 64], ktm[:, 2 + h:3 + h], kng[:],
                                ALU.mult, ALU.mult, [Bps[b], Bktm, Bsmall], [Bst])
                        dma("sp", d_ngk[l, rows, :], st[:, 0:128], [Bst], [])
                        dma("sp", d_ngv[l, rows, :], st[:, 128:256], [Bst], [])
                    else:
                        proj_tm(g, k, 384, 128, t_, b)
                        act(vtok[:, t_, 256:384], ps[b][:, 0:128], AF.Copy, [Bps[b]], Bcat1)
            if substop is not None and piece == substop:
                raise _Stop()
        wfree(k)
        while dq:
            run_dq()

    def exchange(l):
        for c in range(2):
            cp(edge[:, c, 0:15], upad[1][:, c, 15:30], [Bup[1]], [Bedge])
            cp(edge[:, c, 15:16], ppad[1][:, c, 1:2], [Bpp[1]], [Bedge])
            cp(edge[:, c, 16:31], upad[1][:, c, 512:527], [Bup[1]], [Bedge])
            cp(edge[:, c, 31:32], ppad[1][:, c, 512:513], [Bpp[1]], [Bedge])
        bo = d_bounce[l]
        dma("sp", bo[0:256, :].rearrange("(c p) t -> p c t", p=128), kdS[:, :, 256:256 + T], [Bkd[1]], [B_bounce[l]])
        dma("sp", bo[256:384, :], kgS[:, 256:256 + T], [Bkg[1]], [B_bounce[l]])
        vview = bo[384:768, :].rearrange("r c -> (r c)").rearrange("(tt p v) -> p tt v", p=128, v=384)
        dma("sp", vview, vtok, Bcat1, [B_bounce[l]])
        eview = bo[768:784, :].rearrange("r c -> (r c)").rearrange("(c p e) -> p c e", p=128, e=32)
        dma("sp", eview, edge[:], [Bedge], [B_bounce[l]])
        S.add("pool", lambda e: e.collective_compute("AllGather", ALU.bypass, replica_groups=[[0, 1, 2, 3], [4, 5, 6, 7]],
                                                     ins=[bo.opt()], outs=[d_gath[l].opt()]),
              reads=[B_bounce[l]], writes=[B_gath[l]], kind="cc")
    def exchange_recv(l):
        ga = d_gath[l]
        for r in range(4):
            base = r * BROWS
            ks = 256 + r * T
            dma("sp", kdS[:, :, ks:ks + T], ga[base:base + 256, :].rearrange("(c p) t -> p c t", p=128), [B_gath[l]], [Bkgath])
            dma("sp", kgS[:, ks:ks + T], ga[base + 256:base + 384, :], [B_gath[l]], [Bkgath])
            vv = ga[base + 384:base + 768, :].rearrange("r c -> (r c)").rearrange("(tt p v) -> p tt v", p=128, v=384)
            for t_ in range(4):
                ch = 2 + r * 4 + t_
                dma("sp", vaSd[:, ch, :, 0:64], vv[:, t_, 0:256].rearrange("p (h d) -> p h d", h=4), [B_gath[l]], BvaSd)
                dma("sp", vaSg[:, ch, :, 0:64], vv[:, t_, 256:384].rearrange("p (h d) -> p h d", h=2), [B_gath[l]], [Bvgg])
            ev = ga[base + 768:base + 784, :].rearrange("r c -> (r c)").rearrange("(c p e) -> p c e", p=128, e=32)
            dma("sp", egt[:, :, r, :], ev, [B_gath[l]], [Begt])

    def halos():
        for side, lo, mcol in ((0, 16, 8), (1, 0, 12)):
            ts(halo[:, :, side, :], egt[:, :, 0, lo:lo + 16], msk[:, mcol:mcol + 1], ALU.mult, [Begt, Bmsk], [Bhalo])
            for r in range(1, 4):
                stt(halo[:, :, side, :], egt[:, :, r, lo:lo + 16], msk[:, mcol + r:mcol + r + 1], halo[:, :, side, :], ALU.mult, ALU.add,
                    [Begt, Bmsk, Bhalo], [Bhalo])
        cp(upad[1][:, :, 0:15], halo[:, :, 0, 0:15], [Bhalo], [Bup[1]])
        cp(ppad[1][:, :, 0:1], halo[:, :, 0, 15:16], [Bhalo], [Bpp[1]])
        cp(upad[1][:, :, 527:542], halo[:, :, 1, 0:15], [Bhalo], [Bup[1]])
        cp(ppad[1][:, :, 513:514], halo[:, :, 1, 15:16], [Bhalo], [Bpp[1]])

    def attention(g, hook=None):
        nq, nkc = 512, (2 if g == 0 else 18)
        items = [(0, hm) for hm in range(12)]
        state = {"on_prev": None}

        def qinfo(hm):
            if hm < 8:
                return qpk[g][:, hm // 4, :], Bqpk[g][hm // 4], hm % 4, hm // 2, 32 ** -0.5
            sl = hm - 8
            return qpk[g][:, 2 + sl // 2, :], Bqpk[g][2 + sl // 2], 4 + sl % 2, 4 + sl % 2, 64 ** -0.5

        def emit_mask(idx):
            s_, hm = items[idx]
            qsrc, Bq, mcol, h, scale = qinfo(hm)
            qk = idx % 2
            ts(qmr[qk][:], qsrc, msk[:, mcol:mcol + 1], ALU.mult, [Bq, Bmsk], [Bqmr[qk]])

        def kv(s_, hm, h, kc):
            if g == 0:
                ksl = slice(s_ * 256 + kc * 128, s_ * 256 + (kc + 1) * 128)
                kT = kdP[:, hm // 4, ksl] if hm < 8 else kgP[:, ksl]
                Bk = [Bkd[0]] if hm < 8 else [Bkg[0]]
                return kT, Bk, vaP[:, s_ * 2 + kc, h, :], [BvaP]
            ksl = slice(kc * 128, (kc + 1) * 128)
            kT = kdS[:, hm // 4, ksl] if hm < 8 else kgS[:, ksl]
            Bk = [Bkctx] if kc < 2 else [Bkgath]
            if h < 4:
                return kT, Bk, vaSd[:, kc, h, :], BvaSd
            return kT, Bk, vaSg[:, kc, h - 4, :], [Bvgc if kc < 2 else Bvgg]

        def finalize(idx):
            s_, hm = items[idx]
            qs = slice(s_ * nq, (s_ + 1) * nq)
            qsrc, Bq, mcol, h, scale = qinfo(hm)
            ob = 4 + idx % 2
            r_, Br = nt()
            cp(r_[0:64, 0:nq], ps[ob][64:128, 0:nq], [Bps[ob]], [Br])
            act(r_[0:64, 0:nq], r_[0:64, 0:nq], AF.Ln, [Br], [Br])
            act(r_[0:64, 0:nq], r_[0:64, 0:nq], AF.Exp, [Br], [Br], scale=-1.0)
            if hm < 8:
                on, Bon = nt()
                tt(on[0:64, 0:nq], ps[ob][0:64, 0:nq], r_[0:64, 0:nq], ALU.mult, [Bps[ob], Br], [Bon])
                if hm % 2 == 0:
                    state["on_prev"] = (on, Bon)
                else:
                    on0, Bon0 = state["on_prev"]
                    d_, Bd = nt()
                    stt(d_[0:64, 0:nq], on[0:64, 0:nq], lamt[0:64, 0:1], on0[0:64, 0:nq], ALU.mult, ALU.add, [Bon0, Bon, Blam], [Bd])
                    act(qbf[0:64, 0:nq], d_[0:64, 0:nq], AF.Square, [Bd], [Bqbf])

                    def cont(d_=d_, Bd=Bd, h=h, qs=qs):
                        mm(Bps[6], ps[6][0:64, 0:nq], Bcmat, ones_b[0:64, 0:64], Bqbf, qbf[0:64, 0:nq], True, True)
                        rr, Brr = nt()
                        rsqrt_chain(rr[0:64, 0:nq], Brr, ps[6][0:64, 0:nq], 1.0 / 64, [Bps[6]])
                        stt(hb[g][:, h, qs], d_[0:64, 0:nq], lamt[0:64, 1:2], rr[0:64, 0:nq], ALU.mult, ALU.mult, [Bd, Blam, Brr], [Bhb[g][h]])
                    return cont
            else:
                sl = hm - 8
                tt(hb[g][:, 4 + sl, qs], ps[ob][0:64, 0:nq], r_[0:64, 0:nq], ALU.mult, [Bps[ob], Br], [Bhb[g][4 + sl]])
            return None

        contB = [None]

        def fin_step(idx):
            if contB[0] is not None:
                contB[0]()
                contB[0] = None
            if idx >= 0:
                contB[0] = finalize(idx)

        NPT = len(pT)
        stepc = [0]
        emit_mask(0)
        for idx, (s_, hm) in enumerate(items):
            qs = slice(s_ * nq, (s_ + 1) * nq)
            qsrc, Bq, mcol, h, scale = qinfo(hm)
            qk = idx % 2
            ob = 4 + idx % 2
            if idx + 1 < len(items):
                emit_mask(idx + 1)
            pend = []
            if g == 0:
                pks = []
                for kc in range(2):
                    sbk = stepc[0] % 4
                    pk = stepc[0] % NPT
                    stepc[0] += 1
                    pks.append(pk)
                    for s2 in range(2):
                        kT, Bk, va, Bv = kv(s2, hm, h, kc)
                        cs = slice(s2 * 256, (s2 + 1) * 256)
                        S.add("pe", lambda e, o=ps[sbk][:, cs], a=kT, b_=qmr[qk][:, cs]: e.matmul(o, a, b_, start=True, stop=True),
                              reads=Bk + [Bqmr[qk]], writes=[Bps[sbk]])
                    act(pT[pk][:], ps[sbk][:], AF.Exp, [Bps[sbk]], [BpT[pk]], scale=scale)
                fin_step(idx - 1)
                for s2 in range(2):
                    cs = slice(s2 * 256, (s2 + 1) * 256)
                    for kc in range(2):
                        kT, Bk, va, Bv = kv(s2, hm, h, kc)
                        S.add("pe", lambda e, o=ps[ob][:, cs], a=va, b_=pT[pks[kc]][:, cs], st_=(kc == 0), sp_=(kc == 1): e.matmul(o, a, b_, start=st_, stop=sp_),
                              reads=Bv + [BpT[pks[kc]]], writes=[Bps[ob]])
                continue

            def score(kc):
                kT, Bk, va, Bv = kv(s_, hm, h, kc)
                sbk = stepc[0] % 4
                pk = stepc[0] % NPT
                stepc[0] += 1
                S.add("pe", lambda e, o=ps[sbk][:, 0:nq], a=kT, b_=qmr[qk][:, qs]: e.matmul(o, a, b_, start=True, stop=True),
                      reads=Bk + [Bqmr[qk]], writes=[Bps[sbk]])
                act(pT[pk][:, 0:nq], ps[sbk][:, 0:nq], AF.Exp, [Bps[sbk]], [BpT[pk]], scale=scale)
                pend.append((kc, pk, va, Bv))

            def pv():
                kc, pk, va, Bv = pend.pop(0)
                st_, sp_ = (kc == 0), (kc == nkc - 1)
                S.add("pe", lambda e, o=ps[ob][:, 0:nq], a=va, b_=pT[pk][:, 0:nq], st_=st_, sp_=sp_: e.matmul(o, a, b_, start=st_, stop=sp_),
                      reads=Bv + [BpT[pk]], writes=[Bps[ob]])

            DEPTH_PIPE = 2
            for kc in range(nkc):
                score(kc)
                if len(pend) > DEPTH_PIPE:
                    pv()
            fin_step(idx - 1)
            while pend:
                pv()
            if hook is not None:
                hook()
        fin_step(len(items) - 1)
        fin_step(-1)

    def diag(wcol):
        k = dgc[0] % NDG
        dgc[0] += 1
        ts(dgr[k][:], ident, wcol, ALU.mult, [Bcmat, Bsmall], [Bdgr[k]])
        return dgr[k][:], Bdgr[k]

    def convs(g):
        nseq, n = (2, 256) if g == 0 else (1, 512)
        ucs = []
        for c in range(2):
            b = nb()
            for k in range(31):
                dg, Bdg = diag(ccw[:, c, k:k + 1])
                for s in range(nseq):
                    rhs = upad[0][:, c, s, k:k + n] if g == 0 else upad[1][:, c, k:k + n]
                    S.add("pe", lambda e, o=ps[4 + s][:, 0:n], a=dg, r=rhs, st_=(k == 0), sp_=(k == 30): e.matmul(o, a, r, start=st_, stop=sp_),
                          reads=[Bdg, Bup[g]], writes=[Bps[4 + s]])
            uc, Buc = nt()
            for s in range(nseq):
                act(uc[:, s * n:(s + 1) * n], ps[4 + s][:, 0:n], AF.Identity, [Bps[4 + s], Bsmall], [Buc], bias=ccv[:, 0, c:c + 1])
            ucq, Bucq = nt()
            act(ucq[:], uc[:], AF.Square, [Buc], [Bucq])
            ucs.append((uc, Buc, ucq, Bucq))
            for k in range(3):
                dg, Bdg = diag(sconvw[:, c, k:k + 1])
                for s in range(nseq):
                    rhs = ppad[0][:, c, s, k:k + n] if g == 0 else ppad[1][:, c, k:k + n]
                    S.add("pe", lambda e, o=ps[6 + s][:, 0:n], a=dg, r=rhs, st_=(k == 0), sp_=(k == 2): e.matmul(o, a, r, start=st_, stop=sp_),
                          reads=[Bdg, Bpp[g]], writes=[Bps[6 + s]])
            for s in range(nseq):
                tt(cat[g][:, c, s * n:(s + 1) * n], abt[g][:, c, s * n:(s + 1) * n], ps[6 + s][:, 0:n], ALU.mult, Bab[g] + [Bps[6 + s]], [Bcat[g][c]])
        b1 = nb()
        for c in range(2):
            mm(Bps[b1], ps[b1][:], Bcmat, ones_f, ucs[c][1], ucs[c][0][:], c == 0, c == 1)
        b2 = nb()
        for c in range(2):
            mm(Bps[b2], ps[b2][:], Bcmat, ones_f, ucs[c][3], ucs[c][2][:], c == 0, c == 1)
        mean, Bmean = nt()
        ts(mean[:], ps[b1][:], 1.0 / 256, ALU.mult, [Bps[b1]], [Bmean])
        var, Bvar = nt()
        tt(var[:], mean[:], mean[:], ALU.mult, [Bmean], [Bvar])
        stt(var[:], ps[b2][:], 1.0 / 256, var[:], ALU.mult, ALU.subtract, [Bps[b2], Bvar], [Bvar])
        rsqrt_chain(var[:], Bvar, var[:], 1.0, [Bvar])
        for c in range(2):
            uc, Buc, ucq, Bucq = ucs[c]
            tt(ucq[:], uc[:], mean[:], ALU.subtract, [Buc, Bmean], [Bucq])
            tt(ucq[:], ucq[:], var[:], ALU.mult, [Bucq, Bvar], [Bucq])
            act(vb[:, c, :], ucq[:], AF.Silu, [Bucq, Bsmall], [Bvb], scale=ccv[:, 1, c:c + 1], bias=ccv[:, 2, c:c + 1])
        for oc in range(2):
            b = nb()
            for c in range(2):
                mm(Bps[b], ps[b][:], Bccpw, ccpw[:, c, oc * 128:(oc + 1) * 128], Bvb, vb[:, c, :], c == 0, c == 1)
            act(cat[g][:, 2 + oc, :], ps[b][:], AF.Copy, [Bps[b]], [Bcat[g][2 + oc]])

    def mix_out(l, g):
        wk = [wload(d_wmo[l, a]) for a in range(3)]
        pieces = [(cat[g][:, 0, :], Bcat[g][0], 128), (cat[g][:, 1, :], Bcat[g][1], 128)]
        pieces += [(hb[g][:, h, :], Bhb[g][h], 64) for h in range(4)]
        pieces += [(cat[g][:, 2, :], Bcat[g][2], 128), (cat[g][:, 3, :], Bcat[g][3], 128)]
        pieces += [(hb[g][:, 4 + h, :], Bhb[g][4 + h], 64) for h in range(4)]
        for oc in range(8):
            bank = 4 + (oc % 2)
            for pi, (ap_, B_, kk) in enumerate(pieces):
                k = wk[pi // 4]
                w = wsl[k][:].rearrange("p (a n) -> p a n", a=4)
                mm(Bps[bank], ps[bank][:], Bws[k], w[0:kk, pi % 4, oc * 128:(oc + 1) * 128], B_, ap_[0:kk, :], pi == 0, pi == 11)
            flush_sq()
            evac_y(g, oc, bank)
        flush_sq(True)
        for k in wk:
            wfree(k)
        post_norm(g, 1)

    def dumpx(idx, g):
        if DEBUG:
            dma("sp", d_dbg[idx].rearrange("p (c t) -> p c t", c=8), xt[g][:], Bx[g], [])

    stage = [0]

    def chk():
        stage[0] += 1
        if stop is not None and stage[0] >= stop:
            raise _Stop()

    def mixer(l):
        cur[0] = l
        pre_norm(1, 1)
        mix_in(l, 1)
        exchange(l)
        pre_norm(0, 1)
        mix_in(l, 0)
        attention(0)
        exchange_recv(l)
        convs(0)
        mix_out(l, 0)
        hook = None
        gen = None
        if l + 1 < DEPTH:
            gen = ada_gen(l + 1)
            cnt_h = [0]

            def hook():
                n = 2 if cnt_h[0] < 6 else 1
                cnt_h[0] += 1
                for _ in range(n):
                    next(gen, None)
        attention(1, hook)
        if gen is not None:
            for _ in gen:
                pass
        halos()
        convs(1)
        mix_out(l, 1)

    try:
        gen0 = ada_gen(0)
        for _ in range(4):
            next(gen0, None)
        layer_consts(0)
        ffn_chain([(0, 0, 0, 0), (0, 0, 0, 1)], hook=lambda: next(gen0, None))
        for _ in gen0:
            pass
        ctx_v_diff(0)
        chk()
        mixer(0)
        chk()
        layer_consts(1)
        ffn_chain([(0, 1, 2, 0), (0, 1, 2, 1), (1, 0, 0, 0), (1, 0, 0, 1)])
        ctx_v_diff(1)
        mixer(1)
        ffn_chain([(1, 1, 2, 0), (1, 1, 2, 1)])
    except _Stop:
        pass
    for g in range(2):
        for c in range(8):
            dma("sp", d_y[g][:, c, :], xt[g][:, c, :], [Bx[g][c]], [])

    print('sbuf remaining', nc.sbuf_bytes_remaining, flush=True)
    S.finalize()
    with nc.Block() as block:
        @block.tensor
        def _(e):
            S.replay("pe", e, sems)

        @block.scalar
        def _(e):
            S.replay("act", e, sems)

        @block.vector
        def _(e):
            S.replay("dve", e, sems)

        @block.gpsimd
        def _(e):
            S.replay("pool", e, sems)

        @block.sync
        def _(e):
            S.replay("sp", e, sems)
    es.close()
    return nc


def _fm(a):
    t = a.shape[0]
    return np.ascontiguousarray(a.T.reshape(8, 128, t).transpose(1, 0, 2))


def _unfm(a):
    t = a.shape[2]
    return np.ascontiguousarray(a.transpose(1, 0, 2).reshape(1024, t).T)


def _kpieces(w, ncols_piece):
    n = w.shape[1]
    npieces = n // ncols_piece
    a = w.reshape(8, 128, npieces, ncols_piece).transpose(2, 1, 0, 3)
    return np.ascontiguousarray(a.reshape(npieces, 128, 8 * ncols_piece))


def _vec_fm(v, nch):
    return np.ascontiguousarray(v.reshape(nch, 128).T)


def _rope_tables():
    theta = np.float32(10000.0)
    tabs = {}
    for name, dd in (("d", 32), ("g", 64)):
        nf = dd // 4
        inv = (theta ** (-(np.arange(nf, dtype=np.float32) / np.float32(nf)))).astype(np.float32)
        tabs[name] = (dd, nf, inv)
    return tabs


def _host_inputs(inp):
    f32 = np.float32
    shared = {}
    L = DEPTH
    shared["wada"] = np.stack([_kpieces(inp["w_ada"][l], 512) for l in range(L)])
    shared["gpre"] = np.stack([np.concatenate([_vec_fm(inp["norm_pre"][l, i], 8) for i in range(3)], axis=1) for l in range(L)])
    shared["gpost"] = np.stack([np.concatenate([_vec_fm(inp["norm_post"][l, i], 8) for i in range(3)], axis=1) for l in range(L)])
    wgu = np.zeros((L, 2, 11, 128, 4096), f32)
    wdn = np.zeros((L, 2, 8, 128, NFC * 128), f32)
    for l in range(L):
        for wi, (gn, un, dn) in enumerate((("ffn1_gate", "ffn1_up", "ffn1_down"), ("ffn2_gate", "ffn2_up", "ffn2_down"))):
            gp = _kpieces(inp[gn][l], 256).reshape(11, 128, 2048)
            up = _kpieces(inp[un][l], 256).reshape(11, 128, 2048)
            wgu[l, wi] = np.concatenate([gp, up], axis=2)
            wd = inp[dn][l]
            a = wd.reshape(NFC, 128, 8, 128).transpose(2, 1, 0, 3)
            wdn[l, wi] = a.reshape(8, 128, NFC * 128)
    shared["wgu"] = wgu
    shared["wdn"] = wdn
    wmi = np.zeros((L, 5, 128, 4096), f32)
    wmo = np.zeros((L, 3, 128, 4096), f32)
    for l in range(L):
        w = inp["w_mix_in"][l].copy()
        dq = w[:, 2048:2304].reshape(1024, 4, 64)
        w[:, 2048:2304] = dq[:, [0, 2, 1, 3], :].reshape(1024, 256)
        w[:, 0:768] = np.concatenate([w[:, 256:512], w[:, 512:768], w[:, 0:256]], axis=1)
        wmi[l] = _kpieces(w, 512)
        wo = inp["w_mix_out"][l]
        pieces = np.zeros((12, 128, 1024), f32)
        pieces[0] = wo[0:128]; pieces[1] = wo[128:256]
        for h in range(4):
            pieces[2 + h, 0:64] = wo[256 + h * 64:256 + (h + 1) * 64]
        pieces[6] = wo[512:640]; pieces[7] = wo[640:768]
        for slot, h in enumerate((0, 2, 1, 3)):
            pieces[8 + slot, 0:64] = wo[768 + h * 64:768 + (h + 1) * 64]
        wmo[l] = pieces.reshape(3, 4, 128, 1024).transpose(0, 2, 1, 3).reshape(3, 128, 4096)
    shared["wmi"] = wmi
    shared["wmo"] = wmo
    shared["sconvw"] = np.stack([np.ascontiguousarray(inp["sconv_w"][l].reshape(3, 2, 128).transpose(2, 1, 0)).reshape(128, 6) for l in range(L)])
    shared["ccw"] = np.stack([np.ascontiguousarray(inp["ccm_dw_w"][l].reshape(31, 2, 128).transpose(2, 1, 0)).reshape(128, 62) for l in range(L)])
    shared["ccv"] = np.stack([np.concatenate([_vec_fm(inp[n][l], 2) for n in ("ccm_dw_b", "ccm_ln_g", "ccm_ln_b")], axis=1) for l in range(L)])
    shared["ccpw"] = np.stack([np.ascontiguousarray(inp["ccm_pw"][l].reshape(2, 128, 256).transpose(1, 0, 2)).reshape(128, 512) for l in range(L)])
    shared["lamv"] = np.stack([np.tile(np.concatenate([inp[n][l] for n in ("diff_lq1", "diff_lk1", "diff_lq2", "diff_lk2")])[None, :], (128, 1)) for l in range(L)])
    shared["subln"] = np.stack([np.tile(inp["diff_subln"][l], 2).reshape(128, 1) for l in range(L)])
    shared["qkn"] = np.stack([np.stack([np.tile(inp["gqa_qnorm"][l], 2), np.tile(inp["gqa_knorm"][l], 2)], axis=1) for l in range(L)])
    shared["kng"] = np.stack([np.tile(inp["gqa_knorm"][l][None, :], (128, 1)) for l in range(L)])
    cm = np.zeros((128, 5, 128), f32)
    cm[:, 0, :] = 1.0
    cm[0:64, 1, 0:64] = 1.0; cm[64:128, 1, 64:128] = 1.0
    cm[:, 2, :] = np.eye(128, dtype=f32)
    for i in range(128):
        cm[i ^ 8, 3, i] = 1.0
        cm[i ^ 16, 4, i] = 1.0
    shared["cmat"] = cm.reshape(128, 640)
    for k in shared:
        shared[k] = np.ascontiguousarray(shared[k], dtype=f32)

    tabs = _rope_tables()
    maps = []
    for i in range(NCORES):
        b, j = i // 4, i % 4
        m = dict(shared)
        m["xp"] = _fm(inp["x_prompt"][2 * i:2 * i + 2].reshape(512, 1024))
        m["xs"] = _fm(inp["x_sample"][b, 512 * j:512 * (j + 1)])
        m["cnd"] = np.ascontiguousarray(np.stack([_vec_fm(inp["c_ctx"], 8), _vec_fm(inp["c"][b], 8)], axis=2))
        m["bada"] = np.stack([np.repeat(_vec_fm(inp["b_ada"][l], 72), 2, axis=1) for l in range(L)])
        m["ckd"] = np.stack([np.ascontiguousarray(inp["cache_diff_k"][b, l].reshape(256, 2, 128).transpose(2, 1, 0)).reshape(128, 512) for l in range(L)])
        m["ckg"] = np.stack([np.ascontiguousarray(inp["cache_gqa_k"][b, l].reshape(256, 128).T) for l in range(L)])
        m["cvd"] = np.stack([np.ascontiguousarray(inp["cache_diff_v"][b, l].reshape(2, 128, 256).transpose(1, 0, 2)).reshape(128, 512) for l in range(L)])
        m["cvg"] = np.stack([np.ascontiguousarray(inp["cache_gqa_v"][b, l].reshape(2, 128, 128).transpose(1, 0, 2)).reshape(128, 256) for l in range(L)])
        tok = np.arange(512 * j, 512 * (j + 1))
        row = (tok // 64).astype(f32)
        col = (tok % 64).astype(f32)
        rt = np.zeros((128, 4, T), f32)
        for ti, name in enumerate(("d", "g")):
            dd, nf, inv = tabs[name]
            for p in range(128):
                f = p % dd
                pos = row if f < dd // 2 else col
                ang = (pos * inv[f % nf]).astype(f32)
                sgn = -1.0 if (f % (2 * nf)) < nf else 1.0
                rt[p, 2 * ti, :] = np.cos(ang)
                rt[p, 2 * ti + 1, :] = sgn * np.sin(ang)
        m["rope"] = rt.reshape(128, 4 * T)
        mk = np.zeros((128, 16), f32)
        for q in range(4):
            mk[32 * q:32 * (q + 1), q] = 1.0
        mk[0:64, 4] = 1.0
        mk[64:128, 5] = 1.0
        if j > 0:
            mk[:, 8 + j - 1] = 1.0
        if j < 3:
            mk[:, 12 + j + 1] = 1.0
        m["msk"] = mk
        for k in m:
            m[k] = np.ascontiguousarray(m[k], dtype=f32)
        maps.append(m)
    return maps


_NC_CACHE = {}


def kernel(**inputs):
    inp = {k: np.asarray(v) for k, v in inputs.items()}
    if "nc" not in _NC_CACHE:
        _NC_CACHE["nc"] = build_program()
    nc = _NC_CACHE["nc"]
    in_maps = _host_inputs(inp)
    res = run_bass_kernel_spmd(nc, in_maps, core_ids=list(range(NCORES)))
    R = res.results
    y_prompt = np.zeros((16, 256, 1024), np.float32)
    y_sample = np.zeros((2, 2048, 1024), np.float32)
    ndk = np.zeros((16, DEPTH, 256, 4, 2, 32), np.float32)
    ndv = np.zeros((16, DEPTH, 256, 4, 64), np.float32)
    ngk = np.zeros((16, DEPTH, 256, 2, 64), np.float32)
    ngv = np.zeros((16, DEPTH, 256, 2, 64), np.float32)
    for i in range(NCORES):
        b, j = i // 4, i % 4
        y_prompt[2 * i:2 * i + 2] = _unfm(R[i]["yp"]).reshape(2, 256, 1024)
        y_sample[b, 512 * j:512 * (j + 1)] = _unfm(R[i]["ys"])
        for l in range(DEPTH):
            ndk[2 * i:2 * i + 2, l] = R[i]["ndk"][l].reshape(2, 256, 4, 2, 32)
            ndv[2 * i:2 * i + 2, l] = R[i]["ndv"][l].reshape(2, 256, 4, 64)
            ngk[2 * i:2 * i + 2, l] = R[i]["ngk"][l].reshape(2, 256, 2, 64)
            ngv[2 * i:2 * i + 2, l] = R[i]["ngv"][l].reshape(2, 256, 2, 64)
    return (y_prompt, y_sample, ndk, ndv, ngk, ngv)
```

```python
import math
from contextlib import ExitStack
import numpy as np
import concourse.bass as bass
import concourse.mybir as mybir
from concourse.bass_utils import run_bass_kernel_spmd

F32 = mybir.dt.float32
BF16 = mybir.dt.bfloat16
ALU = mybir.AluOpType
AF = mybir.ActivationFunctionType

D = 1024
DFF = 2816
NFC = 22
DEPTH = 2
T = 512
EPS = 1e-6
NCORES = 8
NKS = 2304
BROWS = 784
NW = 3
NDS = 12
DEBUG = False


class Buf:
    __slots__ = ("name", "w", "r", "psum")

    def __init__(self, name, psum=False):
        self.name = name
        self.w = None
        self.r = {}
        self.psum = psum


class Op:
    __slots__ = ("eng", "emit", "deps", "needs_inc", "ticket", "kind", "dsem", "dval", "pre")

    def __init__(self, eng, emit, kind):
        self.eng = eng
        self.emit = emit
        self.deps = []
        self.needs_inc = False
        self.ticket = 0
        self.kind = kind
        self.dsem = None
        self.dval = 0
        self.pre = None


class Sched:
    ENGS = ("pe", "act", "dve", "pool", "sp")

    def __init__(self):
        self.ops = {e: [] for e in self.ENGS}
        self.ndma = {e: 0 for e in self.ENGS}
        self.ncc = 0

    def add(self, eng, emit, reads=(), writes=(), kind="c"):
        op = Op(eng, emit, kind)
        deps = {}

        def dep(o):
            if o is None or o is op:
                return
            if eng == "pe" and kind == "c" and o.eng == "pe" and o.kind == "c":
                return
            deps[id(o)] = o

        for b in reads:
            dep(b.w)
            if b.psum:
                for k_, o in b.r.items():
                    if k_ != eng:
                        dep(o)
        for b in writes:
            dep(b.w)
            for o in b.r.values():
                dep(o)
        for b in writes:
            b.w = op
            b.r = {}
        for b in reads:
            if kind == "c":
                b.r[eng] = op
            else:
                b.r[("d", id(op))] = op
        op.deps = list(deps.values())
        for o in op.deps:
            if o.kind == "c":
                o.needs_inc = True
        if kind == "d":
            n = self.ndma[eng]
            self.ndma[eng] = n + 1
            op.dsem = (eng, n % NDS)
            op.dval = 16 * (n // NDS + 1)
            if n >= NDS:
                op.pre = (op.dsem, 16 * (n // NDS))
        elif kind == "cc":
            self.ncc += 1
            op.dsem = ("cc", self.ncc - 1)
            op.dval = 1
        self.ops[eng].append(op)
        return op

    def finalize(self):
        for e in self.ENGS:
            t = 0
            for op in self.ops[e]:
                if op.kind == "c" and op.needs_inc:
                    t += 1
                    op.ticket = t

    def replay(self, e, eng, sems):
        waited = {}

        def wait(key, val):
            if waited.get(key, 0) < val:
                eng.wait_ge(sems[key], val)
                waited[key] = val

        for op in self.ops[e]:
            for o in op.deps:
                if o.kind == "c":
                    wait((o.eng, "c"), o.ticket)
                else:
                    wait(o.dsem, o.dval)
            if op.pre is not None:
                wait(op.pre[0], op.pre[1])
            ins = op.emit(eng)
            if op.kind == "d":
                ins.then_inc(sems[op.dsem], 16)
            elif op.kind == "cc":
                ins.then_inc(sems[op.dsem])
            elif op.needs_inc:
                ins.then_inc(sems[(e, "c")], 1)
        n = self.ndma[e]
        for k in range(min(n, NDS)):
            cnt = (n - 1 - k) // NDS + 1
            wait((e, k), 16 * cnt)
        if e == "pool":
            for k in range(self.ncc):
                wait(("cc", k), 1)


class _Stop(Exception):
    pass


def build_program(stop=None, substop=None):
    nc = bass.Bass("TRN2", target_bir_lowering=False)
    S = Sched()
    es = ExitStack()

    def din(name, shape, dt=F32):
        return nc.dram_tensor(name, list(shape), dt, kind="ExternalInput").ap()

    def dout(name, shape, dt=F32):
        return nc.dram_tensor(name, list(shape), dt, kind="ExternalOutput").ap()

    d_x = [din("xp", [128, 8, T]), din("xs", [128, 8, T])]
    d_cnd = din("cnd", [128, 8, 2])
    d_wada = din("wada", [DEPTH, 18, 128, 8 * 512])
    d_bada = din("bada", [DEPTH, 128, 144])
    d_gpre = din("gpre", [DEPTH, 128, 24])
    d_gpost = din("gpost", [DEPTH, 128, 24])
    d_wgu = din("wgu", [DEPTH, 2, 11, 128, 4096])
    d_wdn = din("wdn", [DEPTH, 2, 8, 128, NFC * 128])
    d_wmi = din("wmi", [DEPTH, 5, 128, 4096])
    d_wmo = din("wmo", [DEPTH, 3, 128, 4096])
    d_sconv = din("sconvw", [DEPTH, 128, 6])
    d_ccw = din("ccw", [DEPTH, 128, 62])
    d_ccv = din("ccv", [DEPTH, 128, 6])
    d_ccpw = din("ccpw", [DEPTH, 128, 512])
    d_lam = din("lamv", [DEPTH, 128, 128])
    d_subln = din("subln", [DEPTH, 128, 1])
    d_qkn = din("qkn", [DEPTH, 128, 2])
    d_kng = din("kng", [DEPTH, 128, 64])
    d_ckd = din("ckd", [DEPTH, 128, 2 * 256])
    d_ckg = din("ckg", [DEPTH, 128, 256])
    d_cvd = din("cvd", [DEPTH, 128, 2 * 256])
    d_cvg = din("cvg", [DEPTH, 128, 2 * 128])
    d_rope = din("rope", [128, 4 * T])
    d_cmat = din("cmat", [128, 5 * 128])
    d_msk = din("msk", [128, 16])
    d_y = [dout("yp", [128, 8, T]), dout("ys", [128, 8, T])]
    d_ndk = dout("ndk", [DEPTH, T, 256])
    d_ndv = dout("ndv", [DEPTH, T, 256])
    d_ngk = dout("ngk", [DEPTH, T, 128])
    d_ngv = dout("ngv", [DEPTH, T, 128])
    if DEBUG:
        d_dbg = dout("dbg", [8, 128, 8 * T])
        d_dbgb = dout("dbgb", [4, 128, 8 * T], BF16)
    d_bounce = [nc.dram_tensor("bounce%d" % l, [BROWS, 512], BF16).ap() for l in range(DEPTH)]
    d_gath = [nc.dram_tensor("gath%d" % l, [4 * BROWS, 512], BF16).ap() for l in range(DEPTH)]
    B_bounce = [Buf("bounce%d" % l) for l in range(DEPTH)]
    B_gath = [Buf("gath%d" % l) for l in range(DEPTH)]

    def sb(name, shape, dt=F32):
        return es.enter_context(nc.sbuf_tensor("s_" + name, list(shape), dt))

    xt = [sb("xP", [128, 8, T]), sb("xS", [128, 8, T])]
    Bx = [[Buf("x%d_%d" % (g, c)) for c in range(8)] for g in range(2)]
    h1 = sb("h", [128, 8, T], BF16)
    ht = [h1, h1]
    Bh1 = [Buf("h_%d" % c) for c in range(8)]
    Bh = [Bh1, Bh1]
    y1 = sb("y", [128, 8, T])
    yt = [y1, y1]
    By1 = [Buf("y_%d" % c) for c in range(8)]
    By = [By1, By1]
    wsl = [sb("ws%d" % k, [128, 4096], BF16) for k in range(NW)]
    Bws = [Buf("ws%d" % k) for k in range(NW)]
    ps = [es.enter_context(nc.psum_tensor("ps%d" % k, [128, 512], F32)) for k in range(8)]
    Bps = [Buf("ps%d" % k, psum=True) for k in range(8)]
    hmid1 = sb("hmid", [128, NFC, T], BF16)
    Bhm1 = [Buf("hm_%d" % f) for f in range(NFC)]

    NTP = 6
    tpool = [sb("tp%d" % k, [128, T]) for k in range(NTP)]
    Btp = [Buf("tp%d" % k) for k in range(NTP)]
    tpc = [0]

    rstd_t = sb("rstd_t", [128, T]); Brstd_t = Buf("rstd_t")

    def nt():
        k = tpc[0] % NTP
        tpc[0] += 1
        return tpool[k], Btp[k]

    cnd = sb("cnd", [128, 8, 2]); Bcnd = Buf("cnd")
    scnd = sb("scnd", [128, 8, 2], BF16); Bscnd = Buf("scnd")
    bada_l = [sb("bada%d" % l, [128, 144]) for l in range(DEPTH)]; Bbada_l = [Buf("bada%d" % l) for l in range(DEPTH)]
    modt_l = [sb("modt%d" % l, [128, 72, 2]) for l in range(DEPTH)]; Bmod_l = [[Buf("mod%d_%d" % (l, i)) for i in range(3)] for l in range(DEPTH)]
    gpre_l = [sb("gpre%d" % l, [128, 3, 8]) for l in range(DEPTH)]; gpost_l = [sb("gpost%d" % l, [128, 3, 8]) for l in range(DEPTH)]
    Bgp_l = [Buf("gprepost%d" % l) for l in range(DEPTH)]
    Amod_l = [sb("Amod%d" % l, [128, 2, 3, 8]) for l in range(DEPTH)]; Gmod_l = [sb("Gmod%d" % l, [128, 2, 3, 8]) for l in range(DEPTH)]
    BAG_l = [[Buf("AG%d_%d" % (l, i)) for i in range(3)] for l in range(DEPTH)]
    BGG_l = [[Buf("GG%d_%d" % (l, i)) for i in range(3)] for l in range(DEPTH)]
    Bmodg_l = [[Buf("modg%d_%d" % (l, i)) for i in range(3)] for l in range(DEPTH)]
    cur = [0]
    sconvw = sb("sconvw", [128, 2, 3]); ccw = sb("ccw", [128, 2, 31]); ccv = sb("ccv", [128, 3, 2])
    Bsmall = Buf("small")
    ccpw = sb("ccpw", [128, 2, 256], BF16); Bccpw = Buf("ccpw")
    lamv = sb("lamv", [128, 4, 32]); lamt = sb("lamt", [128, 8]); Blam = Buf("lam")
    subln = sb("subln", [128, 1]); qkn = sb("qkn", [128, 2]); kng = sb("kng", [128, 64])
    rope = sb("rope", [128, 4, T]); Brope = Buf("rope")
    cmatf = sb("cmatf", [128, 5, 128]); cmat = sb("cmat", [128, 5, 128], BF16); Bcmat = Buf("cmat")
    msk = sb("msk", [128, 16]); Bmsk = Buf("msk")
    NDG = 6
    dgr = [sb("dgr%d" % k, [128, 128], BF16) for k in range(NDG)]; Bdgr = [Buf("dgr%d" % k) for k in range(NDG)]
    dgc = [0]

    hx = sb("hx", [128, 8, T], BF16)
    Bhx = [Buf("hx_%d" % c) for c in range(8)]
    abt = [hx[:, 0:2, :], hx[:, 2:4, :]]; Bab = [Bhx[0:2], Bhx[2:4]]
    ppad = [sb("ppadP", [128, 2, 2, 258], BF16), sb("ppadS", [128, 2, 514], BF16)]; Bpp = [Buf("pp0"), Buf("pp1")]
    upad = [sb("upadP", [128, 2, 2, 286], BF16), sb("upadS", [128, 2, 542], BF16)]; Bup = [Buf("up0"), Buf("up1")]
    cat1 = hx[:, 4:8, :]
    cat = [cat1, cat1]
    Bcat1 = Bhx[4:8]
    Bcat = [Bcat1, Bcat1]
    hb1 = sb("hb", [64, 8, T], BF16)
    hb = [hb1, hb1]
    Bhb1 = [Buf("hb_%d" % k) for k in range(8)]
    Bhb = [Bhb1, Bhb1]
    qpk = [sb("qpk%d" % g, [128, 4, T], BF16) for g in range(2)]
    Bqpk = [[Buf("qpk%d_%d" % (g, k)) for k in range(4)] for g in range(2)]
    qmr = [sb("qmr%d" % k, [128, T], BF16) for k in range(2)]; Bqmr = [Buf("qmr%d" % k) for k in range(2)]
    kdP = sb("kdP", [128, 2, T], BF16); kgP = sb("kgP", [128, T], BF16)
    kdS = sb("kdS", [128, 2, NKS], BF16); kgS = sb("kgS", [128, NKS], BF16)
    Bkd = [Buf("kdP"), Buf("kdS_own")]; Bkg = [Buf("kgP"), Buf("kgS_own")]
    Bkctx = Buf("kctx"); Bkgath = Buf("kgath")
    vaP = sb("vaP", [128, 4, 6, 128], BF16)
    vaSg = sb("vaSg", [128, 18, 2, 128], BF16)
    vaSd = hmid1[:].rearrange("p f t -> p (f t)")[:, 0:18 * 512].rearrange("p (k h d) -> p k h d", h=4, d=128)
    BvaSd = Bhm1[0:18]
    BvaP = Buf("vaP"); Bvgc = Buf("vgctx"); Bvgg = Buf("vggath")
    vtok = cat1.rearrange("p a t -> p (a t)")[:, 0:1536].rearrange("p (t v) -> p t v", v=384)
    edge = sb("edge", [128, 2, 32], BF16); Bedge = Buf("edge")
    egt = sb("egt", [128, 2, 4, 32], BF16); Begt = Buf("egt")
    halo = sb("halo", [128, 2, 2, 16]); Bhalo = Buf("halo")
    pT = [sb("pT%d" % k, [128, T], BF16) for k in range(2)]; BpT = [Buf("pT%d" % k) for k in range(2)]
    sqr = pT[0:2]; Bsqr = BpT[0:2]
    pT = pT + [hmid1[:, 18, :], hmid1[:, 19, :]]; BpT = BpT + [Bhm1[18], Bhm1[19]]
    vb = sb("vb", [128, 2, T], BF16); Bvb = Buf("vb")
    qbf = sb("qbf", [128, T], BF16); Bqbf = Buf("qbf")
    qbc = sb("qbc", [128, T], BF16); Bqbc = Buf("qbc")
    ktm = sb("ktm", [128, 8]); Bktm = Buf("ktm")

    sems = {}
    for e in Sched.ENGS:
        sems[(e, "c")] = es.enter_context(nc.semaphore("sc_" + e))
    for e in ("sp", "pool", "act"):
        for k in range(NDS):
            sems[(e, k)] = es.enter_context(nc.semaphore("sd_%s%d" % (e, k)))
    for k in range(DEPTH):
        sems[("cc", k)] = es.enter_context(nc.semaphore("scc%d" % k))

    def mm(outb, out, lb, lhsT, rb, rhs, start, stop):
        S.add("pe", lambda e: e.matmul(out, lhsT, rhs, start=start, stop=stop), reads=[lb, rb], writes=[outb])

    def act(out, in_, func, reads, writes, scale=None, bias=None):
        kw = {}
        if scale is not None:
            kw["scale"] = scale
        if bias is not None:
            kw["bias"] = bias
        S.add("act", lambda e: e.activation(out, in_, func, **kw), reads=reads, writes=writes)

    def tt(out, a, b, op, reads, writes, eng="dve"):
        S.add(eng, lambda e: e.tensor_tensor(out, a, b, op), reads=reads, writes=writes)

    def ts(out, a, s1, op0, reads, writes, eng="dve"):
        S.add(eng, lambda e: e.tensor_scalar(out, a, s1, None, op0), reads=reads, writes=writes)

    def stt(out, a, s, b, op0, op1, reads, writes):
        S.add("dve", lambda e: e.scalar_tensor_tensor(out, a, s, b, op0, op1), reads=reads, writes=writes)

    def cp(out, in_, reads, writes, eng="dve"):
        S.add(eng, lambda e: e.tensor_copy(out, in_), reads=reads, writes=writes)

    def dma(q, out, in_, reads, writes):
        S.add(q, lambda e: e.dma_start(out=out, in_=in_), reads=reads, writes=writes, kind="d")

    def memset(ap, val, writes, eng="dve"):
        S.add(eng, lambda e: e.memset(ap, val), reads=[], writes=writes)

    epsb = sb("epsb", [128, 1]); Bepsb = Buf("epsb")
    memset(epsb[:], EPS, [Bepsb])

    def rsqrt_chain(out, Bout, in_, mult, reads_in):
        np_ = out.shape[0]
        act(out, in_, AF.Ln, reads_in + [Bepsb], [Bout], scale=mult, bias=epsb[0:np_, 0:1])
        act(out, out, AF.Exp, [Bout], [Bout], scale=-0.5)

    wfree_list = list(range(NW))

    def wload(src_ap, nel=4096):
        k = wfree_list.pop(0)
        dma("pool", wsl[k][:, 0:nel].rearrange("p (a n) -> p a n", a=4), src_ap[:, 0:nel].rearrange("p (a n) -> p a n", a=4), [], [Bws[k]])
        return k

    def wfree(k):
        wfree_list.append(k)

    for g in range(2):
        for c in range(8):
            dma("sp", xt[g][:, c, :], d_x[g][:, c, :], [], [Bx[g][c]])
    dma("sp", cnd[:], d_cnd[:], [], [Bcnd])
    dma("sp", rope[:].rearrange("p a t -> p (a t)"), d_rope[:], [], [Brope])
    dma("sp", cmatf[:].rearrange("p a t -> p (a t)"), d_cmat[:], [], [Bcmat])
    dma("sp", msk[:], d_msk[:], [], [Bmsk])
    cp(cmat[:], cmatf[:], [Bcmat], [Bcmat])
    ones_b = cmat[:, 0, :]
    blk64 = cmat[:, 1, :]
    ident = cmat[:, 2, :]
    permd = cmat[:, 3, :]
    permg = cmat[:, 4, :]
    ones_f = cmatf[:, 0, :]
    act(scnd[:], cnd[:], AF.Silu, [Bcnd], [Bscnd])
    memset(vaP[:, :, :, 64:128], 1.0, [BvaP], eng="pool")
    memset(vaSg[:, :, :, 64:128], 1.0, [Bvgc, Bvgg], eng="pool")
    memset(ppad[0][:], 0.0, [Bpp[0]], eng="pool")
    memset(upad[0][:], 0.0, [Bup[0]], eng="pool")

    bankrr = [0]

    def nb():
        b = bankrr[0] % 4
        bankrr[0] += 1
        return b

    def sumsq_pre(g):
        for c in range(8):
            k = c % 2
            act(sqr[k][:], xt[g][:, c, :], AF.Square, [Bx[g][c]], [Bsqr[k]])
            mm(Bps[6], ps[6][:], Bcmat, ones_b, Bsqr[k], sqr[k][:], c == 0, c == 7)

    def pre_norm_gen(g, i, l=None, hb_=None):
        l = cur[0] if l is None else l
        hdst, Bhd = (h1, Bh1) if hb_ is None else hb_
        for c in range(8):
            k = c % 2
            act(sqr[k][:], xt[g][:, c, :], AF.Square, [Bx[g][c]], [Bsqr[k]])
            if c > 0:
                mm(Bps[6], ps[6][:], Bcmat, ones_b, Bsqr[1 - k], sqr[1 - k][:], c == 1, False)
            yield
        mm(Bps[6], ps[6][:], Bcmat, ones_b, Bsqr[1], sqr[1][:], False, True)
        rstd, Brstd = rstd_t, Brstd_t
        rsqrt_chain(rstd[:], Brstd, ps[6][:], 1.0 / D, [Bps[6]])
        for c in range(8):
            t_, Bt = nt()
            tt(t_[:], xt[g][:, c, :], rstd[:], ALU.mult, [Bx[g][c], Brstd], [Bt])
            act(hdst[:, c, :], t_[:], AF.Identity, [Bt, BAG_l[l][i], Bmod_l[l][i]], [Bhd[c]],
                scale=Amod_l[l][:, g, i, c:c + 1], bias=modt_l[l][:, (3 * i) * 8 + c, g:g + 1])

    def pre_norm(g, i, l=None, hb_=None):
        for _ in pre_norm_gen(g, i, l, hb_):
            pass

    def evac_y(g, oc, bank):
        act(yt[g][:, oc, :], ps[bank][:], AF.Copy, [Bps[bank]], [By[g][oc]])
        k = oc % 2
        act(sqr[k][:], ps[bank][:], AF.Square, [Bps[bank]], [Bsqr[k]])
        pend_sq.append((oc, k))

    pend_sq = []

    def flush_sq(everything=False):
        while pend_sq and (everything or len(pend_sq) > 0):
            oc, k = pend_sq.pop(0)
            mm(Bps[6], ps[6][:], Bcmat, ones_b, Bsqr[k], sqr[k][:], oc == 0, oc == 7)

    def post_norm(g, i, l=None):
        l = cur[0] if l is None else l
        rstd, Brstd = rstd_t, Brstd_t
        rsqrt_chain(rstd[:], Brstd, ps[6][:], 1.0 / D, [Bps[6]])
        for c in range(8):
            t_, Bt = nt()
            stt(t_[:], yt[g][:, c, :], Gmod_l[l][:, g, i, c:c + 1], rstd[:], ALU.mult, ALU.mult, [By[g][c], BGG_l[l][i], Brstd], [Bt])
            tt(xt[g][:, c, :], xt[g][:, c, :], t_[:], ALU.add, [Bx[g][c], Bt], [Bx[g][c]])

    def ffn_chain(jobs, hook=None):
        hsel = lambda g: (h1, Bh1) if g == 0 else (hx, Bhx)
        l0, w0, i0, g0 = jobs[0]
        pre_norm(g0, i0, l0, hsel(g0))
        for jn, (l, which, i, g) in enumerate(jobs):
            hsrc, Bhs = hsel(g)
            cnt = 0
            png = None
            if jn + 1 < len(jobs):
                ln, wn, in_, gn = jobs[jn + 1]
                png = pre_norm_gen(gn, in_, ln, hsel(gn))
            for f in range(11):
                k = wload(d_wgu[l, which, f])
                w = wsl[k][:].rearrange("p (a c n) -> p a c n", a=2, c=8)
                for j in range(2):
                    fc = 2 * f + j
                    bg, bu = (cnt % 2) * 2, (cnt % 2) * 2 + 1
                    cnt += 1
                    for c in range(8):
                        mm(Bps[bg], ps[bg][:], Bws[k], w[:, 0, c, j * 128:(j + 1) * 128], Bhs[c], hsrc[:, c, :], c == 0, c == 7)
                    for c in range(8):
                        mm(Bps[bu], ps[bu][:], Bws[k], w[:, 1, c, j * 128:(j + 1) * 128], Bhs[c], hsrc[:, c, :], c == 0, c == 7)
                    sk = cnt % 2
                    act(qmr[sk][:], ps[bg][:], AF.Silu, [Bps[bg]], [Bqmr[sk]])
                    tt(hmid1[:, fc, :], qmr[sk][:], ps[bu][:], ALU.mult, [Bqmr[sk], Bps[bu]], [Bhm1[fc]])
                wfree(k)
                if hook is not None:
                    hook()
                if png is not None and f >= 2:
                    next(png, None)
            if png is not None:
                for _ in png:
                    pass
            for oc in range(8):
                k = wload(d_wdn[l, which, oc], NFC * 128)
                w = wsl[k][:, 0:NFC * 128].rearrange("p (f n) -> p f n", f=NFC)
                bank = 4 + (oc % 2)
                for fc in range(NFC):
                    mm(Bps[bank], ps[bank][:], Bws[k], w[:, fc, :], Bhm1[fc], hmid1[:, fc, :], fc == 0, fc == NFC - 1)
                wfree(k)
                flush_sq()
                evac_y(g, oc, bank)
            flush_sq(True)
            post_norm(g, i, l)

    def ffn(l, which, i, g, hook=None):
        pre_norm(g, i)
        cnt = 0
        for f in range(11):
            k = wload(d_wgu[l, which, f])
            w = wsl[k][:].rearrange("p (a c n) -> p a c n", a=2, c=8)
            for j in range(2):
                fc = 2 * f + j
                bg, bu = (cnt % 2) * 2, (cnt % 2) * 2 + 1
                cnt += 1
                for c in range(8):
                    mm(Bps[bg], ps[bg][:], Bws[k], w[:, 0, c, j * 128:(j + 1) * 128], Bh[g][c], ht[g][:, c, :], c == 0, c == 7)
                for c in range(8):
                    mm(Bps[bu], ps[bu][:], Bws[k], w[:, 1, c, j * 128:(j + 1) * 128], Bh[g][c], ht[g][:, c, :], c == 0, c == 7)
                sk = cnt % 2
                act(qmr[sk][:], ps[bg][:], AF.Silu, [Bps[bg]], [Bqmr[sk]])
                tt(hmid1[:, fc, :], qmr[sk][:], ps[bu][:], ALU.mult, [Bqmr[sk], Bps[bu]], [Bhm1[fc]])
            wfree(k)
            if hook is not None:
                hook()
        for oc in range(8):
            k = wload(d_wdn[l, which, oc], NFC * 128)
            w = wsl[k][:, 0:NFC * 128].rearrange("p (f n) -> p f n", f=NFC)
            bank = 4 + (oc % 2)
            for fc in range(NFC):
                mm(Bps[bank], ps[bank][:], Bws[k], w[:, fc, :], Bhm1[fc], hmid1[:, fc, :], fc == 0, fc == NFC - 1)
            wfree(k)
            evac_y(g, oc, bank)
        post_norm(g, i)

    def ada_gen(l):
        bada, modt, gpre, gpost, Amod, Gmod = bada_l[l], modt_l[l], gpre_l[l], gpost_l[l], Amod_l[l], Gmod_l[l]
        Bbada, Bmod, Bgp, BAG = Bbada_l[l], Bmod_l[l], Bgp_l[l], BAG_l[l]
        dma("sp", bada[:], d_bada[l], [], [Bbada])
        dma("sp", gpre[:].rearrange("p a c -> p (a c)"), d_gpre[l], [], [Bgp])
        dma("sp", gpost[:].rearrange("p a c -> p (a c)"), d_gpost[l], [], [Bgp])
        for pa in range(18):
            k = wload(d_wada[l, pa])
            w = wsl[k][:].rearrange("p (c n) -> p c n", c=8)
            for cc in range(4):
                ci = pa * 4 + cc
                for c in range(8):
                    mm(Bps[7], ps[7][:, 2 * ci:2 * ci + 2], Bws[k], w[:, c, cc * 128:(cc + 1) * 128], Bscnd, scnd[:, c, :], c == 0, c == 7)
            wfree(k)
            if pa % 6 == 3:
                i = pa // 6
                lo, hi = (3 * i) * 16, (3 * i + 2) * 16
                tt(modt[:].rearrange("p a k -> p (a k)")[:, lo:hi], ps[7][:, lo:hi], bada[:, lo:hi], ALU.add, [Bps[7], Bbada], [Bmod[i]])
                for g in range(2):
                    stt(Amod[:, g, i, :], modt[:, (3 * i + 1) * 8:(3 * i + 2) * 8, g], 1.0, gpre[:, i, :], ALU.add, ALU.mult,
                        [Bmod[i], Bgp], [BAG[i]])
            if pa % 6 == 5:
                i = pa // 6
                lo, hi = (3 * i + 2) * 16, (3 * i + 3) * 16
                tt(modt[:].rearrange("p a k -> p (a k)")[:, lo:hi], ps[7][:, lo:hi], bada[:, lo:hi], ALU.add, [Bps[7], Bbada], [Bmodg_l[l][i]])
                for g in range(2):
                    stt(Gmod[:, g, i, :], modt[:, (3 * i + 2) * 8:(3 * i + 3) * 8, g], 1.0 if i == 1 else 0.5, gpost[:, i, :],
                        ALU.mult, ALU.mult, [Bmodg_l[l][i], Bgp], [BGG_l[l][i]])
            if pa < 17:
                yield

    def layer_consts(l):
        dma("sp", sconvw[:].rearrange("p a c -> p (a c)"), d_sconv[l], [], [Bsmall])
        dma("sp", ccw[:].rearrange("p a c -> p (a c)"), d_ccw[l], [], [Bsmall])
        dma("sp", ccv[:].rearrange("p a c -> p (a c)"), d_ccv[l], [], [Bsmall])
        dma("sp", subln[:], d_subln[l], [], [Bsmall])
        dma("sp", qkn[:], d_qkn[l], [], [Bsmall])
        dma("sp", kng[:], d_kng[l], [], [Bsmall])
        dma("sp", lamv[:].rearrange("p a c -> p (a c)"), d_lam[l], [], [Blam])
        dma("pool", ccpw[:].rearrange("p a c -> p (a c)"), d_ccpw[l], [], [Bccpw])
        dma("pool", kdS[:, :, 0:256], d_ckd[l].rearrange("p (c t) -> p c t", c=2), [], [Bkctx])
        dma("pool", kgS[:, 0:256], d_ckg[l], [], [Bkctx])
        dma("pool", vaSg[:, 0:2, :, 0:64], d_cvg[l].rearrange("p (c h d) -> p c h d", c=2, h=2), [], [Bvgc])
        lam_init = 0.8 - 0.6 * math.exp(-0.3 * l)
        tt(lamv[:, 0, :], lamv[:, 0, :], lamv[:, 1, :], ALU.mult, [Blam], [Blam])
        tt(lamv[:, 2, :], lamv[:, 2, :], lamv[:, 3, :], ALU.mult, [Blam], [Blam])
        S.add("dve", lambda e: e.tensor_reduce(lamt[:, 2:3], lamv[:, 0, :], mybir.AxisListType.X, ALU.add), reads=[Blam], writes=[Blam])
        S.add("dve", lambda e: e.tensor_reduce(lamt[:, 3:4], lamv[:, 2, :], mybir.AxisListType.X, ALU.add), reads=[Blam], writes=[Blam])
        act(lamt[:, 4:6], lamt[:, 2:4], AF.Exp, [Blam], [Blam])
        tt(lamt[:, 6:7], lamt[:, 5:6], lamt[:, 4:5], ALU.subtract, [Blam], [Blam])
        ts(lamt[:, 0:1], lamt[:, 6:7], -lam_init, ALU.add, [Blam], [Blam])
        ts(lamt[:, 1:2], subln[:], 1.0 - lam_init, ALU.mult, [Bsmall, Blam], [Blam])

    def ctx_v_diff(l):
        memset(vaSd[:, :, :, 64:128], 1.0, BvaSd, eng="pool")
        dma("pool", vaSd[:, 0:2, :, 0:64], d_cvd[l].rearrange("p (c h d) -> p c h d", c=2, h=4), [], BvaSd)

    dq = []

    def run_dq():
        todo = dq[:]
        del dq[:]
        for f_ in todo:
            f_()

    def proj_fm(g, k, col, bank):
        w = wsl[k][:].rearrange("p (c n) -> p c n", c=8)
        for c in range(8):
            mm(Bps[bank], ps[bank][:], Bws[k], w[:, c, col:col + 128], Bh[g][c], ht[g][:, c, :], c == 0, c == 7)
        run_dq()

    def proj_tm(g, k, col, ncol, tile_, bank):
        w = wsl[k][:].rearrange("p (c n) -> p c n", c=8)
        for c in range(8):
            mm(Bps[bank], ps[bank][:, 0:ncol], Bh[g][c], ht[g][:, c, tile_ * 128:(tile_ + 1) * 128], Bws[k], w[:, c, col:col + ncol], c == 0, c == 7)
        run_dq()

    def rope_apply(src, src_reads, kind, dst, Bdst):
        ci, si, pm = (0, 1, permd) if kind == "d" else (2, 3, permg)
        act(qbc[:], src, AF.Copy, src_reads, [Bqbc])

        def tail():
            mm(Bps[6], ps[6][:], Bcmat, pm, Bqbc, qbc[:], True, True)
            t1, B1 = nt()
            t2, B2 = nt()
            tt(t1[:], src, rope[:, ci, :], ALU.mult, src_reads + [Brope], [B1])
            tt(t2[:], ps[6][:], rope[:, si, :], ALU.mult, [Bps[6], Brope], [B2])
            tt(dst, t1[:], t2[:], ALU.add, [B1, B2], [Bdst])
        dq.append(tail)

    def headnorm(bank, col, then):
        act(qbf[:], ps[bank][:], AF.Square, [Bps[bank]], [Bqbf])

        def tail():
            mm(Bps[7], ps[7][:], Bcmat, blk64, Bqbf, qbf[:], True, True)
            r_, Br = nt()
            rsqrt_chain(r_[:], Br, ps[7][:], 1.0 / 64, [Bps[7]])
            o_, Bo = nt()
            stt(o_[:], ps[bank][:], qkn[:, col:col + 1], r_[:], ALU.mult, ALU.mult, [Bps[bank], Bsmall, Br], [Bo])
            then(o_[:], [Bo])
        dq.append(tail)

    def mix_in(l, g):
        for piece in range(5):
            if piece > 0:
                wfree(k)
            k = wload(d_wmi[l, piece])
            if piece == 0:
                for c in range(2):
                    b = nb(); proj_fm(g, k, c * 128, b)
                    ac, Bac = nt()
                    act(ac[:], ps[b][:], AF.Copy, [Bps[b]], [Bac])
                    b2 = nb(); proj_fm(g, k, 256 + c * 128, b2)
                    if g == 0:
                        tt(ppad[0][:, c, :, 1:257], ac[:].rearrange("p (s t) -> p s t", s=2), ps[b2][:].rearrange("p (s t) -> p s t", s=2),
                           ALU.mult, [Bac, Bps[b2]], [Bpp[0]])
                    else:
                        tt(ppad[1][:, c, 1:513], ac[:], ps[b2][:], ALU.mult, [Bac, Bps[b2]], [Bpp[1]])
            elif piece == 1:
                for c in range(2):
                    b = nb(); proj_fm(g, k, c * 128, b)
                    act(abt[g][:, c, :], ps[b][:], AF.Copy, [Bps[b]], Bab[g])
                for c in range(2):
                    b = nb(); proj_fm(g, k, 256 + c * 128, b)
                    if g == 0:
                        act(qpk[g][:, c, :], ps[b][:], AF.Copy, [Bps[b]], [Bqpk[g][c]])
                    else:
                        rope_apply(ps[b][:], [Bps[b]], "d", qpk[g][:, c, :], Bqpk[g][c])
            elif piece == 2:
                for c in range(2):
                    b = nb(); proj_fm(g, k, c * 128, b)
                    if g == 0:
                        act(kdP[:, c, :], ps[b][:], AF.Copy, [Bps[b]], [Bkd[0]])
                    else:
                        rope_apply(ps[b][:], [Bps[b]], "d", kdS[:, c, 256:256 + T], Bkd[1])
                for t_ in range(4):
                    b = nb()
                    rows = slice(t_ * 128, (t_ + 1) * 128)
                    if g == 0:
                        proj_tm(g, k, 0, 512, t_, b)
                        st, Bst = nt()
                        act(st[:], ps[b][:], AF.Copy, [Bps[b]], [Bst])
                        cp(vaP[:, t_, 0:4, 0:64], ps[b][:, 256:512].rearrange("p (h d) -> p h d", h=4), [Bps[b]], [BvaP])
                        dma("sp", d_ndk[l, rows, :], st[:, 0:256], [Bst], [])
                        dma("sp", d_ndv[l, rows, :], st[:, 256:512], [Bst], [])
                    else:
                        proj_tm(g, k, 256, 256, t_, b)
                        act(vtok[:, t_, 0:256], ps[b][:, 0:256], AF.Copy, [Bps[b]], Bcat1)
            elif piece == 3:
                for c in range(2):
                    b = nb(); proj_fm(g, k, 256 + c * 128, b)
                    sg, Bsg = nt()
                    act(sg[:], ps[b][:], AF.Sigmoid, [Bps[b]], [Bsg])
                    b2 = nb(); proj_fm(g, k, c * 128, b2)
                    if g == 0:
                        tt(upad[0][:, c, :, 15:271], sg[:].rearrange("p (s t) -> p s t", s=2),
                           ps[b2][:].rearrange("p (s t) -> p s t", s=2), ALU.mult, [Bsg, Bps[b2]], [Bup[0]])
                    else:
                        tt(upad[1][:, c, 15:527], sg[:], ps[b2][:], ALU.mult, [Bsg, Bps[b2]], [Bup[1]])
            else:
                for c in range(2):
                    b = nb(); proj_fm(g, k, c * 128, b)
                    if g == 0:
                        headnorm(b, 0, lambda src, Bsrc, c=c: cp(qpk[0][:, 2 + c, :], src, Bsrc, [Bqpk[0][2 + c]]))
                    else:
                        headnorm(b, 0, lambda src, Bsrc, c=c: rope_apply(src, Bsrc, "g", qpk[1][:, 2 + c, :], Bqpk[1][2 + c]))
                b = nb(); proj_fm(g, k, 256, b)
                if g == 0:
                    headnorm(b, 1, lambda src, Bsrc: cp(kgP[:], src, Bsrc, [Bkg[0]]))
                else:
                    headnorm(b, 1, lambda src, Bsrc: rope_apply(src, Bsrc, "g", kgS[:, 256:256 + T], Bkg[1]))
                for t_ in range(4):
                    b = nb()
                    rows = slice(t_ * 128, (t_ + 1) * 128)
                    if g == 0:
                        proj_tm(g, k, 256, 256, t_, b)
                        st, Bst = nt()
                        act(st[:, 128:256], ps[b][:, 128:256], AF.Copy, [Bps[b]], [Bst])
                        cp(vaP[:, t_, 4:6, 0:64], ps[b][:, 128:256].rearrange("p (h d) -> p h d", h=2), [Bps[b]], [BvaP])
                        act(st[:, 256:384], ps[b][:, 0:128], AF.Square, [Bps[b]], [Bst])
                        S.add("dve", lambda e, st=st: e.tensor_reduce(ktm[:, 0:2], st[:, 256:384].rearrange("p (h d) -> p h d", h=2),
                                                                      mybir.AxisListType.X, ALU.add), reads=[Bst], writes=[Bktm])
                        rsqrt_chain(ktm[:, 2:4], Bktm, ktm[:, 0:2], 1.0 / 64, [Bktm])
                        for h in range(2):
                            stt(st[:, h * 64:(h + 1) * 64], ps[b][:, h * 64:(h + 1) * 64], ktm[:, 2 + h:3 + h], kng[:],
                                ALU.mult, ALU.mult, [Bps[b], Bktm, Bsmall], [Bst])
                        dma("sp", d_ngk[l, rows, :], st[:, 0:128], [Bst], [])
                        dma("sp", d_ngv[l, rows, :], st[:, 128:256], [Bst], [])
                    else:
                        proj_tm(g, k, 384, 128, t_, b)
                        act(vtok[:, t_, 256:384], ps[b][:, 0:128], AF.Copy, [Bps[b]], Bcat1)
            if substop is not None and piece == substop:
                raise _Stop()
        wfree(k)
        while dq:
            run_dq()

    def exchange(l):
        for c in range(2):
            cp(edge[:, c, 0:15], upad[1][:, c, 15:30], [Bup[1]], [Bedge])
            cp(edge[:, c, 15:16], ppad[1][:, c, 1:2], [Bpp[1]], [Bedge])
            cp(edge[:, c, 16:31], upad[1][:, c, 512:527], [Bup[1]], [Bedge])
            cp(edge[:, c, 31:32], ppad[1][:, c, 512:513], [Bpp[1]], [Bedge])
        bo = d_bounce[l]
        dma("sp", bo[0:256, :].rearrange("(c p) t -> p c t", p=128), kdS[:, :, 256:256 + T], [Bkd[1]], [B_bounce[l]])
        dma("sp", bo[256:384, :], kgS[:, 256:256 + T], [Bkg[1]], [B_bounce[l]])
        vview = bo[384:768, :].rearrange("r c -> (r c)").rearrange("(tt p v) -> p tt v", p=128, v=384)
        dma("sp", vview, vtok, Bcat1, [B_bounce[l]])
        eview = bo[768:784, :].rearrange("r c -> (r c)").rearrange("(c p e) -> p c e", p=128, e=32)
        dma("sp", eview, edge[:], [Bedge], [B_bounce[l]])
        S.add("pool", lambda e: e.collective_compute("AllGather", ALU.bypass, replica_groups=[[0, 1, 2, 3], [4, 5, 6, 7]],
                                                     ins=[bo.opt()], outs=[d_gath[l].opt()]),
              reads=[B_bounce[l]], writes=[B_gath[l]], kind="cc")
    def exchange_recv(l):
        ga = d_gath[l]
        for r in range(4):
            base = r * BROWS
            ks = 256 + r * T
            dma("sp", kdS[:, :, ks:ks + T], ga[base:base + 256, :].rearrange("(c p) t -> p c t", p=128), [B_gath[l]], [Bkgath])
            dma("sp", kgS[:, ks:ks + T], ga[base + 256:base + 384, :], [B_gath[l]], [Bkgath])
            vv = ga[base + 384:base + 768, :].rearrange("r c -> (r c)").rearrange("(tt p v) -> p tt v", p=128, v=384)
            for t_ in range(4):
                ch = 2 + r * 4 + t_
                dma("sp", vaSd[:, ch, :, 0:64], vv[:, t_, 0:256].rearrange("p (h d) -> p h d", h=4), [B_gath[l]], BvaSd)
                dma("sp", vaSg[:, ch, :, 0:64], vv[:, t_, 256:384].rearrange("p (h d) -> p h d", h=2), [B_gath[l]], [Bvgg])
            ev = ga[base + 768:base + 784, :].rearrange("r c -> (r c)").rearrange("(c p e) -> p c e", p=128, e=32)
            dma("sp", egt[:, :, r, :], ev, [B_gath[l]], [Begt])

    def halos():
        for side, lo, mcol in ((0, 16, 8), (1, 0, 12)):
            ts(halo[:, :, side, :], egt[:, :, 0, lo:lo + 16], msk[:, mcol:mcol + 1], ALU.mult, [Begt, Bmsk], [Bhalo])
            for r in range(1, 4):
                stt(halo[:, :, side, :], egt[:, :, r, lo:lo + 16], msk[:, mcol + r:mcol + r + 1], halo[:, :, side, :], ALU.mult, ALU.add,
                    [Begt, Bmsk, Bhalo], [Bhalo])
        cp(upad[1][:, :, 0:15], halo[:, :, 0, 0:15], [Bhalo], [Bup[1]])
        cp(ppad[1][:, :, 0:1], halo[:, :, 0, 15:16], [Bhalo], [Bpp[1]])
        cp(upad[1][:, :, 527:542], halo[:, :, 1, 0:15], [Bhalo], [Bup[1]])
        cp(ppad[1][:, :, 513:514], halo[:, :, 1, 15:16], [Bhalo], [Bpp[1]])

    def attention(g, hook=None, premasked=False):
        nq, nkc = 512, (2 if g == 0 else 18)
        items = [(0, hm) for hm in range(12)]
        state = {"on_prev": None}

        def qinfo(hm):
            if hm < 8:
                return qpk[g][:, hm // 4, :], Bqpk[g][hm // 4], hm % 4, hm // 2, 32 ** -0.5
            sl = hm - 8
            return qpk[g][:, 2 + sl // 2, :], Bqpk[g][2 + sl // 2], 4 + sl % 2, 4 + sl % 2, 64 ** -0.5

        def emit_mask(idx):
            s_, hm = items[idx]
            qsrc, Bq, mcol, h, scale = qinfo(hm)
            qk = idx % 2
            ts(qmr[qk][:], qsrc, msk[:, mcol:mcol + 1], ALU.mult, [Bq, Bmsk], [Bqmr[qk]])

        def kv(s_, hm, h, kc):
            if g == 0:
                ksl = slice(s_ * 256 + kc * 128, s_ * 256 + (kc + 1) * 128)
                kT = kdP[:, hm // 4, ksl] if hm < 8 else kgP[:, ksl]
                Bk = [Bkd[0]] if hm < 8 else [Bkg[0]]
                return kT, Bk, vaP[:, s_ * 2 + kc, h, :], [BvaP]
            ksl = slice(kc * 128, (kc + 1) * 128)
            kT = kdS[:, hm // 4, ksl] if hm < 8 else kgS[:, ksl]
            Bk = [Bkctx] if kc < 2 else [Bkgath]
            if h < 4:
                return kT, Bk, vaSd[:, kc, h, :], BvaSd
            return kT, Bk, vaSg[:, kc, h - 4, :], [Bvgc if kc < 2 else Bvgg]

        def finalize(idx):
            s_, hm = items[idx]
            qs = slice(s_ * nq, (s_ + 1) * nq)
            qsrc, Bq, mcol, h, scale = qinfo(hm)
            ob = 4 + idx % 2
            r_, Br = nt()
            cp(r_[0:64, 0:nq], ps[ob][64:128, 0:nq], [Bps[ob]], [Br])
            act(r_[0:64, 0:nq], r_[0:64, 0:nq], AF.Ln, [Br], [Br])
            act(r_[0:64, 0:nq], r_[0:64, 0:nq], AF.Exp, [Br], [Br], scale=-1.0)
            if hm < 8:
                on, Bon = nt()
                tt(on[0:64, 0:nq], ps[ob][0:64, 0:nq], r_[0:64, 0:nq], ALU.mult, [Bps[ob], Br], [Bon])
                if hm % 2 == 0:
                    state["on_prev"] = (on, Bon)
                else:
                    on0, Bon0 = state["on_prev"]
                    d_, Bd = nt()
                    stt(d_[0:64, 0:nq], on[0:64, 0:nq], lamt[0:64, 0:1], on0[0:64, 0:nq], ALU.mult, ALU.add, [Bon0, Bon, Blam], [Bd])
                    act(qbf[0:64, 0:nq], d_[0:64, 0:nq], AF.Square, [Bd], [Bqbf])

                    def cont(d_=d_, Bd=Bd, h=h, qs=qs):
                        mm(Bps[6], ps[6][0:64, 0:nq], Bcmat, ones_b[0:64, 0:64], Bqbf, qbf[0:64, 0:nq], True, True)
                        rr, Brr = nt()
                        rsqrt_chain(rr[0:64, 0:nq], Brr, ps[6][0:64, 0:nq], 1.0 / 64, [Bps[6]])
                        stt(hb[g][:, h, qs], d_[0:64, 0:nq], lamt[0:64, 1:2], rr[0:64, 0:nq], ALU.mult, ALU.mult, [Bd, Blam, Brr], [Bhb[g][h]])
                    return cont
            else:
                sl = hm - 8
                tt(hb[g][:, 4 + sl, qs], ps[ob][0:64, 0:nq], r_[0:64, 0:nq], ALU.mult, [Bps[ob], Br], [Bhb[g][4 + sl]])
            return None

        contB = [None]

        def fin_step(idx):
            if contB[0] is not None:
                contB[0]()
                contB[0] = None
            if idx >= 0:
                contB[0] = finalize(idx)

        NPT = len(pT)
        stepc = [0]
        if not premasked:
            emit_mask(0)
        for idx, (s_, hm) in enumerate(items):
            qs = slice(s_ * nq, (s_ + 1) * nq)
            qsrc, Bq, mcol, h, scale = qinfo(hm)
            qk = idx % 2
            ob = 4 + idx % 2
            if idx + 1 < len(items):
                emit_mask(idx + 1)
            pend = []
            if g == 0:
                pks = []
                for kc in range(2):
                    sbk = stepc[0] % 4
                    pk = stepc[0] % NPT
                    stepc[0] += 1
                    pks.append(pk)
                    for s2 in range(2):
                        kT, Bk, va, Bv = kv(s2, hm, h, kc)
                        cs = slice(s2 * 256, (s2 + 1) * 256)
                        S.add("pe", lambda e, o=ps[sbk][:, cs], a=kT, b_=qmr[qk][:, cs]: e.matmul(o, a, b_, start=True, stop=True),
                              reads=Bk + [Bqmr[qk]], writes=[Bps[sbk]])
                    act(pT[pk][:], ps[sbk][:], AF.Exp, [Bps[sbk]], [BpT[pk]], scale=scale)
                fin_step(idx - 1)
                for s2 in range(2):
                    cs = slice(s2 * 256, (s2 + 1) * 256)
                    for kc in range(2):
                        kT, Bk, va, Bv = kv(s2, hm, h, kc)
                        S.add("pe", lambda e, o=ps[ob][:, cs], a=va, b_=pT[pks[kc]][:, cs], st_=(kc == 0), sp_=(kc == 1): e.matmul(o, a, b_, start=st_, stop=sp_),
                              reads=Bv + [BpT[pks[kc]]], writes=[Bps[ob]])
                continue

            def score(kc):
                kT, Bk, va, Bv = kv(s_, hm, h, kc)
                sbk = stepc[0] % 4
                pk = stepc[0] % NPT
                stepc[0] += 1
                S.add("pe", lambda e, o=ps[sbk][:, 0:nq], a=kT, b_=qmr[qk][:, qs]: e.matmul(o, a, b_, start=True, stop=True),
                      reads=Bk + [Bqmr[qk]], writes=[Bps[sbk]])
                act(pT[pk][:, 0:nq], ps[sbk][:, 0:nq], AF.Exp, [Bps[sbk]], [BpT[pk]], scale=scale)
                pend.append((kc, pk, va, Bv))

            def pv():
                kc, pk, va, Bv = pend.pop(0)
                st_, sp_ = (kc == 0), (kc == nkc - 1)
                S.add("pe", lambda e, o=ps[ob][:, 0:nq], a=va, b_=pT[pk][:, 0:nq], st_=st_, sp_=sp_: e.matmul(o, a, b_, start=st_, stop=sp_),
                      reads=Bv + [BpT[pk]], writes=[Bps[ob]])

            DEPTH_PIPE = 2
            for kc in range(nkc):
                score(kc)
                if len(pend) > DEPTH_PIPE:
                    pv()
            fin_step(idx - 1)
            while pend:
                pv()
            if hook is not None:
                hook()
        fin_step(len(items) - 1)
        fin_step(-1)

    def diag(wcol):
        k = dgc[0] % NDG
        dgc[0] += 1
        ts(dgr[k][:], ident, wcol, ALU.mult, [Bcmat, Bsmall], [Bdgr[k]])
        return dgr[k][:], Bdgr[k]

    def convs(g):
        nseq, n = (2, 256) if g == 0 else (1, 512)
        ucs = []
        for c in range(2):
            b = nb()
            for k in range(31):
                dg, Bdg = diag(ccw[:, c, k:k + 1])
                for s in range(nseq):
                    rhs = upad[0][:, c, s, k:k + n] if g == 0 else upad[1][:, c, k:k + n]
                    S.add("pe", lambda e, o=ps[4 + s][:, 0:n], a=dg, r=rhs, st_=(k == 0), sp_=(k == 30): e.matmul(o, a, r, start=st_, stop=sp_),
                          reads=[Bdg, Bup[g]], writes=[Bps[4 + s]])
            uc, Buc = nt()
            for s in range(nseq):
                act(uc[:, s * n:(s + 1) * n], ps[4 + s][:, 0:n], AF.Identity, [Bps[4 + s], Bsmall], [Buc], bias=ccv[:, 0, c:c + 1])
            ucq, Bucq = nt()
            act(ucq[:], uc[:], AF.Square, [Buc], [Bucq])
            ucs.append((uc, Buc, ucq, Bucq))
            for k in range(3):
                dg, Bdg = diag(sconvw[:, c, k:k + 1])
                for s in range(nseq):
                    rhs = ppad[0][:, c, s, k:k + n] if g == 0 else ppad[1][:, c, k:k + n]
                    S.add("pe", lambda e, o=ps[6 + s][:, 0:n], a=dg, r=rhs, st_=(k == 0), sp_=(k == 2): e.matmul(o, a, r, start=st_, stop=sp_),
                          reads=[Bdg, Bpp[g]], writes=[Bps[6 + s]])
            for s in range(nseq):
                tt(cat[g][:, c, s * n:(s + 1) * n], abt[g][:, c, s * n:(s + 1) * n], ps[6 + s][:, 0:n], ALU.mult, Bab[g] + [Bps[6 + s]], [Bcat[g][c]])
        b1 = nb()
        for c in range(2):
            mm(Bps[b1], ps[b1][:], Bcmat, ones_f, ucs[c][1], ucs[c][0][:], c == 0, c == 1)
        b2 = nb()
        for c in range(2):
            mm(Bps[b2], ps[b2][:], Bcmat, ones_f, ucs[c][3], ucs[c][2][:], c == 0, c == 1)
        mean, Bmean = nt()
        ts(mean[:], ps[b1][:], 1.0 / 256, ALU.mult, [Bps[b1]], [Bmean])
        var, Bvar = nt()
        tt(var[:], mean[:], mean[:], ALU.mult, [Bmean], [Bvar])
        stt(var[:], ps[b2][:], 1.0 / 256, var[:], ALU.mult, ALU.subtract, [Bps[b2], Bvar], [Bvar])
        rsqrt_chain(var[:], Bvar, var[:], 1.0, [Bvar])
        for c in range(2):
            uc, Buc, ucq, Bucq = ucs[c]
            tt(ucq[:], uc[:], mean[:], ALU.subtract, [Buc, Bmean], [Bucq])
            tt(ucq[:], ucq[:], var[:], ALU.mult, [Bucq, Bvar], [Bucq])
            act(vb[:, c, :], ucq[:], AF.Silu, [Bucq, Bsmall], [Bvb], scale=ccv[:, 1, c:c + 1], bias=ccv[:, 2, c:c + 1])
        for oc in range(2):
            b = nb()
            for c in range(2):
                mm(Bps[b], ps[b][:], Bccpw, ccpw[:, c, oc * 128:(oc + 1) * 128], Bvb, vb[:, c, :], c == 0, c == 1)
            act(cat[g][:, 2 + oc, :], ps[b][:], AF.Copy, [Bps[b]], [Bcat[g][2 + oc]])

    def mix_out(l, g):
        wk = [wload(d_wmo[l, a]) for a in range(3)]
        pieces = [(cat[g][:, 0, :], Bcat[g][0], 128), (cat[g][:, 1, :], Bcat[g][1], 128)]
        pieces += [(hb[g][:, h, :], Bhb[g][h], 64) for h in range(4)]
        pieces += [(cat[g][:, 2, :], Bcat[g][2], 128), (cat[g][:, 3, :], Bcat[g][3], 128)]
        pieces += [(hb[g][:, 4 + h, :], Bhb[g][4 + h], 64) for h in range(4)]
        for oc in range(8):
            bank = 4 + (oc % 2)
            for pi, (ap_, B_, kk) in enumerate(pieces):
                k = wk[pi // 4]
                w = wsl[k][:].rearrange("p (a n) -> p a n", a=4)
                mm(Bps[bank], ps[bank][:], Bws[k], w[0:kk, pi % 4, oc * 128:(oc + 1) * 128], B_, ap_[0:kk, :], pi == 0, pi == 11)
            flush_sq()
            evac_y(g, oc, bank)
        flush_sq(True)
        for k in wk:
            wfree(k)
        post_norm(g, 1)

    def dumpx(idx, g):
        if DEBUG:
            dma("sp", d_dbg[idx].rearrange("p (c t) -> p c t", c=8), xt[g][:], Bx[g], [])

    stage = [0]

    def chk():
        stage[0] += 1
        if stop is not None and stage[0] >= stop:
            raise _Stop()

    def mixer(l):
        cur[0] = l
        pre_norm(1, 1)
        mix_in(l, 1)
        ctx_v_diff(l)
        exchange(l)
        pre_norm(0, 1)
        mix_in(l, 0)
        attention(0)
        exchange_recv(l)
        convs(0)
        ts(qmr[0][:], qpk[1][:, 0, :], msk[:, 0:1], ALU.mult, [Bqpk[1][0], Bmsk], [Bqmr[0]])
        mix_out(l, 0)
        hook = None
        gen = None
        if l + 1 < DEPTH:
            gen = ada_gen(l + 1)
            cnt_h = [0]

            def hook():
                n = 2 if cnt_h[0] < 6 else 1
                cnt_h[0] += 1
                for _ in range(n):
                    next(gen, None)
        attention(1, hook, premasked=True)
        if gen is not None:
            for _ in gen:
                pass
        halos()
        convs(1)
        mix_out(l, 1)

    try:
        gen0 = ada_gen(0)
        for _ in range(4):
            next(gen0, None)
        layer_consts(0)
        ffn_chain([(0, 0, 0, 0), (0, 0, 0, 1)], hook=lambda: next(gen0, None))
        for _ in gen0:
            pass
        chk()
        mixer(0)
        chk()
        layer_consts(1)
        ffn_chain([(0, 1, 2, 0), (0, 1, 2, 1), (1, 0, 0, 0), (1, 0, 0, 1)])
        mixer(1)
        ffn_chain([(1, 1, 2, 0), (1, 1, 2, 1)])
    except _Stop:
        pass
    for g in range(2):
        for c in range(8):
            dma("sp", d_y[g][:, c, :], xt[g][:, c, :], [Bx[g][c]], [])

    print('sbuf remaining', nc.sbuf_bytes_remaining, flush=True)
    S.finalize()
    with nc.Block() as block:
        @block.tensor
        def _(e):
            S.replay("pe", e, sems)

        @block.scalar
        def _(e):
            S.replay("act", e, sems)

        @block.vector
        def _(e):
            S.replay("dve", e, sems)

        @block.gpsimd
        def _(e):
            S.replay("pool", e, sems)

        @block.sync
        def _(e):
            S.replay("sp", e, sems)
    es.close()
    return nc


def _fm(a):
    t = a.shape[0]
    return np.ascontiguousarray(a.T.reshape(8, 128, t).transpose(1, 0, 2))


def _unfm(a):
    t = a.shape[2]
    return np.ascontiguousarray(a.transpose(1, 0, 2).reshape(1024, t).T)


def _kpieces(w, ncols_piece):
    n = w.shape[1]
    npieces = n // ncols_piece
    a = w.reshape(8, 128, npieces, ncols_piece).transpose(2, 1, 0, 3)
    return np.ascontiguousarray(a.reshape(npieces, 128, 8 * ncols_piece))


def _vec_fm(v, nch):
    return np.ascontiguousarray(v.reshape(nch, 128).T)


def _rope_tables():
    theta = np.float32(10000.0)
    tabs = {}
    for name, dd in (("d", 32), ("g", 64)):
        nf = dd // 4
        inv = (theta ** (-(np.arange(nf, dtype=np.float32) / np.float32(nf)))).astype(np.float32)
        tabs[name] = (dd, nf, inv)
    return tabs


def _host_inputs(inp):
    f32 = np.float32
    shared = {}
    L = DEPTH
    shared["wada"] = np.stack([_kpieces(inp["w_ada"][l], 512) for l in range(L)])
    shared["gpre"] = np.stack([np.concatenate([_vec_fm(inp["norm_pre"][l, i], 8) for i in range(3)], axis=1) for l in range(L)])
    shared["gpost"] = np.stack([np.concatenate([_vec_fm(inp["norm_post"][l, i], 8) for i in range(3)], axis=1) for l in range(L)])
    wgu = np.zeros((L, 2, 11, 128, 4096), f32)
    wdn = np.zeros((L, 2, 8, 128, NFC * 128), f32)
    for l in range(L):
        for wi, (gn, un, dn) in enumerate((("ffn1_gate", "ffn1_up", "ffn1_down"), ("ffn2_gate", "ffn2_up", "ffn2_down"))):
            gp = _kpieces(inp[gn][l], 256).reshape(11, 128, 2048)
            up = _kpieces(inp[un][l], 256).reshape(11, 128, 2048)
            wgu[l, wi] = np.concatenate([gp, up], axis=2)
            wd = inp[dn][l]
            a = wd.reshape(NFC, 128, 8, 128).transpose(2, 1, 0, 3)
            wdn[l, wi] = a.reshape(8, 128, NFC * 128)
    shared["wgu"] = wgu
    shared["wdn"] = wdn
    wmi = np.zeros((L, 5, 128, 4096), f32)
    wmo = np.zeros((L, 3, 128, 4096), f32)
    for l in range(L):
        w = inp["w_mix_in"][l].copy()
        dq = w[:, 2048:2304].reshape(1024, 4, 64)
        w[:, 2048:2304] = dq[:, [0, 2, 1, 3], :].reshape(1024, 256)
        w[:, 0:768] = np.concatenate([w[:, 256:512], w[:, 512:768], w[:, 0:256]], axis=1)
        wmi[l] = _kpieces(w, 512)
        wo = inp["w_mix_out"][l]
        pieces = np.zeros((12, 128, 1024), f32)
        pieces[0] = wo[0:128]; pieces[1] = wo[128:256]
        for h in range(4):
            pieces[2 + h, 0:64] = wo[256 + h * 64:256 + (h + 1) * 64]
        pieces[6] = wo[512:640]; pieces[7] = wo[640:768]
        for slot, h in enumerate((0, 2, 1, 3)):
            pieces[8 + slot, 0:64] = wo[768 + h * 64:768 + (h + 1) * 64]
        wmo[l] = pieces.reshape(3, 4, 128, 1024).transpose(0, 2, 1, 3).reshape(3, 128, 4096)
    shared["wmi"] = wmi
    shared["wmo"] = wmo
    shared["sconvw"] = np.stack([np.ascontiguousarray(inp["sconv_w"][l].reshape(3, 2, 128).transpose(2, 1, 0)).reshape(128, 6) for l in range(L)])
    shared["ccw"] = np.stack([np.ascontiguousarray(inp["ccm_dw_w"][l].reshape(31, 2, 128).transpose(2, 1, 0)).reshape(128, 62) for l in range(L)])
    shared["ccv"] = np.stack([np.concatenate([_vec_fm(inp[n][l], 2) for n in ("ccm_dw_b", "ccm_ln_g", "ccm_ln_b")], axis=1) for l in range(L)])
    shared["ccpw"] = np.stack([np.ascontiguousarray(inp["ccm_pw"][l].reshape(2, 128, 256).transpose(1, 0, 2)).reshape(128, 512) for l in range(L)])
    shared["lamv"] = np.stack([np.tile(np.concatenate([inp[n][l] for n in ("diff_lq1", "diff_lk1", "diff_lq2", "diff_lk2")])[None, :], (128, 1)) for l in range(L)])
    shared["subln"] = np.stack([np.tile(inp["diff_subln"][l], 2).reshape(128, 1) for l in range(L)])
    shared["qkn"] = np.stack([np.stack([np.tile(inp["gqa_qnorm"][l], 2), np.tile(inp["gqa_knorm"][l], 2)], axis=1) for l in range(L)])
    shared["kng"] = np.stack([np.tile(inp["gqa_knorm"][l][None, :], (128, 1)) for l in range(L)])
    cm = np.zeros((128, 5, 128), f32)
    cm[:, 0, :] = 1.0
    cm[0:64, 1, 0:64] = 1.0; cm[64:128, 1, 64:128] = 1.0
    cm[:, 2, :] = np.eye(128, dtype=f32)
    for i in range(128):
        cm[i ^ 8, 3, i] = 1.0
        cm[i ^ 16, 4, i] = 1.0
    shared["cmat"] = cm.reshape(128, 640)
    for k in shared:
        shared[k] = np.ascontiguousarray(shared[k], dtype=f32)

    tabs = _rope_tables()
    maps = []
    for i in range(NCORES):
        b, j = i // 4, i % 4
        m = dict(shared)
        m["xp"] = _fm(inp["x_prompt"][2 * i:2 * i + 2].reshape(512, 1024))
        m["xs"] = _fm(inp["x_sample"][b, 512 * j:512 * (j + 1)])
        m["cnd"] = np.ascontiguousarray(np.stack([_vec_fm(inp["c_ctx"], 8), _vec_fm(inp["c"][b], 8)], axis=2))
        m["bada"] = np.stack([np.repeat(_vec_fm(inp["b_ada"][l], 72), 2, axis=1) for l in range(L)])
        m["ckd"] = np.stack([np.ascontiguousarray(inp["cache_diff_k"][b, l].reshape(256, 2, 128).transpose(2, 1, 0)).reshape(128, 512) for l in range(L)])
        m["ckg"] = np.stack([np.ascontiguousarray(inp["cache_gqa_k"][b, l].reshape(256, 128).T) for l in range(L)])
        m["cvd"] = np.stack([np.ascontiguousarray(inp["cache_diff_v"][b, l].reshape(2, 128, 256).transpose(1, 0, 2)).reshape(128, 512) for l in range(L)])
        m["cvg"] = np.stack([np.ascontiguousarray(inp["cache_gqa_v"][b, l].reshape(2, 128, 128).transpose(1, 0, 2)).reshape(128, 256) for l in range(L)])
        tok = np.arange(512 * j, 512 * (j + 1))
        row = (tok // 64).astype(f32)
        col = (tok % 64).astype(f32)
        rt = np.zeros((128, 4, T), f32)
        for ti, name in enumerate(("d", "g")):
            dd, nf, inv = tabs[name]
            for p in range(128):
                f = p % dd
                pos = row if f < dd // 2 else col
                ang = (pos * inv[f % nf]).astype(f32)
                sgn = -1.0 if (f % (2 * nf)) < nf else 1.0
                rt[p, 2 * ti, :] = np.cos(ang)
                rt[p, 2 * ti + 1, :] = sgn * np.sin(ang)
        m["rope"] = rt.reshape(128, 4 * T)
        mk = np.zeros((128, 16), f32)
        for q in range(4):
            mk[32 * q:32 * (q + 1), q] = 1.0
        mk[0:64, 4] = 1.0
        mk[64:128, 5] = 1.0
        if j > 0:
            mk[:, 8 + j - 1] = 1.0
        if j < 3:
            mk[:, 12 + j + 1] = 1.0
        m["msk"] = mk
        for k in m:
            m[k] = np.ascontiguousarray(m[k], dtype=f32)
        maps.append(m)
    return maps


_NC_CACHE = {}


def kernel(**inputs):
    inp = {k: np.asarray(v) for k, v in inputs.items()}
    if "nc" not in _NC_CACHE:
        _NC_CACHE["nc"] = build_program()
    nc = _NC_CACHE["nc"]
    in_maps = _host_inputs(inp)
    res = run_bass_kernel_spmd(nc, in_maps, core_ids=list(range(NCORES)))
    R = res.results
    y_prompt = np.zeros((16, 256, 1024), np.float32)
    y_sample = np.zeros((2, 2048, 1024), np.float32)
    ndk = np.zeros((16, DEPTH, 256, 4, 2, 32), np.float32)
    ndv = np.zeros((16, DEPTH, 256, 4, 64), np.float32)
    ngk = np.zeros((16, DEPTH, 256, 2, 64), np.float32)
    ngv = np.zeros((16, DEPTH, 256, 2, 64), np.float32)
    for i in range(NCORES):
        b, j = i // 4, i % 4
        y_prompt[2 * i:2 * i + 2] = _unfm(R[i]["yp"]).reshape(2, 256, 1024)
        y_sample[b, 512 * j:512 * (j + 1)] = _unfm(R[i]["ys"])
        for l in range(DEPTH):
            ndk[2 * i:2 * i + 2, l] = R[i]["ndk"][l].reshape(2, 256, 4, 2, 32)
            ndv[2 * i:2 * i + 2, l] = R[i]["ndv"][l].reshape(2, 256, 4, 64)
            ngk[2 * i:2 * i + 2, l] = R[i]["ngk"][l].reshape(2, 256, 2, 64)
            ngv[2 * i:2 * i + 2, l] = R[i]["ngv"][l].reshape(2, 256, 2, 64)
    return (y_prompt, y_sample, ndk, ndv, ngk, ngv)
```

```python
import math
from contextlib import ExitStack
import numpy as np
import concourse.bass as bass
import concourse.mybir as mybir
from concourse.bass_utils import run_bass_kernel_spmd

F32 = mybir.dt.float32
BF16 = mybir.dt.bfloat16
ALU = mybir.AluOpType
AF = mybir.ActivationFunctionType

D = 1024
DFF = 2816
NFC = 22
DEPTH = 2
T = 512
EPS = 1e-6
NCORES = 8
NKS = 2304
BROWS = 784
NW = 3
NDS = 12
DEBUG = False


class Buf:
    __slots__ = ("name", "w", "r", "psum")

    def __init__(self, name, psum=False):
        self.name = name
        self.w = None
        self.r = {}
        self.psum = psum


class Op:
    __slots__ = ("eng", "emit", "deps", "needs_inc", "ticket", "kind", "dsem", "dval", "pre")

    def __init__(self, eng, emit, kind):
        self.eng = eng
        self.emit = emit
        self.deps = []
        self.needs_inc = False
        self.ticket = 0
        self.kind = kind
        self.dsem = None
        self.dval = 0
        self.pre = None


class Sched:
    ENGS = ("pe", "act", "dve", "pool", "sp")

    def __init__(self):
        self.ops = {e: [] for e in self.ENGS}
        self.ndma = {e: 0 for e in self.ENGS}
        self.ncc = 0

    def add(self, eng, emit, reads=(), writes=(), kind="c"):
        op = Op(eng, emit, kind)
        deps = {}

        def dep(o):
            if o is None or o is op:
                return
            if eng == "pe" and kind == "c" and o.eng == "pe" and o.kind == "c":
                return
            deps[id(o)] = o

        for b in reads:
            dep(b.w)
            if b.psum:
                for k_, o in b.r.items():
                    if k_ != eng:
                        dep(o)
        for b in writes:
            dep(b.w)
            for o in b.r.values():
                dep(o)
        for b in writes:
            b.w = op
            b.r = {}
        for b in reads:
            if kind == "c":
                b.r[eng] = op
            else:
                b.r[("d", id(op))] = op
        op.deps = list(deps.values())
        for o in op.deps:
            if o.kind == "c":
                o.needs_inc = True
        if kind == "d":
            n = self.ndma[eng]
            self.ndma[eng] = n + 1
            op.dsem = (eng, n % NDS)
            op.dval = 16 * (n // NDS + 1)
            if n >= NDS:
                op.pre = (op.dsem, 16 * (n // NDS))
        elif kind == "cc":
            self.ncc += 1
            op.dsem = ("cc", self.ncc - 1)
            op.dval = 1
        self.ops[eng].append(op)
        return op

    def finalize(self):
        for e in self.ENGS:
            t = 0
            for op in self.ops[e]:
                if op.kind == "c" and op.needs_inc:
                    t += 1
                    op.ticket = t

    def replay(self, e, eng, sems):
        waited = {}

        def wait(key, val):
            if waited.get(key, 0) < val:
                eng.wait_ge(sems[key], val)
                waited[key] = val

        for op in self.ops[e]:
            for o in op.deps:
                if o.kind == "c":
                    wait((o.eng, "c"), o.ticket)
                else:
                    wait(o.dsem, o.dval)
            if op.pre is not None:
                wait(op.pre[0], op.pre[1])
            ins = op.emit(eng)
            if op.kind == "d":
                ins.then_inc(sems[op.dsem], 16)
            elif op.kind == "cc":
                ins.then_inc(sems[op.dsem])
            elif op.needs_inc:
                ins.then_inc(sems[(e, "c")], 1)
        n = self.ndma[e]
        for k in range(min(n, NDS)):
            cnt = (n - 1 - k) // NDS + 1
            wait((e, k), 16 * cnt)
        if e == "pool":
            for k in range(self.ncc):
                wait(("cc", k), 1)


class _Stop(Exception):
    pass


def build_program(stop=None, substop=None):
    nc = bass.Bass("TRN2", target_bir_lowering=False)
    S = Sched()
    es = ExitStack()

    def din(name, shape, dt=F32):
        return nc.dram_tensor(name, list(shape), dt, kind="ExternalInput").ap()

    def dout(name, shape, dt=F32):
        return nc.dram_tensor(name, list(shape), dt, kind="ExternalOutput").ap()

    d_x = [din("xp", [128, 8, T]), din("xs", [128, 8, T])]
    d_cnd = din("cnd", [128, 8, 2])
    d_wada = din("wada", [DEPTH, 18, 128, 8 * 512])
    d_bada = din("bada", [DEPTH, 128, 144])
    d_gpre = din("gpre", [DEPTH, 128, 24])
    d_gpost = din("gpost", [DEPTH, 128, 24])
    d_wgu = din("wgu", [DEPTH, 2, 11, 128, 4096])
    d_wdn = din("wdn", [DEPTH, 2, 8, 128, NFC * 128])
    d_wmi = din("wmi", [DEPTH, 5, 128, 4096])
    d_wmo = din("wmo", [DEPTH, 3, 128, 4096])
    d_sconv = din("sconvw", [DEPTH, 128, 6])
    d_ccw = din("ccw", [DEPTH, 128, 62])
    d_ccv = din("ccv", [DEPTH, 128, 6])
    d_ccpw = din("ccpw", [DEPTH, 128, 512])
    d_lam = din("lamv", [DEPTH, 128, 128])
    d_subln = din("subln", [DEPTH, 128, 1])
    d_qkn = din("qkn", [DEPTH, 128, 2])
    d_kng = din("kng", [DEPTH, 128, 64])
    d_ckd = din("ckd", [DEPTH, 128, 2 * 256])
    d_ckg = din("ckg", [DEPTH, 128, 256])
    d_cvd = din("cvd", [DEPTH, 128, 2 * 256])
    d_cvg = din("cvg", [DEPTH, 128, 2 * 128])
    d_rope = din("rope", [128, 4 * T])
    d_cmat = din("cmat", [128, 5 * 128])
    d_msk = din("msk", [128, 16])
    d_y = [dout("yp", [128, 8, T]), dout("ys", [128, 8, T])]
    d_ndk = dout("ndk", [DEPTH, T, 256])
    d_ndv = dout("ndv", [DEPTH, T, 256])
    d_ngk = dout("ngk", [DEPTH, T, 128])
    d_ngv = dout("ngv", [DEPTH, T, 128])
    if DEBUG:
        d_dbg = dout("dbg", [8, 128, 8 * T])
        d_dbgb = dout("dbgb", [4, 128, 8 * T], BF16)
    d_bounce = [nc.dram_tensor("bounce%d" % l, [BROWS, 512], BF16).ap() for l in range(DEPTH)]
    d_gath = [nc.dram_tensor("gath%d" % l, [4 * BROWS, 512], BF16).ap() for l in range(DEPTH)]
    B_bounce = [Buf("bounce%d" % l) for l in range(DEPTH)]
    B_gath = [Buf("gath%d" % l) for l in range(DEPTH)]

    def sb(name, shape, dt=F32):
        return es.enter_context(nc.sbuf_tensor("s_" + name, list(shape), dt))

    xt = [sb("xP", [128, 8, T]), sb("xS", [128, 8, T])]
    Bx = [[Buf("x%d_%d" % (g, c)) for c in range(8)] for g in range(2)]
    h1 = sb("h", [128, 8, T], BF16)
    ht = [h1, h1]
    Bh1 = [Buf("h_%d" % c) for c in range(8)]
    Bh = [Bh1, Bh1]
    y1 = sb("y", [128, 8, T])
    yt = [y1, y1]
    By1 = [Buf("y_%d" % c) for c in range(8)]
    By = [By1, By1]
    wsl = [sb("ws%d" % k, [128, 4096], BF16) for k in range(NW)]
    Bws = [Buf("ws%d" % k) for k in range(NW)]
    ps = [es.enter_context(nc.psum_tensor("ps%d" % k, [128, 512], F32)) for k in range(8)]
    Bps = [Buf("ps%d" % k, psum=True) for k in range(8)]
    hmid1 = sb("hmid", [128, NFC, T], BF16)
    Bhm1 = [Buf("hm_%d" % f) for f in range(NFC)]

    NTP = 6
    tpool = [sb("tp%d" % k, [128, T]) for k in range(NTP)]
    Btp = [Buf("tp%d" % k) for k in range(NTP)]
    tpc = [0]

    rstd_t = sb("rstd_t", [128, T]); Brstd_t = Buf("rstd_t")

    def nt():
        k = tpc[0] % NTP
        tpc[0] += 1
        return tpool[k], Btp[k]

    cnd = sb("cnd", [128, 8, 2]); Bcnd = Buf("cnd")
    scnd = sb("scnd", [128, 8, 2], BF16); Bscnd = Buf("scnd")
    bada_l = [sb("bada%d" % l, [128, 144]) for l in range(DEPTH)]; Bbada_l = [Buf("bada%d" % l) for l in range(DEPTH)]
    modt_l = [sb("modt%d" % l, [128, 72, 2]) for l in range(DEPTH)]; Bmod_l = [[Buf("mod%d_%d" % (l, i)) for i in range(3)] for l in range(DEPTH)]
    gpre_l = [sb("gpre%d" % l, [128, 3, 8]) for l in range(DEPTH)]; gpost_l = [sb("gpost%d" % l, [128, 3, 8]) for l in range(DEPTH)]
    Bgp_l = [Buf("gprepost%d" % l) for l in range(DEPTH)]
    Amod_l = [sb("Amod%d" % l, [128, 2, 3, 8]) for l in range(DEPTH)]; Gmod_l = [sb("Gmod%d" % l, [128, 2, 3, 8]) for l in range(DEPTH)]
    BAG_l = [[Buf("AG%d_%d" % (l, i)) for i in range(3)] for l in range(DEPTH)]
    BGG_l = [[Buf("GG%d_%d" % (l, i)) for i in range(3)] for l in range(DEPTH)]
    Bmodg_l = [[Buf("modg%d_%d" % (l, i)) for i in range(3)] for l in range(DEPTH)]
    cur = [0]
    sconvw = sb("sconvw", [128, 2, 3]); ccw = sb("ccw", [128, 2, 31]); ccv = sb("ccv", [128, 3, 2])
    Bsmall = Buf("small")
    ccpw = sb("ccpw", [128, 2, 256], BF16); Bccpw = Buf("ccpw")
    lamv = sb("lamv", [128, 4, 32]); lamt = sb("lamt", [128, 8]); Blam = Buf("lam")
    subln = sb("subln", [128, 1]); qkn = sb("qkn", [128, 2]); kng = sb("kng", [128, 64])
    rope = sb("rope", [128, 4, T]); Brope = Buf("rope")
    cmatf = sb("cmatf", [128, 5, 128]); cmat = sb("cmat", [128, 5, 128], BF16); Bcmat = Buf("cmat")
    msk = sb("msk", [128, 16]); Bmsk = Buf("msk")
    NDG = 6
    dgr = [sb("dgr%d" % k, [128, 128], BF16) for k in range(NDG)]; Bdgr = [Buf("dgr%d" % k) for k in range(NDG)]
    dgc = [0]

    hx = sb("hx", [128, 8, T], BF16)
    Bhx = [Buf("hx_%d" % c) for c in range(8)]
    abt = [hx[:, 0:2, :], hx[:, 2:4, :]]; Bab = [Bhx[0:2], Bhx[2:4]]
    ppad = [sb("ppadP", [128, 2, 2, 258], BF16), sb("ppadS", [128, 2, 514], BF16)]; Bpp = [Buf("pp0"), Buf("pp1")]
    upad = [sb("upadP", [128, 2, 2, 286], BF16), sb("upadS", [128, 2, 542], BF16)]; Bup = [Buf("up0"), Buf("up1")]
    cat1 = hx[:, 4:8, :]
    cat = [cat1, cat1]
    Bcat1 = Bhx[4:8]
    Bcat = [Bcat1, Bcat1]
    hb1 = sb("hb", [64, 8, T], BF16)
    hb = [hb1, hb1]
    Bhb1 = [Buf("hb_%d" % k) for k in range(8)]
    Bhb = [Bhb1, Bhb1]
    qpk = [sb("qpk%d" % g, [128, 4, T], BF16) for g in range(2)]
    Bqpk = [[Buf("qpk%d_%d" % (g, k)) for k in range(4)] for g in range(2)]
    qmr = [sb("qmr%d" % k, [128, T], BF16) for k in range(2)]; Bqmr = [Buf("qmr%d" % k) for k in range(2)]
    kdP = sb("kdP", [128, 2, T], BF16); kgP = sb("kgP", [128, T], BF16)
    kdS = sb("kdS", [128, 2, NKS], BF16); kgS = sb("kgS", [128, NKS], BF16)
    Bkd = [Buf("kdP"), Buf("kdS_own")]; Bkg = [Buf("kgP"), Buf("kgS_own")]
    Bkctx = Buf("kctx"); Bkgath = Buf("kgath")
    vaP = sb("vaP", [128, 4, 6, 128], BF16)
    vaSg = sb("vaSg", [128, 18, 2, 128], BF16)
    vaSd = hmid1[:].rearrange("p f t -> p (f t)")[:, 0:18 * 512].rearrange("p (k h d) -> p k h d", h=4, d=128)
    BvaSd = Bhm1[0:18]
    BvaP = Buf("vaP"); Bvgc = Buf("vgctx"); Bvgg = Buf("vggath")
    vtok = cat1.rearrange("p a t -> p (a t)")[:, 0:1536].rearrange("p (t v) -> p t v", v=384)
    edge = sb("edge", [128, 2, 32], BF16); Bedge = Buf("edge")
    egt = sb("egt", [128, 2, 4, 32], BF16); Begt = Buf("egt")
    halo = sb("halo", [128, 2, 2, 16]); Bhalo = Buf("halo")
    pT = [sb("pT%d" % k, [128, T], BF16) for k in range(2)]; BpT = [Buf("pT%d" % k) for k in range(2)]
    sqr = pT[0:2]; Bsqr = BpT[0:2]
    pT = pT + [hmid1[:, 18, :], hmid1[:, 19, :]]; BpT = BpT + [Bhm1[18], Bhm1[19]]
    vb = sb("vb", [128, 2, T], BF16); Bvb = Buf("vb")
    qbf = sb("qbf", [128, T], BF16); Bqbf = Buf("qbf")
    qbc = sb("qbc", [128, T], BF16); Bqbc = Buf("qbc")
    ktm = sb("ktm", [128, 8]); Bktm = Buf("ktm")

    sems = {}
    for e in Sched.ENGS:
        sems[(e, "c")] = es.enter_context(nc.semaphore("sc_" + e))
    for e in ("sp", "pool", "act"):
        for k in range(NDS):
            sems[(e, k)] = es.enter_context(nc.semaphore("sd_%s%d" % (e, k)))
    for k in range(DEPTH):
        sems[("cc", k)] = es.enter_context(nc.semaphore("scc%d" % k))

    def mm(outb, out, lb, lhsT, rb, rhs, start, stop):
        S.add("pe", lambda e: e.matmul(out, lhsT, rhs, start=start, stop=stop), reads=[lb, rb], writes=[outb])

    def act(out, in_, func, reads, writes, scale=None, bias=None):
        kw = {}
        if scale is not None:
            kw["scale"] = scale
        if bias is not None:
            kw["bias"] = bias
        S.add("act", lambda e: e.activation(out, in_, func, **kw), reads=reads, writes=writes)

    def tt(out, a, b, op, reads, writes, eng="dve"):
        S.add(eng, lambda e: e.tensor_tensor(out, a, b, op), reads=reads, writes=writes)

    def ts(out, a, s1, op0, reads, writes, eng="dve"):
        S.add(eng, lambda e: e.tensor_scalar(out, a, s1, None, op0), reads=reads, writes=writes)

    def stt(out, a, s, b, op0, op1, reads, writes):
        S.add("dve", lambda e: e.scalar_tensor_tensor(out, a, s, b, op0, op1), reads=reads, writes=writes)

    def cp(out, in_, reads, writes, eng="dve"):
        S.add(eng, lambda e: e.tensor_copy(out, in_), reads=reads, writes=writes)

    def dma(q, out, in_, reads, writes):
        S.add(q, lambda e: e.dma_start(out=out, in_=in_), reads=reads, writes=writes, kind="d")

    def memset(ap, val, writes, eng="dve"):
        S.add(eng, lambda e: e.memset(ap, val), reads=[], writes=writes)

    epsb = sb("epsb", [128, 1]); Bepsb = Buf("epsb")
    memset(epsb[:], EPS, [Bepsb])

    def rsqrt_chain(out, Bout, in_, mult, reads_in):
        np_ = out.shape[0]
        act(out, in_, AF.Ln, reads_in + [Bepsb], [Bout], scale=mult, bias=epsb[0:np_, 0:1])
        act(out, out, AF.Exp, [Bout], [Bout], scale=-0.5)

    wfree_list = list(range(NW))

    def wload(src_ap, nel=4096):
        k = wfree_list.pop(0)
        dma("pool", wsl[k][:, 0:nel].rearrange("p (a n) -> p a n", a=4), src_ap[:, 0:nel].rearrange("p (a n) -> p a n", a=4), [], [Bws[k]])
        return k

    def wfree(k):
        wfree_list.append(k)

    for g in range(2):
        for c in range(8):
            dma("sp", xt[g][:, c, :], d_x[g][:, c, :], [], [Bx[g][c]])
    dma("sp", cnd[:], d_cnd[:], [], [Bcnd])
    dma("sp", rope[:].rearrange("p a t -> p (a t)"), d_rope[:], [], [Brope])
    dma("sp", cmatf[:].rearrange("p a t -> p (a t)"), d_cmat[:], [], [Bcmat])
    dma("sp", msk[:], d_msk[:], [], [Bmsk])
    cp(cmat[:], cmatf[:], [Bcmat], [Bcmat])
    ones_b = cmat[:, 0, :]
    blk64 = cmat[:, 1, :]
    ident = cmat[:, 2, :]
    permd = cmat[:, 3, :]
    permg = cmat[:, 4, :]
    ones_f = cmatf[:, 0, :]
    act(scnd[:], cnd[:], AF.Silu, [Bcnd], [Bscnd])
    memset(vaP[:, :, :, 64:128], 1.0, [BvaP], eng="pool")
    memset(vaSg[:, :, :, 64:128], 1.0, [Bvgc, Bvgg], eng="pool")
    memset(ppad[0][:], 0.0, [Bpp[0]], eng="pool")
    memset(upad[0][:], 0.0, [Bup[0]], eng="pool")

    bankrr = [0]

    def nb():
        b = bankrr[0] % 4
        bankrr[0] += 1
        return b

    def sumsq_pre(g):
        for c in range(8):
            k = c % 2
            act(sqr[k][:], xt[g][:, c, :], AF.Square, [Bx[g][c]], [Bsqr[k]])
            mm(Bps[6], ps[6][:], Bcmat, ones_b, Bsqr[k], sqr[k][:], c == 0, c == 7)

    def pre_norm_gen(g, i, l=None, hb_=None):
        l = cur[0] if l is None else l
        hdst, Bhd = (h1, Bh1) if hb_ is None else hb_
        for c in range(8):
            k = c % 2
            act(sqr[k][:], xt[g][:, c, :], AF.Square, [Bx[g][c]], [Bsqr[k]])
            if c > 0:
                mm(Bps[6], ps[6][:], Bcmat, ones_b, Bsqr[1 - k], sqr[1 - k][:], c == 1, False)
            yield
        mm(Bps[6], ps[6][:], Bcmat, ones_b, Bsqr[1], sqr[1][:], False, True)
        rstd, Brstd = rstd_t, Brstd_t
        rsqrt_chain(rstd[:], Brstd, ps[6][:], 1.0 / D, [Bps[6]])
        for c in range(8):
            t_, Bt = nt()
            tt(t_[:], xt[g][:, c, :], rstd[:], ALU.mult, [Bx[g][c], Brstd], [Bt])
            act(hdst[:, c, :], t_[:], AF.Identity, [Bt, BAG_l[l][i], Bmod_l[l][i]], [Bhd[c]],
                scale=Amod_l[l][:, g, i, c:c + 1], bias=modt_l[l][:, (3 * i) * 8 + c, g:g + 1])

    def pre_norm(g, i, l=None, hb_=None):
        for _ in pre_norm_gen(g, i, l, hb_):
            pass

    def evac_y(g, oc, bank):
        act(yt[g][:, oc, :], ps[bank][:], AF.Copy, [Bps[bank]], [By[g][oc]])
        k = oc % 2
        act(sqr[k][:], ps[bank][:], AF.Square, [Bps[bank]], [Bsqr[k]])
        pend_sq.append((oc, k))

    pend_sq = []

    def flush_sq(everything=False):
        while pend_sq and (everything or len(pend_sq) > 0):
            oc, k = pend_sq.pop(0)
            mm(Bps[6], ps[6][:], Bcmat, ones_b, Bsqr[k], sqr[k][:], oc == 0, oc == 7)

    def post_norm(g, i, l=None):
        l = cur[0] if l is None else l
        rstd, Brstd = rstd_t, Brstd_t
        rsqrt_chain(rstd[:], Brstd, ps[6][:], 1.0 / D, [Bps[6]])
        for c in range(8):
            t_, Bt = nt()
            stt(t_[:], yt[g][:, c, :], Gmod_l[l][:, g, i, c:c + 1], rstd[:], ALU.mult, ALU.mult, [By[g][c], BGG_l[l][i], Brstd], [Bt])
            tt(xt[g][:, c, :], xt[g][:, c, :], t_[:], ALU.add, [Bx[g][c], Bt], [Bx[g][c]])

    def ffn_chain(jobs, hook=None, first_pre_done=False):
        hsel = lambda g: (h1, Bh1) if g == 0 else (hx, Bhx)
        l0, w0, i0, g0 = jobs[0]
        if not first_pre_done:
            pre_norm(g0, i0, l0, hsel(g0))
        for jn, (l, which, i, g) in enumerate(jobs):
            hsrc, Bhs = hsel(g)
            cnt = 0
            png = None
            if jn + 1 < len(jobs):
                ln, wn, in_, gn = jobs[jn + 1]
                png = pre_norm_gen(gn, in_, ln, hsel(gn))
            for f in range(11):
                k = wload(d_wgu[l, which, f])
                w = wsl[k][:].rearrange("p (a c n) -> p a c n", a=2, c=8)
                for j in range(2):
                    fc = 2 * f + j
                    bg, bu = (cnt % 2) * 2, (cnt % 2) * 2 + 1
                    cnt += 1
                    for c in range(8):
                        mm(Bps[bg], ps[bg][:], Bws[k], w[:, 0, c, j * 128:(j + 1) * 128], Bhs[c], hsrc[:, c, :], c == 0, c == 7)
                    for c in range(8):
                        mm(Bps[bu], ps[bu][:], Bws[k], w[:, 1, c, j * 128:(j + 1) * 128], Bhs[c], hsrc[:, c, :], c == 0, c == 7)
                    sk = cnt % 2
                    act(qmr[sk][:], ps[bg][:], AF.Silu, [Bps[bg]], [Bqmr[sk]])
                    tt(hmid1[:, fc, :], qmr[sk][:], ps[bu][:], ALU.mult, [Bqmr[sk], Bps[bu]], [Bhm1[fc]])
                wfree(k)
                if hook is not None:
                    hook()
                if png is not None and f >= 2:
                    next(png, None)
            if png is not None:
                for _ in png:
                    pass
            for oc in range(8):
                k = wload(d_wdn[l, which, oc], NFC * 128)
                w = wsl[k][:, 0:NFC * 128].rearrange("p (f n) -> p f n", f=NFC)
                bank = 4 + (oc % 2)
                for fc in range(NFC):
                    mm(Bps[bank], ps[bank][:], Bws[k], w[:, fc, :], Bhm1[fc], hmid1[:, fc, :], fc == 0, fc == NFC - 1)
                wfree(k)
                flush_sq()
                evac_y(g, oc, bank)
            flush_sq(True)
            post_norm(g, i, l)

    def ffn(l, which, i, g, hook=None):
        pre_norm(g, i)
        cnt = 0
        for f in range(11):
            k = wload(d_wgu[l, which, f])
            w = wsl[k][:].rearrange("p (a c n) -> p a c n", a=2, c=8)
            for j in range(2):
                fc = 2 * f + j
                bg, bu = (cnt % 2) * 2, (cnt % 2) * 2 + 1
                cnt += 1
                for c in range(8):
                    mm(Bps[bg], ps[bg][:], Bws[k], w[:, 0, c, j * 128:(j + 1) * 128], Bh[g][c], ht[g][:, c, :], c == 0, c == 7)
                for c in range(8):
                    mm(Bps[bu], ps[bu][:], Bws[k], w[:, 1, c, j * 128:(j + 1) * 128], Bh[g][c], ht[g][:, c, :], c == 0, c == 7)
                sk = cnt % 2
                act(qmr[sk][:], ps[bg][:], AF.Silu, [Bps[bg]], [Bqmr[sk]])
                tt(hmid1[:, fc, :], qmr[sk][:], ps[bu][:], ALU.mult, [Bqmr[sk], Bps[bu]], [Bhm1[fc]])
            wfree(k)
            if hook is not None:
                hook()
        for oc in range(8):
            k = wload(d_wdn[l, which, oc], NFC * 128)
            w = wsl[k][:, 0:NFC * 128].rearrange("p (f n) -> p f n", f=NFC)
            bank = 4 + (oc % 2)
            for fc in range(NFC):
                mm(Bps[bank], ps[bank][:], Bws[k], w[:, fc, :], Bhm1[fc], hmid1[:, fc, :], fc == 0, fc == NFC - 1)
            wfree(k)
            evac_y(g, oc, bank)
        post_norm(g, i)

    def ada_gen(l):
        bada, modt, gpre, gpost, Amod, Gmod = bada_l[l], modt_l[l], gpre_l[l], gpost_l[l], Amod_l[l], Gmod_l[l]
        Bbada, Bmod, Bgp, BAG = Bbada_l[l], Bmod_l[l], Bgp_l[l], BAG_l[l]
        dma("sp", bada[:], d_bada[l], [], [Bbada])
        dma("sp", gpre[:].rearrange("p a c -> p (a c)"), d_gpre[l], [], [Bgp])
        dma("sp", gpost[:].rearrange("p a c -> p (a c)"), d_gpost[l], [], [Bgp])
        for pa in range(18):
            k = wload(d_wada[l, pa])
            w = wsl[k][:].rearrange("p (c n) -> p c n", c=8)
            for cc in range(4):
                ci = pa * 4 + cc
                for c in range(8):
                    mm(Bps[7], ps[7][:, 2 * ci:2 * ci + 2], Bws[k], w[:, c, cc * 128:(cc + 1) * 128], Bscnd, scnd[:, c, :], c == 0, c == 7)
            wfree(k)
            if pa % 6 == 3:
                i = pa // 6
                lo, hi = (3 * i) * 16, (3 * i + 2) * 16
                tt(modt[:].rearrange("p a k -> p (a k)")[:, lo:hi], ps[7][:, lo:hi], bada[:, lo:hi], ALU.add, [Bps[7], Bbada], [Bmod[i]])
                for g in range(2):
                    stt(Amod[:, g, i, :], modt[:, (3 * i + 1) * 8:(3 * i + 2) * 8, g], 1.0, gpre[:, i, :], ALU.add, ALU.mult,
                        [Bmod[i], Bgp], [BAG[i]])
            if pa % 6 == 5:
                i = pa // 6
                lo, hi = (3 * i + 2) * 16, (3 * i + 3) * 16
                tt(modt[:].rearrange("p a k -> p (a k)")[:, lo:hi], ps[7][:, lo:hi], bada[:, lo:hi], ALU.add, [Bps[7], Bbada], [Bmodg_l[l][i]])
                for g in range(2):
                    stt(Gmod[:, g, i, :], modt[:, (3 * i + 2) * 8:(3 * i + 3) * 8, g], 1.0 if i == 1 else 0.5, gpost[:, i, :],
                        ALU.mult, ALU.mult, [Bmodg_l[l][i], Bgp], [BGG_l[l][i]])
            if pa < 17:
                yield

    def layer_consts(l):
        dma("sp", sconvw[:].rearrange("p a c -> p (a c)"), d_sconv[l], [], [Bsmall])
        dma("sp", ccw[:].rearrange("p a c -> p (a c)"), d_ccw[l], [], [Bsmall])
        dma("sp", ccv[:].rearrange("p a c -> p (a c)"), d_ccv[l], [], [Bsmall])
        dma("sp", subln[:], d_subln[l], [], [Bsmall])
        dma("sp", qkn[:], d_qkn[l], [], [Bsmall])
        dma("sp", kng[:], d_kng[l], [], [Bsmall])
        dma("sp", lamv[:].rearrange("p a c -> p (a c)"), d_lam[l], [], [Blam])
        dma("pool", ccpw[:].rearrange("p a c -> p (a c)"), d_ccpw[l], [], [Bccpw])
        dma("pool", kdS[:, :, 0:256], d_ckd[l].rearrange("p (c t) -> p c t", c=2), [], [Bkctx])
        dma("pool", kgS[:, 0:256], d_ckg[l], [], [Bkctx])
        dma("pool", vaSg[:, 0:2, :, 0:64], d_cvg[l].rearrange("p (c h d) -> p c h d", c=2, h=2), [], [Bvgc])
        lam_init = 0.8 - 0.6 * math.exp(-0.3 * l)
        tt(lamv[:, 0, :], lamv[:, 0, :], lamv[:, 1, :], ALU.mult, [Blam], [Blam])
        tt(lamv[:, 2, :], lamv[:, 2, :], lamv[:, 3, :], ALU.mult, [Blam], [Blam])
        S.add("dve", lambda e: e.tensor_reduce(lamt[:, 2:3], lamv[:, 0, :], mybir.AxisListType.X, ALU.add), reads=[Blam], writes=[Blam])
        S.add("dve", lambda e: e.tensor_reduce(lamt[:, 3:4], lamv[:, 2, :], mybir.AxisListType.X, ALU.add), reads=[Blam], writes=[Blam])
        act(lamt[:, 4:6], lamt[:, 2:4], AF.Exp, [Blam], [Blam])
        tt(lamt[:, 6:7], lamt[:, 5:6], lamt[:, 4:5], ALU.subtract, [Blam], [Blam])
        ts(lamt[:, 0:1], lamt[:, 6:7], -lam_init, ALU.add, [Blam], [Blam])
        ts(lamt[:, 1:2], subln[:], 1.0 - lam_init, ALU.mult, [Bsmall, Blam], [Blam])

    def ctx_v_diff(l):
        memset(vaSd[:, :, :, 64:128], 1.0, BvaSd, eng="pool")
        dma("pool", vaSd[:, 0:2, :, 0:64], d_cvd[l].rearrange("p (c h d) -> p c h d", c=2, h=4), [], BvaSd)

    dq = []

    def run_dq():
        todo = dq[:]
        del dq[:]
        for f_ in todo:
            f_()

    def proj_fm(g, k, col, bank):
        w = wsl[k][:].rearrange("p (c n) -> p c n", c=8)
        for c in range(8):
            mm(Bps[bank], ps[bank][:], Bws[k], w[:, c, col:col + 128], Bh[g][c], ht[g][:, c, :], c == 0, c == 7)
        run_dq()

    def proj_tm(g, k, col, ncol, tile_, bank):
        w = wsl[k][:].rearrange("p (c n) -> p c n", c=8)
        for c in range(8):
            mm(Bps[bank], ps[bank][:, 0:ncol], Bh[g][c], ht[g][:, c, tile_ * 128:(tile_ + 1) * 128], Bws[k], w[:, c, col:col + ncol], c == 0, c == 7)
        run_dq()

    def rope_apply(src, src_reads, kind, dst, Bdst):
        ci, si, pm = (0, 1, permd) if kind == "d" else (2, 3, permg)
        act(qbc[:], src, AF.Copy, src_reads, [Bqbc])

        def tail():
            mm(Bps[6], ps[6][:], Bcmat, pm, Bqbc, qbc[:], True, True)
            t1, B1 = nt()
            t2, B2 = nt()
            tt(t1[:], src, rope[:, ci, :], ALU.mult, src_reads + [Brope], [B1])
            tt(t2[:], ps[6][:], rope[:, si, :], ALU.mult, [Bps[6], Brope], [B2])
            tt(dst, t1[:], t2[:], ALU.add, [B1, B2], [Bdst])
        dq.append(tail)

    def headnorm(bank, col, then):
        act(qbf[:], ps[bank][:], AF.Square, [Bps[bank]], [Bqbf])

        def tail():
            mm(Bps[7], ps[7][:], Bcmat, blk64, Bqbf, qbf[:], True, True)
            r_, Br = nt()
            rsqrt_chain(r_[:], Br, ps[7][:], 1.0 / 64, [Bps[7]])
            o_, Bo = nt()
            stt(o_[:], ps[bank][:], qkn[:, col:col + 1], r_[:], ALU.mult, ALU.mult, [Bps[bank], Bsmall, Br], [Bo])
            then(o_[:], [Bo])
        dq.append(tail)

    def mix_in(l, g):
        for piece in range(5):
            if piece > 0:
                wfree(k)
            k = wload(d_wmi[l, piece])
            if piece == 0:
                for c in range(2):
                    b = nb(); proj_fm(g, k, c * 128, b)
                    ac, Bac = nt()
                    act(ac[:], ps[b][:], AF.Copy, [Bps[b]], [Bac])
                    b2 = nb(); proj_fm(g, k, 256 + c * 128, b2)
                    if g == 0:
                        tt(ppad[0][:, c, :, 1:257], ac[:].rearrange("p (s t) -> p s t", s=2), ps[b2][:].rearrange("p (s t) -> p s t", s=2),
                           ALU.mult, [Bac, Bps[b2]], [Bpp[0]])
                    else:
                        tt(ppad[1][:, c, 1:513], ac[:], ps[b2][:], ALU.mult, [Bac, Bps[b2]], [Bpp[1]])
            elif piece == 1:
                for c in range(2):
                    b = nb(); proj_fm(g, k, c * 128, b)
                    act(abt[g][:, c, :], ps[b][:], AF.Copy, [Bps[b]], Bab[g])
                for c in range(2):
                    b = nb(); proj_fm(g, k, 256 + c * 128, b)
                    if g == 0:
                        act(qpk[g][:, c, :], ps[b][:], AF.Copy, [Bps[b]], [Bqpk[g][c]])
                    else:
                        rope_apply(ps[b][:], [Bps[b]], "d", qpk[g][:, c, :], Bqpk[g][c])
            elif piece == 2:
                for c in range(2):
                    b = nb(); proj_fm(g, k, c * 128, b)
                    if g == 0:
                        act(kdP[:, c, :], ps[b][:], AF.Copy, [Bps[b]], [Bkd[0]])
                    else:
                        rope_apply(ps[b][:], [Bps[b]], "d", kdS[:, c, 256:256 + T], Bkd[1])
                for t_ in range(4):
                    b = nb()
                    rows = slice(t_ * 128, (t_ + 1) * 128)
                    if g == 0:
                        proj_tm(g, k, 0, 512, t_, b)
                        st, Bst = nt()
                        act(st[:], ps[b][:], AF.Copy, [Bps[b]], [Bst])
                        cp(vaP[:, t_, 0:4, 0:64], ps[b][:, 256:512].rearrange("p (h d) -> p h d", h=4), [Bps[b]], [BvaP])
                        dma("sp", d_ndk[l, rows, :], st[:, 0:256], [Bst], [])
                        dma("sp", d_ndv[l, rows, :], st[:, 256:512], [Bst], [])
                    else:
                        proj_tm(g, k, 256, 256, t_, b)
                        act(vtok[:, t_, 0:256], ps[b][:, 0:256], AF.Copy, [Bps[b]], Bcat1)
            elif piece == 3:
                for c in range(2):
                    b = nb(); proj_fm(g, k, 256 + c * 128, b)
                    sg, Bsg = nt()
                    act(sg[:], ps[b][:], AF.Sigmoid, [Bps[b]], [Bsg])
                    b2 = nb(); proj_fm(g, k, c * 128, b2)
                    if g == 0:
                        tt(upad[0][:, c, :, 15:271], sg[:].rearrange("p (s t) -> p s t", s=2),
                           ps[b2][:].rearrange("p (s t) -> p s t", s=2), ALU.mult, [Bsg, Bps[b2]], [Bup[0]])
                    else:
                        tt(upad[1][:, c, 15:527], sg[:], ps[b2][:], ALU.mult, [Bsg, Bps[b2]], [Bup[1]])
            else:
                for c in range(2):
                    b = nb(); proj_fm(g, k, c * 128, b)
                    if g == 0:
                        headnorm(b, 0, lambda src, Bsrc, c=c: cp(qpk[0][:, 2 + c, :], src, Bsrc, [Bqpk[0][2 + c]]))
                    else:
                        headnorm(b, 0, lambda src, Bsrc, c=c: rope_apply(src, Bsrc, "g", qpk[1][:, 2 + c, :], Bqpk[1][2 + c]))
                b = nb(); proj_fm(g, k, 256, b)
                if g == 0:
                    headnorm(b, 1, lambda src, Bsrc: cp(kgP[:], src, Bsrc, [Bkg[0]]))
                else:
                    headnorm(b, 1, lambda src, Bsrc: rope_apply(src, Bsrc, "g", kgS[:, 256:256 + T], Bkg[1]))
                for t_ in range(4):
                    b = nb()
                    rows = slice(t_ * 128, (t_ + 1) * 128)
                    if g == 0:
                        proj_tm(g, k, 256, 256, t_, b)
                        st, Bst = nt()
                        act(st[:, 128:256], ps[b][:, 128:256], AF.Copy, [Bps[b]], [Bst])
                        cp(vaP[:, t_, 4:6, 0:64], ps[b][:, 128:256].rearrange("p (h d) -> p h d", h=2), [Bps[b]], [BvaP])
                        act(st[:, 256:384], ps[b][:, 0:128], AF.Square, [Bps[b]], [Bst])
                        S.add("dve", lambda e, st=st: e.tensor_reduce(ktm[:, 0:2], st[:, 256:384].rearrange("p (h d) -> p h d", h=2),
                                                                      mybir.AxisListType.X, ALU.add), reads=[Bst], writes=[Bktm])
                        rsqrt_chain(ktm[:, 2:4], Bktm, ktm[:, 0:2], 1.0 / 64, [Bktm])
                        for h in range(2):
                            stt(st[:, h * 64:(h + 1) * 64], ps[b][:, h * 64:(h + 1) * 64], ktm[:, 2 + h:3 + h], kng[:],
                                ALU.mult, ALU.mult, [Bps[b], Bktm, Bsmall], [Bst])
                        dma("sp", d_ngk[l, rows, :], st[:, 0:128], [Bst], [])
                        dma("sp", d_ngv[l, rows, :], st[:, 128:256], [Bst], [])
                    else:
                        proj_tm(g, k, 384, 128, t_, b)
                        act(vtok[:, t_, 256:384], ps[b][:, 0:128], AF.Copy, [Bps[b]], Bcat1)
            if substop is not None and piece == substop:
                raise _Stop()
        wfree(k)
        while dq:
            run_dq()

    def exchange(l):
        for c in range(2):
            cp(edge[:, c, 0:15], upad[1][:, c, 15:30], [Bup[1]], [Bedge])
            cp(edge[:, c, 15:16], ppad[1][:, c, 1:2], [Bpp[1]], [Bedge])
            cp(edge[:, c, 16:31], upad[1][:, c, 512:527], [Bup[1]], [Bedge])
            cp(edge[:, c, 31:32], ppad[1][:, c, 512:513], [Bpp[1]], [Bedge])
        bo = d_bounce[l]
        dma("sp", bo[0:256, :].rearrange("(c p) t -> p c t", p=128), kdS[:, :, 256:256 + T], [Bkd[1]], [B_bounce[l]])
        dma("sp", bo[256:384, :], kgS[:, 256:256 + T], [Bkg[1]], [B_bounce[l]])
        vview = bo[384:768, :].rearrange("r c -> (r c)").rearrange("(tt p v) -> p tt v", p=128, v=384)
        dma("sp", vview, vtok, Bcat1, [B_bounce[l]])
        eview = bo[768:784, :].rearrange("r c -> (r c)").rearrange("(c p e) -> p c e", p=128, e=32)
        dma("sp", eview, edge[:], [Bedge], [B_bounce[l]])
        S.add("pool", lambda e: e.collective_compute("AllGather", ALU.bypass, replica_groups=[[0, 1, 2, 3], [4, 5, 6, 7]],
                                                     ins=[bo.opt()], outs=[d_gath[l].opt()]),
              reads=[B_bounce[l]], writes=[B_gath[l]], kind="cc")
    def exchange_recv(l):
        ga = d_gath[l]
        for r in range(4):
            base = r * BROWS
            ks = 256 + r * T
            dma("sp", kdS[:, :, ks:ks + T], ga[base:base + 256, :].rearrange("(c p) t -> p c t", p=128), [B_gath[l]], [Bkgath])
            dma("sp", kgS[:, ks:ks + T], ga[base + 256:base + 384, :], [B_gath[l]], [Bkgath])
            vv = ga[base + 384:base + 768, :].rearrange("r c -> (r c)").rearrange("(tt p v) -> p tt v", p=128, v=384)
            for t_ in range(4):
                ch = 2 + r * 4 + t_
                dma("sp", vaSd[:, ch, :, 0:64], vv[:, t_, 0:256].rearrange("p (h d) -> p h d", h=4), [B_gath[l]], BvaSd)
                dma("sp", vaSg[:, ch, :, 0:64], vv[:, t_, 256:384].rearrange("p (h d) -> p h d", h=2), [B_gath[l]], [Bvgg])
            ev = ga[base + 768:base + 784, :].rearrange("r c -> (r c)").rearrange("(c p e) -> p c e", p=128, e=32)
            dma("sp", egt[:, :, r, :], ev, [B_gath[l]], [Begt])

    def halos():
        for side, lo, mcol in ((0, 16, 8), (1, 0, 12)):
            ts(halo[:, :, side, :], egt[:, :, 0, lo:lo + 16], msk[:, mcol:mcol + 1], ALU.mult, [Begt, Bmsk], [Bhalo])
            for r in range(1, 4):
                stt(halo[:, :, side, :], egt[:, :, r, lo:lo + 16], msk[:, mcol + r:mcol + r + 1], halo[:, :, side, :], ALU.mult, ALU.add,
                    [Begt, Bmsk, Bhalo], [Bhalo])
        cp(upad[1][:, :, 0:15], halo[:, :, 0, 0:15], [Bhalo], [Bup[1]])
        cp(ppad[1][:, :, 0:1], halo[:, :, 0, 15:16], [Bhalo], [Bpp[1]])
        cp(upad[1][:, :, 527:542], halo[:, :, 1, 0:15], [Bhalo], [Bup[1]])
        cp(ppad[1][:, :, 513:514], halo[:, :, 1, 15:16], [Bhalo], [Bpp[1]])

    def attention(g, hook=None, premasked=False):
        nq, nkc = 512, (2 if g == 0 else 18)
        items = [(0, hm) for hm in range(12)]
        state = {"on_prev": None}

        def qinfo(hm):
            if hm < 8:
                return qpk[g][:, hm // 4, :], Bqpk[g][hm // 4], hm % 4, hm // 2, 32 ** -0.5
            sl = hm - 8
            return qpk[g][:, 2 + sl // 2, :], Bqpk[g][2 + sl // 2], 4 + sl % 2, 4 + sl % 2, 64 ** -0.5

        def emit_mask(idx):
            s_, hm = items[idx]
            qsrc, Bq, mcol, h, scale = qinfo(hm)
            qk = idx % 2
            ts(qmr[qk][:], qsrc, msk[:, mcol:mcol + 1], ALU.mult, [Bq, Bmsk], [Bqmr[qk]])

        def kv(s_, hm, h, kc):
            if g == 0:
                ksl = slice(s_ * 256 + kc * 128, s_ * 256 + (kc + 1) * 128)
                kT = kdP[:, hm // 4, ksl] if hm < 8 else kgP[:, ksl]
                Bk = [Bkd[0]] if hm < 8 else [Bkg[0]]
                return kT, Bk, vaP[:, s_ * 2 + kc, h, :], [BvaP]
            ksl = slice(kc * 128, (kc + 1) * 128)
            kT = kdS[:, hm // 4, ksl] if hm < 8 else kgS[:, ksl]
            Bk = [Bkctx] if kc < 2 else [Bkgath]
            if h < 4:
                return kT, Bk, vaSd[:, kc, h, :], BvaSd
            return kT, Bk, vaSg[:, kc, h - 4, :], [Bvgc if kc < 2 else Bvgg]

        def finalize(idx):
            s_, hm = items[idx]
            qs = slice(s_ * nq, (s_ + 1) * nq)
            qsrc, Bq, mcol, h, scale = qinfo(hm)
            ob = 4 + idx % 2
            r_, Br = nt()
            cp(r_[0:64, 0:nq], ps[ob][64:128, 0:nq], [Bps[ob]], [Br])
            act(r_[0:64, 0:nq], r_[0:64, 0:nq], AF.Ln, [Br], [Br])
            act(r_[0:64, 0:nq], r_[0:64, 0:nq], AF.Exp, [Br], [Br], scale=-1.0)
            if hm < 8:
                on, Bon = nt()
                tt(on[0:64, 0:nq], ps[ob][0:64, 0:nq], r_[0:64, 0:nq], ALU.mult, [Bps[ob], Br], [Bon])
                if hm % 2 == 0:
                    state["on_prev"] = (on, Bon)
                else:
                    on0, Bon0 = state["on_prev"]
                    d_, Bd = nt()
                    stt(d_[0:64, 0:nq], on[0:64, 0:nq], lamt[0:64, 0:1], on0[0:64, 0:nq], ALU.mult, ALU.add, [Bon0, Bon, Blam], [Bd])
                    act(qbf[0:64, 0:nq], d_[0:64, 0:nq], AF.Square, [Bd], [Bqbf])

                    def cont(d_=d_, Bd=Bd, h=h, qs=qs):
                        mm(Bps[6], ps[6][0:64, 0:nq], Bcmat, ones_b[0:64, 0:64], Bqbf, qbf[0:64, 0:nq], True, True)
                        rr, Brr = nt()
                        rsqrt_chain(rr[0:64, 0:nq], Brr, ps[6][0:64, 0:nq], 1.0 / 64, [Bps[6]])
                        stt(hb[g][:, h, qs], d_[0:64, 0:nq], lamt[0:64, 1:2], rr[0:64, 0:nq], ALU.mult, ALU.mult, [Bd, Blam, Brr], [Bhb[g][h]])
                    return cont
            else:
                sl = hm - 8
                tt(hb[g][:, 4 + sl, qs], ps[ob][0:64, 0:nq], r_[0:64, 0:nq], ALU.mult, [Bps[ob], Br], [Bhb[g][4 + sl]])
            return None

        contB = [None]

        def fin_step(idx):
            if contB[0] is not None:
                contB[0]()
                contB[0] = None
            if idx >= 0:
                contB[0] = finalize(idx)

        NPT = len(pT)
        stepc = [0]
        if not premasked:
            emit_mask(0)
        for idx, (s_, hm) in enumerate(items):
            qs = slice(s_ * nq, (s_ + 1) * nq)
            qsrc, Bq, mcol, h, scale = qinfo(hm)
            qk = idx % 2
            ob = 4 + idx % 2
            if idx + 1 < len(items):
                emit_mask(idx + 1)
            pend = []
            if g == 0:
                pks = []
                for kc in range(2):
                    sbk = stepc[0] % 4
                    pk = stepc[0] % NPT
                    stepc[0] += 1
                    pks.append(pk)
                    for s2 in range(2):
                        kT, Bk, va, Bv = kv(s2, hm, h, kc)
                        cs = slice(s2 * 256, (s2 + 1) * 256)
                        S.add("pe", lambda e, o=ps[sbk][:, cs], a=kT, b_=qmr[qk][:, cs]: e.matmul(o, a, b_, start=True, stop=True),
                              reads=Bk + [Bqmr[qk]], writes=[Bps[sbk]])
                    act(pT[pk][:], ps[sbk][:], AF.Exp, [Bps[sbk]], [BpT[pk]], scale=scale)
                fin_step(idx - 1)
                for s2 in range(2):
                    cs = slice(s2 * 256, (s2 + 1) * 256)
                    for kc in range(2):
                        kT, Bk, va, Bv = kv(s2, hm, h, kc)
                        S.add("pe", lambda e, o=ps[ob][:, cs], a=va, b_=pT[pks[kc]][:, cs], st_=(kc == 0), sp_=(kc == 1): e.matmul(o, a, b_, start=st_, stop=sp_),
                              reads=Bv + [BpT[pks[kc]]], writes=[Bps[ob]])
                continue

            def score(kc):
                kT, Bk, va, Bv = kv(s_, hm, h, kc)
                sbk = stepc[0] % 4
                pk = stepc[0] % NPT
                stepc[0] += 1
                S.add("pe", lambda e, o=ps[sbk][:, 0:nq], a=kT, b_=qmr[qk][:, qs]: e.matmul(o, a, b_, start=True, stop=True),
                      reads=Bk + [Bqmr[qk]], writes=[Bps[sbk]])
                act(pT[pk][:, 0:nq], ps[sbk][:, 0:nq], AF.Exp, [Bps[sbk]], [BpT[pk]], scale=scale)
                pend.append((kc, pk, va, Bv))

            def pv():
                kc, pk, va, Bv = pend.pop(0)
                st_, sp_ = (kc == 0), (kc == nkc - 1)
                S.add("pe", lambda e, o=ps[ob][:, 0:nq], a=va, b_=pT[pk][:, 0:nq], st_=st_, sp_=sp_: e.matmul(o, a, b_, start=st_, stop=sp_),
                      reads=Bv + [BpT[pk]], writes=[Bps[ob]])

            DEPTH_PIPE = 2
            for kc in range(nkc):
                score(kc)
                if len(pend) > DEPTH_PIPE:
                    pv()
            fin_step(idx - 1)
            while pend:
                pv()
            if hook is not None:
                hook()
        fin_step(len(items) - 1)
        fin_step(-1)

    def diag(wcol):
        k = dgc[0] % NDG
        dgc[0] += 1
        ts(dgr[k][:], ident, wcol, ALU.mult, [Bcmat, Bsmall], [Bdgr[k]])
        return dgr[k][:], Bdgr[k]

    def convs(g):
        nseq, n = (2, 256) if g == 0 else (1, 512)
        ucs = []
        for c in range(2):
            b = nb()
            for k in range(31):
                dg, Bdg = diag(ccw[:, c, k:k + 1])
                for s in range(nseq):
                    rhs = upad[0][:, c, s, k:k + n] if g == 0 else upad[1][:, c, k:k + n]
                    S.add("pe", lambda e, o=ps[4 + s][:, 0:n], a=dg, r=rhs, st_=(k == 0), sp_=(k == 30): e.matmul(o, a, r, start=st_, stop=sp_),
                          reads=[Bdg, Bup[g]], writes=[Bps[4 + s]])
            uc, Buc = nt()
            for s in range(nseq):
                act(uc[:, s * n:(s + 1) * n], ps[4 + s][:, 0:n], AF.Identity, [Bps[4 + s], Bsmall], [Buc], bias=ccv[:, 0, c:c + 1])
            ucq, Bucq = nt()
            act(ucq[:], uc[:], AF.Square, [Buc], [Bucq])
            ucs.append((uc, Buc, ucq, Bucq))
            for k in range(3):
                dg, Bdg = diag(sconvw[:, c, k:k + 1])
                for s in range(nseq):
                    rhs = ppad[0][:, c, s, k:k + n] if g == 0 else ppad[1][:, c, k:k + n]
                    S.add("pe", lambda e, o=ps[6 + s][:, 0:n], a=dg, r=rhs, st_=(k == 0), sp_=(k == 2): e.matmul(o, a, r, start=st_, stop=sp_),
                          reads=[Bdg, Bpp[g]], writes=[Bps[6 + s]])
            for s in range(nseq):
                tt(cat[g][:, c, s * n:(s + 1) * n], abt[g][:, c, s * n:(s + 1) * n], ps[6 + s][:, 0:n], ALU.mult, Bab[g] + [Bps[6 + s]], [Bcat[g][c]])
        b1 = nb()
        for c in range(2):
            mm(Bps[b1], ps[b1][:], Bcmat, ones_f, ucs[c][1], ucs[c][0][:], c == 0, c == 1)
        b2 = nb()
        for c in range(2):
            mm(Bps[b2], ps[b2][:], Bcmat, ones_f, ucs[c][3], ucs[c][2][:], c == 0, c == 1)
        mean, Bmean = nt()
        ts(mean[:], ps[b1][:], 1.0 / 256, ALU.mult, [Bps[b1]], [Bmean])
        var, Bvar = nt()
        tt(var[:], mean[:], mean[:], ALU.mult, [Bmean], [Bvar])
        stt(var[:], ps[b2][:], 1.0 / 256, var[:], ALU.mult, ALU.subtract, [Bps[b2], Bvar], [Bvar])
        rsqrt_chain(var[:], Bvar, var[:], 1.0, [Bvar])
        for c in range(2):
            uc, Buc, ucq, Bucq = ucs[c]
            tt(ucq[:], uc[:], mean[:], ALU.subtract, [Buc, Bmean], [Bucq])
            tt(ucq[:], ucq[:], var[:], ALU.mult, [Bucq, Bvar], [Bucq])
            act(vb[:, c, :], ucq[:], AF.Silu, [Bucq, Bsmall], [Bvb], scale=ccv[:, 1, c:c + 1], bias=ccv[:, 2, c:c + 1])
        for oc in range(2):
            b = nb()
            for c in range(2):
                mm(Bps[b], ps[b][:], Bccpw, ccpw[:, c, oc * 128:(oc + 1) * 128], Bvb, vb[:, c, :], c == 0, c == 1)
            act(cat[g][:, 2 + oc, :], ps[b][:], AF.Copy, [Bps[b]], [Bcat[g][2 + oc]])

    def mix_out(l, g):
        wk = [wload(d_wmo[l, a]) for a in range(3)]
        pieces = [(cat[g][:, 0, :], Bcat[g][0], 128), (cat[g][:, 1, :], Bcat[g][1], 128)]
        pieces += [(hb[g][:, h, :], Bhb[g][h], 64) for h in range(4)]
        pieces += [(cat[g][:, 2, :], Bcat[g][2], 128), (cat[g][:, 3, :], Bcat[g][3], 128)]
        pieces += [(hb[g][:, 4 + h, :], Bhb[g][4 + h], 64) for h in range(4)]
        for oc in range(8):
            bank = 4 + (oc % 2)
            for pi, (ap_, B_, kk) in enumerate(pieces):
                k = wk[pi // 4]
                w = wsl[k][:].rearrange("p (a n) -> p a n", a=4)
                mm(Bps[bank], ps[bank][:], Bws[k], w[0:kk, pi % 4, oc * 128:(oc + 1) * 128], B_, ap_[0:kk, :], pi == 0, pi == 11)
            flush_sq()
            evac_y(g, oc, bank)
        flush_sq(True)
        for k in wk:
            wfree(k)
        post_norm(g, 1)

    def dumpx(idx, g):
        if DEBUG:
            dma("sp", d_dbg[idx].rearrange("p (c t) -> p c t", c=8), xt[g][:], Bx[g], [])

    stage = [0]

    def chk():
        stage[0] += 1
        if stop is not None and stage[0] >= stop:
            raise _Stop()

    def mixer(l):
        cur[0] = l
        pre_norm(1, 1)
        mix_in(l, 1)
        ctx_v_diff(l)
        exchange(l)
        pre_norm(0, 1)
        mix_in(l, 0)
        attention(0)
        exchange_recv(l)
        convs(0)
        ts(qmr[0][:], qpk[1][:, 0, :], msk[:, 0:1], ALU.mult, [Bqpk[1][0], Bmsk], [Bqmr[0]])
        mix_out(l, 0)
        hook = None
        gen = None
        if l + 1 < DEPTH:
            gen = ada_gen(l + 1)
            cnt_h = [0]

            def hook():
                n = 2 if cnt_h[0] < 6 else 1
                cnt_h[0] += 1
                for _ in range(n):
                    next(gen, None)
        attention(1, hook, premasked=True)
        if gen is not None:
            for _ in gen:
                pass
        halos()
        convs(1)
        pre_norm(0, 2, l, (h1, Bh1))
        mix_out(l, 1)

    try:
        gen0 = ada_gen(0)
        for _ in range(4):
            next(gen0, None)
        layer_consts(0)
        ffn_chain([(0, 0, 0, 0), (0, 0, 0, 1)], hook=lambda: next(gen0, None))
        for _ in gen0:
            pass
        chk()
        mixer(0)
        chk()
        layer_consts(1)
        ffn_chain([(0, 1, 2, 0), (0, 1, 2, 1), (1, 0, 0, 0), (1, 0, 0, 1)], first_pre_done=True)
        mixer(1)
        ffn_chain([(1, 1, 2, 0), (1, 1, 2, 1)], first_pre_done=True)
    except _Stop:
        pass
    for g in range(2):
        for c in range(8):
            dma("sp", d_y[g][:, c, :], xt[g][:, c, :], [Bx[g][c]], [])

    print('sbuf remaining', nc.sbuf_bytes_remaining, flush=True)
    S.finalize()
    with nc.Block() as block:
        @block.tensor
        def _(e):
            S.replay("pe", e, sems)

        @block.scalar
        def _(e):
            S.replay("act", e, sems)

        @block.vector
        def _(e):
            S.replay("dve", e, sems)

        @block.gpsimd
        def _(e):
            S.replay("pool", e, sems)

        @block.sync
        def _(e):
            S.replay("sp", e, sems)
    es.close()
    return nc


def _fm(a):
    t = a.shape[0]
    return np.ascontiguousarray(a.T.reshape(8, 128, t).transpose(1, 0, 2))


def _unfm(a):
    t = a.shape[2]
    return np.ascontiguousarray(a.transpose(1, 0, 2).reshape(1024, t).T)


def _kpieces(w, ncols_piece):
    n = w.shape[1]
    npieces = n // ncols_piece
    a = w.reshape(8, 128, npieces, ncols_piece).transpose(2, 1, 0, 3)
    return np.ascontiguousarray(a.reshape(npieces, 128, 8 * ncols_piece))


def _vec_fm(v, nch):
    return np.ascontiguousarray(v.reshape(nch, 128).T)


def _rope_tables():
    theta = np.float32(10000.0)
    tabs = {}
    for name, dd in (("d", 32), ("g", 64)):
        nf = dd // 4
        inv = (theta ** (-(np.arange(nf, dtype=np.float32) / np.float32(nf)))).astype(np.float32)
        tabs[name] = (dd, nf, inv)
    return tabs


def _host_inputs(inp):
    f32 = np.float32
    shared = {}
    L = DEPTH
    shared["wada"] = np.stack([_kpieces(inp["w_ada"][l], 512) for l in range(L)])
    shared["gpre"] = np.stack([np.concatenate([_vec_fm(inp["norm_pre"][l, i], 8) for i in range(3)], axis=1) for l in range(L)])
    shared["gpost"] = np.stack([np.concatenate([_vec_fm(inp["norm_post"][l, i], 8) for i in range(3)], axis=1) for l in range(L)])
    wgu = np.zeros((L, 2, 11, 128, 4096), f32)
    wdn = np.zeros((L, 2, 8, 128, NFC * 128), f32)
    for l in range(L):
        for wi, (gn, un, dn) in enumerate((("ffn1_gate", "ffn1_up", "ffn1_down"), ("ffn2_gate", "ffn2_up", "ffn2_down"))):
            gp = _kpieces(inp[gn][l], 256).reshape(11, 128, 2048)
            up = _kpieces(inp[un][l], 256).reshape(11, 128, 2048)
            wgu[l, wi] = np.concatenate([gp, up], axis=2)
            wd = inp[dn][l]
            a = wd.reshape(NFC, 128, 8, 128).transpose(2, 1, 0, 3)
            wdn[l, wi] = a.reshape(8, 128, NFC * 128)
    shared["wgu"] = wgu
    shared["wdn"] = wdn
    wmi = np.zeros((L, 5, 128, 4096), f32)
    wmo = np.zeros((L, 3, 128, 4096), f32)
    for l in range(L):
        w = inp["w_mix_in"][l].copy()
        dq = w[:, 2048:2304].reshape(1024, 4, 64)
        w[:, 2048:2304] = dq[:, [0, 2, 1, 3], :].reshape(1024, 256)
        w[:, 0:768] = np.concatenate([w[:, 256:512], w[:, 512:768], w[:, 0:256]], axis=1)
        wmi[l] = _kpieces(w, 512)
        wo = inp["w_mix_out"][l]
        pieces = np.zeros((12, 128, 1024), f32)
        pieces[0] = wo[0:128]; pieces[1] = wo[128:256]
        for h in range(4):
            pieces[2 + h, 0:64] = wo[256 + h * 64:256 + (h + 1) * 64]
        pieces[6] = wo[512:640]; pieces[7] = wo[640:768]
        for slot, h in enumerate((0, 2, 1, 3)):
            pieces[8 + slot, 0:64] = wo[768 + h * 64:768 + (h + 1) * 64]
        wmo[l] = pieces.reshape(3, 4, 128, 1024).transpose(0, 2, 1, 3).reshape(3, 128, 4096)
    shared["wmi"] = wmi
    shared["wmo"] = wmo
    shared["sconvw"] = np.stack([np.ascontiguousarray(inp["sconv_w"][l].reshape(3, 2, 128).transpose(2, 1, 0)).reshape(128, 6) for l in range(L)])
    shared["ccw"] = np.stack([np.ascontiguousarray(inp["ccm_dw_w"][l].reshape(31, 2, 128).transpose(2, 1, 0)).reshape(128, 62) for l in range(L)])
    shared["ccv"] = np.stack([np.concatenate([_vec_fm(inp[n][l], 2) for n in ("ccm_dw_b", "ccm_ln_g", "ccm_ln_b")], axis=1) for l in range(L)])
    shared["ccpw"] = np.stack([np.ascontiguousarray(inp["ccm_pw"][l].reshape(2, 128, 256).transpose(1, 0, 2)).reshape(128, 512) for l in range(L)])
    shared["lamv"] = np.stack([np.tile(np.concatenate([inp[n][l] for n in ("diff_lq1", "diff_lk1", "diff_lq2", "diff_lk2")])[None, :], (128, 1)) for l in range(L)])
    shared["subln"] = np.stack([np.tile(inp["diff_subln"][l], 2).reshape(128, 1) for l in range(L)])
    shared["qkn"] = np.stack([np.stack([np.tile(inp["gqa_qnorm"][l], 2), np.tile(inp["gqa_knorm"][l], 2)], axis=1) for l in range(L)])
    shared["kng"] = np.stack([np.tile(inp["gqa_knorm"][l][None, :], (128, 1)) for l in range(L)])
    cm = np.zeros((128, 5, 128), f32)
    cm[:, 0, :] = 1.0
    cm[0:64, 1, 0:64] = 1.0; cm[64:128, 1, 64:128] = 1.0
    cm[:, 2, :] = np.eye(128, dtype=f32)
    for i in range(128):
        cm[i ^ 8, 3, i] = 1.0
        cm[i ^ 16, 4, i] = 1.0
    shared["cmat"] = cm.reshape(128, 640)
    for k in shared:
        shared[k] = np.ascontiguousarray(shared[k], dtype=f32)

    tabs = _rope_tables()
    maps = []
    for i in range(NCORES):
        b, j = i // 4, i % 4
        m = dict(shared)
        m["xp"] = _fm(inp["x_prompt"][2 * i:2 * i + 2].reshape(512, 1024))
        m["xs"] = _fm(inp["x_sample"][b, 512 * j:512 * (j + 1)])
        m["cnd"] = np.ascontiguousarray(np.stack([_vec_fm(inp["c_ctx"], 8), _vec_fm(inp["c"][b], 8)], axis=2))
        m["bada"] = np.stack([np.repeat(_vec_fm(inp["b_ada"][l], 72), 2, axis=1) for l in range(L)])
        m["ckd"] = np.stack([np.ascontiguousarray(inp["cache_diff_k"][b, l].reshape(256, 2, 128).transpose(2, 1, 0)).reshape(128, 512) for l in range(L)])
        m["ckg"] = np.stack([np.ascontiguousarray(inp["cache_gqa_k"][b, l].reshape(256, 128).T) for l in range(L)])
        m["cvd"] = np.stack([np.ascontiguousarray(inp["cache_diff_v"][b, l].reshape(2, 128, 256).transpose(1, 0, 2)).reshape(128, 512) for l in range(L)])
        m["cvg"] = np.stack([np.ascontiguousarray(inp["cache_gqa_v"][b, l].reshape(2, 128, 128).transpose(1, 0, 2)).reshape(128, 256) for l in range(L)])
        tok = np.arange(512 * j, 512 * (j + 1))
        row = (tok // 64).astype(f32)
        col = (tok % 64).astype(f32)
        rt = np.zeros((128, 4, T), f32)
        for ti, name in enumerate(("d", "g")):
            dd, nf, inv = tabs[name]
            for p in range(128):
                f = p % dd
                pos = row if f < dd // 2 else col
                ang = (pos * inv[f % nf]).astype(f32)
                sgn = -1.0 if (f % (2 * nf)) < nf else 1.0
                rt[p, 2 * ti, :] = np.cos(ang)
                rt[p, 2 * ti + 1, :] = sgn * np.sin(ang)
        m["rope"] = rt.reshape(128, 4 * T)
        mk = np.zeros((128, 16), f32)
        for q in range(4):
            mk[32 * q:32 * (q + 1), q] = 1.0
        mk[0:64, 4] = 1.0
        mk[64:128, 5] = 1.0
        if j > 0:
            mk[:, 8 + j - 1] = 1.0
        if j < 3:
            mk[:, 12 + j + 1] = 1.0
        m["msk"] = mk
        for k in m:
            m[k] = np.ascontiguousarray(m[k], dtype=f32)
        maps.append(m)
    return maps


_NC_CACHE = {}


def kernel(**inputs):
    inp = {k: np.asarray(v) for k, v in inputs.items()}
    if "nc" not in _NC_CACHE:
        _NC_CACHE["nc"] = build_program()
    nc = _NC_CACHE["nc"]
    in_maps = _host_inputs(inp)
    res = run_bass_kernel_spmd(nc, in_maps, core_ids=list(range(NCORES)))
    R = res.results
    y_prompt = np.zeros((16, 256, 1024), np.float32)
    y_sample = np.zeros((2, 2048, 1024), np.float32)
    ndk = np.zeros((16, DEPTH, 256, 4, 2, 32), np.float32)
    ndv = np.zeros((16, DEPTH, 256, 4, 64), np.float32)
    ngk = np.zeros((16, DEPTH, 256, 2, 64), np.float32)
    ngv = np.zeros((16, DEPTH, 256, 2, 64), np.float32)
    for i in range(NCORES):
        b, j = i // 4, i % 4
        y_prompt[2 * i:2 * i + 2] = _unfm(R[i]["yp"]).reshape(2, 256, 1024)
        y_sample[b, 512 * j:512 * (j + 1)] = _unfm(R[i]["ys"])
        for l in range(DEPTH):
            ndk[2 * i:2 * i + 2, l] = R[i]["ndk"][l].reshape(2, 256, 4, 2, 32)
            ndv[2 * i:2 * i + 2, l] = R[i]["ndv"][l].reshape(2, 256, 4, 64)
            ngk[2 * i:2 * i + 2, l] = R[i]["ngk"][l].reshape(2, 256, 2, 64)
            ngv[2 * i:2 * i + 2, l] = R[i]["ngv"][l].reshape(2, 256, 2, 64)
    return (y_prompt, y_sample, ndk, ndv, ngk, ngv)
```

```python
import math
from contextlib import ExitStack
import numpy as np
import concourse.bass as bass
import concourse.mybir as mybir
from concourse.bass_utils import run_bass_kernel_spmd

F32 = mybir.dt.float32
BF16 = mybir.dt.bfloat16
ALU = mybir.AluOpType
AF = mybir.ActivationFunctionType

D = 1024
DFF = 2816
NFC = 22
DEPTH = 2
T = 512
EPS = 1e-6
NCORES = 8
NKS = 2304
BROWS = 784
NW = 3
NDS = 12
DEBUG = False


class Buf:
    __slots__ = ("name", "w", "r", "psum")

    def __init__(self, name, psum=False):
        self.name = name
        self.w = None
        self.r = {}
        self.psum = psum


class Op:
    __slots__ = ("eng", "emit", "deps", "needs_inc", "ticket", "kind", "dsem", "dval", "pre")

    def __init__(self, eng, emit, kind):
        self.eng = eng
        self.emit = emit
        self.deps = []
        self.needs_inc = False
        self.ticket = 0
        self.kind = kind
        self.dsem = None
        self.dval = 0
        self.pre = None


class Sched:
    ENGS = ("pe", "act", "dve", "pool", "sp")

    def __init__(self):
        self.ops = {e: [] for e in self.ENGS}
        self.ndma = {e: 0 for e in self.ENGS}
        self.ncc = 0

    def add(self, eng, emit, reads=(), writes=(), kind="c"):
        op = Op(eng, emit, kind)
        deps = {}

        def dep(o):
            if o is None or o is op:
                return
            if eng == "pe" and kind == "c" and o.eng == "pe" and o.kind == "c":
                return
            deps[id(o)] = o

        for b in reads:
            dep(b.w)
            if b.psum:
                for k_, o in b.r.items():
                    if k_ != eng:
                        dep(o)
        for b in writes:
            dep(b.w)
            for o in b.r.values():
                dep(o)
        for b in writes:
            b.w = op
            b.r = {}
        for b in reads:
            if kind == "c":
                b.r[eng] = op
            else:
                b.r[("d", id(op))] = op
        op.deps = list(deps.values())
        for o in op.deps:
            if o.kind == "c":
                o.needs_inc = True
        if kind == "d":
            n = self.ndma[eng]
            self.ndma[eng] = n + 1
            op.dsem = (eng, n % NDS)
            op.dval = 16 * (n // NDS + 1)
            if n >= NDS:
                op.pre = (op.dsem, 16 * (n // NDS))
        elif kind == "cc":
            self.ncc += 1
            op.dsem = ("cc", self.ncc - 1)
            op.dval = 1
        self.ops[eng].append(op)
        return op

    def finalize(self):
        for e in self.ENGS:
            t = 0
            for op in self.ops[e]:
                if op.kind == "c" and op.needs_inc:
                    t += 1
                    op.ticket = t

    def replay(self, e, eng, sems):
        waited = {}

        def wait(key, val):
            if waited.get(key, 0) < val:
                eng.wait_ge(sems[key], val)
                waited[key] = val

        for op in self.ops[e]:
            for o in op.deps:
                if o.kind == "c":
                    wait((o.eng, "c"), o.ticket)
                else:
                    wait(o.dsem, o.dval)
            if op.pre is not None:
                wait(op.pre[0], op.pre[1])
            ins = op.emit(eng)
            if op.kind == "d":
                ins.then_inc(sems[op.dsem], 16)
            elif op.kind == "cc":
                ins.then_inc(sems[op.dsem])
            elif op.needs_inc:
                ins.then_inc(sems[(e, "c")], 1)
        n = self.ndma[e]
        for k in range(min(n, NDS)):
            cnt = (n - 1 - k) // NDS + 1
            wait((e, k), 16 * cnt)
        if e == "pool":
            for k in range(self.ncc):
                wait(("cc", k), 1)


class _Stop(Exception):
    pass


def build_program(stop=None, substop=None):
    nc = bass.Bass("TRN2", target_bir_lowering=False)
    S = Sched()
    es = ExitStack()

    def din(name, shape, dt=F32):
        return nc.dram_tensor(name, list(shape), dt, kind="ExternalInput").ap()

    def dout(name, shape, dt=F32):
        return nc.dram_tensor(name, list(shape), dt, kind="ExternalOutput").ap()

    d_x = [din("xp", [128, 8, T]), din("xs", [128, 8, T])]
    d_cnd = din("cnd", [128, 8, 2])
    d_wada = din("wada", [DEPTH, 18, 128, 8 * 512])
    d_bada = din("bada", [DEPTH, 128, 144])
    d_gpre = din("gpre", [DEPTH, 128, 24])
    d_gpost = din("gpost", [DEPTH, 128, 24])
    d_wgu = din("wgu", [DEPTH, 2, 11, 128, 4096])
    d_wdn = din("wdn", [DEPTH, 2, 8, 128, NFC * 128])
    d_wmi = din("wmi", [DEPTH, 5, 128, 4096])
    d_wmo = din("wmo", [DEPTH, 3, 128, 4096])
    d_sconv = din("sconvw", [DEPTH, 128, 6])
    d_ccw = din("ccw", [DEPTH, 128, 62])
    d_ccv = din("ccv", [DEPTH, 128, 6])
    d_ccpw = din("ccpw", [DEPTH, 128, 512])
    d_lam = din("lamv", [DEPTH, 128, 128])
    d_subln = din("subln", [DEPTH, 128, 1])
    d_qkn = din("qkn", [DEPTH, 128, 2])
    d_kng = din("kng", [DEPTH, 128, 64])
    d_ckd = din("ckd", [DEPTH, 128, 2 * 256])
    d_ckg = din("ckg", [DEPTH, 128, 256])
    d_cvd = din("cvd", [DEPTH, 128, 2 * 256])
    d_cvg = din("cvg", [DEPTH, 128, 2 * 128])
    d_rope = din("rope", [128, 4 * T])
    d_cmat = din("cmat", [128, 5 * 128])
    d_msk = din("msk", [128, 16])
    d_y = [dout("yp", [128, 8, T]), dout("ys", [128, 8, T])]
    d_ndk = dout("ndk", [DEPTH, T, 256])
    d_ndv = dout("ndv", [DEPTH, T, 256])
    d_ngk = dout("ngk", [DEPTH, T, 128])
    d_ngv = dout("ngv", [DEPTH, T, 128])
    if DEBUG:
        d_dbg = dout("dbg", [8, 128, 8 * T])
        d_dbgb = dout("dbgb", [4, 128, 8 * T], BF16)
    d_bounce = [nc.dram_tensor("bounce%d" % l, [BROWS, 512], BF16).ap() for l in range(DEPTH)]
    d_gath = [nc.dram_tensor("gath%d" % l, [4 * BROWS, 512], BF16).ap() for l in range(DEPTH)]
    B_bounce = [Buf("bounce%d" % l) for l in range(DEPTH)]
    B_gath = [Buf("gath%d" % l) for l in range(DEPTH)]

    def sb(name, shape, dt=F32):
        return es.enter_context(nc.sbuf_tensor("s_" + name, list(shape), dt))

    xt = [sb("xP", [128, 8, T]), sb("xS", [128, 8, T])]
    Bx = [[Buf("x%d_%d" % (g, c)) for c in range(8)] for g in range(2)]
    h1 = sb("h", [128, 8, T], BF16)
    ht = [h1, h1]
    Bh1 = [Buf("h_%d" % c) for c in range(8)]
    Bh = [Bh1, Bh1]
    y1 = sb("y", [128, 8, T])
    yt = [y1, y1]
    By1 = [Buf("y_%d" % c) for c in range(8)]
    By = [By1, By1]
    wsl = [sb("ws%d" % k, [128, 4096], BF16) for k in range(NW)]
    Bws = [Buf("ws%d" % k) for k in range(NW)]
    psA = es.enter_context(nc.psum_tensor("psA", [128, 2048], F32))
    ps = [psA[:, k * 512:(k + 1) * 512] for k in range(4)] + [es.enter_context(nc.psum_tensor("ps%d" % k, [128, 512], F32)) for k in range(4, 8)]
    Bps = [Buf("ps%d" % k, psum=True) for k in range(8)]
    hmid1 = sb("hmid", [128, NFC, T], BF16)
    Bhm1 = [Buf("hm_%d" % f) for f in range(NFC)]

    NTP = 6
    tpool = [sb("tp%d" % k, [128, T]) for k in range(NTP)]
    Btp = [Buf("tp%d" % k) for k in range(NTP)]
    tpc = [0]

    rstd_t = sb("rstd_t", [128, T]); Brstd_t = Buf("rstd_t")

    def nt():
        k = tpc[0] % NTP
        tpc[0] += 1
        return tpool[k], Btp[k]

    cnd = sb("cnd", [128, 8, 2]); Bcnd = Buf("cnd")
    scnd = sb("scnd", [128, 8, 2], BF16); Bscnd = Buf("scnd")
    bada_l = [sb("bada%d" % l, [128, 144]) for l in range(DEPTH)]; Bbada_l = [Buf("bada%d" % l) for l in range(DEPTH)]
    modt_l = [sb("modt%d" % l, [128, 72, 2]) for l in range(DEPTH)]; Bmod_l = [[Buf("mod%d_%d" % (l, i)) for i in range(3)] for l in range(DEPTH)]
    gpre_l = [sb("gpre%d" % l, [128, 3, 8]) for l in range(DEPTH)]; gpost_l = [sb("gpost%d" % l, [128, 3, 8]) for l in range(DEPTH)]
    Bgp_l = [Buf("gprepost%d" % l) for l in range(DEPTH)]
    Amod_l = [sb("Amod%d" % l, [128, 2, 3, 8]) for l in range(DEPTH)]; Gmod_l = [sb("Gmod%d" % l, [128, 2, 3, 8]) for l in range(DEPTH)]
    BAG_l = [[Buf("AG%d_%d" % (l, i)) for i in range(3)] for l in range(DEPTH)]
    BGG_l = [[Buf("GG%d_%d" % (l, i)) for i in range(3)] for l in range(DEPTH)]
    Bmodg_l = [[Buf("modg%d_%d" % (l, i)) for i in range(3)] for l in range(DEPTH)]
    cur = [0]
    sconvw = sb("sconvw", [128, 2, 3]); ccw = sb("ccw", [128, 2, 31]); ccv = sb("ccv", [128, 3, 2])
    Bsmall = Buf("small")
    ccpw = sb("ccpw", [128, 2, 256], BF16); Bccpw = Buf("ccpw")
    lamv = sb("lamv", [128, 4, 32]); lamt = sb("lamt", [128, 8]); Blam = Buf("lam")
    subln = sb("subln", [128, 1]); qkn = sb("qkn", [128, 2]); kng = sb("kng", [128, 64])
    rope = sb("rope", [128, 4, T]); Brope = Buf("rope")
    cmatf = sb("cmatf", [128, 5, 128]); cmat = sb("cmat", [128, 5, 128], BF16); Bcmat = Buf("cmat")
    msk = sb("msk", [128, 16]); Bmsk = Buf("msk")
    NDG = 6
    dgr = [sb("dgr%d" % k, [128, 128], BF16) for k in range(NDG)]; Bdgr = [Buf("dgr%d" % k) for k in range(NDG)]
    dgc = [0]

    hx = sb("hx", [128, 8, T], BF16)
    Bhx = [Buf("hx_%d" % c) for c in range(8)]
    abt = [hx[:, 0:2, :], hx[:, 2:4, :]]; Bab = [Bhx[0:2], Bhx[2:4]]
    ppad = [sb("ppadP", [128, 2, 2, 258], BF16), sb("ppadS", [128, 2, 514], BF16)]; Bpp = [Buf("pp0"), Buf("pp1")]
    upad = [sb("upadP", [128, 2, 2, 286], BF16), sb("upadS", [128, 2, 542], BF16)]; Bup = [Buf("up0"), Buf("up1")]
    cat1 = hx[:, 4:8, :]
    cat = [cat1, cat1]
    Bcat1 = Bhx[4:8]
    Bcat = [Bcat1, Bcat1]
    hb1 = sb("hb", [64, 8, T], BF16)
    hb = [hb1, hb1]
    Bhb1 = [Buf("hb_%d" % k) for k in range(8)]
    Bhb = [Bhb1, Bhb1]
    qpk = [sb("qpk%d" % g, [128, 4, T], BF16) for g in range(2)]
    Bqpk = [[Buf("qpk%d_%d" % (g, k)) for k in range(4)] for g in range(2)]
    qmr = [sb("qmr%d" % k, [128, T], BF16) for k in range(2)]; Bqmr = [Buf("qmr%d" % k) for k in range(2)]
    kdP = sb("kdP", [128, 2, T], BF16); kgP = sb("kgP", [128, T], BF16)
    kdS = sb("kdS", [128, 2, NKS], BF16); kgS = sb("kgS", [128, NKS], BF16)
    Bkd = [Buf("kdP"), Buf("kdS_own")]; Bkg = [Buf("kgP"), Buf("kgS_own")]
    Bkctx = Buf("kctx"); Bkgath = Buf("kgath")
    vaP = sb("vaP", [128, 4, 6, 128], BF16)
    vaSg = sb("vaSg", [128, 18, 2, 128], BF16)
    vaSd = hmid1[:].rearrange("p f t -> p (f t)")[:, 0:18 * 512].rearrange("p (k h d) -> p k h d", h=4, d=128)
    BvaSd = Bhm1[0:18]
    BvaP = Buf("vaP"); Bvgc = Buf("vgctx"); Bvgg = Buf("vggath")
    vtok = cat1.rearrange("p a t -> p (a t)")[:, 0:1536].rearrange("p (t v) -> p t v", v=384)
    edge = sb("edge", [128, 2, 32], BF16); Bedge = Buf("edge")
    egt = sb("egt", [128, 2, 4, 32], BF16); Begt = Buf("egt")
    halo = sb("halo", [128, 2, 2, 16]); Bhalo = Buf("halo")
    pT = [sb("pT%d" % k, [128, T], BF16) for k in range(2)]; BpT = [Buf("pT%d" % k) for k in range(2)]
    sqr = pT[0:2]; Bsqr = BpT[0:2]
    pT = pT + [hmid1[:, 18, :], hmid1[:, 19, :]]; BpT = BpT + [Bhm1[18], Bhm1[19]]
    vb = sb("vb", [128, 2, T], BF16); Bvb = Buf("vb")
    qbf = sb("qbf", [128, T], BF16); Bqbf = Buf("qbf")
    qbc = sb("qbc", [128, T], BF16); Bqbc = Buf("qbc")
    ktm = sb("ktm", [128, 8]); Bktm = Buf("ktm")

    sems = {}
    for e in Sched.ENGS:
        sems[(e, "c")] = es.enter_context(nc.semaphore("sc_" + e))
    for e in ("sp", "pool", "act"):
        for k in range(NDS):
            sems[(e, k)] = es.enter_context(nc.semaphore("sd_%s%d" % (e, k)))
    for k in range(DEPTH):
        sems[("cc", k)] = es.enter_context(nc.semaphore("scc%d" % k))

    def mm(outb, out, lb, lhsT, rb, rhs, start, stop):
        S.add("pe", lambda e: e.matmul(out, lhsT, rhs, start=start, stop=stop), reads=[lb, rb], writes=[outb])

    def act(out, in_, func, reads, writes, scale=None, bias=None):
        kw = {}
        if scale is not None:
            kw["scale"] = scale
        if bias is not None:
            kw["bias"] = bias
        S.add("act", lambda e: e.activation(out, in_, func, **kw), reads=reads, writes=writes)

    def tt(out, a, b, op, reads, writes, eng="dve"):
        S.add(eng, lambda e: e.tensor_tensor(out, a, b, op), reads=reads, writes=writes)

    def ts(out, a, s1, op0, reads, writes, eng="dve"):
        S.add(eng, lambda e: e.tensor_scalar(out, a, s1, None, op0), reads=reads, writes=writes)

    def stt(out, a, s, b, op0, op1, reads, writes):
        S.add("dve", lambda e: e.scalar_tensor_tensor(out, a, s, b, op0, op1), reads=reads, writes=writes)

    def cp(out, in_, reads, writes, eng="dve"):
        S.add(eng, lambda e: e.tensor_copy(out, in_), reads=reads, writes=writes)

    def dma(q, out, in_, reads, writes):
        S.add(q, lambda e: e.dma_start(out=out, in_=in_), reads=reads, writes=writes, kind="d")

    def memset(ap, val, writes, eng="dve"):
        S.add(eng, lambda e: e.memset(ap, val), reads=[], writes=writes)

    epsb = sb("epsb", [128, 1]); Bepsb = Buf("epsb")
    memset(epsb[:], EPS, [Bepsb])

    def rsqrt_chain(out, Bout, in_, mult, reads_in):
        np_ = out.shape[0]
        act(out, in_, AF.Ln, reads_in + [Bepsb], [Bout], scale=mult, bias=epsb[0:np_, 0:1])
        act(out, out, AF.Exp, [Bout], [Bout], scale=-0.5)

    wfree_list = list(range(NW))

    def wload(src_ap, nel=4096):
        k = wfree_list.pop(0)
        dma("pool", wsl[k][:, 0:nel].rearrange("p (a n) -> p a n", a=4), src_ap[:, 0:nel].rearrange("p (a n) -> p a n", a=4), [], [Bws[k]])
        return k

    def wfree(k):
        wfree_list.append(k)

    for g in range(2):
        for c in range(8):
            dma("sp", xt[g][:, c, :], d_x[g][:, c, :], [], [Bx[g][c]])
    dma("sp", cnd[:], d_cnd[:], [], [Bcnd])
    dma("sp", rope[:].rearrange("p a t -> p (a t)"), d_rope[:], [], [Brope])
    dma("sp", cmatf[:].rearrange("p a t -> p (a t)"), d_cmat[:], [], [Bcmat])
    dma("sp", msk[:], d_msk[:], [], [Bmsk])
    cp(cmat[:], cmatf[:], [Bcmat], [Bcmat])
    ones_b = cmat[:, 0, :]
    blk64 = cmat[:, 1, :]
    ident = cmat[:, 2, :]
    permd = cmat[:, 3, :]
    permg = cmat[:, 4, :]
    ones_f = cmatf[:, 0, :]
    act(scnd[:], cnd[:], AF.Silu, [Bcnd], [Bscnd])
    memset(vaP[:, :, :, 64:128], 1.0, [BvaP], eng="pool")
    memset(vaSg[:, :, :, 64:128], 1.0, [Bvgc, Bvgg], eng="pool")
    memset(ppad[0][:], 0.0, [Bpp[0]], eng="pool")
    memset(upad[0][:], 0.0, [Bup[0]], eng="pool")

    bankrr = [0]

    def nb():
        b = bankrr[0] % 4
        bankrr[0] += 1
        return b

    def sumsq_pre(g):
        for c in range(8):
            k = c % 2
            act(sqr[k][:], xt[g][:, c, :], AF.Square, [Bx[g][c]], [Bsqr[k]])
            mm(Bps[6], ps[6][:], Bcmat, ones_b, Bsqr[k], sqr[k][:], c == 0, c == 7)

    def pre_norm_gen(g, i, l=None, hb_=None):
        l = cur[0] if l is None else l
        hdst, Bhd = (h1, Bh1) if hb_ is None else hb_
        for c in range(8):
            k = c % 2
            act(sqr[k][:], xt[g][:, c, :], AF.Square, [Bx[g][c]], [Bsqr[k]])
            if c > 0:
                mm(Bps[6], ps[6][:], Bcmat, ones_b, Bsqr[1 - k], sqr[1 - k][:], c == 1, False)
            yield
        mm(Bps[6], ps[6][:], Bcmat, ones_b, Bsqr[1], sqr[1][:], False, True)
        rstd, Brstd = rstd_t, Brstd_t
        rsqrt_chain(rstd[:], Brstd, ps[6][:], 1.0 / D, [Bps[6]])
        for c in range(8):
            t_, Bt = nt()
            tt(t_[:], xt[g][:, c, :], rstd[:], ALU.mult, [Bx[g][c], Brstd], [Bt])
            act(hdst[:, c, :], t_[:], AF.Identity, [Bt, BAG_l[l][i], Bmod_l[l][i]], [Bhd[c]],
                scale=Amod_l[l][:, g, i, c:c + 1], bias=modt_l[l][:, (3 * i) * 8 + c, g:g + 1])

    def pre_norm(g, i, l=None, hb_=None):
        for _ in pre_norm_gen(g, i, l, hb_):
            pass

    def evac_y(g, oc, bank):
        act(yt[g][:, oc, :], ps[bank][:], AF.Copy, [Bps[bank]], [By[g][oc]])
        k = oc % 2
        act(sqr[k][:], ps[bank][:], AF.Square, [Bps[bank]], [Bsqr[k]])
        pend_sq.append((oc, k))

    pend_sq = []

    def flush_sq(everything=False):
        while pend_sq and (everything or len(pend_sq) > 0):
            oc, k = pend_sq.pop(0)
            mm(Bps[6], ps[6][:], Bcmat, ones_b, Bsqr[k], sqr[k][:], oc == 0, oc == 7)

    def post_norm(g, i, l=None):
        l = cur[0] if l is None else l
        rstd, Brstd = rstd_t, Brstd_t
        rsqrt_chain(rstd[:], Brstd, ps[6][:], 1.0 / D, [Bps[6]])
        for c in range(8):
            t_, Bt = nt()
            stt(t_[:], yt[g][:, c, :], Gmod_l[l][:, g, i, c:c + 1], rstd[:], ALU.mult, ALU.mult, [By[g][c], BGG_l[l][i], Brstd], [Bt])
            tt(xt[g][:, c, :], xt[g][:, c, :], t_[:], ALU.add, [Bx[g][c], Bt], [Bx[g][c]])

    def ffn_chain(jobs, hook=None):
        hsel = lambda g: (h1, Bh1) if g == 0 else (hx, Bhx)
        l0, w0, i0, g0 = jobs[0]
        pre_norm(g0, i0, l0, hsel(g0))
        for jn, (l, which, i, g) in enumerate(jobs):
            hsrc, Bhs = hsel(g)
            cnt = 0
            png = None
            if jn + 1 < len(jobs):
                ln, wn, in_, gn = jobs[jn + 1]
                png = pre_norm_gen(gn, in_, ln, hsel(gn))
            for f in range(11):
                k = wload(d_wgu[l, which, f])
                w = wsl[k][:].rearrange("p (a c n) -> p a c n", a=2, c=8)
                for j in range(2):
                    fc = 2 * f + j
                    bg, bu = (cnt % 2) * 2, (cnt % 2) * 2 + 1
                    cnt += 1
                    for c in range(8):
                        mm(Bps[bg], ps[bg][:], Bws[k], w[:, 0, c, j * 128:(j + 1) * 128], Bhs[c], hsrc[:, c, :], c == 0, c == 7)
                    for c in range(8):
                        mm(Bps[bu], ps[bu][:], Bws[k], w[:, 1, c, j * 128:(j + 1) * 128], Bhs[c], hsrc[:, c, :], c == 0, c == 7)
                    sk = cnt % 2
                    act(qmr[sk][:], ps[bg][:], AF.Silu, [Bps[bg]], [Bqmr[sk]])
                    tt(hmid1[:, fc, :], qmr[sk][:], ps[bu][:], ALU.mult, [Bqmr[sk], Bps[bu]], [Bhm1[fc]])
                wfree(k)
                if hook is not None:
                    hook()
                if png is not None and f >= 2:
                    next(png, None)
            if png is not None:
                for _ in png:
                    pass
            for oc in range(8):
                k = wload(d_wdn[l, which, oc], NFC * 128)
                w = wsl[k][:, 0:NFC * 128].rearrange("p (f n) -> p f n", f=NFC)
                bank = 4 + (oc % 2)
                for fc in range(NFC):
                    mm(Bps[bank], ps[bank][:], Bws[k], w[:, fc, :], Bhm1[fc], hmid1[:, fc, :], fc == 0, fc == NFC - 1)
                wfree(k)
                flush_sq()
                evac_y(g, oc, bank)
            flush_sq(True)
            post_norm(g, i, l)

    def ffn(l, which, i, g, hook=None):
        pre_norm(g, i)
        cnt = 0
        for f in range(11):
            k = wload(d_wgu[l, which, f])
            w = wsl[k][:].rearrange("p (a c n) -> p a c n", a=2, c=8)
            for j in range(2):
                fc = 2 * f + j
                bg, bu = (cnt % 2) * 2, (cnt % 2) * 2 + 1
                cnt += 1
                for c in range(8):
                    mm(Bps[bg], ps[bg][:], Bws[k], w[:, 0, c, j * 128:(j + 1) * 128], Bh[g][c], ht[g][:, c, :], c == 0, c == 7)
                for c in range(8):
                    mm(Bps[bu], ps[bu][:], Bws[k], w[:, 1, c, j * 128:(j + 1) * 128], Bh[g][c], ht[g][:, c, :], c == 0, c == 7)
                sk = cnt % 2
                act(qmr[sk][:], ps[bg][:], AF.Silu, [Bps[bg]], [Bqmr[sk]])
                tt(hmid1[:, fc, :], qmr[sk][:], ps[bu][:], ALU.mult, [Bqmr[sk], Bps[bu]], [Bhm1[fc]])
            wfree(k)
            if hook is not None:
                hook()
        for oc in range(8):
            k = wload(d_wdn[l, which, oc], NFC * 128)
            w = wsl[k][:, 0:NFC * 128].rearrange("p (f n) -> p f n", f=NFC)
            bank = 4 + (oc % 2)
            for fc in range(NFC):
                mm(Bps[bank], ps[bank][:], Bws[k], w[:, fc, :], Bhm1[fc], hmid1[:, fc, :], fc == 0, fc == NFC - 1)
            wfree(k)
            evac_y(g, oc, bank)
        post_norm(g, i)

    def ada_gen(l):
        bada, modt, gpre, gpost, Amod, Gmod = bada_l[l], modt_l[l], gpre_l[l], gpost_l[l], Amod_l[l], Gmod_l[l]
        Bbada, Bmod, Bgp, BAG = Bbada_l[l], Bmod_l[l], Bgp_l[l], BAG_l[l]
        dma("sp", bada[:], d_bada[l], [], [Bbada])
        dma("sp", gpre[:].rearrange("p a c -> p (a c)"), d_gpre[l], [], [Bgp])
        dma("sp", gpost[:].rearrange("p a c -> p (a c)"), d_gpost[l], [], [Bgp])
        for pa in range(18):
            k = wload(d_wada[l, pa])
            w = wsl[k][:].rearrange("p (c n) -> p c n", c=8)
            for cc in range(4):
                ci = pa * 4 + cc
                for c in range(8):
                    mm(Bps[7], ps[7][:, 2 * ci:2 * ci + 2], Bws[k], w[:, c, cc * 128:(cc + 1) * 128], Bscnd, scnd[:, c, :], c == 0, c == 7)
            wfree(k)
            if pa % 6 == 3:
                i = pa // 6
                lo, hi = (3 * i) * 16, (3 * i + 2) * 16
                tt(modt[:].rearrange("p a k -> p (a k)")[:, lo:hi], ps[7][:, lo:hi], bada[:, lo:hi], ALU.add, [Bps[7], Bbada], [Bmod[i]])
                for g in range(2):
                    stt(Amod[:, g, i, :], modt[:, (3 * i + 1) * 8:(3 * i + 2) * 8, g], 1.0, gpre[:, i, :], ALU.add, ALU.mult,
                        [Bmod[i], Bgp], [BAG[i]])
            if pa % 6 == 5:
                i = pa // 6
                lo, hi = (3 * i + 2) * 16, (3 * i + 3) * 16
                tt(modt[:].rearrange("p a k -> p (a k)")[:, lo:hi], ps[7][:, lo:hi], bada[:, lo:hi], ALU.add, [Bps[7], Bbada], [Bmodg_l[l][i]])
                for g in range(2):
                    stt(Gmod[:, g, i, :], modt[:, (3 * i + 2) * 8:(3 * i + 3) * 8, g], 1.0 if i == 1 else 0.5, gpost[:, i, :],
                        ALU.mult, ALU.mult, [Bmodg_l[l][i], Bgp], [BGG_l[l][i]])
            if pa < 17:
                yield

    def layer_consts(l):
        dma("sp", sconvw[:].rearrange("p a c -> p (a c)"), d_sconv[l], [], [Bsmall])
        dma("sp", ccw[:].rearrange("p a c -> p (a c)"), d_ccw[l], [], [Bsmall])
        dma("sp", ccv[:].rearrange("p a c -> p (a c)"), d_ccv[l], [], [Bsmall])
        dma("sp", subln[:], d_subln[l], [], [Bsmall])
        dma("sp", qkn[:], d_qkn[l], [], [Bsmall])
        dma("sp", kng[:], d_kng[l], [], [Bsmall])
        dma("sp", lamv[:].rearrange("p a c -> p (a c)"), d_lam[l], [], [Blam])
        dma("pool", ccpw[:].rearrange("p a c -> p (a c)"), d_ccpw[l], [], [Bccpw])
        dma("pool", kdS[:, :, 0:256], d_ckd[l].rearrange("p (c t) -> p c t", c=2), [], [Bkctx])
        dma("pool", kgS[:, 0:256], d_ckg[l], [], [Bkctx])
        dma("pool", vaSg[:, 0:2, :, 0:64], d_cvg[l].rearrange("p (c h d) -> p c h d", c=2, h=2), [], [Bvgc])
        lam_init = 0.8 - 0.6 * math.exp(-0.3 * l)
        tt(lamv[:, 0, :], lamv[:, 0, :], lamv[:, 1, :], ALU.mult, [Blam], [Blam])
        tt(lamv[:, 2, :], lamv[:, 2, :], lamv[:, 3, :], ALU.mult, [Blam], [Blam])
        S.add("dve", lambda e: e.tensor_reduce(lamt[:, 2:3], lamv[:, 0, :], mybir.AxisListType.X, ALU.add), reads=[Blam], writes=[Blam])
        S.add("dve", lambda e: e.tensor_reduce(lamt[:, 3:4], lamv[:, 2, :], mybir.AxisListType.X, ALU.add), reads=[Blam], writes=[Blam])
        act(lamt[:, 4:6], lamt[:, 2:4], AF.Exp, [Blam], [Blam])
        tt(lamt[:, 6:7], lamt[:, 5:6], lamt[:, 4:5], ALU.subtract, [Blam], [Blam])
        ts(lamt[:, 0:1], lamt[:, 6:7], -lam_init, ALU.add, [Blam], [Blam])
        ts(lamt[:, 1:2], subln[:], 1.0 - lam_init, ALU.mult, [Bsmall, Blam], [Blam])

    def ctx_v_diff(l):
        memset(vaSd[:, :, :, 64:128], 1.0, BvaSd, eng="pool")
        dma("pool", vaSd[:, 0:2, :, 0:64], d_cvd[l].rearrange("p (c h d) -> p c h d", c=2, h=4), [], BvaSd)

    dq = []

    def run_dq():
        todo = dq[:]
        del dq[:]
        for f_ in todo:
            f_()

    def proj_fm(g, k, col, bank):
        w = wsl[k][:].rearrange("p (c n) -> p c n", c=8)
        for c in range(8):
            mm(Bps[bank], ps[bank][:], Bws[k], w[:, c, col:col + 128], Bh[g][c], ht[g][:, c, :], c == 0, c == 7)
        run_dq()

    def proj_tm(g, k, col, ncol, tile_, bank):
        w = wsl[k][:].rearrange("p (c n) -> p c n", c=8)
        for c in range(8):
            mm(Bps[bank], ps[bank][:, 0:ncol], Bh[g][c], ht[g][:, c, tile_ * 128:(tile_ + 1) * 128], Bws[k], w[:, c, col:col + ncol], c == 0, c == 7)
        run_dq()

    def rope_apply(src, src_reads, kind, dst, Bdst):
        ci, si, pm = (0, 1, permd) if kind == "d" else (2, 3, permg)
        act(qbc[:], src, AF.Copy, src_reads, [Bqbc])

        def tail():
            mm(Bps[6], ps[6][:], Bcmat, pm, Bqbc, qbc[:], True, True)
            t1, B1 = nt()
            t2, B2 = nt()
            tt(t1[:], src, rope[:, ci, :], ALU.mult, src_reads + [Brope], [B1])
            tt(t2[:], ps[6][:], rope[:, si, :], ALU.mult, [Bps[6], Brope], [B2])
            tt(dst, t1[:], t2[:], ALU.add, [B1, B2], [Bdst])
        dq.append(tail)

    def headnorm(bank, col, then):
        act(qbf[:], ps[bank][:], AF.Square, [Bps[bank]], [Bqbf])

        def tail():
            mm(Bps[7], ps[7][:], Bcmat, blk64, Bqbf, qbf[:], True, True)
            r_, Br = nt()
            rsqrt_chain(r_[:], Br, ps[7][:], 1.0 / 64, [Bps[7]])
            o_, Bo = nt()
            stt(o_[:], ps[bank][:], qkn[:, col:col + 1], r_[:], ALU.mult, ALU.mult, [Bps[bank], Bsmall, Br], [Bo])
            then(o_[:], [Bo])
        dq.append(tail)

    def mix_in(l, g):
        for piece in range(5):
            if piece > 0:
                wfree(k)
            k = wload(d_wmi[l, piece])
            if piece == 0:
                for c in range(2):
                    b = nb(); proj_fm(g, k, c * 128, b)
                    ac, Bac = nt()
                    act(ac[:], ps[b][:], AF.Copy, [Bps[b]], [Bac])
                    b2 = nb(); proj_fm(g, k, 256 + c * 128, b2)
                    if g == 0:
                        tt(ppad[0][:, c, :, 1:257], ac[:].rearrange("p (s t) -> p s t", s=2), ps[b2][:].rearrange("p (s t) -> p s t", s=2),
                           ALU.mult, [Bac, Bps[b2]], [Bpp[0]])
                    else:
                        tt(ppad[1][:, c, 1:513], ac[:], ps[b2][:], ALU.mult, [Bac, Bps[b2]], [Bpp[1]])
            elif piece == 1:
                for c in range(2):
                    b = nb(); proj_fm(g, k, c * 128, b)
                    act(abt[g][:, c, :], ps[b][:], AF.Copy, [Bps[b]], Bab[g])
                for c in range(2):
                    b = nb(); proj_fm(g, k, 256 + c * 128, b)
                    if g == 0:
                        act(qpk[g][:, c, :], ps[b][:], AF.Copy, [Bps[b]], [Bqpk[g][c]])
                    else:
                        rope_apply(ps[b][:], [Bps[b]], "d", qpk[g][:, c, :], Bqpk[g][c])
            elif piece == 2:
                for c in range(2):
                    b = nb(); proj_fm(g, k, c * 128, b)
                    if g == 0:
                        act(kdP[:, c, :], ps[b][:], AF.Copy, [Bps[b]], [Bkd[0]])
                    else:
                        rope_apply(ps[b][:], [Bps[b]], "d", kdS[:, c, 256:256 + T], Bkd[1])
                for t_ in range(4):
                    b = nb()
                    rows = slice(t_ * 128, (t_ + 1) * 128)
                    if g == 0:
                        proj_tm(g, k, 0, 512, t_, b)
                        st, Bst = nt()
                        act(st[:], ps[b][:], AF.Copy, [Bps[b]], [Bst])
                        cp(vaP[:, t_, 0:4, 0:64], ps[b][:, 256:512].rearrange("p (h d) -> p h d", h=4), [Bps[b]], [BvaP])
                        dma("sp", d_ndk[l, rows, :], st[:, 0:256], [Bst], [])
                        dma("sp", d_ndv[l, rows, :], st[:, 256:512], [Bst], [])
                    else:
                        proj_tm(g, k, 256, 256, t_, b)
                        act(vtok[:, t_, 0:256], ps[b][:, 0:256], AF.Copy, [Bps[b]], Bcat1)
            elif piece == 3:
                for c in range(2):
                    b = nb(); proj_fm(g, k, 256 + c * 128, b)
                    sg, Bsg = nt()
                    act(sg[:], ps[b][:], AF.Sigmoid, [Bps[b]], [Bsg])
                    b2 = nb(); proj_fm(g, k, c * 128, b2)
                    if g == 0:
                        tt(upad[0][:, c, :, 15:271], sg[:].rearrange("p (s t) -> p s t", s=2),
                           ps[b2][:].rearrange("p (s t) -> p s t", s=2), ALU.mult, [Bsg, Bps[b2]], [Bup[0]])
                    else:
                        tt(upad[1][:, c, 15:527], sg[:], ps[b2][:], ALU.mult, [Bsg, Bps[b2]], [Bup[1]])
            else:
                for c in range(2):
                    b = nb(); proj_fm(g, k, c * 128, b)
                    if g == 0:
                        headnorm(b, 0, lambda src, Bsrc, c=c: cp(qpk[0][:, 2 + c, :], src, Bsrc, [Bqpk[0][2 + c]]))
                    else:
                        headnorm(b, 0, lambda src, Bsrc, c=c: rope_apply(src, Bsrc, "g", qpk[1][:, 2 + c, :], Bqpk[1][2 + c]))
                b = nb(); proj_fm(g, k, 256, b)
                if g == 0:
                    headnorm(b, 1, lambda src, Bsrc: cp(kgP[:], src, Bsrc, [Bkg[0]]))
                else:
                    headnorm(b, 1, lambda src, Bsrc: rope_apply(src, Bsrc, "g", kgS[:, 256:256 + T], Bkg[1]))
                for t_ in range(4):
                    b = nb()
                    rows = slice(t_ * 128, (t_ + 1) * 128)
                    if g == 0:
                        proj_tm(g, k, 256, 256, t_, b)
                        st, Bst = nt()
                        act(st[:, 128:256], ps[b][:, 128:256], AF.Copy, [Bps[b]], [Bst])
                        cp(vaP[:, t_, 4:6, 0:64], ps[b][:, 128:256].rearrange("p (h d) -> p h d", h=2), [Bps[b]], [BvaP])
                        act(st[:, 256:384], ps[b][:, 0:128], AF.Square, [Bps[b]], [Bst])
                        S.add("dve", lambda e, st=st: e.tensor_reduce(ktm[:, 0:2], st[:, 256:384].rearrange("p (h d) -> p h d", h=2),
                                                                      mybir.AxisListType.X, ALU.add), reads=[Bst], writes=[Bktm])
                        rsqrt_chain(ktm[:, 2:4], Bktm, ktm[:, 0:2], 1.0 / 64, [Bktm])
                        for h in range(2):
                            stt(st[:, h * 64:(h + 1) * 64], ps[b][:, h * 64:(h + 1) * 64], ktm[:, 2 + h:3 + h], kng[:],
                                ALU.mult, ALU.mult, [Bps[b], Bktm, Bsmall], [Bst])
                        dma("sp", d_ngk[l, rows, :], st[:, 0:128], [Bst], [])
                        dma("sp", d_ngv[l, rows, :], st[:, 128:256], [Bst], [])
                    else:
                        proj_tm(g, k, 384, 128, t_, b)
                        act(vtok[:, t_, 256:384], ps[b][:, 0:128], AF.Copy, [Bps[b]], Bcat1)
            if substop is not None and piece == substop:
                raise _Stop()
        wfree(k)
        while dq:
            run_dq()

    def exchange(l):
        for c in range(2):
            cp(edge[:, c, 0:15], upad[1][:, c, 15:30], [Bup[1]], [Bedge])
            cp(edge[:, c, 15:16], ppad[1][:, c, 1:2], [Bpp[1]], [Bedge])
            cp(edge[:, c, 16:31], upad[1][:, c, 512:527], [Bup[1]], [Bedge])
            cp(edge[:, c, 31:32], ppad[1][:, c, 512:513], [Bpp[1]], [Bedge])
        bo = d_bounce[l]
        dma("sp", bo[0:256, :].rearrange("(c p) t -> p c t", p=128), kdS[:, :, 256:256 + T], [Bkd[1]], [B_bounce[l]])
        dma("sp", bo[256:384, :], kgS[:, 256:256 + T], [Bkg[1]], [B_bounce[l]])
        vview = bo[384:768, :].rearrange("r c -> (r c)").rearrange("(tt p v) -> p tt v", p=128, v=384)
        dma("sp", vview, vtok, Bcat1, [B_bounce[l]])
        eview = bo[768:784, :].rearrange("r c -> (r c)").rearrange("(c p e) -> p c e", p=128, e=32)
        dma("sp", eview, edge[:], [Bedge], [B_bounce[l]])
        S.add("pool", lambda e: e.collective_compute("AllGather", ALU.bypass, replica_groups=[[0, 1, 2, 3], [4, 5, 6, 7]],
                                                     ins=[bo.opt()], outs=[d_gath[l].opt()]),
              reads=[B_bounce[l]], writes=[B_gath[l]], kind="cc")
    def exchange_recv(l):
        ga = d_gath[l]
        for r in range(4):
            base = r * BROWS
            ks = 256 + r * T
            dma("sp", kdS[:, :, ks:ks + T], ga[base:base + 256, :].rearrange("(c p) t -> p c t", p=128), [B_gath[l]], [Bkgath])
            dma("sp", kgS[:, ks:ks + T], ga[base + 256:base + 384, :], [B_gath[l]], [Bkgath])
            vv = ga[base + 384:base + 768, :].rearrange("r c -> (r c)").rearrange("(tt p v) -> p tt v", p=128, v=384)
            for t_ in range(4):
                ch = 2 + r * 4 + t_
                dma("sp", vaSd[:, ch, :, 0:64], vv[:, t_, 0:256].rearrange("p (h d) -> p h d", h=4), [B_gath[l]], BvaSd)
                dma("sp", vaSg[:, ch, :, 0:64], vv[:, t_, 256:384].rearrange("p (h d) -> p h d", h=2), [B_gath[l]], [Bvgg])
            ev = ga[base + 768:base + 784, :].rearrange("r c -> (r c)").rearrange("(c p e) -> p c e", p=128, e=32)
            dma("sp", egt[:, :, r, :], ev, [B_gath[l]], [Begt])

    def halos():
        for side, lo, mcol in ((0, 16, 8), (1, 0, 12)):
            ts(halo[:, :, side, :], egt[:, :, 0, lo:lo + 16], msk[:, mcol:mcol + 1], ALU.mult, [Begt, Bmsk], [Bhalo])
            for r in range(1, 4):
                stt(halo[:, :, side, :], egt[:, :, r, lo:lo + 16], msk[:, mcol + r:mcol + r + 1], halo[:, :, side, :], ALU.mult, ALU.add,
                    [Begt, Bmsk, Bhalo], [Bhalo])
        cp(upad[1][:, :, 0:15], halo[:, :, 0, 0:15], [Bhalo], [Bup[1]])
        cp(ppad[1][:, :, 0:1], halo[:, :, 0, 15:16], [Bhalo], [Bpp[1]])
        cp(upad[1][:, :, 527:542], halo[:, :, 1, 0:15], [Bhalo], [Bup[1]])
        cp(ppad[1][:, :, 513:514], halo[:, :, 1, 15:16], [Bhalo], [Bpp[1]])

    def attention(g, hook=None, premasked=False):
        nq, nkc = 512, (2 if g == 0 else 18)
        items = [(0, hm) for hm in range(12)]
        state = {"on_prev": None}

        def qinfo(hm):
            if hm < 8:
                return qpk[g][:, hm // 4, :], Bqpk[g][hm // 4], hm % 4, hm // 2, 32 ** -0.5
            sl = hm - 8
            return qpk[g][:, 2 + sl // 2, :], Bqpk[g][2 + sl // 2], 4 + sl % 2, 4 + sl % 2, 64 ** -0.5

        def emit_mask(idx):
            s_, hm = items[idx]
            qsrc, Bq, mcol, h, scale = qinfo(hm)
            qk = idx % 2
            ts(qmr[qk][:], qsrc, msk[:, mcol:mcol + 1], ALU.mult, [Bq, Bmsk], [Bqmr[qk]])

        def kv(s_, hm, h, kc):
            if g == 0:
                ksl = slice(s_ * 256 + kc * 128, s_ * 256 + (kc + 1) * 128)
                kT = kdP[:, hm // 4, ksl] if hm < 8 else kgP[:, ksl]
                Bk = [Bkd[0]] if hm < 8 else [Bkg[0]]
                return kT, Bk, vaP[:, s_ * 2 + kc, h, :], [BvaP]
            ksl = slice(kc * 128, (kc + 1) * 128)
            kT = kdS[:, hm // 4, ksl] if hm < 8 else kgS[:, ksl]
            Bk = [Bkctx] if kc < 2 else [Bkgath]
            if h < 4:
                return kT, Bk, vaSd[:, kc, h, :], BvaSd
            return kT, Bk, vaSg[:, kc, h - 4, :], [Bvgc if kc < 2 else Bvgg]

        def finalize(idx):
            s_, hm = items[idx]
            qs = slice(s_ * nq, (s_ + 1) * nq)
            qsrc, Bq, mcol, h, scale = qinfo(hm)
            ob = 4 + idx % 2
            r_, Br = nt()
            cp(r_[0:64, 0:nq], ps[ob][64:128, 0:nq], [Bps[ob]], [Br])
            act(r_[0:64, 0:nq], r_[0:64, 0:nq], AF.Ln, [Br], [Br])
            act(r_[0:64, 0:nq], r_[0:64, 0:nq], AF.Exp, [Br], [Br], scale=-1.0)
            if hm < 8:
                on, Bon = nt()
                tt(on[0:64, 0:nq], ps[ob][0:64, 0:nq], r_[0:64, 0:nq], ALU.mult, [Bps[ob], Br], [Bon])
                if hm % 2 == 0:
                    state["on_prev"] = (on, Bon)
                else:
                    on0, Bon0 = state["on_prev"]
                    d_, Bd = nt()
                    stt(d_[0:64, 0:nq], on[0:64, 0:nq], lamt[0:64, 0:1], on0[0:64, 0:nq], ALU.mult, ALU.add, [Bon0, Bon, Blam], [Bd])
                    act(qbf[0:64, 0:nq], d_[0:64, 0:nq], AF.Square, [Bd], [Bqbf])

                    def cont(d_=d_, Bd=Bd, h=h, qs=qs):
                        mm(Bps[6], ps[6][0:64, 0:nq], Bcmat, ones_b[0:64, 0:64], Bqbf, qbf[0:64, 0:nq], True, True)
                        rr, Brr = nt()
                        rsqrt_chain(rr[0:64, 0:nq], Brr, ps[6][0:64, 0:nq], 1.0 / 64, [Bps[6]])
                        stt(hb[g][:, h, qs], d_[0:64, 0:nq], lamt[0:64, 1:2], rr[0:64, 0:nq], ALU.mult, ALU.mult, [Bd, Blam, Brr], [Bhb[g][h]])
                    return cont
            else:
                sl = hm - 8
                tt(hb[g][:, 4 + sl, qs], ps[ob][0:64, 0:nq], r_[0:64, 0:nq], ALU.mult, [Bps[ob], Br], [Bhb[g][4 + sl]])
            return None

        contB = [None]

        def fin_step(idx):
            if contB[0] is not None:
                contB[0]()
                contB[0] = None
            if idx >= 0:
                contB[0] = finalize(idx)

        NPT = len(pT)
        pairc = [0]
        stepc = [0]
        if not premasked:
            emit_mask(0)
        for idx, (s_, hm) in enumerate(items):
            qs = slice(s_ * nq, (s_ + 1) * nq)
            qsrc, Bq, mcol, h, scale = qinfo(hm)
            qk = idx % 2
            ob = 4 + idx % 2
            if idx + 1 < len(items):
                emit_mask(idx + 1)
            pend = []
            if g == 0:
                pks = []
                for kc in range(2):
                    sbk = stepc[0] % 4
                    pk = stepc[0] % NPT
                    stepc[0] += 1
                    pks.append(pk)
                    for s2 in range(2):
                        kT, Bk, va, Bv = kv(s2, hm, h, kc)
                        cs = slice(s2 * 256, (s2 + 1) * 256)
                        S.add("pe", lambda e, o=ps[sbk][:, cs], a=kT, b_=qmr[qk][:, cs]: e.matmul(o, a, b_, start=True, stop=True),
                              reads=Bk + [Bqmr[qk]], writes=[Bps[sbk]])
                    act(pT[pk][:], ps[sbk][:], AF.Exp, [Bps[sbk]], [BpT[pk]], scale=scale)
                fin_step(idx - 1)
                for s2 in range(2):
                    cs = slice(s2 * 256, (s2 + 1) * 256)
                    for kc in range(2):
                        kT, Bk, va, Bv = kv(s2, hm, h, kc)
                        S.add("pe", lambda e, o=ps[ob][:, cs], a=va, b_=pT[pks[kc]][:, cs], st_=(kc == 0), sp_=(kc == 1): e.matmul(o, a, b_, start=st_, stop=sp_),
                              reads=Bv + [BpT[pks[kc]]], writes=[Bps[ob]])
                continue

            def score(kc):
                kT, Bk, va, Bv = kv(s_, hm, h, kc)
                sbk = stepc[0] % 4
                pk = stepc[0] % NPT
                stepc[0] += 1
                S.add("pe", lambda e, o=ps[sbk][:, 0:nq], a=kT, b_=qmr[qk][:, qs]: e.matmul(o, a, b_, start=True, stop=True),
                      reads=Bk + [Bqmr[qk]], writes=[Bps[sbk]])
                act(pT[pk][:, 0:nq], ps[sbk][:, 0:nq], AF.Exp, [Bps[sbk]], [BpT[pk]], scale=scale)
                pend.append((kc, pk, va, Bv))

            def pv():
                kc, pap, Bpt, va, Bv = pend.pop(0)
                st_, sp_ = (kc == 0), (kc == nkc - 1)
                S.add("pe", lambda e, o=ps[ob][:, 0:nq], a=va, b_=pap, st_=st_, sp_=sp_: e.matmul(o, a, b_, start=st_, stop=sp_),
                      reads=Bv + Bpt, writes=[Bps[ob]])

            for kp in range(nkc // 2):
                jb = pairc[0] % 2
                pairc[0] += 1
                ptile = hmid1[:, 18 + 2 * jb:20 + 2 * jb, :].rearrange("p a t -> p (a t)")
                Bpt = [Bhm1[18 + 2 * jb], Bhm1[19 + 2 * jb]]
                for u in range(2):
                    kc = 2 * kp + u
                    kT, Bk, va, Bv = kv(s_, hm, h, kc)
                    S.add("pe", lambda e, o=ps[2 * jb + u][:, 0:nq], a=kT, b_=qmr[qk][:, qs]: e.matmul(o, a, b_, start=True, stop=True),
                          reads=Bk + [Bqmr[qk]], writes=[Bps[2 * jb + u]])
                    pend.append((kc, ptile[:, u * 512:(u + 1) * 512], Bpt, va, Bv))
                act(ptile, psA[:, jb * 1024:(jb + 1) * 1024], AF.Exp, [Bps[2 * jb], Bps[2 * jb + 1]], Bpt, scale=scale)
                while len(pend) > 2:
                    pv()
            fin_step(idx - 1)
            while pend:
                pv()
            if hook is not None:
                hook()
        fin_step(len(items) - 1)
        fin_step(-1)

    def diag(wcol):
        k = dgc[0] % NDG
        dgc[0] += 1
        ts(dgr[k][:], ident, wcol, ALU.mult, [Bcmat, Bsmall], [Bdgr[k]])
        return dgr[k][:], Bdgr[k]

    def convs(g):
        nseq, n = (2, 256) if g == 0 else (1, 512)
        ucs = []
        for c in range(2):
            b = nb()
            for k in range(31):
                dg, Bdg = diag(ccw[:, c, k:k + 1])
                for s in range(nseq):
                    rhs = upad[0][:, c, s, k:k + n] if g == 0 else upad[1][:, c, k:k + n]
                    S.add("pe", lambda e, o=ps[4 + s][:, 0:n], a=dg, r=rhs, st_=(k == 0), sp_=(k == 30): e.matmul(o, a, r, start=st_, stop=sp_),
                          reads=[Bdg, Bup[g]], writes=[Bps[4 + s]])
            uc, Buc = nt()
            for s in range(nseq):
                act(uc[:, s * n:(s + 1) * n], ps[4 + s][:, 0:n], AF.Identity, [Bps[4 + s], Bsmall], [Buc], bias=ccv[:, 0, c:c + 1])
            ucq, Bucq = nt()
            act(ucq[:], uc[:], AF.Square, [Buc], [Bucq])
            ucs.append((uc, Buc, ucq, Bucq))
            for k in range(3):
                dg, Bdg = diag(sconvw[:, c, k:k + 1])
                for s in range(nseq):
                    rhs = ppad[0][:, c, s, k:k + n] if g == 0 else ppad[1][:, c, k:k + n]
                    S.add("pe", lambda e, o=ps[6 + s][:, 0:n], a=dg, r=rhs, st_=(k == 0), sp_=(k == 2): e.matmul(o, a, r, start=st_, stop=sp_),
                          reads=[Bdg, Bpp[g]], writes=[Bps[6 + s]])
            for s in range(nseq):
                tt(cat[g][:, c, s * n:(s + 1) * n], abt[g][:, c, s * n:(s + 1) * n], ps[6 + s][:, 0:n], ALU.mult, Bab[g] + [Bps[6 + s]], [Bcat[g][c]])
        b1 = nb()
        for c in range(2):
            mm(Bps[b1], ps[b1][:], Bcmat, ones_f, ucs[c][1], ucs[c][0][:], c == 0, c == 1)
        b2 = nb()
        for c in range(2):
            mm(Bps[b2], ps[b2][:], Bcmat, ones_f, ucs[c][3], ucs[c][2][:], c == 0, c == 1)
        mean, Bmean = nt()
        ts(mean[:], ps[b1][:], 1.0 / 256, ALU.mult, [Bps[b1]], [Bmean])
        var, Bvar = nt()
        tt(var[:], mean[:], mean[:], ALU.mult, [Bmean], [Bvar])
        stt(var[:], ps[b2][:], 1.0 / 256, var[:], ALU.mult, ALU.subtract, [Bps[b2], Bvar], [Bvar])
        rsqrt_chain(var[:], Bvar, var[:], 1.0, [Bvar])
        for c in range(2):
            uc, Buc, ucq, Bucq = ucs[c]
            tt(ucq[:], uc[:], mean[:], ALU.subtract, [Buc, Bmean], [Bucq])
            tt(ucq[:], ucq[:], var[:], ALU.mult, [Bucq, Bvar], [Bucq])
            act(vb[:, c, :], ucq[:], AF.Silu, [Bucq, Bsmall], [Bvb], scale=ccv[:, 1, c:c + 1], bias=ccv[:, 2, c:c + 1])
        for oc in range(2):
            b = nb()
            for c in range(2):
                mm(Bps[b], ps[b][:], Bccpw, ccpw[:, c, oc * 128:(oc + 1) * 128], Bvb, vb[:, c, :], c == 0, c == 1)
            act(cat[g][:, 2 + oc, :], ps[b][:], AF.Copy, [Bps[b]], [Bcat[g][2 + oc]])

    def mix_out(l, g):
        wk = [wload(d_wmo[l, a]) for a in range(3)]
        pieces = [(cat[g][:, 0, :], Bcat[g][0], 128), (cat[g][:, 1, :], Bcat[g][1], 128)]
        pieces += [(hb[g][:, h, :], Bhb[g][h], 64) for h in range(4)]
        pieces += [(cat[g][:, 2, :], Bcat[g][2], 128), (cat[g][:, 3, :], Bcat[g][3], 128)]
        pieces += [(hb[g][:, 4 + h, :], Bhb[g][4 + h], 64) for h in range(4)]
        for oc in range(8):
            bank = 4 + (oc % 2)
            for pi, (ap_, B_, kk) in enumerate(pieces):
                k = wk[pi // 4]
                w = wsl[k][:].rearrange("p (a n) -> p a n", a=4)
                mm(Bps[bank], ps[bank][:], Bws[k], w[0:kk, pi % 4, oc * 128:(oc + 1) * 128], B_, ap_[0:kk, :], pi == 0, pi == 11)
            flush_sq()
            evac_y(g, oc, bank)
        flush_sq(True)
        for k in wk:
            wfree(k)
        post_norm(g, 1)

    def dumpx(idx, g):
        if DEBUG:
            dma("sp", d_dbg[idx].rearrange("p (c t) -> p c t", c=8), xt[g][:], Bx[g], [])

    stage = [0]

    def chk():
        stage[0] += 1
        if stop is not None and stage[0] >= stop:
            raise _Stop()

    def mixer(l):
        cur[0] = l
        pre_norm(1, 1)
        mix_in(l, 1)
        ctx_v_diff(l)
        exchange(l)
        pre_norm(0, 1)
        mix_in(l, 0)
        attention(0)
        exchange_recv(l)
        convs(0)
        ts(qmr[0][:], qpk[1][:, 0, :], msk[:, 0:1], ALU.mult, [Bqpk[1][0], Bmsk], [Bqmr[0]])
        mix_out(l, 0)
        hook = None
        gen = None
        if l + 1 < DEPTH:
            gen = ada_gen(l + 1)
            cnt_h = [0]

            def hook():
                n = 2 if cnt_h[0] < 6 else 1
                cnt_h[0] += 1
                for _ in range(n):
                    next(gen, None)
        attention(1, hook, premasked=True)
        if gen is not None:
            for _ in gen:
                pass
        halos()
        convs(1)
        mix_out(l, 1)

    try:
        gen0 = ada_gen(0)
        for _ in range(4):
            next(gen0, None)
        layer_consts(0)
        ffn_chain([(0, 0, 0, 0), (0, 0, 0, 1)], hook=lambda: next(gen0, None))
        for _ in gen0:
            pass
        chk()
        mixer(0)
        chk()
        layer_consts(1)
        ffn_chain([(0, 1, 2, 0), (0, 1, 2, 1), (1, 0, 0, 0), (1, 0, 0, 1)])
        mixer(1)
        ffn_chain([(1, 1, 2, 0), (1, 1, 2, 1)])
    except _Stop:
        pass
    for g in range(2):
        for c in range(8):
            dma("sp", d_y[g][:, c, :], xt[g][:, c, :], [Bx[g][c]], [])

    print('sbuf remaining', nc.sbuf_bytes_remaining, flush=True)
    S.finalize()
    with nc.Block() as block:
        @block.tensor
        def _(e):
            S.replay("pe", e, sems)

        @block.scalar
        def _(e):
            S.replay("act", e, sems)

        @block.vector
        def _(e):
            S.replay("dve", e, sems)

        @block.gpsimd
        def _(e):
            S.replay("pool", e, sems)

        @block.sync
        def _(e):
            S.replay("sp", e, sems)
    es.close()
    return nc


def _fm(a):
    t = a.shape[0]
    return np.ascontiguousarray(a.T.reshape(8, 128, t).transpose(1, 0, 2))


def _unfm(a):
    t = a.shape[2]
    return np.ascontiguousarray(a.transpose(1, 0, 2).reshape(1024, t).T)


def _kpieces(w, ncols_piece):
    n = w.shape[1]
    npieces = n // ncols_piece
    a = w.reshape(8, 128, npieces, ncols_piece).transpose(2, 1, 0, 3)
    return np.ascontiguousarray(a.reshape(npieces, 128, 8 * ncols_piece))


def _vec_fm(v, nch):
    return np.ascontiguousarray(v.reshape(nch, 128).T)


def _rope_tables():
    theta = np.float32(10000.0)
    tabs = {}
    for name, dd in (("d", 32), ("g", 64)):
        nf = dd // 4
        inv = (theta ** (-(np.arange(nf, dtype=np.float32) / np.float32(nf)))).astype(np.float32)
        tabs[name] = (dd, nf, inv)
    return tabs


def _host_inputs(inp):
    f32 = np.float32
    shared = {}
    L = DEPTH
    shared["wada"] = np.stack([_kpieces(inp["w_ada"][l], 512) for l in range(L)])
    shared["gpre"] = np.stack([np.concatenate([_vec_fm(inp["norm_pre"][l, i], 8) for i in range(3)], axis=1) for l in range(L)])
    shared["gpost"] = np.stack([np.concatenate([_vec_fm(inp["norm_post"][l, i], 8) for i in range(3)], axis=1) for l in range(L)])
    wgu = np.zeros((L, 2, 11, 128, 4096), f32)
    wdn = np.zeros((L, 2, 8, 128, NFC * 128), f32)
    for l in range(L):
        for wi, (gn, un, dn) in enumerate((("ffn1_gate", "ffn1_up", "ffn1_down"), ("ffn2_gate", "ffn2_up", "ffn2_down"))):
            gp = _kpieces(inp[gn][l], 256).reshape(11, 128, 2048)
            up = _kpieces(inp[un][l], 256).reshape(11, 128, 2048)
            wgu[l, wi] = np.concatenate([gp, up], axis=2)
            wd = inp[dn][l]
            a = wd.reshape(NFC, 128, 8, 128).transpose(2, 1, 0, 3)
            wdn[l, wi] = a.reshape(8, 128, NFC * 128)
    shared["wgu"] = wgu
    shared["wdn"] = wdn
    wmi = np.zeros((L, 5, 128, 4096), f32)
    wmo = np.zeros((L, 3, 128, 4096), f32)
    for l in range(L):
        w = inp["w_mix_in"][l].copy()
        dq = w[:, 2048:2304].reshape(1024, 4, 64)
        w[:, 2048:2304] = dq[:, [0, 2, 1, 3], :].reshape(1024, 256)
        w[:, 0:768] = np.concatenate([w[:, 256:512], w[:, 512:768], w[:, 0:256]], axis=1)
        wmi[l] = _kpieces(w, 512)
        wo = inp["w_mix_out"][l]
        pieces = np.zeros((12, 128, 1024), f32)
        pieces[0] = wo[0:128]; pieces[1] = wo[128:256]
        for h in range(4):
            pieces[2 + h, 0:64] = wo[256 + h * 64:256 + (h + 1) * 64]
        pieces[6] = wo[512:640]; pieces[7] = wo[640:768]
        for slot, h in enumerate((0, 2, 1, 3)):
            pieces[8 + slot, 0:64] = wo[768 + h * 64:768 + (h + 1) * 64]
        wmo[l] = pieces.reshape(3, 4, 128, 1024).transpose(0, 2, 1, 3).reshape(3, 128, 4096)
    shared["wmi"] = wmi
    shared["wmo"] = wmo
    shared["sconvw"] = np.stack([np.ascontiguousarray(inp["sconv_w"][l].reshape(3, 2, 128).transpose(2, 1, 0)).reshape(128, 6) for l in range(L)])
    shared["ccw"] = np.stack([np.ascontiguousarray(inp["ccm_dw_w"][l].reshape(31, 2, 128).transpose(2, 1, 0)).reshape(128, 62) for l in range(L)])
    shared["ccv"] = np.stack([np.concatenate([_vec_fm(inp[n][l], 2) for n in ("ccm_dw_b", "ccm_ln_g", "ccm_ln_b")], axis=1) for l in range(L)])
    shared["ccpw"] = np.stack([np.ascontiguousarray(inp["ccm_pw"][l].reshape(2, 128, 256).transpose(1, 0, 2)).reshape(128, 512) for l in range(L)])
    shared["lamv"] = np.stack([np.tile(np.concatenate([inp[n][l] for n in ("diff_lq1", "diff_lk1", "diff_lq2", "diff_lk2")])[None, :], (128, 1)) for l in range(L)])
    shared["subln"] = np.stack([np.tile(inp["diff_subln"][l], 2).reshape(128, 1) for l in range(L)])
    shared["qkn"] = np.stack([np.stack([np.tile(inp["gqa_qnorm"][l], 2), np.tile(inp["gqa_knorm"][l], 2)], axis=1) for l in range(L)])
    shared["kng"] = np.stack([np.tile(inp["gqa_knorm"][l][None, :], (128, 1)) for l in range(L)])
    cm = np.zeros((128, 5, 128), f32)
    cm[:, 0, :] = 1.0
    cm[0:64, 1, 0:64] = 1.0; cm[64:128, 1, 64:128] = 1.0
    cm[:, 2, :] = np.eye(128, dtype=f32)
    for i in range(128):
        cm[i ^ 8, 3, i] = 1.0
        cm[i ^ 16, 4, i] = 1.0
    shared["cmat"] = cm.reshape(128, 640)
    for k in shared:
        shared[k] = np.ascontiguousarray(shared[k], dtype=f32)

    tabs = _rope_tables()
    maps = []
    for i in range(NCORES):
        b, j = i // 4, i % 4
        m = dict(shared)
        m["xp"] = _fm(inp["x_prompt"][2 * i:2 * i + 2].reshape(512, 1024))
        m["xs"] = _fm(inp["x_sample"][b, 512 * j:512 * (j + 1)])
        m["cnd"] = np.ascontiguousarray(np.stack([_vec_fm(inp["c_ctx"], 8), _vec_fm(inp["c"][b], 8)], axis=2))
        m["bada"] = np.stack([np.repeat(_vec_fm(inp["b_ada"][l], 72), 2, axis=1) for l in range(L)])
        m["ckd"] = np.stack([np.ascontiguousarray(inp["cache_diff_k"][b, l].reshape(256, 2, 128).transpose(2, 1, 0)).reshape(128, 512) for l in range(L)])
        m["ckg"] = np.stack([np.ascontiguousarray(inp["cache_gqa_k"][b, l].reshape(256, 128).T) for l in range(L)])
        m["cvd"] = np.stack([np.ascontiguousarray(inp["cache_diff_v"][b, l].reshape(2, 128, 256).transpose(1, 0, 2)).reshape(128, 512) for l in range(L)])
        m["cvg"] = np.stack([np.ascontiguousarray(inp["cache_gqa_v"][b, l].reshape(2, 128, 128).transpose(1, 0, 2)).reshape(128, 256) for l in range(L)])
        tok = np.arange(512 * j, 512 * (j + 1))
        row = (tok // 64).astype(f32)
        col = (tok % 64).astype(f32)
        rt = np.zeros((128, 4, T), f32)
        for ti, name in enumerate(("d", "g")):
            dd, nf, inv = tabs[name]
            for p in range(128):
                f = p % dd
                pos = row if f < dd // 2 else col
                ang = (pos * inv[f % nf]).astype(f32)
                sgn = -1.0 if (f % (2 * nf)) < nf else 1.0
                rt[p, 2 * ti, :] = np.cos(ang)
                rt[p, 2 * ti + 1, :] = sgn * np.sin(ang)
        m["rope"] = rt.reshape(128, 4 * T)
        mk = np.zeros((128, 16), f32)
        for q in range(4):
            mk[32 * q:32 * (q + 1), q] = 1.0
        mk[0:64, 4] = 1.0
        mk[64:128, 5] = 1.0
        if j > 0:
            mk[:, 8 + j - 1] = 1.0
        if j < 3:
            mk[:, 12 + j + 1] = 1.0
        m["msk"] = mk
        for k in m:
            m[k] = np.ascontiguousarray(m[k], dtype=f32)
        maps.append(m)
    return maps


_NC_CACHE = {}


def kernel(**inputs):
    inp = {k: np.asarray(v) for k, v in inputs.items()}
    if "nc" not in _NC_CACHE:
        _NC_CACHE["nc"] = build_program()
    nc = _NC_CACHE["nc"]
    in_maps = _host_inputs(inp)
    res = run_bass_kernel_spmd(nc, in_maps, core_ids=list(range(NCORES)))
    R = res.results
    y_prompt = np.zeros((16, 256, 1024), np.float32)
    y_sample = np.zeros((2, 2048, 1024), np.float32)
    ndk = np.zeros((16, DEPTH, 256, 4, 2, 32), np.float32)
    ndv = np.zeros((16, DEPTH, 256, 4, 64), np.float32)
    ngk = np.zeros((16, DEPTH, 256, 2, 64), np.float32)
    ngv = np.zeros((16, DEPTH, 256, 2, 64), np.float32)
    for i in range(NCORES):
        b, j = i // 4, i % 4
        y_prompt[2 * i:2 * i + 2] = _unfm(R[i]["yp"]).reshape(2, 256, 1024)
        y_sample[b, 512 * j:512 * (j + 1)] = _unfm(R[i]["ys"])
        for l in range(DEPTH):
            ndk[2 * i:2 * i + 2, l] = R[i]["ndk"][l].reshape(2, 256, 4, 2, 32)
            ndv[2 * i:2 * i + 2, l] = R[i]["ndv"][l].reshape(2, 256, 4, 64)
            ngk[2 * i:2 * i + 2, l] = R[i]["ngk"][l].reshape(2, 256, 2, 64)
            ngv[2 * i:2 * i + 2, l] = R[i]["ngv"][l].reshape(2, 256, 2, 64)
    return (y_prompt, y_sample, ndk, ndv, ngk, ngv)
```
